# Optimizing a Trainium2 kernel written in Bass

```python
import math
import jax, jax.numpy as jnp
from jax import lax
import numpy as np

D_MODEL = 1024
BATCH = 8
SEQ = 2048
DEPTH = 2
DEC_BATCH = 128
DEC_SEQ = 4
PAST_LEN = 16384
PAGE_SIZE = 128

D_RNN = D_MODEL
N_LRU_BLOCKS = 8
LRU_BLOCK = D_RNN // N_LRU_BLOCKS
CONV_LRU = 4
LRU_C = 8.0
N_HEADS = 8
N_KV = 2
HEAD_DIM = D_MODEL // N_HEADS
WINDOW = 128
N_BUCKETS = 32
MAX_EXACT = N_BUCKETS // 2
MAX_DISTANCE = 128
D_FF = 4 * D_MODEL
CONV_FF = 3
EPS = 1e-6

Q_COLS = N_HEADS * HEAD_DIM
KV_COLS = N_KV * HEAD_DIM
IN_COLS = D_RNN + Q_COLS + 2 * KV_COLS + 2 * D_MODEL

kernel_name = "hawk_swa_sink_convffn_step"


def _rmsnorm(x, g):
    x32 = x.astype(jnp.float32)
    y = x32 * lax.rsqrt(jnp.mean(x32 * x32, axis=-1, keepdims=True) + EPS) * g.astype(jnp.float32)
    return y.astype(x.dtype)


def _causal_dwconv(x, prev, w, b):
    width = w.shape[0]
    xx = jnp.concatenate([prev.astype(x.dtype), x], axis=1)
    y = lax.conv_general_dilated(xx, w[:, None, :].astype(x.dtype), window_strides=(1,), padding='VALID',
                                 dimension_numbers=('NWC', 'WIO', 'NWC'), feature_group_count=x.shape[-1])
    return y + b.astype(x.dtype), xx[:, -(width - 1):]


def _rg_lru(x, h0, wr, br, wi, bi, lam, pos0):
    B, T, _ = x.shape
    x32 = x.astype(jnp.float32)
    xb = x32.reshape(B, T, N_LRU_BLOCKS, LRU_BLOCK)
    r = jax.nn.sigmoid(jnp.einsum('btnc,ncd->btnd', xb, wr.astype(jnp.float32)).reshape(B, T, D_RNN) + br.astype(jnp.float32))
    i = jax.nn.sigmoid(jnp.einsum('btnc,ncd->btnd', xb, wi.astype(jnp.float32)).reshape(B, T, D_RNN) + bi.astype(jnp.float32))
    log_a = -LRU_C * r * jax.nn.softplus(-lam.astype(jnp.float32))
    a = jnp.exp(log_a)
    mult = jnp.sqrt(jnp.maximum(-jnp.expm1(2.0 * log_a), 0.0))
    pos = pos0 + jnp.arange(T)
    mult = jnp.where((pos == 0)[None, :, None], 1.0, mult)
    bterm = mult * (i * x32)
    bterm = jnp.concatenate([bterm[:, :1] + a[:, :1] * h0.astype(jnp.float32)[:, None], bterm[:, 1:]], axis=1)

    def combine(left, right):
        a1, b1 = left
        a2, b2 = right
        return a1 * a2, a2 * b1 + b2

    _, h = lax.associative_scan(combine, (a, bterm), axis=1)
    return h.astype(x.dtype), h[:, -1]


def _t5_bucket(d):
    n = jnp.maximum(d, 0)
    nf = jnp.maximum(n, 1).astype(jnp.float32)
    large = MAX_EXACT + (jnp.log(nf / MAX_EXACT) / math.log(MAX_DISTANCE / MAX_EXACT)
                         * (N_BUCKETS - MAX_EXACT)).astype(jnp.int32)
    large = jnp.minimum(large, N_BUCKETS - 1)
    return jnp.where(n < MAX_EXACT, n, large)


def _band_attention(qb, kb, vb, d, valid, sink, rel_bias):
    B, N, Tq, H, HD = qb.shape
    Tk = kb.shape[2]
    G = H // N_KV
    qg = qb.reshape(B, N, Tq, N_KV, G, HD)
    s = jnp.einsum('bnqkgd,bnskd->bnkgqs', qg, kb).astype(jnp.float32) * (HD ** -0.5)
    bias = rel_bias.astype(jnp.float32)[_t5_bucket(d)]
    bias = jnp.transpose(bias, (2, 0, 1)).reshape(N_KV, G, Tq, Tk)
    s = jnp.where(valid[None, :, None, None], s + bias, -jnp.inf)
    sk = sink.astype(jnp.float32).reshape(1, 1, N_KV, G, 1, 1)
    m = jnp.maximum(jnp.max(s, axis=-1, keepdims=True), sk)
    p = jnp.exp(s - m)
    p = p / (jnp.sum(p, axis=-1, keepdims=True) + jnp.exp(sk - m))
    o = jnp.einsum('bnkgqs,bnskd->bnqkgd', p.astype(vb.dtype), vb)
    return o.reshape(B, N * Tq, H * HD)


def _mixer(h, lru_h0, lru_conv_prev, k_cache, v_cache, pos0, w_in, conv_w, conv_b, wr, br, wi, bi, lam,
           w_lru_o, w_attn_o, w_out, sink, rel_bias):
    B, T, _ = h.shape
    proj = h @ w_in
    c0 = D_RNN
    c1 = c0 + Q_COLS
    c2 = c1 + KV_COLS
    c3 = c2 + KV_COLS
    xr, q, k, v, g = jnp.split(proj, [c0, c1, c2, c3], axis=-1)
    g_lru, g_attn = jnp.split(g, 2, axis=-1)
    xc, conv_state = _causal_dwconv(xr, lru_conv_prev, conv_w, conv_b)
    lru_out, h_last = _rg_lru(xc, lru_h0, wr, br, wi, bi, lam, pos0)
    q = q.reshape(B, T, N_HEADS, HEAD_DIM)
    k = k.reshape(B, T, N_KV, HEAD_DIM)
    v = v.reshape(B, T, N_KV, HEAD_DIM)
    if k_cache is None:
        nb = T // WINDOW
        qb = q.reshape(B, nb, WINDOW, N_HEADS, HEAD_DIM)

        def blocks(z):
            zpad = jnp.concatenate([jnp.zeros((B, WINDOW) + z.shape[2:], z.dtype), z], axis=1)
            prev = zpad[:, :T].reshape(B, nb, WINDOW, N_KV, HEAD_DIM)
            cur = z.reshape(B, nb, WINDOW, N_KV, HEAD_DIM)
            return jnp.concatenate([prev, cur], axis=2)

        kb, vb = blocks(k), blocks(v)
        d = (jnp.arange(WINDOW)[:, None] + WINDOW) - jnp.arange(2 * WINDOW)[None, :]
        in_band = (d >= 0) & (d < WINDOW)
        key_exists = (jnp.arange(nb)[:, None, None] > 0) | (jnp.arange(2 * WINDOW)[None, None, :] >= WINDOW)
        valid = in_band[None] & key_exists
        attn = _band_attention(qb, kb, vb, d, valid, sink, rel_bias)
        k_state, v_state = k[:, -WINDOW:], v[:, -WINDOW:]
    else:
        kk = jnp.concatenate([k_cache.astype(k.dtype), k], axis=1)
        vv = jnp.concatenate([v_cache.astype(v.dtype), v], axis=1)
        d = (jnp.arange(T)[:, None] + WINDOW) - jnp.arange(WINDOW + T)[None, :]
        valid = ((d >= 0) & (d < WINDOW))[None]
        attn = _band_attention(q[:, None], kk[:, None], vv[:, None], d, valid, sink, rel_bias)
        k_state, v_state = kk[:, -WINDOW:], vv[:, -WINDOW:]
    merged = jax.nn.sigmoid(g_lru) * (lru_out @ w_lru_o) + jax.nn.sigmoid(g_attn) * (attn @ w_attn_o)
    return merged @ w_out, h_last, conv_state, k_state, v_state


def _conv_ffn(h, prev, w_up, conv_w, conv_b, w_down):
    u = h @ w_up
    u, conv_state = _causal_dwconv(u, prev, conv_w, conv_b)
    val, gate = jnp.split(u, 2, axis=-1)
    return (jax.nn.gelu(gate, approximate=True) * val) @ w_down, conv_state


def _layer(x, lru_h0, lru_conv_prev, k_cache, v_cache, ffn_prev, pos0, l,
           norm_mix_pre, norm_mix_post, norm_ffn_pre, norm_ffn_post, w_in, conv_lru_w, conv_lru_b,
           lru_wr, lru_br, lru_wi, lru_bi, lru_lambda, w_lru_o, w_attn_o, w_out, attn_sink, rel_bias,
           w_up, ffn_conv_w, ffn_conv_b, w_down):
    m, h_last, conv_state, k_state, v_state = _mixer(
        _rmsnorm(x, norm_mix_pre[l]), lru_h0, lru_conv_prev, k_cache, v_cache, pos0,
        w_in[l], conv_lru_w[l], conv_lru_b[l], lru_wr[l], lru_br[l], lru_wi[l], lru_bi[l], lru_lambda[l],
        w_lru_o[l], w_attn_o[l], w_out[l], attn_sink[l], rel_bias)
    x = x + _rmsnorm(m, norm_mix_post[l])
    f, ffn_state = _conv_ffn(_rmsnorm(x, norm_ffn_pre[l]), ffn_prev, w_up[l], ffn_conv_w[l], ffn_conv_b[l], w_down[l])
    x = x + _rmsnorm(f, norm_ffn_post[l])
    return x, h_last, conv_state, k_state, v_state, ffn_state


def setup_inputs(seed: int = 0) -> dict:
    key = jax.random.key(seed)
    ks = jax.random.split(key, 32)
    nrm = lambda k, shape, s: s * jax.random.normal(k, shape, jnp.float32)
    u = jax.random.uniform(ks[20], (DEPTH, D_RNN), jnp.float32, 0.9, 0.999)
    sa = u ** (1.0 / LRU_C)
    return {
        "x_prompt": nrm(ks[0], (BATCH, SEQ, D_MODEL), 1.0),
        "x_sample": nrm(ks[1], (DEC_BATCH, DEC_SEQ, D_MODEL), 1.0),
        "state_lru_h": nrm(ks[2], (DEPTH, DEC_BATCH, D_RNN), 0.5),
        "state_lru_conv": nrm(ks[3], (DEPTH, DEC_BATCH, CONV_LRU - 1, D_RNN), 1.0),
        "cache_win_k": nrm(ks[4], (DEPTH, DEC_BATCH, WINDOW, N_KV, HEAD_DIM), 1.0),
        "cache_win_v": nrm(ks[5], (DEPTH, DEC_BATCH, WINDOW, N_KV, HEAD_DIM), 1.0),
        "state_ffn_conv": nrm(ks[6], (DEPTH, DEC_BATCH, CONV_FF - 1, 2 * D_FF), 1.0),
        "norm_mix_pre": 1.0 + nrm(ks[7], (DEPTH, D_MODEL), 0.05),
        "norm_mix_post": 1.0 + nrm(ks[8], (DEPTH, D_MODEL), 0.05),
        "norm_ffn_pre": 1.0 + nrm(ks[9], (DEPTH, D_MODEL), 0.05),
        "norm_ffn_post": 1.0 + nrm(ks[10], (DEPTH, D_MODEL), 0.05),
        "w_in": nrm(ks[11], (DEPTH, D_MODEL, IN_COLS), D_MODEL ** -0.5),
        "conv_lru_w": nrm(ks[12], (DEPTH, CONV_LRU, D_RNN), CONV_LRU ** -0.5),
        "conv_lru_b": nrm(ks[13], (DEPTH, D_RNN), 0.01),
        "lru_wr": nrm(ks[14], (DEPTH, N_LRU_BLOCKS, LRU_BLOCK, LRU_BLOCK), LRU_BLOCK ** -0.5),
        "lru_br": nrm(ks[15], (DEPTH, D_RNN), 0.1),
        "lru_wi": nrm(ks[16], (DEPTH, N_LRU_BLOCKS, LRU_BLOCK, LRU_BLOCK), LRU_BLOCK ** -0.5),
        "lru_bi": nrm(ks[17], (DEPTH, D_RNN), 0.1),
        "lru_lambda": jnp.log(sa) - jnp.log1p(-sa),
        "w_lru_o": nrm(ks[18], (DEPTH, D_RNN, D_MODEL), D_RNN ** -0.5),
        "w_attn_o": nrm(ks[19], (DEPTH, Q_COLS, D_MODEL), Q_COLS ** -0.5),
        "w_out": nrm(ks[21], (DEPTH, D_MODEL, D_MODEL), D_MODEL ** -0.5),
        "attn_sink": nrm(ks[22], (DEPTH, N_HEADS), 1.0),
        "rel_bias": nrm(ks[23], (N_BUCKETS, N_HEADS), 0.5),
        "w_up": nrm(ks[24], (DEPTH, D_MODEL, 2 * D_FF), D_MODEL ** -0.5),
        "ffn_conv_w": nrm(ks[25], (DEPTH, CONV_FF, 2 * D_FF), CONV_FF ** -0.5),
        "ffn_conv_b": nrm(ks[26], (DEPTH, 2 * D_FF), 0.01),
        "w_down": nrm(ks[27], (DEPTH, D_FF, D_MODEL), D_FF ** -0.5),
    }


def reference(x_prompt, x_sample, state_lru_h, state_lru_conv, cache_win_k, cache_win_v, state_ffn_conv,
              norm_mix_pre, norm_mix_post, norm_ffn_pre, norm_ffn_post, w_in, conv_lru_w, conv_lru_b,
              lru_wr, lru_br, lru_wi, lru_bi, lru_lambda, w_lru_o, w_attn_o, w_out, attn_sink, rel_bias,
              w_up, ffn_conv_w, ffn_conv_b, w_down):
    weights = (norm_mix_pre, norm_mix_post, norm_ffn_pre, norm_ffn_post, w_in, conv_lru_w, conv_lru_b,
               lru_wr, lru_br, lru_wi, lru_bi, lru_lambda, w_lru_o, w_attn_o, w_out, attn_sink, rel_bias,
               w_up, ffn_conv_w, ffn_conv_b, w_down)
    bp = x_prompt.shape[0]
    yp, ys = x_prompt, x_sample
    p_h, p_c, p_k, p_v, p_f = [], [], [], [], []
    s_h, s_c, s_k, s_v, s_f = [], [], [], [], []
    for l in range(DEPTH):
        yp, h1, c1, k1, v1, f1 = _layer(
            yp, jnp.zeros((bp, D_RNN), jnp.float32), jnp.zeros((bp, CONV_LRU - 1, D_RNN), yp.dtype),
            None, None, jnp.zeros((bp, CONV_FF - 1, 2 * D_FF), yp.dtype), 0, l, *weights)
        p_h.append(h1); p_c.append(c1); p_k.append(k1); p_v.append(v1); p_f.append(f1)
        ys, h2, c2, k2, v2, f2 = _layer(
            ys, state_lru_h[l], state_lru_conv[l], cache_win_k[l], cache_win_v[l], state_ffn_conv[l],
            PAST_LEN, l, *weights)
        s_h.append(h2); s_c.append(c2); s_k.append(k2); s_v.append(v2); s_f.append(f2)
    return (yp, ys,
            jnp.stack(p_h), jnp.stack(p_c), jnp.stack(p_k), jnp.stack(p_v), jnp.stack(p_f),
            jnp.stack(s_h), jnp.stack(s_c), jnp.stack(s_k), jnp.stack(s_v), jnp.stack(s_f))
```

```python
import numpy as np
import concourse.bass as bass
import concourse.mybir as mybir
from concourse.bass_utils import run_bass_kernel_spmd

F32 = mybir.dt.float32
BF16 = mybir.dt.bfloat16
AF = mybir.ActivationFunctionType
ALU = mybir.AluOpType
AX = mybir.AxisListType

D = 1024
NCH = 8
SEQ = 2048
TP = 512
NPASS = SEQ // TP
NB = 16
NS = 64
DEPTH = 2
NEG = -30000.0
EPS = 1e-6
QSCALE = 128 ** -0.5
NSLOT = 5
GR = 256
SB_BASE = 16512
SB_TOP = 229344

SM_NMP, SM_NMPOST, SM_NFP, SM_NFPOST = 0, 8, 16, 24
SM_CLW = 32
SM_CLB = 64
SM_BR, SM_BI, SM_LAM = 72, 80, 88
SM_FCW = 96
SM_FCB = 288
SM_N = 352


class _Op:
    __slots__ = ("id", "eng", "fn", "deps", "signals", "dsem", "sidx", "ninst")


class Sched:
    ENGS = ("pe", "act", "dve", "pool", "sp")

    def __init__(self, nc):
        self.nc = nc
        self.ops = []
        self.last_writer = {}
        self.readers = {}
        self.dma_sems = {}
        self.alias = {}

    def reg(self, key, off, nbytes):
        g0 = off // GR
        g1 = (off + nbytes + GR - 1) // GR
        self.alias[key] = [("g", g) for g in range(g0, g1)]

    def _expand(self, keys):
        out = []
        for k in keys:
            a = self.alias.get(k)
            if a is None:
                assert isinstance(k, tuple) and k[0] in ("ps", "w", "dram"), ("unregistered key", k)
                out.append(k)
            else:
                out.extend(a)
        return out

    def add(self, eng, fn, reads=(), writes=(), dsem=None, ndma=1):
        reads = self._expand(reads)
        writes = self._expand(writes)
        deps = set()
        for r in reads:
            lw = self.last_writer.get(r)
            if lw is not None:
                deps.add(lw)
        for w in writes:
            lw = self.last_writer.get(w)
            if lw is not None:
                deps.add(lw)
            for rd in self.readers.get(w, ()):
                deps.add(rd)
        op = _Op()
        op.id = len(self.ops)
        op.eng = eng
        op.fn = fn
        op.deps = deps
        op.dsem = dsem
        op.signals = dsem is not None
        op.sidx = None
        op.ninst = ndma
        deps.discard(op.id)
        self.ops.append(op)
        for r in reads:
            self.readers.setdefault(r, []).append(op.id)
        for w in writes:
            self.last_writer[w] = op.id
            self.readers[w] = []
        return op.id

    def emit(self):
        nc = self.nc
        ops = self.ops
        for op in ops:
            for d in op.deps:
                p = ops[d]
                if p.eng == "pe" and op.eng == "pe" and p.dsem is None:
                    continue
                p.signals = True
        esem = {e: nc.alloc_semaphore("s_" + e) for e in ("pe", "act", "dve", "pool")}
        ecount = {e: 0 for e in esem}
        dcount = {}
        for op in ops:
            if op.dsem is not None:
                if op.dsem not in self.dma_sems:
                    self.dma_sems[op.dsem] = nc.alloc_semaphore("d_%d" % len(self.dma_sems))
                    dcount[op.dsem] = 0
                dcount[op.dsem] += 16 * op.ninst
                op.sidx = (self.dma_sems[op.dsem], dcount[op.dsem])
            elif op.signals:
                ecount[op.eng] += 1
                op.sidx = (esem[op.eng], ecount[op.eng])
        final_waits = {k: (self.dma_sems[k], v) for k, v in dcount.items()}
        by_eng = {e: [op for op in ops if op.eng == e] for e in self.ENGS}

        def run(engine, ename):
            waited = {}
            for op in by_eng[ename]:
                need = {}
                for d in op.deps:
                    p = ops[d]
                    if p.eng == "pe" and ename == "pe" and p.dsem is None:
                        continue
                    sem, val = p.sidx
                    k = id(sem)
                    if waited.get(k, 0) >= val:
                        continue
                    if k not in need or need[k][1] < val:
                        need[k] = (sem, val)
                for k, (sem, val) in need.items():
                    engine.wait_ge(sem, val)
                    waited[k] = val
                r = op.fn(engine)
                insts = r if isinstance(r, (list, tuple)) else [r]
                if op.dsem is not None:
                    assert len(insts) == op.ninst, (len(insts), op.ninst)
                    for i in insts:
                        i.then_inc(op.sidx[0], 16)
                elif op.signals:
                    insts[-1].then_inc(op.sidx[0], 1)
            if ename == "sp":
                for k, (sem, val) in final_waits.items():
                    engine.wait_ge(sem, val)

        with nc.Block() as block:
            @block.sync
            def _(e):
                run(e, "sp")

            @block.gpsimd
            def _(e):
                run(e, "pool")

            @block.tensor
            def _(e):
                run(e, "pe")

            @block.scalar
            def _(e):
                run(e, "act")

            @block.vector
            def _(e):
                run(e, "dve")


class Banks:
    def __init__(self, tensors):
        self.t = tensors
        self.free_list = list(range(len(tensors)))

    def alloc(self):
        assert self.free_list, "out of PSUM banks"
        return self.free_list.pop(0)

    def free(self, b):
        assert b not in self.free_list
        self.free_list.append(b)


class WStream:
    def __init__(self, S, nc, seq, slots, first_reads=()):
        self.S = S
        self.seq = seq
        self.slots = slots
        self.first_reads = list(first_reads)
        self.next_dma = 0
        self.released = 0
        self.pos = 0
        self._pump()

    def _pump(self):
        while self.next_dma < len(self.seq) and self.next_dma - NSLOT < self.released:
            n = self.next_dma
            key, src, ncol = self.seq[n]
            sl = n % NSLOT
            dst = self.slots[sl]
            nd = ncol // 2048

            def fn(e, src=src, dst=dst, nd=nd):
                return [e.dma_start(out=dst[:, i * 2048:(i + 1) * 2048], in_=src[:, i * 2048:(i + 1) * 2048])
                        for i in range(nd)]
            self.S.add("pool", fn, reads=(self.first_reads if n == 0 else ()), writes=[("w", sl)], dsem=("w", sl), ndma=nd)
            self.next_dma += 1

    def get(self, key):
        i = self.pos
        assert self.seq[i][0] == key, (self.seq[i][0], key)
        self.pos += 1
        self._pump()
        assert i < self.next_dma
        sl = i % NSLOT
        return self.slots[sl], ("w", sl)

    def release(self):
        self.released += 1
        self._pump()


def layer_block_keys(l):
    ks = [("win", l, 0), ("win", l, 1), ("win", l, 2), ("win", l, 3), ("win", l, 4)]
    ks += [("wlo", l, 0), ("win", l, 5), ("wlo", l, 1), ("win", l, 6)]
    ks += [("wao", l, 0), ("win", l, 7), ("wao", l, 1), ("win", l, 8)]
    ks += [("wout", l, 0), ("wout", l, 1)]
    for pb in range(8):
        ks += [("wup", l, pb), ("wup", l, pb + 8)]
    ks += [("wdn", l, oc) for oc in range(8)]
    return ks


def build():
    nc = bass.Bass("TRN2", target_bir_lowering=False)

    def din(name, shape):
        return nc.dram_tensor(name, list(shape), F32, kind="ExternalInput").ap()

    def dout(name, shape):
        return nc.dram_tensor(name, list(shape), F32, kind="ExternalOutput").ap()

    xp = din("xp", [SEQ, D])
    xs = din("xs", [NS, D])
    st_h = din("st_h", [DEPTH, NB, D])
    st_c = din("st_c", [DEPTH, NB * 3, D])
    ck = din("ck", [DEPTH, NB, 128, 256])
    cv = din("cv", [DEPTH, NB, 128, 256])
    st_f = din("st_f", [DEPTH, NB * 2, 8192])
    win_r = din("win_r", [DEPTH * 9, 128, 4096])
    gates_r = din("gates_r", [DEPTH, 128, 2048])
    wlo_r = din("wlo_r", [DEPTH * 2, 128, 4096])
    wao_r = din("wao_r", [DEPTH * 2, 128, 4096])
    wout_r = din("wout_r", [DEPTH * 2, 128, 4096])
    wup_r = din("wup_r", [DEPTH * 16, 128, 4096])
    wdn_r = din("wdn_r", [DEPTH * 8, 128, 4096])
    smalls_d = din("smalls", [DEPTH, 128, SM_N])
    sinks_d = din("sinks", [DEPTH, 128, 10])
    biasp_d = din("biasp", [128, 8 * 256])
    biass_d = din("biass", [16, 2 * 132])
    ident_d = din("ident", [128, 128])

    yp = dout("yp", [SEQ, D])
    ys = dout("ys", [NS, D])
    o_plh = dout("o_plh", [DEPTH, D])
    o_plc = dout("o_plc", [DEPTH, 3, D])
    o_pk = dout("o_pk", [DEPTH, 128, 256])
    o_pv = dout("o_pv", [DEPTH, 128, 256])
    o_pf = dout("o_pf", [DEPTH, 2, 8192])
    o_slh = dout("o_slh", [DEPTH, NB, D])
    o_slc = dout("o_slc", [DEPTH, NB * 3, D])
    o_sk = dout("o_sk", [DEPTH, NB, 128, 256])
    o_sv = dout("o_sv", [DEPTH, NB, 128, 256])
    o_sf = dout("o_sf", [DEPTH, NB * 2, 8192])

    S = Sched(nc)
    TMAX = TP + NS
    XRW = 3 + TP + NB * 7
    UW = 2 + TP + NB * 6

    def esz(dt):
        return 2 if dt == BF16 else 4

    def mk(name, shape, dt, off, key=None, chunks=None):
        assert off % 32 == 0
        nbytes = int(np.prod(shape[1:])) * esz(dt)
        assert SB_BASE + off + nbytes <= SB_TOP, (name, off, nbytes)
        t = nc.alloc_sbuf_tensor_at(name, list(shape), dt, offset=SB_BASE + off)
        key = key or name
        S.reg(key, off, nbytes)
        if chunks:
            cb = nbytes // chunks
            for c in range(chunks):
                S.reg((key, c), off + c * cb, cb)
        return t

    cur = [0]

    def P(name, shape, dt=F32, chunks=None, key=None):
        nbytes = int(np.prod(shape[1:])) * esz(dt)
        off = cur[0]
        cur[0] += (nbytes + GR - 1) // GR * GR
        return mk(name, shape, dt, off, key=key, chunks=chunks)

    identf = P("identf", [128, 128])
    identb = P("identb", [128, 128], BF16)
    ones_b = P("ones_b", [128, 128], BF16)
    epsc = P("epsc", [128, 1])
    smalls = [P("smalls%d" % l, [128, SM_N], key=("smalls", l)) for l in range(DEPTH)]
    lamc = [P("lamc%d" % l, [128, 16], key=("lamc", l)) for l in range(DEPTH)]
    sinks = [P("sinks%d" % l, [128, 10], key=("sinks", l)) for l in range(DEPTH)]
    biasp = P("biasp", [128, 8, 256])
    biass = P("biass", [16, 2, 132])
    convhist = [P("convhist%d" % l, [128, NCH, 3], key=("convhist", l)) for l in range(DEPTH)]
    hstate = [P("hstate%d" % l, [128, NCH], key=("hstate", l)) for l in range(DEPTH)]
    ffnhist = [P("ffnhist%d" % l, [128, 64, 2], key=("ffnhist", l)) for l in range(DEPTH)]
    khist = [P("khist%d" % l, [128, 2, 128], BF16, key=("khist", l)) for l in range(DEPTH)]
    vhist = [P("vhist%d" % l, [128, 256], BF16, key=("vhist", l)) for l in range(DEPTH)]
    gatesw = [P("gatesw%d" % l, [128, 2048], BF16, key=("gatesw", l)) for l in range(DEPTH)]
    x = P("x", [128, NCH, TMAX], chunks=NCH)
    h = P("h", [128, NCH, TMAX], BF16, chunks=NCH)
    rstd = P("rstd", [128, TMAX])
    sdv = P("sdv", [128, TMAX])
    stout = P("stout", [128, 1024])
    SFs = [P("SFs%d" % i, [128, 8, NB * 2], key=("SFs", i)) for i in range(2)]
    pst = P("pst", [128, 128])
    h0s = P("h0s", [128, NCH, NB])
    hs_last = P("hs_last", [128, NCH, NB])
    cs_stage = P("cs_stage", [128, NCH, NB * 3])
    tmp16 = P("tmp16", [128, NB])
    qTs = P("qTs", [128, 2, NB, 16], BF16)
    wslots = [P("wslot%d" % i, [128, 4096], BF16) for i in range(NSLOT)]
    SCR = cur[0]
    RA = SCR
    RB = SCR + 36864
    assert SB_BASE + RB + 61696 <= SB_TOP, (SCR, SB_TOP - SB_BASE)

    lro = mk("lro", [128, NCH, TMAX], BF16, RA + 0, chunks=NCH)
    qT = mk("qT", [128, 8, TMAX], BF16, RA + 9216, chunks=8)
    merged = mk("merged", [128, NCH, TMAX], BF16, RA + 9216, chunks=NCH)
    attn = mk("attn", [128, 8, TMAX], BF16, RA + 18432, chunks=8)
    kT = mk("kT", [128, 2, 128 + TP], BF16, RA + 27648)
    Vt = mk("Vt", [128, 5, 256], BF16, RA + 30208)
    kvnew = mk("kvnew", [64, 512], F32, RA + 32768)
    kvlast = mk("kvlast", [128, 512], F32, RA + 34816)
    A = mk("A", [128, 32, TMAX], BF16, RA + 0, chunks=32)
    sq = mk("sq", [128, NCH, TMAX], BF16, RB + 0, chunks=NCH)
    tok = mk("tok", [128, 4, D], F32, RB + 36864)
    tokx = mk("tokx", [128, 4, D], F32, RB + 20480)
    toks = mk("toks", [64, D], F32, RB + 16384)
    XR = mk("XR", [128, NCH, XRW], F32, RB + 0, chunks=NCH)
    LS = 20736
    xcg = [mk("xc%d" % s_, [128, 2, TMAX], F32, RB + 20224 + s_ * LS, key=("xc", s_), chunks=2) for s_ in range(2)]
    xcb = [mk("xcb%d" % s_, [128, 2, TMAX], BF16, RB + 20224 + s_ * LS + 4608, key=("xcb", s_), chunks=2) for s_ in range(2)]
    rr = [mk("rr%d" % s_, [128, 2, TMAX], F32, RB + 20224 + s_ * LS + 6912, key=("rr", s_), chunks=2) for s_ in range(2)]
    ii = [mk("ii%d" % s_, [128, 2, TMAX], F32, RB + 20224 + s_ * LS + 11520, key=("ii", s_), chunks=2) for s_ in range(2)]
    aa = [mk("aa%d" % s_, [128, 2, TMAX], F32, RB + 20224 + s_ * LS + 16128, key=("aa", s_), chunks=2) for s_ in range(2)]
    kTs = mk("kTs", [128, 2, NB, 132], BF16, RB + 0)
    Vs = mk("Vs", [128, NB, 256], BF16, RB + 8448)
    Vn4 = mk("Vn4", [4, NB, 256], BF16, RB + 16640)
    cstage = mk("cstage", [128, NB, 256], F32, RB + 24832)
    o3 = RB + 41216
    NSET = 8
    SBW = 64 + 260
    Sbuf, Pn, PT, stat = [], [], [], []
    for i in range(NSET):
        o = o3 + i * 2816
        Sbuf.append(mk("Sbuf%d" % i, [128, SBW], F32, o, key=("Sb", i)))
        S.reg(("Sb", i, "h"), o, 256)
        S.reg(("Sb", i, "b"), o + 256, 4 * 260)
        Pn.append(mk("Pn%d" % i, [128, 256], BF16, o + 1536, key=("Pn", i)))
        PT.append(mk("PT%d" % i, [128, 2, 128], BF16, o + 2048, key=("PT", i)))
        stat.append(mk("stat%d" % i, [128, 4], F32, o + 2560, key=("st", i)))
    sgA = [mk("sgA%d" % i, [128, TP], F32, RB + 9216 + i * 2048, key=("sgA", i)) for i in range(2)]
    sgB = [mk("sgB%d" % i, [128, TP], F32, RB + 13312 + i * 2048, key=("sgB", i)) for i in range(2)]
    t1b = [mk("t1b%d" % i, [128, TP], F32, RB + 17408 + i * 2048, key=("t1b", i)) for i in range(2)]
    mbuf = mk("mbuf", [128, NCH, TMAX], F32, RB + 40448, chunks=NCH)
    def mkU(name, i, off):
        t = mk("%s%d" % (name, i), [128, 64 + TP], F32, off, key=(name, i))
        S.reg((name, i, "h"), off, 256)
        S.reg((name, i, "b"), off + 256, 4 * TP)
        return t
    Uv = [mkU("Uv", 0, RB + 9216), mkU("Uv", 1, RB + 9216 + 2560), mkU("Uv", 2, RB + 41472)]
    Ug = [mkU("Ug", 0, RB + 14336), mkU("Ug", 1, RB + 14336 + 2560), mkU("Ug", 2, RB + 41472 + 2304)]
    cvv = [mk("cvv%d" % i, [128, TMAX], F32, RB + 19456 + i * 2304, key=("cvv", i)) for i in range(2)]
    cvg = [mk("cvg%d" % i, [128, TMAX], F32, RB + 24064 + i * 2304, key=("cvg", i)) for i in range(2)]
    ggb = [mk("ggb%d" % i, [128, TMAX], F32, RB + 28672 + i * 2304, key=("gg", i)) for i in range(2)]
    cvv.append(mk("cvv2", [128, TMAX], F32, RB + 0, key=("cvv", 2)))
    cvg.append(mk("cvg2", [128, TMAX], F32, RB + 2304, key=("cvg", 2)))
    ggb.append(mk("ggb2", [128, TMAX], F32, RB + 4608, key=("gg", 2)))
    FH = mk("FH", [128, 64, NB * 2], F32, RB + 33280)
    stf = mk("stf", [32, 2048], F32, RB + 0)
    Us = [mk("Us%d" % i, [128, 4, NB * 6], F32, RB + 58880 + i * 1536, key=("Us", i)) for i in range(2)]
    cvs = [mk("cvs%d" % i, [128, 4, NS], F32, RB + 6912 + i * 1024, key=("cvs", i)) for i in range(2)]
    print("SBUF map: persistent=%d scratch_avail=%d" % (SCR, SB_TOP - SB_BASE - SCR))

    ps_t = [nc.alloc_psum_tensor("ps%d" % i, [128, 512], F32) for i in range(8)]
    banks = Banks(ps_t)

    def PSR(b):
        return ("ps", b)

    def keys(name, n):
        return [(name, c) for c in range(n)]

    wsrc = {"win": (win_r, 9), "wlo": (wlo_r, 2), "wao": (wao_r, 2), "wout": (wout_r, 2),
            "wup": (wup_r, 16), "wdn": (wdn_r, 8)}
    seq = []
    for p in range(NPASS):
        for l in range(DEPTH):
            for key in layer_block_keys(l):
                t, n = wsrc[key[0]]
                seq.append(((p,) + key, t[l * n + key[2]], 4096))
    S.add("sp", lambda e: e.dma_start(out=tokx[:], in_=xp[0:TP, :].rearrange("(j r) d -> r j d", r=128)), writes=["tokx"], dsem="xin")
    W = WStream(S, nc, seq, wslots, first_reads=["tokx"])
    for l in range(DEPTH):
        S.add("pool", lambda e, l=l: e.dma_start(out=gatesw[l][:], in_=gates_r[l]), writes=[("gatesw", l)], dsem=("gw", l))

    S.add("sp", lambda e: e.dma_start(out=identf[:], in_=ident_d), writes=["identf"], dsem="c0")
    S.add("sp", lambda e: e.dma_start(out=biasp[:].rearrange("p h k -> p (h k)"), in_=biasp_d), writes=["biasp"], dsem="c1")
    S.add("sp", lambda e: e.dma_start(out=biass[:].rearrange("p h k -> p (h k)"), in_=biass_d), writes=["biass"], dsem="c2")
    for l in range(DEPTH):
        S.add("sp", lambda e, l=l: e.dma_start(out=smalls[l][:], in_=smalls_d[l]), writes=[("smalls", l)], dsem=("c3", l))
        S.add("sp", lambda e, l=l: e.dma_start(out=sinks[l][:], in_=sinks_d[l]), writes=[("sinks", l)], dsem=("c4", l))
    S.add("dve", lambda e: e.tensor_scalar(biasp[:], biasp[:], -1.0, None, ALU.mult), writes=["biasp"])
    S.add("dve", lambda e: e.tensor_scalar(biass[:], biass[:], -1.0, None, ALU.mult), writes=["biass"])
    for l in range(DEPTH):
        S.add("dve", lambda e, l=l: e.tensor_scalar(sinks[l][:], sinks[l][:], -1.0, None, ALU.mult), writes=[("sinks", l)])
    S.add("dve", lambda e: e.tensor_copy(out=identb[:], in_=identf[:]), reads=["identf"], writes=["identb"])
    S.add("dve", lambda e: e.memset(ones_b[:], 1.0), writes=["ones_b"])
    S.add("dve", lambda e: e.memset(epsc[:], EPS), writes=["epsc"])
    for l in range(DEPTH):
        S.add("act", lambda e, l=l: e.activation(out=lamc[l][:, 0:8], in_=smalls[l][:, SM_LAM:SM_LAM + 8], func=AF.Exp, scale=-1.0),
              reads=[("smalls", l)], writes=[("lamc", l)])
        S.add("act", lambda e, l=l: e.activation(out=lamc[l][:, 0:8], in_=lamc[l][:, 0:8], func=AF.Ln, bias=1.0),
              writes=[("lamc", l)])
        S.add("dve", lambda e, l=l: e.tensor_scalar(lamc[l][:, 8:16], lamc[l][:, 0:8], -16.0, None, ALU.mult), writes=[("lamc", l)])
        S.add("dve", lambda e, l=l: e.tensor_scalar(lamc[l][:, 0:8], lamc[l][:, 0:8], -8.0, None, ALU.mult), writes=[("lamc", l)])

    rot = {"S": 0, "sg": 0, "u": 0}
    XK = keys("x", NCH)
    HR = keys("h", NCH)

    def transpose_out(src_fn, ncols, dst_aps, rd, tag):
        i = 0
        while i < len(dst_aps):
            grp = list(range(i, min(i + 4, len(dst_aps))))
            b = banks.alloc()

            def tr(e, grp=grp, b=b):
                r = None
                for gi, k in enumerate(grp):
                    n = ncols[k]
                    r = e.transpose(ps_t[b][0:n, gi * 128:(gi + 1) * 128], src_fn(k), identf[:])
                return r
            S.add("pe", tr, reads=list(rd) + ["identf"], writes=[PSR(b)])
            nmax = max(ncols[k] for k in grp)
            S.add("act", lambda e, b=b, grp=grp, nmax=nmax: e.copy(out=stout[0:nmax, 0:128 * len(grp)], in_=ps_t[b][0:nmax, 0:128 * len(grp)]),
                  writes=[PSR(b), "stout"])
            banks.free(b)
            for gi, k in enumerate(grp):
                n = ncols[k]
                S.add("sp", lambda e, gi=gi, k=k, n=n: e.dma_start(out=dst_aps[k], in_=stout[0:n, gi * 128:(gi + 1) * 128]),
                      reads=["stout"], dsem=("so", tag))
            i += 4

    def norm_stats(T, subs):
        for (c0, n) in subs:
            b = banks.alloc()
            for c in range(NCH):
                S.add("pe", lambda e, b=b, c0=c0, n=n, c=c: e.matmul(ps_t[b][:, 0:n], ones_b[:], sq[:, c, c0:c0 + n], start=(c == 0), stop=(c == NCH - 1)),
                      reads=[("sq", c), "ones_b"], writes=[PSR(b)])
            S.add("act", lambda e, b=b, c0=c0, n=n: e.activation(out=sdv[:, c0:c0 + n], in_=ps_t[b][:, 0:n], func=AF.Ln,
                                                                 scale=1.0 / D, bias=epsc[:, 0:1]),
                  reads=["epsc"], writes=[PSR(b), "sdv"])
            S.add("act", lambda e, c0=c0, n=n: e.activation(out=rstd[:, c0:c0 + n], in_=sdv[:, c0:c0 + n], func=AF.Exp, scale=-0.5),
                  reads=["sdv"], writes=["rstd"])
            banks.free(b)

    def split_eng(c):
        return "dve"

    def rmsnorm_to_h(l, gcol, T, subs, squares_done=False):
        if not squares_done:
            for c in range(NCH):
                S.add("act", lambda e, c=c: e.activation(out=sq[:, c, 0:T], in_=x[:, c, 0:T], func=AF.Square), reads=[("x", c)], writes=[("sq", c)])
        norm_stats(T, subs)
        for c in range(NCH):
            S.add("dve", lambda e, c=c: e.scalar_tensor_tensor(out=h[:, c, 0:T], in0=x[:, c, 0:T],
                                                                      scalar=smalls[l][:, gcol + c:gcol + c + 1], in1=rstd[:, 0:T],
                                                                      op0=ALU.mult, op1=ALU.mult),
                  reads=[("x", c), "rstd", ("smalls", l)], writes=[("h", c)])

    def postnorm_residual(l, T, subs, next_squares):
        norm_stats(T, subs)
        for c in range(NCH):
            eng = split_eng(c)
            S.add(eng, lambda e, c=c: e.tensor_tensor(out=mbuf[:, c, 0:T], in0=mbuf[:, c, 0:T], in1=rstd[:, 0:T], op=ALU.mult),
                  reads=["rstd"], writes=[("mbuf", c)])
            S.add(eng, lambda e, c=c: e.tensor_tensor(out=x[:, c, 0:T], in0=x[:, c, 0:T], in1=mbuf[:, c, 0:T], op=ALU.add),
                  reads=[("mbuf", c)], writes=[("x", c)])
            if next_squares:
                S.add("act", lambda e, c=c: e.activation(out=sq[:, c, 0:T], in_=x[:, c, 0:T], func=AF.Square), reads=[("x", c)], writes=[("sq", c)])

    def proj(wt, wres, col, rhs, rhs_res, nk, subs, evac):
        for (c0, n) in subs:
            b = banks.alloc()
            for k in range(nk):
                S.add("pe", lambda e, b=b, c0=c0, n=n, k=k: e.matmul(ps_t[b][:, 0:n], wt[:, k, col:col + 128], rhs[:, k, c0:c0 + n],
                                                                     start=(k == 0), stop=(k == nk - 1)),
                      reads=[wres, rhs_res[k]], writes=[PSR(b)])
            evac(b, c0, n)
            banks.free(b)

    def attn_waves(units):
        waves = [units[i:i + 4] for i in range(0, len(units), 4)]

        def sets(w, i):
            return (w % 2) * 4 + i

        def front(w):
            wv = waves[w]
            bl = []
            for i, u in enumerate(wv):
                b = banks.alloc()
                bl.append(b)
                NQ, NK = u["NQ"], u["NK"]
                S.add("pe", lambda e, u=u, b=b, NQ=NQ, NK=NK: e.matmul(ps_t[b][0:NQ, 0:NK], u["qap"], u["kap"], start=True, stop=True),
                      reads=u["rd"], writes=[PSR(b)])
            for i, u in enumerate(wv):
                si = sets(w, i)
                NQ = u["NQ"]
                S.add("pool", lambda e, u=u, si=si, NQ=NQ: e.tensor_copy(out=Sbuf[si][0:NQ, 63:64], in_=u["sink_ap"]), reads=u["rd"], writes=[("Sb", si, "h")])
            for i, u in enumerate(wv):
                si = sets(w, i)
                b = bl[i]
                NQ, NK = u["NQ"], u["NK"]
                S.add("dve", lambda e, u=u, si=si, b=b, NQ=NQ, NK=NK: e.tensor_tensor(out=Sbuf[si][0:NQ, 64:64 + NK], in0=u["bias_ap"], in1=ps_t[b][0:NQ, 0:NK], op=ALU.subtract),
                      reads=[u["bias_key"]], writes=[PSR(b), ("Sb", si, "b")])
                banks.free(b)
            for i, u in enumerate(wv):
                si = sets(w, i)
                NQ, NK = u["NQ"], u["NK"]
                S.add("dve", lambda e, si=si, NQ=NQ, NK=NK: e.tensor_reduce(out=stat[si][0:NQ, 0:1], in_=Sbuf[si][0:NQ, 63:64 + NK], axis=AX.X, op=ALU.min),
                      reads=[("Sb", si)], writes=[("st", si)])

        def mid(w):
            wv = waves[w]
            for i, u in enumerate(wv):
                si = sets(w, i)
                NQ, NK = u["NQ"], u["NK"]
                S.add("act", lambda e, si=si, NQ=NQ, NK=NK: e.activation(out=Sbuf[si][0:NQ, 63:64 + NK], in_=Sbuf[si][0:NQ, 63:64 + NK], func=AF.Exp,
                                                                         bias=stat[si][0:NQ, 0:1], scale=-1.0, accum_out=stat[si][0:NQ, 2:3]),
                      writes=[("Sb", si), ("st", si)])
            for i, u in enumerate(wv):
                si = sets(w, i)
                NQ = u["NQ"]
                S.add("dve", lambda e, si=si, NQ=NQ: e.reciprocal(out=stat[si][0:NQ, 3:4], in_=stat[si][0:NQ, 2:3]), writes=[("st", si)])
            for i, u in enumerate(wv):
                si = sets(w, i)
                NQ, NK = u["NQ"], u["NK"]
                S.add("act", lambda e, si=si, NQ=NQ, NK=NK: e.activation(out=Pn[si][0:NQ, 0:NK], in_=Sbuf[si][0:NQ, 64:64 + NK], func=AF.Copy, scale=stat[si][0:NQ, 3:4]),
                      reads=[("Sb", si), ("st", si)], writes=[("Pn", si)])

        def back(w):
            wv = waves[w]
            tbl = []
            for i, u in enumerate(wv):
                si = sets(w, i)
                tb = banks.alloc()
                tbl.append(tb)
                NQ = u["NQ"]

                def tr(e, u=u, si=si, tb=tb, NQ=NQ):
                    tps = ps_t[tb].bitcast(BF16)
                    r = None
                    for vi, (vap, k0, nk) in enumerate(u["vblocks"]):
                        r = e.transpose(tps[0:nk, vi * 128:vi * 128 + NQ], Pn[si][0:NQ, k0:k0 + nk], identb[0:NQ, 0:NQ])
                    return r
                S.add("pe", tr, reads=[("Pn", si), "identb"], writes=[PSR(tb)])
            for i, u in enumerate(wv):
                si = sets(w, i)
                tb = tbl[i]
                NQ = u["NQ"]
                vb = u["vblocks"]
                if NQ == 128 and all(nk == 128 for (_, _, nk) in vb):
                    nv = len(vb)
                    S.add("act", lambda e, si=si, tb=tb, nv=nv: e.copy(out=PT[si][:, 0:nv, :],
                                                                      in_=ps_t[tb].bitcast(BF16)[:, 0:nv * 128].rearrange("p (v q) -> p v q", q=128)),
                          writes=[PSR(tb), ("PT", si)])
                else:
                    for vi, (vap, k0, nk) in enumerate(vb):
                        S.add("act", lambda e, si=si, tb=tb, vi=vi, nk=nk, NQ=NQ: e.copy(out=PT[si][0:nk, vi, 0:NQ], in_=ps_t[tb].bitcast(BF16)[0:nk, vi * 128:vi * 128 + NQ]),
                              writes=[PSR(tb), ("PT", si)])
                banks.free(tb)
            for i, u in enumerate(wv):
                si = sets(w, i)
                g = u["grp"]
                if g.get("ob") is None:
                    g["ob"] = banks.alloc()
                ob = g["ob"]
                NQ = u["NQ"]
                c0 = u["out_c0"]

                def pv(e, u=u, si=si, ob=ob, NQ=NQ, c0=c0):
                    r = None
                    vb = u["vblocks"]
                    for vi, (vap, k0, nk) in enumerate(vb):
                        r = e.matmul(ps_t[ob][:, c0:c0 + NQ], vap, PT[si][0:nk, vi, 0:NQ], start=(vi == 0), stop=(vi == len(vb) - 1))
                    return r
                S.add("pe", pv, reads=[("PT", si)] + list(u["rd"]), writes=[PSR(ob)])
                if u["last"]:
                    u["evac"](ob)
                    banks.free(ob)
                    g["ob"] = None

        nw = len(waves)
        front(0)
        for w in range(nw):
            mid(w)
            if w + 1 < nw:
                front(w + 1)
            back(w)

    def do_pass(p):
        has_s = (p == 0)
        T = TP + (NS if has_s else 0)
        subs = [(0, TP)] + ([(TP, NS)] if has_s else [])
        last_pass = (p == NPASS - 1)
        XRs = XR[:, :, 3 + TP:3 + TP + NB * 7].rearrange("p c (b k) -> p c b k", k=7)

        if has_s:
            S.add("sp", lambda e: e.dma_start(out=toks[:], in_=xs), writes=["toks"], dsem="xin2")
        for c in range(NCH):
            b = banks.alloc()

            def tr(e, b=b, c=c):
                r = None
                for j in range(4):
                    r = e.transpose(ps_t[b][:, j * 128:(j + 1) * 128], tokx[:, j, c * 128:(c + 1) * 128], identf[:])
                return r
            S.add("pe", tr, reads=["tokx", "identf"], writes=[PSR(b)])
            S.add("act", lambda e, b=b, c=c: e.copy(out=x[:, c, 0:TP], in_=ps_t[b][:, 0:TP]), writes=[PSR(b), ("x", c)])
            banks.free(b)
            if has_s:
                b = banks.alloc()
                S.add("pe", lambda e, b=b, c=c: e.transpose(ps_t[b][:, 0:NS], toks[:, c * 128:(c + 1) * 128], identf[0:NS, 0:NS]),
                      reads=["toks", "identf"], writes=[PSR(b)])
                S.add("act", lambda e, b=b, c=c: e.copy(out=x[:, c, TP:TP + NS], in_=ps_t[b][:, 0:NS]), writes=[PSR(b), ("x", c)])
                banks.free(b)

        for l in range(DEPTH):
            do_layer(p, l, T, subs, has_s, last_pass, XRs)
        finish_pass(p, has_s)

    def do_layer(p, l, T, subs, has_s, last_pass, XRs):
        if True:
            sm = smalls[l]
            SMR = ("smalls", l)

            rmsnorm_to_h(l, SM_NMP, T, subs, squares_done=(l > 0))

            XRall = keys("XR", NCH)
            if has_s:
                S.add("sp", lambda e, l=l: e.dma_start(out=stout[0:48, 0:D], in_=st_c[l]), writes=["stout"], dsem="stin")
                for c in range(NCH):
                    b = banks.alloc()
                    S.add("pe", lambda e, b=b, c=c: e.transpose(ps_t[b][:, 0:48], stout[0:48, c * 128:(c + 1) * 128], identf[0:48, 0:48]),
                          reads=["stout", "identf"], writes=[PSR(b)])
                    S.add("act", lambda e, b=b, c=c: e.copy(out=XRs[:, c, :, 0:3], in_=ps_t[b][:, 0:48].rearrange("p (b k) -> p b k", k=3)),
                          writes=[PSR(b), ("XR", c)])
                    banks.free(b)
                S.add("sp", lambda e, l=l: e.dma_start(out=stout[0:16, 0:D], in_=st_h[l]), writes=["stout"], dsem="stin")
                for c in range(NCH):
                    b = banks.alloc()
                    S.add("pe", lambda e, b=b, c=c: e.transpose(ps_t[b][:, 0:16], stout[0:16, c * 128:(c + 1) * 128], identf[0:16, 0:16]),
                          reads=["stout", "identf"], writes=[PSR(b)])
                    S.add("act", lambda e, b=b, c=c: e.copy(out=h0s[:, c, :], in_=ps_t[b][:, 0:16]), writes=[PSR(b), "h0s"])
                    banks.free(b)

            if p == 0:
                S.add("dve", lambda e: e.memset(XR[:, :, 0:3], 0.0), writes=XRall)
            else:
                S.add("dve", lambda e, l=l: e.tensor_copy(out=XR[:, :, 0:3], in_=convhist[l][:]),
                      reads=[("convhist", l)], writes=XRall)
            for blk in range(2):
                wt, wres = W.get((p, "win", l, blk))
                wv = wt[:].rearrange("p (k n) -> p k n", n=512)
                for cc in range(4):
                    c = blk * 4 + cc

                    def ev(b, c0, n, c=c):
                        if c0 == 0:
                            S.add("act", lambda e: e.copy(out=XR[:, c, 3:3 + TP], in_=ps_t[b][:, 0:TP]), writes=[PSR(b), ("XR", c)])
                        else:
                            S.add("act", lambda e: e.copy(out=XRs[:, c, :, 3:7], in_=ps_t[b][:, 0:NS].rearrange("p (b t) -> p b t", t=4)),
                                  writes=[PSR(b), ("XR", c)])
                    proj(wv, wres, cc * 128, h, HR, NCH, subs, ev)
                W.release()
            S.add("dve", lambda e, l=l: e.tensor_copy(out=convhist[l][:], in_=XR[:, :, TP:TP + 3]), reads=XRall, writes=[("convhist", l)])
            if has_s:
                S.add("dve", lambda e: e.tensor_copy(out=cs_stage[:].rearrange("p c (b k) -> p c b k", k=3), in_=XRs[:, :, :, 4:7]),
                      reads=XRall, writes=["cs_stage"])

            for blk in (2, 3):
                wt, wres = W.get((p, "win", l, blk))
                wv = wt[:].rearrange("p (k n) -> p k n", n=512)
                for cc in range(4):
                    hh = (blk - 2) * 4 + cc

                    def ev(b, c0, n, hh=hh):
                        S.add("act", lambda e: e.activation(out=qT[:, hh, c0:c0 + n], in_=ps_t[b][:, 0:n], func=AF.Copy, scale=QSCALE),
                              writes=[PSR(b), ("qT", hh)])
                    proj(wv, wres, cc * 128, h, HR, NCH, subs, ev)
                W.release()
            wkv_t, wkv_res = W.get((p, "win", l, 4))
            wkv = wkv_t[:].rearrange("p (k n) -> p k n", n=512)
            if p == 0:
                S.add("dve", lambda e: e.memset(kT[:, :, 0:128], 0.0), writes=["kT"])
                S.add("dve", lambda e: e.memset(Vt[:, 0, :], 0.0), writes=["Vt"])
            else:
                S.add("dve", lambda e, l=l: e.tensor_copy(out=kT[:, :, 0:128], in_=khist[l][:]), reads=[("khist", l)], writes=["kT"])
                S.add("dve", lambda e, l=l: e.tensor_copy(out=Vt[:, 0, :], in_=vhist[l][:]), reads=[("vhist", l)], writes=["Vt"])
            for kv in range(2):
                def ev(b, c0, n, kv=kv):
                    S.add("act", lambda e: e.copy(out=kT[:, kv, 128:128 + TP], in_=ps_t[b][:, 0:TP]), writes=[PSR(b), "kT"])
                proj(wkv, wkv_res, kv * 128, h, HR, NCH, [(0, TP)], ev)
            for j in range(4):
                full = last_pass and j == 3
                b = banks.alloc()
                c0w, nw = (0, 512) if full else (256, 256)

                def mm(e, b=b, j=j, c0w=c0w, nw=nw):
                    r = None
                    for k in range(NCH):
                        r = e.matmul(ps_t[b][:, 0:nw], h[:, k, j * 128:(j + 1) * 128], wkv[:, k, c0w:c0w + nw], start=(k == 0), stop=(k == NCH - 1))
                    return r
                S.add("pe", mm, reads=[wkv_res] + HR, writes=[PSR(b)])
                voff = 256 if full else 0
                S.add("act", lambda e, b=b, j=j, voff=voff: e.copy(out=Vt[:, j + 1, :], in_=ps_t[b][:, voff:voff + 256]), writes=[PSR(b), "Vt"])
                if full:
                    S.add("dve", lambda e, b=b: e.tensor_copy(out=kvlast[:], in_=ps_t[b][:, 0:512]), writes=[PSR(b), "kvlast"])
                    S.add("sp", lambda e, l=l: e.dma_start(out=o_pk[l], in_=kvlast[:, 0:256]), reads=["kvlast"], dsem="okv")
                    S.add("sp", lambda e, l=l: e.dma_start(out=o_pv[l], in_=kvlast[:, 256:512]), reads=["kvlast"], dsem="okv")
                banks.free(b)
            S.add("dve", lambda e, l=l: e.tensor_copy(out=khist[l][:], in_=kT[:, :, TP:TP + 128]), reads=["kT"], writes=[("khist", l)])
            S.add("dve", lambda e, l=l: e.tensor_copy(out=vhist[l][:], in_=Vt[:, 4, :]), reads=["Vt"], writes=[("vhist", l)])
            if has_s:
                b = banks.alloc()

                def mm(e, b=b):
                    r = None
                    for k in range(NCH):
                        r = e.matmul(ps_t[b][0:NS, 0:512], h[:, k, TP:TP + NS], wkv[:, k, 0:512], start=(k == 0), stop=(k == NCH - 1))
                    return r
                S.add("pe", mm, reads=[wkv_res] + HR, writes=[PSR(b)])
                S.add("act", lambda e, b=b: e.copy(out=kvnew[:], in_=ps_t[b][0:NS, 0:512]), writes=[PSR(b), "kvnew"])
                banks.free(b)
                def kvout(e, l=l):
                    r = []
                    for bb in range(NB):
                        r.append(e.dma_start(out=o_sk[l][bb, 124:128, :], in_=kvnew[bb * 4:(bb + 1) * 4, 0:256]))
                        r.append(e.dma_start(out=o_sv[l][bb, 124:128, :], in_=kvnew[bb * 4:(bb + 1) * 4, 256:512]))
                    return r
                S.add("sp", kvout, reads=["kvnew"], writes=[("dram", "osv", l)], dsem="okv2", ndma=2 * NB)
                S.add("sp", lambda e, l=l: e.dma_start(out=o_sk[l][:, 0:124, :], in_=ck[l][:, 4:128, :]), dsem="d2d")
                S.add("sp", lambda e, l=l: e.dma_start(out=o_sv[l][:, 0:124, :], in_=cv[l][:, 4:128, :]), dsem="d2d")

            gv = gatesw[l][:].rearrange("p (g c n) -> p g c n", g=2, n=128)

            def K(name, st, cc):
                return ((name, st), cc)

            def convS(g):
                st = g % 2
                for cc in range(2):
                    c = g * 2 + cc
                    wcol = lambda k, c=c: sm[:, SM_CLW + k * 8 + c:SM_CLW + k * 8 + c + 1]
                    bcol = sm[:, SM_CLB + c:SM_CLB + c + 1]
                    xo = xcg[st]
                    S.add("dve", lambda e, c=c, cc=cc, xo=xo, wcol=wcol, bcol=bcol: e.tensor_scalar(xo[:, cc, 0:TP], XR[:, c, 3:3 + TP], wcol(3), bcol, ALU.mult, ALU.add),
                          reads=[("XR", c), SMR], writes=[K("xc", st, cc)])
                    for k in range(3):
                        S.add("dve", lambda e, c=c, cc=cc, k=k, xo=xo, wcol=wcol: e.scalar_tensor_tensor(out=xo[:, cc, 0:TP], in0=XR[:, c, k:k + TP], scalar=wcol(k),
                                                                                                       in1=xo[:, cc, 0:TP], op0=ALU.mult, op1=ALU.add),
                              reads=[("XR", c), SMR], writes=[K("xc", st, cc)])
                    if has_s:
                        xcs = xo[:, cc, TP:TP + NS].rearrange("p (b t) -> p b t", t=4)
                        S.add("dve", lambda e, c=c, wcol=wcol, bcol=bcol, xcs=xcs: e.tensor_scalar(xcs, XRs[:, c, :, 3:7], wcol(3), bcol, ALU.mult, ALU.add),
                              reads=[("XR", c), SMR], writes=[K("xc", st, cc)])
                        for k in range(3):
                            S.add("dve", lambda e, c=c, k=k, wcol=wcol, xcs=xcs: e.scalar_tensor_tensor(out=xcs, in0=XRs[:, c, :, k:k + 4], scalar=wcol(k),
                                                                                                      in1=xcs, op0=ALU.mult, op1=ALU.add),
                                  reads=[("XR", c), SMR], writes=[K("xc", st, cc)])
                    S.add("act", lambda e, cc=cc, xo=xo, st=st: e.copy(out=xcb[st][:, cc, 0:T], in_=xo[:, cc, 0:T]),
                          reads=[K("xc", st, cc)], writes=[K("xcb", st, cc)])

            def gatesS(g):
                st = g % 2
                for cc in range(2):
                    c = g * 2 + cc
                    for (c0, n) in subs:
                        for gi_, dst, dkey, bcolbase in ((0, rr[st], "rr", SM_BR), (1, ii[st], "ii", SM_BI)):
                            b = banks.alloc()
                            S.add("pe", lambda e, b=b, gi_=gi_, c=c, cc=cc, c0=c0, n=n, st=st: e.matmul(ps_t[b][:, 0:n], gv[:, gi_, c, :], xcb[st][:, cc, c0:c0 + n], start=True, stop=True),
                                  reads=[("gatesw", l), K("xcb", st, cc)], writes=[PSR(b)])
                            S.add("act", lambda e, b=b, dst=dst, c=c, cc=cc, c0=c0, n=n, bcolbase=bcolbase: e.activation(
                                out=dst[:, cc, c0:c0 + n], in_=ps_t[b][:, 0:n], func=AF.Sigmoid, bias=sm[:, bcolbase + c:bcolbase + c + 1], scale=1.0),
                                reads=[SMR], writes=[PSR(b), K(dkey, st, cc)])
                            banks.free(b)

            def expS(g):
                st = g % 2
                for cc in range(2):
                    c = g * 2 + cc
                    S.add("act", lambda e, c=c, cc=cc, st=st: e.activation(out=aa[st][:, cc, 0:T], in_=rr[st][:, cc, 0:T], func=AF.Exp, scale=lamc[l][:, c:c + 1]),
                          reads=[K("rr", st, cc), ("lamc", l)], writes=[K("aa", st, cc)])
                for cc in range(2):
                    S.add("dve", lambda e, cc=cc, st=st: e.scalar_tensor_tensor(out=rr[st][:, cc, 0:T], in0=aa[st][:, cc, 0:T], scalar=0.99999994,
                                                                                in1=aa[st][:, cc, 0:T], op0=ALU.min, op1=ALU.mult),
                          reads=[K("aa", st, cc)], writes=[K("rr", st, cc)])
                for cc in range(2):
                    S.add("act", lambda e, cc=cc, st=st: e.activation(out=rr[st][:, cc, 0:T], in_=rr[st][:, cc, 0:T], func=AF.Ln, scale=-1.0, bias=1.0),
                          writes=[K("rr", st, cc)])
                for cc in range(2):
                    S.add("act", lambda e, cc=cc, st=st: e.activation(out=rr[st][:, cc, 0:T], in_=rr[st][:, cc, 0:T], func=AF.Exp, scale=0.5),
                          writes=[K("rr", st, cc)])

            def dveS(g):
                st = g % 2
                xo = xcg[st]
                for cc in range(2):
                    c = g * 2 + cc
                    if p == 0:
                        S.add("dve", lambda e, cc=cc, st=st: e.memset(rr[st][:, cc, 0:1], 1.0), writes=[K("rr", st, cc)])
                    S.add("dve", lambda e, cc=cc, st=st, xo=xo: e.tensor_tensor(out=ii[st][:, cc, 0:T], in0=ii[st][:, cc, 0:T], in1=xo[:, cc, 0:T], op=ALU.mult),
                          reads=[K("xc", st, cc)], writes=[K("ii", st, cc)])
                    S.add("dve", lambda e, cc=cc, st=st: e.tensor_tensor(out=ii[st][:, cc, 0:T], in0=ii[st][:, cc, 0:T], in1=rr[st][:, cc, 0:T], op=ALU.mult),
                          reads=[K("rr", st, cc)], writes=[K("ii", st, cc)])
                    init = 0.0 if p == 0 else hstate[l][:, c:c + 1]
                    S.add("dve", lambda e, cc=cc, st=st, xo=xo, init=init: e.tensor_tensor_scan(out=xo[:, cc, 0:TP], data0=aa[st][:, cc, 0:TP], data1=ii[st][:, cc, 0:TP],
                                                                                                 initial=init, op0=ALU.mult, op1=ALU.add),
                          reads=[K("aa", st, cc), K("ii", st, cc), ("hstate", l)], writes=[K("xc", st, cc)])
                    if has_s:
                        aas = aa[st][:, cc, TP:TP + NS].rearrange("p (b t) -> p b t", t=4)
                        iis = ii[st][:, cc, TP:TP + NS].rearrange("p (b t) -> p b t", t=4)
                        S.add("dve", lambda e, c=c, aas=aas: e.tensor_tensor(out=tmp16[:], in0=aas[:, :, 0], in1=h0s[:, c, :], op=ALU.mult),
                              reads=[K("aa", st, cc), "h0s"], writes=["tmp16"])
                        S.add("dve", lambda e, iis=iis: e.tensor_tensor(out=iis[:, :, 0], in0=iis[:, :, 0], in1=tmp16[:], op=ALU.add),
                              reads=["tmp16"], writes=[K("ii", st, cc)])
                        S.add("dve", lambda e, aas=aas: e.memset(aas[:, :, 0], 0.0), writes=[K("aa", st, cc)])
                        S.add("dve", lambda e, cc=cc, st=st, xo=xo: e.tensor_tensor_scan(out=xo[:, cc, TP:TP + NS], data0=aa[st][:, cc, TP:TP + NS], data1=ii[st][:, cc, TP:TP + NS],
                                                                                         initial=0.0, op0=ALU.mult, op1=ALU.add),
                              reads=[K("aa", st, cc), K("ii", st, cc)], writes=[K("xc", st, cc)])
                    S.add("act", lambda e, c=c, cc=cc, xo=xo: e.copy(out=lro[:, c, 0:T], in_=xo[:, cc, 0:T]), reads=[K("xc", st, cc)], writes=[("lro", c)])
                XCK = [K("xc", st, cc) for cc in range(2)]
                S.add("dve", lambda e, g=g, xo=xo: e.tensor_copy(out=hstate[l][:, g * 2:(g + 1) * 2], in_=xo[:, :, TP - 1]), reads=XCK, writes=[("hstate", l)])
                if has_s:
                    S.add("dve", lambda e, g=g, xo=xo: e.tensor_copy(out=hs_last[:, g * 2:(g + 1) * 2, :],
                                                                     in_=xo[:, :, TP:TP + NS].rearrange("p c (b t) -> p c b t", t=4)[:, :, :, 3]),
                          reads=XCK, writes=["hs_last"])

            convS(0)
            gatesS(0)
            convS(1)
            expS(0)
            gatesS(1)
            dveS(0)
            convS(2)
            expS(1)
            gatesS(2)
            dveS(1)
            convS(3)
            expS(2)
            gatesS(3)
            dveS(2)
            expS(3)
            dveS(3)
            if has_s:
                transpose_out(lambda k: hs_last[:, k, :], [NB] * NCH,
                              [o_slh[l][:, k * 128:(k + 1) * 128] for k in range(NCH)], ["hs_last"], "slh")
                transpose_out(lambda k: cs_stage[:, k, :], [NB * 3] * NCH,
                              [o_slc[l][:, k * 128:(k + 1) * 128] for k in range(NCH)], ["cs_stage"], "slc")
            if last_pass:
                transpose_out(lambda k, l=l: hstate[l][:, :], [NCH], [o_plh[l].rearrange("(c p) -> c p", p=128)], [("hstate", l)], "plh")
                S.add("dve", lambda e, l=l: e.tensor_copy(out=pst[:, 0:24].rearrange("p (k c) -> p k c", c=NCH),
                                                          in_=convhist[l][:].rearrange("p c k -> p k c")),
                      reads=[("convhist", l)], writes=["pst"])
                transpose_out(lambda k: pst[:, 0:24], [24], [o_plc[l].rearrange("k (c p) -> (k c) p", p=128)], ["pst"], "plc")

            if has_s:
                S.add("sp", lambda e, l=l: e.dma_start(out=cstage[:], in_=ck[l].rearrange("b k d -> k b d")), writes=["cstage"], dsem="stin3")
                S.add("pool", lambda e, l=l: e.dma_start(out=Vs[:], in_=cv[l].rearrange("b k d -> k b d")), writes=["Vs"], dsem="vs_in")
                for kv in range(2):
                    def ev(b, c0, n, kv=kv):
                        S.add("act", lambda e: e.copy(out=kTs[:, kv, :, 128:132], in_=ps_t[b][:, 0:NS].rearrange("p (b t) -> p b t", t=4)),
                              writes=[PSR(b), "kTs"])
                    proj(wkv, wkv_res, kv * 128, h, HR, NCH, [(TP, NS)], ev)
                S.add("pool", lambda e, l=l: e.dma_start(out=Vn4[:], in_=o_sv[l][:, 124:128, :].rearrange("b t d -> t b d")),
                      reads=[("dram", "osv", l)], writes=["Vn4"], dsem="vn4")
                for kv in range(2):
                    S.add("dve", lambda e, kv=kv: e.tensor_copy(
                        out=qTs[:, kv, :, :].rearrange("p b (g t) -> p b g t", t=4),
                        in_=qT[:, kv * 4:kv * 4 + 4, TP:TP + NS].rearrange("p g (b t) -> p b g t", t=4)),
                        reads=[("qT", kv * 4 + g) for g in range(4)], writes=["qTs"])
            W.release()

            units = []
            for hh in range(8):
                kv = hh // 4
                grp = {}
                for j in range(4):
                    first_blk = (p == 0 and j == 0)
                    if first_blk:
                        kap = kT[:, kv, 128:256]
                        bias_ap = biasp[:, hh, 128:256]
                        vbl = [(Vt[:, 1, kv * 128:(kv + 1) * 128], 0, 128)]
                        NK = 128
                    else:
                        kap = kT[:, kv, j * 128:j * 128 + 256]
                        bias_ap = biasp[:, hh, :]
                        vbl = [(Vt[:, j, kv * 128:(kv + 1) * 128], 0, 128), (Vt[:, j + 1, kv * 128:(kv + 1) * 128], 128, 128)]
                        NK = 256

                    def evp(ob, hh=hh):
                        S.add("act", lambda e: e.copy(out=attn[:, hh, 0:TP], in_=ps_t[ob][:, 0:TP]), writes=[PSR(ob), ("attn", hh)])
                    units.append(dict(qap=qT[:, hh, j * 128:(j + 1) * 128], kap=kap, NQ=128, NK=NK, bias_ap=bias_ap, bias_key="biasp",
                                      sink_ap=sinks[l][:, hh:hh + 1], vblocks=vbl, rd=[("qT", hh), "kT", "Vt", ("sinks", l)],
                                      grp=grp, out_c0=j * 128, last=(j == 3), evac=evp))
            attn_waves(units)
            units = []
            if has_s:
                for bb in range(NB):
                    b = banks.alloc()

                    def tr(e, b=b, bb=bb):
                        r = None
                        for kv in range(2):
                            r = e.transpose(ps_t[b][:, kv * 128:(kv + 1) * 128], cstage[:, bb, kv * 128:(kv + 1) * 128], identf[:])
                        return r
                    S.add("pe", tr, reads=["cstage", "identf"], writes=[PSR(b)])
                    S.add("act", lambda e, b=b, bb=bb: e.copy(out=kTs[:, :, bb, 0:128], in_=ps_t[b][:, 0:256].rearrange("p (v k) -> p v k", k=128)),
                          writes=[PSR(b), "kTs"])
                    banks.free(b)
                for kv in range(2):
                    grp = {}
                    for bb in range(NB):
                        vbl = [(Vs[:, bb, kv * 128:(kv + 1) * 128], 0, 128), (Vn4[0:4, bb, kv * 128:(kv + 1) * 128], 128, 4)]

                        def evs(ob, kv=kv):
                            S.add("act", lambda e: e.copy(
                                out=attn[:, kv * 4:kv * 4 + 4, TP:TP + NS].rearrange("p g (b t) -> p b g t", t=4),
                                in_=ps_t[ob][:, 0:256].rearrange("p (b g t) -> p b g t", g=4, t=4)),
                                writes=[PSR(ob)] + [("attn", kv * 4 + g) for g in range(4)])
                        units.append(dict(qap=qTs[:, kv, bb, :], kap=kTs[:, kv, bb, :], NQ=16, NK=132, bias_ap=biass[:, kv, :], bias_key="biass",
                                          sink_ap=sinks[l][0:16, 8 + kv:9 + kv], vblocks=vbl, rd=["qTs", "kTs", "Vs", "Vn4", ("sinks", l)],
                                          grp=grp, out_c0=bb * 16, last=(bb == NB - 1), evac=evs))
            if units:
                attn_waves(units)

            LR = keys("lro", NCH)
            AR = keys("attn", 8)

            def mkmm(bk, wv_, src, col, c0, n):
                def mm(e):
                    r = None
                    for k in range(NCH):
                        r = e.matmul(ps_t[bk][:, 0:n], wv_[:, k, col:col + 128], src[:, k, c0:c0 + n], start=(k == 0), stop=(k == NCH - 1))
                    return r
                return mm
            for br, (wname, gbase, src, SR) in enumerate((("wlo", 5, lro, LR), ("wao", 7, attn, AR))):
                for hf in range(2):
                    wo_t, wo_res = W.get((p, wname, l, hf))
                    wg_t, wg_res = W.get((p, "win", l, gbase + hf))
                    v_o = wo_t[:].rearrange("p (k n) -> p k n", n=512)
                    v_g = wg_t[:].rearrange("p (k n) -> p k n", n=512)
                    for cc in range(4):
                        oc = hf * 4 + cc
                        for (c0, n) in subs:
                            gi = rot["sg"] % 2
                            rot["sg"] += 1
                            bO, bG = banks.alloc(), banks.alloc()
                            S.add("pe", mkmm(bG, v_g, h, cc * 128, c0, n), reads=[wg_res] + HR, writes=[PSR(bG)])
                            for k in range(NCH):
                                S.add("pe", lambda e, bO=bO, v_o=v_o, src=src, cc=cc, c0=c0, n=n, k=k: e.matmul(
                                    ps_t[bO][:, 0:n], v_o[:, k, cc * 128:cc * 128 + 128], src[:, k, c0:c0 + n], start=(k == 0), stop=(k == NCH - 1)),
                                    reads=[wo_res, SR[k]], writes=[PSR(bO)])
                            S.add("act", lambda e, gi=gi, bG=bG, n=n: e.activation(out=sgA[gi][:, 0:n], in_=ps_t[bG][:, 0:n], func=AF.Sigmoid),
                                  writes=[PSR(bG), ("sgA", gi)])
                            if br == 0:
                                S.add("dve", lambda e, gi=gi, bO=bO, oc=oc, c0=c0, n=n: e.tensor_tensor(out=merged[:, oc, c0:c0 + n], in0=sgA[gi][:, 0:n], in1=ps_t[bO][:, 0:n], op=ALU.mult),
                                      reads=[("sgA", gi)], writes=[PSR(bO), ("merged", oc)])
                            else:
                                S.add("dve", lambda e, gi=gi, bO=bO, n=n: e.tensor_tensor(out=t1b[gi][:, 0:n], in0=sgA[gi][:, 0:n], in1=ps_t[bO][:, 0:n], op=ALU.mult),
                                      reads=[("sgA", gi)], writes=[PSR(bO), ("t1b", gi)])
                                S.add("dve", lambda e, gi=gi, oc=oc, c0=c0, n=n: e.tensor_tensor(out=merged[:, oc, c0:c0 + n], in0=t1b[gi][:, 0:n], in1=merged[:, oc, c0:c0 + n], op=ALU.add),
                                      reads=[("t1b", gi)], writes=[("merged", oc)])
                            banks.free(bO)
                            banks.free(bG)
                    W.release()
                    W.release()
            MR = keys("merged", NCH)
            for hf in range(2):
                wt, wres = W.get((p, "wout", l, hf))
                wv = wt[:].rearrange("p (k n) -> p k n", n=512)
                for cc in range(4):
                    oc = hf * 4 + cc

                    def ev(b, c0, n, oc=oc):
                        S.add("act", lambda e: e.activation(out=sq[:, oc, c0:c0 + n], in_=ps_t[b][:, 0:n], func=AF.Square), writes=[PSR(b), ("sq", oc)])
                        S.add("act", lambda e: e.activation(out=mbuf[:, oc, c0:c0 + n], in_=ps_t[b][:, 0:n], func=AF.Identity,
                                                            scale=sm[:, SM_NMPOST + oc:SM_NMPOST + oc + 1]),
                              reads=[SMR], writes=[PSR(b), ("mbuf", oc)])
                    proj(wv, wres, cc * 128, merged, MR, NCH, subs, ev)
                W.release()
            postnorm_residual(l, T, subs, True)

            rmsnorm_to_h(l, SM_NFP, T, subs, squares_done=True)
            if has_s:
                for q4 in range(4):
                    S.add("sp", lambda e, l=l, q4=q4: e.dma_start(out=stf[:], in_=st_f[l][:, q4 * 2048:(q4 + 1) * 2048]),
                          writes=["stf"], dsem="stin2")
                    for g4 in range(4):
                        b = banks.alloc()

                        def tr(e, b=b, g4=g4):
                            r = None
                            for i4 in range(4):
                                jj = g4 * 4 + i4
                                r = e.transpose(ps_t[b][:, i4 * 32:(i4 + 1) * 32], stf[0:32, jj * 128:(jj + 1) * 128], identf[0:32, 0:32])
                            return r
                        S.add("pe", tr, reads=["stf", "identf"], writes=[PSR(b)])
                        j0 = q4 * 16 + g4 * 4
                        S.add("act", lambda e, b=b, j0=j0: e.copy(out=FH[:, j0:j0 + 4, :], in_=ps_t[b][:, 0:128].rearrange("p (j n) -> p j n", n=32)),
                              writes=[PSR(b), "FH"])
                        banks.free(b)
            pend = []
            pend_t = []
            for pb in range(8):
                wv_t, wv_res = W.get((p, "wup", l, pb))
                wg_t, wg_res = W.get((p, "wup", l, pb + 8))
                vv = wv_t[:].rearrange("p (k n) -> p k n", n=512)
                vg = wg_t[:].rearrange("p (k n) -> p k n", n=512)
                wsc = lambda k, jj: sm[:, SM_FCW + k * 64 + jj:SM_FCW + k * 64 + jj + 1]
                bsc = lambda jj: sm[:, SM_FCB + jj:SM_FCB + jj + 1]
                if has_s:
                    for half in range(2):
                        S.add("pool", lambda e, half=half, pb=pb: e.tensor_copy(
                            out=Us[half][:].rearrange("p c (b k) -> p c b k", k=6)[:, :, :, 0:2],
                            in_=FH[:, pb * 4 + 32 * half:pb * 4 + 32 * half + 4, :].rearrange("p c (b k) -> p c b k", k=2)),
                            reads=["FH"], writes=[("Us", half)])
                for cc in range(4):
                    j = pb * 4 + cc
                    ci = rot["u"] % 3
                    ui = ci
                    rot["u"] += 1
                    for (wview, wres_, Ub, Un, jj, cvb, cres) in ((vv, wv_res, Uv[ui], "Uv", j, cvv[ci], ("cvv", ci)),
                                                                 (vg, wg_res, Ug[ui], "Ug", 32 + j, cvg[ci], ("cvg", ci))):
                        UH, UB = (Un, ui, "h"), (Un, ui, "b")
                        if p == 0:
                            S.add("pool", lambda e, Ub=Ub: e.memset(Ub[:, 62:64], 0.0), writes=[UH])
                        else:
                            S.add("pool", lambda e, Ub=Ub, jj=jj: e.tensor_copy(out=Ub[:, 62:64], in_=ffnhist[l][:, jj, :]),
                                  reads=[("ffnhist", l)], writes=[UH])

                        def ev(b, c0, n, Ub=Ub, UB=UB, cvb=cvb, cres=cres, jj=jj):
                            S.add("act", lambda e: e.copy(out=Ub[:, 64:64 + TP], in_=ps_t[b][:, 0:TP]), writes=[PSR(b), UB])
                            S.add("act", lambda e: e.activation(out=cvb[:, 0:TP], in_=ps_t[b][:, 0:TP], func=AF.Identity, scale=wsc(2, jj), bias=bsc(jj)),
                                  reads=[SMR], writes=[PSR(b), cres])
                        proj(wview, wres_, cc * 128, h, HR, NCH, [(0, TP)], ev)
                        S.add("pool", lambda e, Ub=Ub, jj=jj: e.tensor_copy(out=ffnhist[l][:, jj, :], in_=Ub[:, 62 + TP:64 + TP]),
                              reads=[UB], writes=[("ffnhist", l)])
                        for k in range(2):
                            S.add("dve", lambda e, Ub=Ub, cvb=cvb, jj=jj, k=k: e.scalar_tensor_tensor(out=cvb[:, 0:TP], in0=Ub[:, 62 + k:62 + k + TP], scalar=wsc(k, jj),
                                                                                                      in1=cvb[:, 0:TP], op0=ALU.mult, op1=ALU.add),
                                  reads=[UH, UB, SMR], writes=[cres])

                    def tail(ci=ci, j=j):
                        S.add("act", lambda e: e.activation(out=ggb[ci][:, 0:TP], in_=cvg[ci][:, 0:TP], func=AF.Gelu_apprx_tanh),
                              reads=[("cvg", ci)], writes=[("gg", ci)])
                        S.add("dve", lambda e: e.tensor_tensor(out=A[:, j, 0:TP], in0=cvv[ci][:, 0:TP], in1=ggb[ci][:, 0:TP], op=ALU.mult),
                              reads=[("cvv", ci), ("gg", ci)], writes=[("A", j)])
                    if pend:
                        pend.pop()()
                    pend.append(tail)
                if has_s and pend_t:
                    pend_t.pop(0)()
                if has_s:
                    for half, (wview, wres_) in enumerate(((vv, wv_res), (vg, wg_res))):
                        jj0 = pb * 4 + 32 * half
                        Uh = Us[half][:].rearrange("p c (b k) -> p c b k", k=6)
                        b = banks.alloc()

                        def mm(e, b=b, wview=wview):
                            r = None
                            for c4 in range(4):
                                for k in range(NCH):
                                    r = e.matmul(ps_t[b][:, c4 * NS:(c4 + 1) * NS], wview[:, k, c4 * 128:(c4 + 1) * 128], h[:, k, TP:TP + NS],
                                                 start=(k == 0), stop=(k == NCH - 1))
                            return r
                        S.add("pe", mm, reads=[wres_] + HR, writes=[PSR(b)])
                        S.add("act", lambda e, Uh=Uh, b=b: e.copy(out=Uh[:, :, :, 2:6], in_=ps_t[b][:, 0:4 * NS].rearrange("p (c b t) -> p c b t", c=4, t=4)),
                              writes=[PSR(b), ("Us", half)])
                        for c4 in range(4):
                            S.add("act", lambda e, b=b, c4=c4, half=half, jj0=jj0: e.activation(out=cvs[half][:, c4, :], in_=ps_t[b][:, c4 * NS:(c4 + 1) * NS],
                                                                                               func=AF.Identity, scale=wsc(2, jj0 + c4), bias=bsc(jj0 + c4)),
                                  reads=[SMR], writes=[PSR(b), ("cvs", half)])
                        banks.free(b)
                        S.add("pool", lambda e, Uh=Uh, half=half, pb=pb: e.tensor_copy(
                            out=SFs[pb % 2][:, half * 4:(half + 1) * 4, :].rearrange("p c (b k) -> p c b k", k=2), in_=Uh[:, :, :, 4:6]),
                            reads=[("Us", half)], writes=[("SFs", pb % 2)])
                        for c4 in range(4):
                            cv4 = cvs[half][:, c4, :].rearrange("p (b t) -> p b t", t=4)
                            for k in range(2):
                                S.add("dve", lambda e, Uh=Uh, cv4=cv4, c4=c4, k=k, jj0=jj0: e.scalar_tensor_tensor(out=cv4, in0=Uh[:, c4, :, k:k + 4], scalar=wsc(k, jj0 + c4),
                                                                                                                 in1=cv4, op0=ALU.mult, op1=ALU.add),
                                      reads=[("Us", half), SMR], writes=[("cvs", half)])
                    S.add("act", lambda e: e.activation(out=cvs[1][:], in_=cvs[1][:], func=AF.Gelu_apprx_tanh), writes=[("cvs", 1)])
                    S.add("dve", lambda e, pb=pb: e.tensor_tensor(out=A[:, pb * 4:(pb + 1) * 4, TP:TP + NS], in0=cvs[0][:], in1=cvs[1][:], op=ALU.mult),
                          reads=[("cvs", 0), ("cvs", 1)], writes=[("A", pb * 4 + i) for i in range(4)])
                W.release()
                W.release()
                if has_s:
                    def tout(pb=pb):
                        jl = [pb * 4 + i for i in range(4)] + [32 + pb * 4 + i for i in range(4)]
                        transpose_out(lambda k, pb=pb: SFs[pb % 2][:, k, :], [NB * 2] * 8,
                                      [o_sf[l][:, jj * 128:(jj + 1) * 128] for jj in jl], [("SFs", pb % 2)], "sf")
                    pend_t.append(tout)
            while pend_t:
                pend_t.pop(0)()
            if last_pass:
                S.add("dve", lambda e, l=l: e.tensor_copy(out=pst[:, 0:128].rearrange("p (k c) -> p k c", c=64),
                                                          in_=ffnhist[l][:].rearrange("p c k -> p k c")),
                      reads=[("ffnhist", l)], writes=["pst"])
                transpose_out(lambda k: pst[:, 0:128], [128], [o_pf[l].rearrange("k (c p) -> (k c) p", p=128)], ["pst"], "pf")
            if pend:
                pend.pop()()
            if l == DEPTH - 1 and p + 1 < NPASS:
                S.add("sp", lambda e, p=p: e.dma_start(out=tokx[:], in_=xp[(p + 1) * TP:(p + 2) * TP, :].rearrange("(j r) d -> r j d", r=128)),
                      writes=["tokx"], dsem="xin")
            ARs = keys("A", 32)
            for oc in range(8):
                wt, wres = W.get((p, "wdn", l, oc))
                wv = wt[:].rearrange("p (k n) -> p k n", n=128)

                def ev(b, c0, n, oc=oc):
                    S.add("act", lambda e: e.activation(out=sq[:, oc, c0:c0 + n], in_=ps_t[b][:, 0:n], func=AF.Square), writes=[PSR(b), ("sq", oc)])
                    S.add("act", lambda e: e.activation(out=mbuf[:, oc, c0:c0 + n], in_=ps_t[b][:, 0:n], func=AF.Identity,
                                                        scale=sm[:, SM_NFPOST + oc:SM_NFPOST + oc + 1]),
                          reads=[SMR], writes=[PSR(b), ("mbuf", oc)])
                proj(wv, wres, 0, A, ARs, 32, subs, ev)
                W.release()
            postnorm_residual(l, T, subs, l + 1 < DEPTH)

    def finish_pass(p, has_s):
        for j in range(4):
            for half in range(2):
                b = banks.alloc()

                def tr(e, b=b, j=j, half=half):
                    r = None
                    for c4 in range(4):
                        c = half * 4 + c4
                        r = e.transpose(ps_t[b][:, c4 * 128:(c4 + 1) * 128], x[:, c, j * 128:(j + 1) * 128], identf[:])
                    return r
                S.add("pe", tr, reads=XK + ["identf"], writes=[PSR(b)])
                S.add("act", lambda e, b=b, j=j, half=half: e.copy(out=tok[:, j, half * 512:(half + 1) * 512], in_=ps_t[b][:, 0:512]),
                      writes=[PSR(b), "tok"])
                banks.free(b)
        S.add("sp", lambda e, p=p: e.dma_start(out=yp[p * TP:(p + 1) * TP, :].rearrange("(j r) d -> r j d", r=128), in_=tok[:]),
              reads=["tok"], dsem="yout")
        if has_s:
            for half in range(2):
                b = banks.alloc()

                def tr(e, b=b, half=half):
                    r = None
                    for c4 in range(4):
                        c = half * 4 + c4
                        r = e.transpose(ps_t[b][0:NS, c4 * 128:(c4 + 1) * 128], x[:, c, TP:TP + NS], identf[:])
                    return r
                S.add("pe", tr, reads=XK + ["identf"], writes=[PSR(b)])
                S.add("act", lambda e, b=b, half=half: e.copy(out=toks[0:NS, half * 512:(half + 1) * 512], in_=ps_t[b][0:NS, 0:512]),
                      writes=[PSR(b), "toks"])
                banks.free(b)
            S.add("sp", lambda e: e.dma_start(out=ys, in_=toks[0:NS, :]), reads=["toks"], dsem="yout2")

    for p in range(NPASS):
        do_pass(p)
    assert W.pos == len(seq)
    S.emit()
    return nc


def _t5_bucket(d):
    n = max(d, 0)
    if n < 16:
        return n
    import math
    nf = np.float32(max(n, 1))
    v = np.float32(np.log(nf / np.float32(16)) / np.float32(math.log(128 / 16))) * np.float32(16)
    return min(16 + int(v), 31)


_NC_CACHE = {}


def kernel(x_prompt, x_sample, state_lru_h, state_lru_conv, cache_win_k, cache_win_v, state_ffn_conv,
           norm_mix_pre, norm_mix_post, norm_ffn_pre, norm_ffn_post, w_in, conv_lru_w, conv_lru_b,
           lru_wr, lru_br, lru_wi, lru_bi, lru_lambda, w_lru_o, w_attn_o, w_out, attn_sink, rel_bias,
           w_up, ffn_conv_w, ffn_conv_b, w_down):
    f32 = lambda a: np.ascontiguousarray(np.asarray(a, dtype=np.float32))
    x_prompt, x_sample = f32(x_prompt), f32(x_sample)
    n = 8

    def blk(w, nb, ncols):
        L = w.shape[0]
        kc = w.shape[1] // 128
        return f32(np.asarray(w).reshape(L, kc, 128, nb, ncols).transpose(0, 3, 2, 1, 4).reshape(L * nb, 128, kc * ncols))

    win_r = blk(f32(w_in), 9, 512)
    wlo_r = blk(f32(w_lru_o), 2, 512)
    wao_r = blk(f32(w_attn_o), 2, 512)
    wout_r = blk(f32(w_out), 2, 512)
    wup_r = blk(f32(w_up), 16, 512)
    wdn_r = blk(f32(w_down), 8, 128)
    gates_r = f32(np.stack([f32(lru_wr), f32(lru_wi)], axis=1).transpose(0, 3, 1, 2, 4).reshape(DEPTH, 128, 2048))

    def pc(v):
        v = f32(v)
        return v.reshape(v.shape[0], -1, 128).transpose(0, 2, 1)

    def pck(v):
        v = f32(v)
        L, K = v.shape[0], v.shape[1]
        return v.reshape(L, K, -1, 128).transpose(0, 3, 1, 2).reshape(L, 128, -1)

    smalls = f32(np.concatenate([pc(norm_mix_pre), pc(norm_mix_post), pc(norm_ffn_pre), pc(norm_ffn_post),
                                 pck(conv_lru_w), pc(conv_lru_b), pc(lru_br), pc(lru_bi), pc(lru_lambda),
                                 pck(ffn_conv_w), pc(ffn_conv_b)], axis=2))
    assert smalls.shape == (DEPTH, 128, SM_N), smalls.shape
    sk = f32(attn_sink)
    sinks = np.zeros((DEPTH, 128, 10), np.float32)
    sinks[:, :, 0:8] = sk[:, None, :]
    for kv in range(2):
        for g in range(4):
            sinks[:, g * 4:(g + 1) * 4, 8 + kv] = sk[:, kv * 4 + g][:, None]
    rb = f32(rel_bias)
    bidx = np.array([_t5_bucket(d) for d in range(128)])
    qq = np.arange(128)[:, None]
    jj = np.arange(256)[None, :]
    dd = qq + 128 - jj
    valid = (dd >= 0) & (dd < 128)
    gat = rb[bidx[np.clip(dd, 0, 127)]]
    biasp = np.where(valid[:, :, None], gat, np.float32(NEG)).transpose(0, 2, 1)
    biasp = f32(biasp).reshape(128, 8 * 256)
    tt = np.arange(4)[:, None]
    js = np.arange(132)[None, :]
    ds = tt + 128 - js
    vs = (ds >= 0) & (ds < 128)
    gs = rb[bidx[np.clip(ds, 0, 127)]]
    bs = np.where(vs[:, :, None], gs, np.float32(NEG))
    biass = np.zeros((16, 2, 132), np.float32)
    for kv in range(2):
        for g in range(4):
            biass[g * 4:(g + 1) * 4, kv, :] = bs[:, :, kv * 4 + g]
    biass = f32(biass).reshape(16, 2 * 132)
    ident = np.eye(128, dtype=np.float32)

    st_h, st_c, ckk, cvv, st_f = f32(state_lru_h), f32(state_lru_conv), f32(cache_win_k), f32(cache_win_v), f32(state_ffn_conv)
    in_maps = []
    for i in range(n):
        sl = slice(i * NB, (i + 1) * NB)
        in_maps.append({
            "xp": x_prompt[i], "xs": f32(x_sample[sl].reshape(NS, D)),
            "st_h": f32(st_h[:, sl]), "st_c": f32(st_c[:, sl].reshape(DEPTH, NB * 3, D)),
            "ck": f32(ckk[:, sl].reshape(DEPTH, NB, 128, 256)), "cv": f32(cvv[:, sl].reshape(DEPTH, NB, 128, 256)),
            "st_f": f32(st_f[:, sl].reshape(DEPTH, NB * 2, 8192)),
            "win_r": win_r, "gates_r": gates_r, "wlo_r": wlo_r, "wao_r": wao_r, "wout_r": wout_r, "wup_r": wup_r, "wdn_r": wdn_r,
            "smalls": smalls, "sinks": sinks, "biasp": biasp, "biass": biass, "ident": ident,
        })
    if "nc" not in _NC_CACHE:
        _NC_CACHE["nc"] = build()
    nc = _NC_CACHE["nc"]
    res = run_bass_kernel_spmd(nc, in_maps, core_ids=list(range(n)))
    R = res.results
    y_prompt = np.stack([R[i]["yp"] for i in range(n)], axis=0)
    y_sample = np.concatenate([R[i]["ys"].reshape(NB, 4, D) for i in range(n)], axis=0)
    p_lru_h = np.stack([R[i]["o_plh"] for i in range(n)], axis=1)
    p_lru_conv = np.stack([R[i]["o_plc"] for i in range(n)], axis=1)
    p_win_k = np.stack([R[i]["o_pk"].reshape(DEPTH, 128, 2, 128) for i in range(n)], axis=1)
    p_win_v = np.stack([R[i]["o_pv"].reshape(DEPTH, 128, 2, 128) for i in range(n)], axis=1)
    p_ffn = np.stack([R[i]["o_pf"] for i in range(n)], axis=1)
    s_lru_h = np.concatenate([R[i]["o_slh"] for i in range(n)], axis=1)
    s_lru_conv = np.concatenate([R[i]["o_slc"].reshape(DEPTH, NB, 3, D) for i in range(n)], axis=1)
    s_win_k = np.concatenate([R[i]["o_sk"].reshape(DEPTH, NB, 128, 2, 128) for i in range(n)], axis=1)
    s_win_v = np.concatenate([R[i]["o_sv"].reshape(DEPTH, NB, 128, 2, 128) for i in range(n)], axis=1)
    s_ffn = np.concatenate([R[i]["o_sf"].reshape(DEPTH, NB, 2, 8192) for i in range(n)], axis=1)
    outs = (y_prompt, y_sample, p_lru_h, p_lru_conv, p_win_k, p_win_v, p_ffn, s_lru_h, s_lru_conv, s_win_k, s_win_v, s_ffn)
    return tuple(np.ascontiguousarray(o, dtype=np.float32) for o in outs)
```

```python
import numpy as np
import concourse.bass as bass
import concourse.mybir as mybir
from concourse.bass_utils import run_bass_kernel_spmd

F32 = mybir.dt.float32
BF16 = mybir.dt.bfloat16
AF = mybir.ActivationFunctionType
ALU = mybir.AluOpType
AX = mybir.AxisListType

D = 1024
NCH = 8
SEQ = 2048
TP = 512
NPASS = SEQ // TP
NB = 16
NS = 64
DEPTH = 2
NEG = -30000.0
EPS = 1e-6
QSCALE = 128 ** -0.5
NSLOT = 5
GR = 256
SB_BASE = 16512
SB_TOP = 229344

SM_NMP, SM_NMPOST, SM_NFP, SM_NFPOST = 0, 8, 16, 24
SM_CLW = 32
SM_CLB = 64
SM_BR, SM_BI, SM_LAM = 72, 80, 88
SM_FCW = 96
SM_FCB = 288
SM_N = 352


class _Op:
    __slots__ = ("id", "eng", "fn", "deps", "signals", "dsem", "sidx", "ninst")


class Sched:
    ENGS = ("pe", "act", "dve", "pool", "sp")

    def __init__(self, nc):
        self.nc = nc
        self.ops = []
        self.last_writer = {}
        self.readers = {}
        self.dma_sems = {}
        self.alias = {}

    def reg(self, key, off, nbytes):
        g0 = off // GR
        g1 = (off + nbytes + GR - 1) // GR
        self.alias[key] = [("g", g) for g in range(g0, g1)]

    def _expand(self, keys):
        out = []
        for k in keys:
            a = self.alias.get(k)
            if a is None:
                assert isinstance(k, tuple) and k[0] in ("ps", "w", "dram"), ("unregistered key", k)
                out.append(k)
            else:
                out.extend(a)
        return out

    def add(self, eng, fn, reads=(), writes=(), dsem=None, ndma=1):
        reads = self._expand(reads)
        writes = self._expand(writes)
        deps = set()
        for r in reads:
            lw = self.last_writer.get(r)
            if lw is not None:
                deps.add(lw)
        for w in writes:
            lw = self.last_writer.get(w)
            if lw is not None:
                deps.add(lw)
            for rd in self.readers.get(w, ()):
                deps.add(rd)
        op = _Op()
        op.id = len(self.ops)
        op.eng = eng
        op.fn = fn
        op.deps = deps
        op.dsem = dsem
        op.signals = dsem is not None
        op.sidx = None
        op.ninst = ndma
        deps.discard(op.id)
        self.ops.append(op)
        for r in reads:
            self.readers.setdefault(r, []).append(op.id)
        for w in writes:
            self.last_writer[w] = op.id
            self.readers[w] = []
        return op.id

    def emit(self):
        nc = self.nc
        ops = self.ops
        for op in ops:
            for d in op.deps:
                p = ops[d]
                if p.eng == "pe" and op.eng == "pe" and p.dsem is None:
                    continue
                p.signals = True
        esem = {e: nc.alloc_semaphore("s_" + e) for e in ("pe", "act", "dve", "pool")}
        ecount = {e: 0 for e in esem}
        dcount = {}
        for op in ops:
            if op.dsem is not None:
                if op.dsem not in self.dma_sems:
                    self.dma_sems[op.dsem] = nc.alloc_semaphore("d_%d" % len(self.dma_sems))
                    dcount[op.dsem] = 0
                dcount[op.dsem] += 16 * op.ninst
                op.sidx = (self.dma_sems[op.dsem], dcount[op.dsem])
            elif op.signals:
                ecount[op.eng] += 1
                op.sidx = (esem[op.eng], ecount[op.eng])
        final_waits = {k: (self.dma_sems[k], v) for k, v in dcount.items()}
        by_eng = {e: [op for op in ops if op.eng == e] for e in self.ENGS}

        def run(engine, ename):
            waited = {}
            for op in by_eng[ename]:
                need = {}
                for d in op.deps:
                    p = ops[d]
                    if p.eng == "pe" and ename == "pe" and p.dsem is None:
                        continue
                    sem, val = p.sidx
                    k = id(sem)
                    if waited.get(k, 0) >= val:
                        continue
                    if k not in need or need[k][1] < val:
                        need[k] = (sem, val)
                for k, (sem, val) in need.items():
                    engine.wait_ge(sem, val)
                    waited[k] = val
                r = op.fn(engine)
                insts = r if isinstance(r, (list, tuple)) else [r]
                if op.dsem is not None:
                    assert len(insts) == op.ninst, (len(insts), op.ninst)
                    for i in insts:
                        i.then_inc(op.sidx[0], 16)
                elif op.signals:
                    insts[-1].then_inc(op.sidx[0], 1)
            if ename == "sp":
                for k, (sem, val) in final_waits.items():
                    engine.wait_ge(sem, val)

        with nc.Block() as block:
            @block.sync
            def _(e):
                run(e, "sp")

            @block.gpsimd
            def _(e):
                run(e, "pool")

            @block.tensor
            def _(e):
                run(e, "pe")

            @block.scalar
            def _(e):
                run(e, "act")

            @block.vector
            def _(e):
                run(e, "dve")


class Banks:
    def __init__(self, tensors):
        self.t = tensors
        self.free_list = list(range(len(tensors)))

    def alloc(self):
        assert self.free_list, "out of PSUM banks"
        return self.free_list.pop(0)

    def free(self, b):
        assert b not in self.free_list
        self.free_list.append(b)


class WStream:
    def __init__(self, S, nc, seq, slots):
        self.S = S
        self.seq = seq
        self.slots = slots
        self.next_dma = 0
        self.released = 0
        self.pos = 0
        self._pump()

    def _pump(self):
        while self.next_dma < len(self.seq) and self.next_dma - NSLOT < self.released:
            n = self.next_dma
            key, src, ncol = self.seq[n]
            sl = n % NSLOT
            dst = self.slots[sl]
            nd = ncol // 2048

            def fn(e, src=src, dst=dst, nd=nd):
                return [e.dma_start(out=dst[:, i * 2048:(i + 1) * 2048], in_=src[:, i * 2048:(i + 1) * 2048])
                        for i in range(nd)]
            self.S.add("pool", fn, writes=[("w", sl)], dsem=("w", sl), ndma=nd)
            self.next_dma += 1

    def get(self, key):
        i = self.pos
        assert self.seq[i][0] == key, (self.seq[i][0], key)
        self.pos += 1
        self._pump()
        assert i < self.next_dma
        sl = i % NSLOT
        return self.slots[sl], ("w", sl)

    def release(self):
        self.released += 1
        self._pump()


def layer_block_keys(l):
    ks = [("win", l, 0), ("win", l, 1), ("win", l, 2), ("win", l, 3), ("win", l, 4)]
    ks += [("wlo", l, 0), ("win", l, 5), ("wlo", l, 1), ("win", l, 6)]
    ks += [("wao", l, 0), ("win", l, 7), ("wao", l, 1), ("win", l, 8)]
    ks += [("wout", l, 0), ("wout", l, 1)]
    for pb in range(8):
        ks += [("wup", l, pb), ("wup", l, pb + 8)]
    ks += [("wdn", l, oc) for oc in range(8)]
    return ks


def build():
    nc = bass.Bass("TRN2", target_bir_lowering=False)

    def din(name, shape):
        return nc.dram_tensor(name, list(shape), F32, kind="ExternalInput").ap()

    def dout(name, shape):
        return nc.dram_tensor(name, list(shape), F32, kind="ExternalOutput").ap()

    xp = din("xp", [SEQ, D])
    xs = din("xs", [NS, D])
    st_h = din("st_h", [DEPTH, NB, D])
    st_c = din("st_c", [DEPTH, NB * 3, D])
    ck = din("ck", [DEPTH, NB, 128, 256])
    cv = din("cv", [DEPTH, NB, 128, 256])
    st_f = din("st_f", [DEPTH, NB * 2, 8192])
    win_r = din("win_r", [DEPTH * 9, 128, 4096])
    gates_r = din("gates_r", [DEPTH, 128, 2048])
    wlo_r = din("wlo_r", [DEPTH * 2, 128, 4096])
    wao_r = din("wao_r", [DEPTH * 2, 128, 4096])
    wout_r = din("wout_r", [DEPTH * 2, 128, 4096])
    wup_r = din("wup_r", [DEPTH * 16, 128, 4096])
    wdn_r = din("wdn_r", [DEPTH * 8, 128, 4096])
    smalls_d = din("smalls", [DEPTH, 128, SM_N])
    sinks_d = din("sinks", [DEPTH, 128, 10])
    biasp_d = din("biasp", [128, 8 * 256])
    biass_d = din("biass", [16, 2 * 132])
    ident_d = din("ident", [128, 128])

    yp = dout("yp", [SEQ, D])
    ys = dout("ys", [NS, D])
    o_plh = dout("o_plh", [DEPTH, D])
    o_plc = dout("o_plc", [DEPTH, 3, D])
    o_pk = dout("o_pk", [DEPTH, 128, 256])
    o_pv = dout("o_pv", [DEPTH, 128, 256])
    o_pf = dout("o_pf", [DEPTH, 2, 8192])
    o_slh = dout("o_slh", [DEPTH, NB, D])
    o_slc = dout("o_slc", [DEPTH, NB * 3, D])
    o_sk = dout("o_sk", [DEPTH, NB, 128, 256])
    o_sv = dout("o_sv", [DEPTH, NB, 128, 256])
    o_sf = dout("o_sf", [DEPTH, NB * 2, 8192])

    S = Sched(nc)
    TMAX = TP + NS
    XRW = 3 + TP + NB * 7
    UW = 2 + TP + NB * 6

    def esz(dt):
        return 2 if dt == BF16 else 4

    def mk(name, shape, dt, off, key=None, chunks=None):
        assert off % 32 == 0
        nbytes = int(np.prod(shape[1:])) * esz(dt)
        assert SB_BASE + off + nbytes <= SB_TOP, (name, off, nbytes)
        t = nc.alloc_sbuf_tensor_at(name, list(shape), dt, offset=SB_BASE + off)
        key = key or name
        S.reg(key, off, nbytes)
        if chunks:
            cb = nbytes // chunks
            for c in range(chunks):
                S.reg((key, c), off + c * cb, cb)
        return t

    cur = [0]

    def P(name, shape, dt=F32, chunks=None, key=None):
        nbytes = int(np.prod(shape[1:])) * esz(dt)
        off = cur[0]
        cur[0] += (nbytes + GR - 1) // GR * GR
        return mk(name, shape, dt, off, key=key, chunks=chunks)

    identf = P("identf", [128, 128])
    identb = P("identb", [128, 128], BF16)
    ones_b = P("ones_b", [128, 128], BF16)
    epsc = P("epsc", [128, 1])
    smalls = [P("smalls%d" % l, [128, SM_N], key=("smalls", l)) for l in range(DEPTH)]
    lamc = [P("lamc%d" % l, [128, 16], key=("lamc", l)) for l in range(DEPTH)]
    sinks = [P("sinks%d" % l, [128, 10], key=("sinks", l)) for l in range(DEPTH)]
    biasp = P("biasp", [128, 8, 256])
    biass = P("biass", [16, 2, 132])
    convhist = [P("convhist%d" % l, [128, NCH, 3], key=("convhist", l)) for l in range(DEPTH)]
    hstate = [P("hstate%d" % l, [128, NCH], key=("hstate", l)) for l in range(DEPTH)]
    ffnhist = [P("ffnhist%d" % l, [128, 64, 2], key=("ffnhist", l)) for l in range(DEPTH)]
    khist = [P("khist%d" % l, [128, 2, 128], BF16, key=("khist", l)) for l in range(DEPTH)]
    vhist = [P("vhist%d" % l, [128, 256], BF16, key=("vhist", l)) for l in range(DEPTH)]
    gatesw = [P("gatesw%d" % l, [128, 2048], BF16, key=("gatesw", l)) for l in range(DEPTH)]
    x = P("x", [128, NCH, TMAX], chunks=NCH)
    h = P("h", [128, NCH, TMAX], BF16, chunks=NCH)
    rstd = P("rstd", [128, TMAX])
    sdv = P("sdv", [128, TMAX])
    stout = P("stout", [128, 1024])
    SFs = [P("SFs%d" % i, [128, 8, NB * 2], key=("SFs", i)) for i in range(2)]
    pst = P("pst", [128, 128])
    h0s = P("h0s", [128, NCH, NB])
    hs_last = P("hs_last", [128, NCH, NB])
    cs_stage = P("cs_stage", [128, NCH, NB * 3])
    tmp16 = P("tmp16", [128, NB])
    qTs = P("qTs", [128, 2, NB, 16], BF16)
    wslots = [P("wslot%d" % i, [128, 4096], BF16) for i in range(NSLOT)]
    SCR = cur[0]
    RA = SCR
    RB = SCR + 36864
    assert SB_BASE + RB + 61696 <= SB_TOP, (SCR, SB_TOP - SB_BASE)

    lro = mk("lro", [128, NCH, TMAX], BF16, RA + 0, chunks=NCH)
    qT = mk("qT", [128, 8, TMAX], BF16, RA + 9216, chunks=8)
    merged = mk("merged", [128, NCH, TMAX], BF16, RA + 9216, chunks=NCH)
    attn = mk("attn", [128, 8, TMAX], BF16, RA + 18432, chunks=8)
    kT = mk("kT", [128, 2, 128 + TP], BF16, RA + 27648)
    Vt = mk("Vt", [128, 5, 256], BF16, RA + 30208)
    kvnew = mk("kvnew", [64, 512], F32, RA + 32768)
    kvlast = mk("kvlast", [128, 512], F32, RA + 34816)
    A = mk("A", [128, 32, TMAX], BF16, RA + 0, chunks=32)
    sq = mk("sq", [128, NCH, TMAX], BF16, RB + 0, chunks=NCH)
    tok = mk("tok", [128, 4, D], F32, RB + 36864)
    tokx = mk("tokx", [128, 4, D], F32, RB + 20480)
    toks = mk("toks", [64, D], F32, RB + 16384)
    XR = mk("XR", [128, NCH, XRW], F32, RB + 0, chunks=NCH)
    LS = 20736
    xcg = [mk("xc%d" % s_, [128, 2, TMAX], F32, RB + 20224 + s_ * LS, key=("xc", s_), chunks=2) for s_ in range(2)]
    xcb = [mk("xcb%d" % s_, [128, 2, TMAX], BF16, RB + 20224 + s_ * LS + 4608, key=("xcb", s_), chunks=2) for s_ in range(2)]
    rr = [mk("rr%d" % s_, [128, 2, TMAX], F32, RB + 20224 + s_ * LS + 6912, key=("rr", s_), chunks=2) for s_ in range(2)]
    ii = [mk("ii%d" % s_, [128, 2, TMAX], F32, RB + 20224 + s_ * LS + 11520, key=("ii", s_), chunks=2) for s_ in range(2)]
    aa = [mk("aa%d" % s_, [128, 2, TMAX], F32, RB + 20224 + s_ * LS + 16128, key=("aa", s_), chunks=2) for s_ in range(2)]
    kTs = mk("kTs", [128, 2, NB, 132], BF16, RB + 0)
    Vs = mk("Vs", [128, NB, 256], BF16, RB + 8448)
    Vn4 = mk("Vn4", [4, NB, 256], BF16, RB + 16640)
    cstage = mk("cstage", [128, NB, 256], F32, RB + 24832)
    o3 = RB + 41216
    NSET = 8
    SBW = 64 + 260
    Sbuf, Pn, PT, stat = [], [], [], []
    for i in range(NSET):
        o = o3 + i * 2816
        Sbuf.append(mk("Sbuf%d" % i, [128, SBW], F32, o, key=("Sb", i)))
        S.reg(("Sb", i, "h"), o, 256)
        S.reg(("Sb", i, "b"), o + 256, 4 * 260)
        Pn.append(mk("Pn%d" % i, [128, 256], BF16, o + 1536, key=("Pn", i)))
        PT.append(mk("PT%d" % i, [128, 2, 128], BF16, o + 2048, key=("PT", i)))
        stat.append(mk("stat%d" % i, [128, 4], F32, o + 2560, key=("st", i)))
    sgA = [mk("sgA%d" % i, [128, TP], F32, RB + 9216 + i * 2048, key=("sgA", i)) for i in range(2)]
    sgB = [mk("sgB%d" % i, [128, TP], F32, RB + 13312 + i * 2048, key=("sgB", i)) for i in range(2)]
    t1b = [mk("t1b%d" % i, [128, TP], F32, RB + 17408 + i * 2048, key=("t1b", i)) for i in range(2)]
    mbuf = mk("mbuf", [128, NCH, TMAX], F32, RB + 40448, chunks=NCH)
    def mkU(name, i, off):
        t = mk("%s%d" % (name, i), [128, 64 + TP], F32, off, key=(name, i))
        S.reg((name, i, "h"), off, 256)
        S.reg((name, i, "b"), off + 256, 4 * TP)
        return t
    Uv = [mkU("Uv", 0, RB + 9216), mkU("Uv", 1, RB + 9216 + 2560), mkU("Uv", 2, RB + 41472)]
    Ug = [mkU("Ug", 0, RB + 14336), mkU("Ug", 1, RB + 14336 + 2560), mkU("Ug", 2, RB + 41472 + 2304)]
    cvv = [mk("cvv%d" % i, [128, TMAX], F32, RB + 19456 + i * 2304, key=("cvv", i)) for i in range(2)]
    cvg = [mk("cvg%d" % i, [128, TMAX], F32, RB + 24064 + i * 2304, key=("cvg", i)) for i in range(2)]
    ggb = [mk("ggb%d" % i, [128, TMAX], F32, RB + 28672 + i * 2304, key=("gg", i)) for i in range(2)]
    cvv.append(mk("cvv2", [128, TMAX], F32, RB + 0, key=("cvv", 2)))
    cvg.append(mk("cvg2", [128, TMAX], F32, RB + 2304, key=("cvg", 2)))
    ggb.append(mk("ggb2", [128, TMAX], F32, RB + 4608, key=("gg", 2)))
    FH = mk("FH", [128, 64, NB * 2], F32, RB + 33280)
    stf = mk("stf", [32, 2048], F32, RB + 0)
    Us = [mk("Us%d" % i, [128, 4, NB * 6], F32, RB + 58880 + i * 1536, key=("Us", i)) for i in range(2)]
    cvs = [mk("cvs%d" % i, [128, 4, NS], F32, RB + 6912 + i * 1024, key=("cvs", i)) for i in range(2)]
    print("SBUF map: persistent=%d scratch_avail=%d" % (SCR, SB_TOP - SB_BASE - SCR))

    ps_t = [nc.alloc_psum_tensor("ps%d" % i, [128, 512], F32) for i in range(8)]
    banks = Banks(ps_t)

    def PSR(b):
        return ("ps", b)

    def keys(name, n):
        return [(name, c) for c in range(n)]

    wsrc = {"win": (win_r, 9), "wlo": (wlo_r, 2), "wao": (wao_r, 2), "wout": (wout_r, 2),
            "wup": (wup_r, 16), "wdn": (wdn_r, 8)}
    seq = []
    for p in range(NPASS):
        for l in range(DEPTH):
            for key in layer_block_keys(l):
                t, n = wsrc[key[0]]
                seq.append(((p,) + key, t[l * n + key[2]], 4096))
    for l in range(DEPTH):
        S.add("pool", lambda e, l=l: e.dma_start(out=gatesw[l][:], in_=gates_r[l]), writes=[("gatesw", l)], dsem=("gw", l))
    W = WStream(S, nc, seq, wslots)

    S.add("sp", lambda e: e.dma_start(out=identf[:], in_=ident_d), writes=["identf"], dsem="c0")
    S.add("sp", lambda e: e.dma_start(out=biasp[:].rearrange("p h k -> p (h k)"), in_=biasp_d), writes=["biasp"], dsem="c1")
    S.add("sp", lambda e: e.dma_start(out=biass[:].rearrange("p h k -> p (h k)"), in_=biass_d), writes=["biass"], dsem="c2")
    for l in range(DEPTH):
        S.add("sp", lambda e, l=l: e.dma_start(out=smalls[l][:], in_=smalls_d[l]), writes=[("smalls", l)], dsem=("c3", l))
        S.add("sp", lambda e, l=l: e.dma_start(out=sinks[l][:], in_=sinks_d[l]), writes=[("sinks", l)], dsem=("c4", l))
    S.add("dve", lambda e: e.tensor_scalar(biasp[:], biasp[:], -1.0, None, ALU.mult), writes=["biasp"])
    S.add("dve", lambda e: e.tensor_scalar(biass[:], biass[:], -1.0, None, ALU.mult), writes=["biass"])
    for l in range(DEPTH):
        S.add("dve", lambda e, l=l: e.tensor_scalar(sinks[l][:], sinks[l][:], -1.0, None, ALU.mult), writes=[("sinks", l)])
    S.add("dve", lambda e: e.tensor_copy(out=identb[:], in_=identf[:]), reads=["identf"], writes=["identb"])
    S.add("dve", lambda e: e.memset(ones_b[:], 1.0), writes=["ones_b"])
    S.add("dve", lambda e: e.memset(epsc[:], EPS), writes=["epsc"])
    for l in range(DEPTH):
        S.add("act", lambda e, l=l: e.activation(out=lamc[l][:, 0:8], in_=smalls[l][:, SM_LAM:SM_LAM + 8], func=AF.Exp, scale=-1.0),
              reads=[("smalls", l)], writes=[("lamc", l)])
        S.add("act", lambda e, l=l: e.activation(out=lamc[l][:, 0:8], in_=lamc[l][:, 0:8], func=AF.Ln, bias=1.0),
              writes=[("lamc", l)])
        S.add("dve", lambda e, l=l: e.tensor_scalar(lamc[l][:, 8:16], lamc[l][:, 0:8], -16.0, None, ALU.mult), writes=[("lamc", l)])
        S.add("dve", lambda e, l=l: e.tensor_scalar(lamc[l][:, 0:8], lamc[l][:, 0:8], -8.0, None, ALU.mult), writes=[("lamc", l)])

    rot = {"S": 0, "sg": 0, "u": 0}
    XK = keys("x", NCH)
    HR = keys("h", NCH)

    def transpose_out(src_fn, ncols, dst_aps, rd, tag):
        i = 0
        while i < len(dst_aps):
            grp = list(range(i, min(i + 4, len(dst_aps))))
            b = banks.alloc()

            def tr(e, grp=grp, b=b):
                r = None
                for gi, k in enumerate(grp):
                    n = ncols[k]
                    r = e.transpose(ps_t[b][0:n, gi * 128:(gi + 1) * 128], src_fn(k), identf[:])
                return r
            S.add("pe", tr, reads=list(rd) + ["identf"], writes=[PSR(b)])
            nmax = max(ncols[k] for k in grp)
            S.add("act", lambda e, b=b, grp=grp, nmax=nmax: e.copy(out=stout[0:nmax, 0:128 * len(grp)], in_=ps_t[b][0:nmax, 0:128 * len(grp)]),
                  writes=[PSR(b), "stout"])
            banks.free(b)
            for gi, k in enumerate(grp):
                n = ncols[k]
                S.add("sp", lambda e, gi=gi, k=k, n=n: e.dma_start(out=dst_aps[k], in_=stout[0:n, gi * 128:(gi + 1) * 128]),
                      reads=["stout"], dsem=("so", tag))
            i += 4

    def norm_stats(T, subs):
        for (c0, n) in subs:
            b = banks.alloc()
            for c in range(NCH):
                S.add("pe", lambda e, b=b, c0=c0, n=n, c=c: e.matmul(ps_t[b][:, 0:n], ones_b[:], sq[:, c, c0:c0 + n], start=(c == 0), stop=(c == NCH - 1)),
                      reads=[("sq", c), "ones_b"], writes=[PSR(b)])
            S.add("act", lambda e, b=b, c0=c0, n=n: e.activation(out=sdv[:, c0:c0 + n], in_=ps_t[b][:, 0:n], func=AF.Ln,
                                                                 scale=1.0 / D, bias=epsc[:, 0:1]),
                  reads=["epsc"], writes=[PSR(b), "sdv"])
            S.add("act", lambda e, c0=c0, n=n: e.activation(out=rstd[:, c0:c0 + n], in_=sdv[:, c0:c0 + n], func=AF.Exp, scale=-0.5),
                  reads=["sdv"], writes=["rstd"])
            banks.free(b)

    def split_eng(c):
        return "dve"

    def rmsnorm_to_h(l, gcol, T, subs, squares_done=False):
        if not squares_done:
            for c in range(NCH):
                S.add("act", lambda e, c=c: e.activation(out=sq[:, c, 0:T], in_=x[:, c, 0:T], func=AF.Square), reads=[("x", c)], writes=[("sq", c)])
        norm_stats(T, subs)
        for c in range(NCH):
            S.add("dve", lambda e, c=c: e.scalar_tensor_tensor(out=h[:, c, 0:T], in0=x[:, c, 0:T],
                                                                      scalar=smalls[l][:, gcol + c:gcol + c + 1], in1=rstd[:, 0:T],
                                                                      op0=ALU.mult, op1=ALU.mult),
                  reads=[("x", c), "rstd", ("smalls", l)], writes=[("h", c)])

    def postnorm_residual(l, T, subs, next_squares):
        norm_stats(T, subs)
        for c in range(NCH):
            eng = split_eng(c)
            S.add(eng, lambda e, c=c: e.tensor_tensor(out=mbuf[:, c, 0:T], in0=mbuf[:, c, 0:T], in1=rstd[:, 0:T], op=ALU.mult),
                  reads=["rstd"], writes=[("mbuf", c)])
            S.add(eng, lambda e, c=c: e.tensor_tensor(out=x[:, c, 0:T], in0=x[:, c, 0:T], in1=mbuf[:, c, 0:T], op=ALU.add),
                  reads=[("mbuf", c)], writes=[("x", c)])
            if next_squares:
                S.add("act", lambda e, c=c: e.activation(out=sq[:, c, 0:T], in_=x[:, c, 0:T], func=AF.Square), reads=[("x", c)], writes=[("sq", c)])

    def proj(wt, wres, col, rhs, rhs_res, nk, subs, evac):
        for (c0, n) in subs:
            b = banks.alloc()
            for k in range(nk):
                S.add("pe", lambda e, b=b, c0=c0, n=n, k=k: e.matmul(ps_t[b][:, 0:n], wt[:, k, col:col + 128], rhs[:, k, c0:c0 + n],
                                                                     start=(k == 0), stop=(k == nk - 1)),
                      reads=[wres, rhs_res[k]], writes=[PSR(b)])
            evac(b, c0, n)
            banks.free(b)

    def attn_waves(units):
        waves = [units[i:i + 4] for i in range(0, len(units), 4)]

        def sets(w, i):
            return (w % 2) * 4 + i

        def front(w):
            wv = waves[w]
            bl = []
            for i, u in enumerate(wv):
                b = banks.alloc()
                bl.append(b)
                NQ, NK = u["NQ"], u["NK"]
                S.add("pe", lambda e, u=u, b=b, NQ=NQ, NK=NK: e.matmul(ps_t[b][0:NQ, 0:NK], u["qap"], u["kap"], start=True, stop=True),
                      reads=u["rd"], writes=[PSR(b)])
            for i, u in enumerate(wv):
                si = sets(w, i)
                NQ = u["NQ"]
                S.add("pool", lambda e, u=u, si=si, NQ=NQ: e.tensor_copy(out=Sbuf[si][0:NQ, 63:64], in_=u["sink_ap"]), reads=u["rd"], writes=[("Sb", si, "h")])
            for i, u in enumerate(wv):
                si = sets(w, i)
                b = bl[i]
                NQ, NK = u["NQ"], u["NK"]
                S.add("dve", lambda e, u=u, si=si, b=b, NQ=NQ, NK=NK: e.tensor_tensor(out=Sbuf[si][0:NQ, 64:64 + NK], in0=u["bias_ap"], in1=ps_t[b][0:NQ, 0:NK], op=ALU.subtract),
                      reads=[u["bias_key"]], writes=[PSR(b), ("Sb", si, "b")])
                banks.free(b)
            for i, u in enumerate(wv):
                si = sets(w, i)
                NQ, NK = u["NQ"], u["NK"]
                S.add("dve", lambda e, si=si, NQ=NQ, NK=NK: e.tensor_reduce(out=stat[si][0:NQ, 0:1], in_=Sbuf[si][0:NQ, 63:64 + NK], axis=AX.X, op=ALU.min),
                      reads=[("Sb", si)], writes=[("st", si)])

        def mid(w):
            wv = waves[w]
            for i, u in enumerate(wv):
                si = sets(w, i)
                NQ, NK = u["NQ"], u["NK"]
                S.add("act", lambda e, si=si, NQ=NQ, NK=NK: e.activation(out=Sbuf[si][0:NQ, 63:64 + NK], in_=Sbuf[si][0:NQ, 63:64 + NK], func=AF.Exp,
                                                                         bias=stat[si][0:NQ, 0:1], scale=-1.0, accum_out=stat[si][0:NQ, 2:3]),
                      writes=[("Sb", si), ("st", si)])
            for i, u in enumerate(wv):
                si = sets(w, i)
                NQ = u["NQ"]
                S.add("dve", lambda e, si=si, NQ=NQ: e.reciprocal(out=stat[si][0:NQ, 3:4], in_=stat[si][0:NQ, 2:3]), writes=[("st", si)])
            for i, u in enumerate(wv):
                si = sets(w, i)
                NQ, NK = u["NQ"], u["NK"]
                S.add("act", lambda e, si=si, NQ=NQ, NK=NK: e.activation(out=Pn[si][0:NQ, 0:NK], in_=Sbuf[si][0:NQ, 64:64 + NK], func=AF.Copy, scale=stat[si][0:NQ, 3:4]),
                      reads=[("Sb", si), ("st", si)], writes=[("Pn", si)])

        def back(w):
            wv = waves[w]
            tbl = []
            for i, u in enumerate(wv):
                si = sets(w, i)
                tb = banks.alloc()
                tbl.append(tb)
                NQ = u["NQ"]

                def tr(e, u=u, si=si, tb=tb, NQ=NQ):
                    tps = ps_t[tb].bitcast(BF16)
                    r = None
                    for vi, (vap, k0, nk) in enumerate(u["vblocks"]):
                        r = e.transpose(tps[0:nk, vi * 128:vi * 128 + NQ], Pn[si][0:NQ, k0:k0 + nk], identb[0:NQ, 0:NQ])
                    return r
                S.add("pe", tr, reads=[("Pn", si), "identb"], writes=[PSR(tb)])
            for i, u in enumerate(wv):
                si = sets(w, i)
                tb = tbl[i]
                NQ = u["NQ"]
                vb = u["vblocks"]
                if NQ == 128 and all(nk == 128 for (_, _, nk) in vb):
                    nv = len(vb)
                    S.add("act", lambda e, si=si, tb=tb, nv=nv: e.copy(out=PT[si][:, 0:nv, :],
                                                                      in_=ps_t[tb].bitcast(BF16)[:, 0:nv * 128].rearrange("p (v q) -> p v q", q=128)),
                          writes=[PSR(tb), ("PT", si)])
                else:
                    for vi, (vap, k0, nk) in enumerate(vb):
                        S.add("act", lambda e, si=si, tb=tb, vi=vi, nk=nk, NQ=NQ: e.copy(out=PT[si][0:nk, vi, 0:NQ], in_=ps_t[tb].bitcast(BF16)[0:nk, vi * 128:vi * 128 + NQ]),
                              writes=[PSR(tb), ("PT", si)])
                banks.free(tb)
            for i, u in enumerate(wv):
                si = sets(w, i)
                g = u["grp"]
                if g.get("ob") is None:
                    g["ob"] = banks.alloc()
                ob = g["ob"]
                NQ = u["NQ"]
                c0 = u["out_c0"]

                def pv(e, u=u, si=si, ob=ob, NQ=NQ, c0=c0):
                    r = None
                    vb = u["vblocks"]
                    for vi, (vap, k0, nk) in enumerate(vb):
                        r = e.matmul(ps_t[ob][:, c0:c0 + NQ], vap, PT[si][0:nk, vi, 0:NQ], start=(vi == 0), stop=(vi == len(vb) - 1))
                    return r
                S.add("pe", pv, reads=[("PT", si)] + list(u["rd"]), writes=[PSR(ob)])
                if u["last"]:
                    u["evac"](ob)
                    banks.free(ob)
                    g["ob"] = None

        nw = len(waves)
        front(0)
        for w in range(nw):
            mid(w)
            if w + 1 < nw:
                front(w + 1)
            back(w)

    def do_pass(p):
        has_s = (p == 0)
        T = TP + (NS if has_s else 0)
        subs = [(0, TP)] + ([(TP, NS)] if has_s else [])
        last_pass = (p == NPASS - 1)
        XRs = XR[:, :, 3 + TP:3 + TP + NB * 7].rearrange("p c (b k) -> p c b k", k=7)

        if p == 0:
            S.add("sp", lambda e, p=p: e.dma_start(out=tokx[:], in_=xp[p * TP:(p + 1) * TP, :].rearrange("(j r) d -> r j d", r=128)),
                  writes=["tokx"], dsem="xin")
        if has_s:
            S.add("sp", lambda e: e.dma_start(out=toks[:], in_=xs), writes=["toks"], dsem="xin2")
        for c in range(NCH):
            b = banks.alloc()

            def tr(e, b=b, c=c):
                r = None
                for j in range(4):
                    r = e.transpose(ps_t[b][:, j * 128:(j + 1) * 128], tokx[:, j, c * 128:(c + 1) * 128], identf[:])
                return r
            S.add("pe", tr, reads=["tokx", "identf"], writes=[PSR(b)])
            S.add("act", lambda e, b=b, c=c: e.copy(out=x[:, c, 0:TP], in_=ps_t[b][:, 0:TP]), writes=[PSR(b), ("x", c)])
            banks.free(b)
            if has_s:
                b = banks.alloc()
                S.add("pe", lambda e, b=b, c=c: e.transpose(ps_t[b][:, 0:NS], toks[:, c * 128:(c + 1) * 128], identf[0:NS, 0:NS]),
                      reads=["toks", "identf"], writes=[PSR(b)])
                S.add("act", lambda e, b=b, c=c: e.copy(out=x[:, c, TP:TP + NS], in_=ps_t[b][:, 0:NS]), writes=[PSR(b), ("x", c)])
                banks.free(b)

        for l in range(DEPTH):
            do_layer(p, l, T, subs, has_s, last_pass, XRs)
        finish_pass(p, has_s)

    def do_layer(p, l, T, subs, has_s, last_pass, XRs):
        if True:
            sm = smalls[l]
            SMR = ("smalls", l)

            rmsnorm_to_h(l, SM_NMP, T, subs, squares_done=(l > 0))

            XRall = keys("XR", NCH)
            if has_s:
                S.add("sp", lambda e, l=l: e.dma_start(out=stout[0:48, 0:D], in_=st_c[l]), writes=["stout"], dsem="stin")
                for c in range(NCH):
                    b = banks.alloc()
                    S.add("pe", lambda e, b=b, c=c: e.transpose(ps_t[b][:, 0:48], stout[0:48, c * 128:(c + 1) * 128], identf[0:48, 0:48]),
                          reads=["stout", "identf"], writes=[PSR(b)])
                    S.add("act", lambda e, b=b, c=c: e.copy(out=XRs[:, c, :, 0:3], in_=ps_t[b][:, 0:48].rearrange("p (b k) -> p b k", k=3)),
                          writes=[PSR(b), ("XR", c)])
                    banks.free(b)
                S.add("sp", lambda e, l=l: e.dma_start(out=stout[0:16, 0:D], in_=st_h[l]), writes=["stout"], dsem="stin")
                for c in range(NCH):
                    b = banks.alloc()
                    S.add("pe", lambda e, b=b, c=c: e.transpose(ps_t[b][:, 0:16], stout[0:16, c * 128:(c + 1) * 128], identf[0:16, 0:16]),
                          reads=["stout", "identf"], writes=[PSR(b)])
                    S.add("act", lambda e, b=b, c=c: e.copy(out=h0s[:, c, :], in_=ps_t[b][:, 0:16]), writes=[PSR(b), "h0s"])
                    banks.free(b)

            if p == 0:
                S.add("dve", lambda e: e.memset(XR[:, :, 0:3], 0.0), writes=XRall)
            else:
                S.add("dve", lambda e, l=l: e.tensor_copy(out=XR[:, :, 0:3], in_=convhist[l][:]),
                      reads=[("convhist", l)], writes=XRall)
            for blk in range(2):
                wt, wres = W.get((p, "win", l, blk))
                wv = wt[:].rearrange("p (k n) -> p k n", n=512)
                for cc in range(4):
                    c = blk * 4 + cc

                    def ev(b, c0, n, c=c):
                        if c0 == 0:
                            S.add("act", lambda e: e.copy(out=XR[:, c, 3:3 + TP], in_=ps_t[b][:, 0:TP]), writes=[PSR(b), ("XR", c)])
                        else:
                            S.add("act", lambda e: e.copy(out=XRs[:, c, :, 3:7], in_=ps_t[b][:, 0:NS].rearrange("p (b t) -> p b t", t=4)),
                                  writes=[PSR(b), ("XR", c)])
                    proj(wv, wres, cc * 128, h, HR, NCH, subs, ev)
                W.release()
            S.add("dve", lambda e, l=l: e.tensor_copy(out=convhist[l][:], in_=XR[:, :, TP:TP + 3]), reads=XRall, writes=[("convhist", l)])
            if has_s:
                S.add("dve", lambda e: e.tensor_copy(out=cs_stage[:].rearrange("p c (b k) -> p c b k", k=3), in_=XRs[:, :, :, 4:7]),
                      reads=XRall, writes=["cs_stage"])

            for blk in (2, 3):
                wt, wres = W.get((p, "win", l, blk))
                wv = wt[:].rearrange("p (k n) -> p k n", n=512)
                for cc in range(4):
                    hh = (blk - 2) * 4 + cc

                    def ev(b, c0, n, hh=hh):
                        S.add("act", lambda e: e.activation(out=qT[:, hh, c0:c0 + n], in_=ps_t[b][:, 0:n], func=AF.Copy, scale=QSCALE),
                              writes=[PSR(b), ("qT", hh)])
                    proj(wv, wres, cc * 128, h, HR, NCH, subs, ev)
                W.release()
            wkv_t, wkv_res = W.get((p, "win", l, 4))
            wkv = wkv_t[:].rearrange("p (k n) -> p k n", n=512)
            if p == 0:
                S.add("dve", lambda e: e.memset(kT[:, :, 0:128], 0.0), writes=["kT"])
                S.add("dve", lambda e: e.memset(Vt[:, 0, :], 0.0), writes=["Vt"])
            else:
                S.add("dve", lambda e, l=l: e.tensor_copy(out=kT[:, :, 0:128], in_=khist[l][:]), reads=[("khist", l)], writes=["kT"])
                S.add("dve", lambda e, l=l: e.tensor_copy(out=Vt[:, 0, :], in_=vhist[l][:]), reads=[("vhist", l)], writes=["Vt"])
            for kv in range(2):
                def ev(b, c0, n, kv=kv):
                    S.add("act", lambda e: e.copy(out=kT[:, kv, 128:128 + TP], in_=ps_t[b][:, 0:TP]), writes=[PSR(b), "kT"])
                proj(wkv, wkv_res, kv * 128, h, HR, NCH, [(0, TP)], ev)
            for j in range(4):
                full = last_pass and j == 3
                b = banks.alloc()
                c0w, nw = (0, 512) if full else (256, 256)

                def mm(e, b=b, j=j, c0w=c0w, nw=nw):
                    r = None
                    for k in range(NCH):
                        r = e.matmul(ps_t[b][:, 0:nw], h[:, k, j * 128:(j + 1) * 128], wkv[:, k, c0w:c0w + nw], start=(k == 0), stop=(k == NCH - 1))
                    return r
                S.add("pe", mm, reads=[wkv_res] + HR, writes=[PSR(b)])
                voff = 256 if full else 0
                S.add("act", lambda e, b=b, j=j, voff=voff: e.copy(out=Vt[:, j + 1, :], in_=ps_t[b][:, voff:voff + 256]), writes=[PSR(b), "Vt"])
                if full:
                    S.add("dve", lambda e, b=b: e.tensor_copy(out=kvlast[:], in_=ps_t[b][:, 0:512]), writes=[PSR(b), "kvlast"])
                    S.add("sp", lambda e, l=l: e.dma_start(out=o_pk[l], in_=kvlast[:, 0:256]), reads=["kvlast"], dsem="okv")
                    S.add("sp", lambda e, l=l: e.dma_start(out=o_pv[l], in_=kvlast[:, 256:512]), reads=["kvlast"], dsem="okv")
                banks.free(b)
            S.add("dve", lambda e, l=l: e.tensor_copy(out=khist[l][:], in_=kT[:, :, TP:TP + 128]), reads=["kT"], writes=[("khist", l)])
            S.add("dve", lambda e, l=l: e.tensor_copy(out=vhist[l][:], in_=Vt[:, 4, :]), reads=["Vt"], writes=[("vhist", l)])
            if has_s:
                b = banks.alloc()

                def mm(e, b=b):
                    r = None
                    for k in range(NCH):
                        r = e.matmul(ps_t[b][0:NS, 0:512], h[:, k, TP:TP + NS], wkv[:, k, 0:512], start=(k == 0), stop=(k == NCH - 1))
                    return r
                S.add("pe", mm, reads=[wkv_res] + HR, writes=[PSR(b)])
                S.add("act", lambda e, b=b: e.copy(out=kvnew[:], in_=ps_t[b][0:NS, 0:512]), writes=[PSR(b), "kvnew"])
                banks.free(b)
                def kvout(e, l=l):
                    r = []
                    for bb in range(NB):
                        r.append(e.dma_start(out=o_sk[l][bb, 124:128, :], in_=kvnew[bb * 4:(bb + 1) * 4, 0:256]))
                        r.append(e.dma_start(out=o_sv[l][bb, 124:128, :], in_=kvnew[bb * 4:(bb + 1) * 4, 256:512]))
                    return r
                S.add("sp", kvout, reads=["kvnew"], writes=[("dram", "osv", l)], dsem="okv2", ndma=2 * NB)
                S.add("sp", lambda e, l=l: e.dma_start(out=o_sk[l][:, 0:124, :], in_=ck[l][:, 4:128, :]), dsem="d2d")
                S.add("sp", lambda e, l=l: e.dma_start(out=o_sv[l][:, 0:124, :], in_=cv[l][:, 4:128, :]), dsem="d2d")

            gv = gatesw[l][:].rearrange("p (g c n) -> p g c n", g=2, n=128)

            def K(name, st, cc):
                return ((name, st), cc)

            def convS(g):
                st = g % 2
                for cc in range(2):
                    c = g * 2 + cc
                    wcol = lambda k, c=c: sm[:, SM_CLW + k * 8 + c:SM_CLW + k * 8 + c + 1]
                    bcol = sm[:, SM_CLB + c:SM_CLB + c + 1]
                    xo = xcg[st]
                    S.add("dve", lambda e, c=c, cc=cc, xo=xo, wcol=wcol, bcol=bcol: e.tensor_scalar(xo[:, cc, 0:TP], XR[:, c, 3:3 + TP], wcol(3), bcol, ALU.mult, ALU.add),
                          reads=[("XR", c), SMR], writes=[K("xc", st, cc)])
                    for k in range(3):
                        S.add("dve", lambda e, c=c, cc=cc, k=k, xo=xo, wcol=wcol: e.scalar_tensor_tensor(out=xo[:, cc, 0:TP], in0=XR[:, c, k:k + TP], scalar=wcol(k),
                                                                                                       in1=xo[:, cc, 0:TP], op0=ALU.mult, op1=ALU.add),
                              reads=[("XR", c), SMR], writes=[K("xc", st, cc)])
                    if has_s:
                        xcs = xo[:, cc, TP:TP + NS].rearrange("p (b t) -> p b t", t=4)
                        S.add("dve", lambda e, c=c, wcol=wcol, bcol=bcol, xcs=xcs: e.tensor_scalar(xcs, XRs[:, c, :, 3:7], wcol(3), bcol, ALU.mult, ALU.add),
                              reads=[("XR", c), SMR], writes=[K("xc", st, cc)])
                        for k in range(3):
                            S.add("dve", lambda e, c=c, k=k, wcol=wcol, xcs=xcs: e.scalar_tensor_tensor(out=xcs, in0=XRs[:, c, :, k:k + 4], scalar=wcol(k),
                                                                                                      in1=xcs, op0=ALU.mult, op1=ALU.add),
                                  reads=[("XR", c), SMR], writes=[K("xc", st, cc)])
                    S.add("act", lambda e, cc=cc, xo=xo, st=st: e.copy(out=xcb[st][:, cc, 0:T], in_=xo[:, cc, 0:T]),
                          reads=[K("xc", st, cc)], writes=[K("xcb", st, cc)])

            def gatesS(g):
                st = g % 2
                for cc in range(2):
                    c = g * 2 + cc
                    for (c0, n) in subs:
                        for gi_, dst, dkey, bcolbase in ((0, rr[st], "rr", SM_BR), (1, ii[st], "ii", SM_BI)):
                            b = banks.alloc()
                            S.add("pe", lambda e, b=b, gi_=gi_, c=c, cc=cc, c0=c0, n=n, st=st: e.matmul(ps_t[b][:, 0:n], gv[:, gi_, c, :], xcb[st][:, cc, c0:c0 + n], start=True, stop=True),
                                  reads=[("gatesw", l), K("xcb", st, cc)], writes=[PSR(b)])
                            S.add("act", lambda e, b=b, dst=dst, c=c, cc=cc, c0=c0, n=n, bcolbase=bcolbase: e.activation(
                                out=dst[:, cc, c0:c0 + n], in_=ps_t[b][:, 0:n], func=AF.Sigmoid, bias=sm[:, bcolbase + c:bcolbase + c + 1], scale=1.0),
                                reads=[SMR], writes=[PSR(b), K(dkey, st, cc)])
                            banks.free(b)

            def expS(g):
                st = g % 2
                for cc in range(2):
                    c = g * 2 + cc
                    S.add("act", lambda e, c=c, cc=cc, st=st: e.activation(out=aa[st][:, cc, 0:T], in_=rr[st][:, cc, 0:T], func=AF.Exp, scale=lamc[l][:, c:c + 1]),
                          reads=[K("rr", st, cc), ("lamc", l)], writes=[K("aa", st, cc)])
                for cc in range(2):
                    S.add("dve", lambda e, cc=cc, st=st: e.scalar_tensor_tensor(out=rr[st][:, cc, 0:T], in0=aa[st][:, cc, 0:T], scalar=0.99999994,
                                                                                in1=aa[st][:, cc, 0:T], op0=ALU.min, op1=ALU.mult),
                          reads=[K("aa", st, cc)], writes=[K("rr", st, cc)])
                for cc in range(2):
                    S.add("act", lambda e, cc=cc, st=st: e.activation(out=rr[st][:, cc, 0:T], in_=rr[st][:, cc, 0:T], func=AF.Ln, scale=-1.0, bias=1.0),
                          writes=[K("rr", st, cc)])
                for cc in range(2):
                    S.add("act", lambda e, cc=cc, st=st: e.activation(out=rr[st][:, cc, 0:T], in_=rr[st][:, cc, 0:T], func=AF.Exp, scale=0.5),
                          writes=[K("rr", st, cc)])

            def dveS(g):
                st = g % 2
                xo = xcg[st]
                for cc in range(2):
                    c = g * 2 + cc
                    if p == 0:
                        S.add("dve", lambda e, cc=cc, st=st: e.memset(rr[st][:, cc, 0:1], 1.0), writes=[K("rr", st, cc)])
                    S.add("dve", lambda e, cc=cc, st=st, xo=xo: e.tensor_tensor(out=ii[st][:, cc, 0:T], in0=ii[st][:, cc, 0:T], in1=xo[:, cc, 0:T], op=ALU.mult),
                          reads=[K("xc", st, cc)], writes=[K("ii", st, cc)])
                    S.add("dve", lambda e, cc=cc, st=st: e.tensor_tensor(out=ii[st][:, cc, 0:T], in0=ii[st][:, cc, 0:T], in1=rr[st][:, cc, 0:T], op=ALU.mult),
                          reads=[K("rr", st, cc)], writes=[K("ii", st, cc)])
                    init = 0.0 if p == 0 else hstate[l][:, c:c + 1]
                    S.add("dve", lambda e, cc=cc, st=st, xo=xo, init=init: e.tensor_tensor_scan(out=xo[:, cc, 0:TP], data0=aa[st][:, cc, 0:TP], data1=ii[st][:, cc, 0:TP],
                                                                                                 initial=init, op0=ALU.mult, op1=ALU.add),
                          reads=[K("aa", st, cc), K("ii", st, cc), ("hstate", l)], writes=[K("xc", st, cc)])
                    if has_s:
                        aas = aa[st][:, cc, TP:TP + NS].rearrange("p (b t) -> p b t", t=4)
                        iis = ii[st][:, cc, TP:TP + NS].rearrange("p (b t) -> p b t", t=4)
                        S.add("dve", lambda e, c=c, aas=aas: e.tensor_tensor(out=tmp16[:], in0=aas[:, :, 0], in1=h0s[:, c, :], op=ALU.mult),
                              reads=[K("aa", st, cc), "h0s"], writes=["tmp16"])
                        S.add("dve", lambda e, iis=iis: e.tensor_tensor(out=iis[:, :, 0], in0=iis[:, :, 0], in1=tmp16[:], op=ALU.add),
                              reads=["tmp16"], writes=[K("ii", st, cc)])
                        S.add("dve", lambda e, aas=aas: e.memset(aas[:, :, 0], 0.0), writes=[K("aa", st, cc)])
                        S.add("dve", lambda e, cc=cc, st=st, xo=xo: e.tensor_tensor_scan(out=xo[:, cc, TP:TP + NS], data0=aa[st][:, cc, TP:TP + NS], data1=ii[st][:, cc, TP:TP + NS],
                                                                                         initial=0.0, op0=ALU.mult, op1=ALU.add),
                              reads=[K("aa", st, cc), K("ii", st, cc)], writes=[K("xc", st, cc)])
                    S.add("act", lambda e, c=c, cc=cc, xo=xo: e.copy(out=lro[:, c, 0:T], in_=xo[:, cc, 0:T]), reads=[K("xc", st, cc)], writes=[("lro", c)])
                XCK = [K("xc", st, cc) for cc in range(2)]
                S.add("dve", lambda e, g=g, xo=xo: e.tensor_copy(out=hstate[l][:, g * 2:(g + 1) * 2], in_=xo[:, :, TP - 1]), reads=XCK, writes=[("hstate", l)])
                if has_s:
                    S.add("dve", lambda e, g=g, xo=xo: e.tensor_copy(out=hs_last[:, g * 2:(g + 1) * 2, :],
                                                                     in_=xo[:, :, TP:TP + NS].rearrange("p c (b t) -> p c b t", t=4)[:, :, :, 3]),
                          reads=XCK, writes=["hs_last"])

            convS(0)
            gatesS(0)
            convS(1)
            expS(0)
            gatesS(1)
            dveS(0)
            convS(2)
            expS(1)
            gatesS(2)
            dveS(1)
            convS(3)
            expS(2)
            gatesS(3)
            dveS(2)
            expS(3)
            dveS(3)
            if has_s:
                transpose_out(lambda k: hs_last[:, k, :], [NB] * NCH,
                              [o_slh[l][:, k * 128:(k + 1) * 128] for k in range(NCH)], ["hs_last"], "slh")
                transpose_out(lambda k: cs_stage[:, k, :], [NB * 3] * NCH,
                              [o_slc[l][:, k * 128:(k + 1) * 128] for k in range(NCH)], ["cs_stage"], "slc")
            if last_pass:
                transpose_out(lambda k, l=l: hstate[l][:, :], [NCH], [o_plh[l].rearrange("(c p) -> c p", p=128)], [("hstate", l)], "plh")
                S.add("dve", lambda e, l=l: e.tensor_copy(out=pst[:, 0:24].rearrange("p (k c) -> p k c", c=NCH),
                                                          in_=convhist[l][:].rearrange("p c k -> p k c")),
                      reads=[("convhist", l)], writes=["pst"])
                transpose_out(lambda k: pst[:, 0:24], [24], [o_plc[l].rearrange("k (c p) -> (k c) p", p=128)], ["pst"], "plc")

            if has_s:
                S.add("sp", lambda e, l=l: e.dma_start(out=cstage[:], in_=ck[l].rearrange("b k d -> k b d")), writes=["cstage"], dsem="stin3")
                S.add("pool", lambda e, l=l: e.dma_start(out=Vs[:], in_=cv[l].rearrange("b k d -> k b d")), writes=["Vs"], dsem="vs_in")
                for kv in range(2):
                    def ev(b, c0, n, kv=kv):
                        S.add("act", lambda e: e.copy(out=kTs[:, kv, :, 128:132], in_=ps_t[b][:, 0:NS].rearrange("p (b t) -> p b t", t=4)),
                              writes=[PSR(b), "kTs"])
                    proj(wkv, wkv_res, kv * 128, h, HR, NCH, [(TP, NS)], ev)
                S.add("pool", lambda e, l=l: e.dma_start(out=Vn4[:], in_=o_sv[l][:, 124:128, :].rearrange("b t d -> t b d")),
                      reads=[("dram", "osv", l)], writes=["Vn4"], dsem="vn4")
                for kv in range(2):
                    S.add("dve", lambda e, kv=kv: e.tensor_copy(
                        out=qTs[:, kv, :, :].rearrange("p b (g t) -> p b g t", t=4),
                        in_=qT[:, kv * 4:kv * 4 + 4, TP:TP + NS].rearrange("p g (b t) -> p b g t", t=4)),
                        reads=[("qT", kv * 4 + g) for g in range(4)], writes=["qTs"])
            W.release()

            units = []
            for hh in range(8):
                kv = hh // 4
                grp = {}
                for j in range(4):
                    first_blk = (p == 0 and j == 0)
                    if first_blk:
                        kap = kT[:, kv, 128:256]
                        bias_ap = biasp[:, hh, 128:256]
                        vbl = [(Vt[:, 1, kv * 128:(kv + 1) * 128], 0, 128)]
                        NK = 128
                    else:
                        kap = kT[:, kv, j * 128:j * 128 + 256]
                        bias_ap = biasp[:, hh, :]
                        vbl = [(Vt[:, j, kv * 128:(kv + 1) * 128], 0, 128), (Vt[:, j + 1, kv * 128:(kv + 1) * 128], 128, 128)]
                        NK = 256

                    def evp(ob, hh=hh):
                        S.add("act", lambda e: e.copy(out=attn[:, hh, 0:TP], in_=ps_t[ob][:, 0:TP]), writes=[PSR(ob), ("attn", hh)])
                    units.append(dict(qap=qT[:, hh, j * 128:(j + 1) * 128], kap=kap, NQ=128, NK=NK, bias_ap=bias_ap, bias_key="biasp",
                                      sink_ap=sinks[l][:, hh:hh + 1], vblocks=vbl, rd=[("qT", hh), "kT", "Vt", ("sinks", l)],
                                      grp=grp, out_c0=j * 128, last=(j == 3), evac=evp))
            attn_waves(units)
            units = []
            if has_s:
                for bb in range(NB):
                    b = banks.alloc()

                    def tr(e, b=b, bb=bb):
                        r = None
                        for kv in range(2):
                            r = e.transpose(ps_t[b][:, kv * 128:(kv + 1) * 128], cstage[:, bb, kv * 128:(kv + 1) * 128], identf[:])
                        return r
                    S.add("pe", tr, reads=["cstage", "identf"], writes=[PSR(b)])
                    S.add("act", lambda e, b=b, bb=bb: e.copy(out=kTs[:, :, bb, 0:128], in_=ps_t[b][:, 0:256].rearrange("p (v k) -> p v k", k=128)),
                          writes=[PSR(b), "kTs"])
                    banks.free(b)
                for kv in range(2):
                    grp = {}
                    for bb in range(NB):
                        vbl = [(Vs[:, bb, kv * 128:(kv + 1) * 128], 0, 128), (Vn4[0:4, bb, kv * 128:(kv + 1) * 128], 128, 4)]

                        def evs(ob, kv=kv):
                            S.add("act", lambda e: e.copy(
                                out=attn[:, kv * 4:kv * 4 + 4, TP:TP + NS].rearrange("p g (b t) -> p b g t", t=4),
                                in_=ps_t[ob][:, 0:256].rearrange("p (b g t) -> p b g t", g=4, t=4)),
                                writes=[PSR(ob)] + [("attn", kv * 4 + g) for g in range(4)])
                        units.append(dict(qap=qTs[:, kv, bb, :], kap=kTs[:, kv, bb, :], NQ=16, NK=132, bias_ap=biass[:, kv, :], bias_key="biass",
                                          sink_ap=sinks[l][0:16, 8 + kv:9 + kv], vblocks=vbl, rd=["qTs", "kTs", "Vs", "Vn4", ("sinks", l)],
                                          grp=grp, out_c0=bb * 16, last=(bb == NB - 1), evac=evs))
            if units:
                attn_waves(units)

            LR = keys("lro", NCH)
            AR = keys("attn", 8)

            def mkmm(bk, wv_, src, col, c0, n):
                def mm(e):
                    r = None
                    for k in range(NCH):
                        r = e.matmul(ps_t[bk][:, 0:n], wv_[:, k, col:col + 128], src[:, k, c0:c0 + n], start=(k == 0), stop=(k == NCH - 1))
                    return r
                return mm
            for br, (wname, gbase, src, SR) in enumerate((("wlo", 5, lro, LR), ("wao", 7, attn, AR))):
                for hf in range(2):
                    wo_t, wo_res = W.get((p, wname, l, hf))
                    wg_t, wg_res = W.get((p, "win", l, gbase + hf))
                    v_o = wo_t[:].rearrange("p (k n) -> p k n", n=512)
                    v_g = wg_t[:].rearrange("p (k n) -> p k n", n=512)
                    for cc in range(4):
                        oc = hf * 4 + cc
                        for (c0, n) in subs:
                            gi = rot["sg"] % 2
                            rot["sg"] += 1
                            bO, bG = banks.alloc(), banks.alloc()
                            S.add("pe", mkmm(bG, v_g, h, cc * 128, c0, n), reads=[wg_res] + HR, writes=[PSR(bG)])
                            for k in range(NCH):
                                S.add("pe", lambda e, bO=bO, v_o=v_o, src=src, cc=cc, c0=c0, n=n, k=k: e.matmul(
                                    ps_t[bO][:, 0:n], v_o[:, k, cc * 128:cc * 128 + 128], src[:, k, c0:c0 + n], start=(k == 0), stop=(k == NCH - 1)),
                                    reads=[wo_res, SR[k]], writes=[PSR(bO)])
                            S.add("act", lambda e, gi=gi, bG=bG, n=n: e.activation(out=sgA[gi][:, 0:n], in_=ps_t[bG][:, 0:n], func=AF.Sigmoid),
                                  writes=[PSR(bG), ("sgA", gi)])
                            if br == 0:
                                S.add("dve", lambda e, gi=gi, bO=bO, oc=oc, c0=c0, n=n: e.tensor_tensor(out=merged[:, oc, c0:c0 + n], in0=sgA[gi][:, 0:n], in1=ps_t[bO][:, 0:n], op=ALU.mult),
                                      reads=[("sgA", gi)], writes=[PSR(bO), ("merged", oc)])
                            else:
                                S.add("dve", lambda e, gi=gi, bO=bO, n=n: e.tensor_tensor(out=t1b[gi][:, 0:n], in0=sgA[gi][:, 0:n], in1=ps_t[bO][:, 0:n], op=ALU.mult),
                                      reads=[("sgA", gi)], writes=[PSR(bO), ("t1b", gi)])
                                S.add("dve", lambda e, gi=gi, oc=oc, c0=c0, n=n: e.tensor_tensor(out=merged[:, oc, c0:c0 + n], in0=t1b[gi][:, 0:n], in1=merged[:, oc, c0:c0 + n], op=ALU.add),
                                      reads=[("t1b", gi)], writes=[("merged", oc)])
                            banks.free(bO)
                            banks.free(bG)
                    W.release()
                    W.release()
            MR = keys("merged", NCH)
            for hf in range(2):
                wt, wres = W.get((p, "wout", l, hf))
                wv = wt[:].rearrange("p (k n) -> p k n", n=512)
                for cc in range(4):
                    oc = hf * 4 + cc

                    def ev(b, c0, n, oc=oc):
                        S.add("act", lambda e: e.activation(out=sq[:, oc, c0:c0 + n], in_=ps_t[b][:, 0:n], func=AF.Square), writes=[PSR(b), ("sq", oc)])
                        S.add("act", lambda e: e.activation(out=mbuf[:, oc, c0:c0 + n], in_=ps_t[b][:, 0:n], func=AF.Identity,
                                                            scale=sm[:, SM_NMPOST + oc:SM_NMPOST + oc + 1]),
                              reads=[SMR], writes=[PSR(b), ("mbuf", oc)])
                    proj(wv, wres, cc * 128, merged, MR, NCH, subs, ev)
                W.release()
            postnorm_residual(l, T, subs, True)

            rmsnorm_to_h(l, SM_NFP, T, subs, squares_done=True)
            if has_s:
                for q4 in range(4):
                    S.add("sp", lambda e, l=l, q4=q4: e.dma_start(out=stf[:], in_=st_f[l][:, q4 * 2048:(q4 + 1) * 2048]),
                          writes=["stf"], dsem="stin2")
                    for g4 in range(4):
                        b = banks.alloc()

                        def tr(e, b=b, g4=g4):
                            r = None
                            for i4 in range(4):
                                jj = g4 * 4 + i4
                                r = e.transpose(ps_t[b][:, i4 * 32:(i4 + 1) * 32], stf[0:32, jj * 128:(jj + 1) * 128], identf[0:32, 0:32])
                            return r
                        S.add("pe", tr, reads=["stf", "identf"], writes=[PSR(b)])
                        j0 = q4 * 16 + g4 * 4
                        S.add("act", lambda e, b=b, j0=j0: e.copy(out=FH[:, j0:j0 + 4, :], in_=ps_t[b][:, 0:128].rearrange("p (j n) -> p j n", n=32)),
                              writes=[PSR(b), "FH"])
                        banks.free(b)
            pend = []
            pend_t = []

            def hist_in(Un, ui_, jj_):
                Ub_ = (Uv if Un == "Uv" else Ug)[ui_]
                if p == 0:
                    S.add("pool", lambda e: e.memset(Ub_[:, 62:64], 0.0), writes=[(Un, ui_, "h")])
                else:
                    S.add("pool", lambda e: e.tensor_copy(out=Ub_[:, 62:64], in_=ffnhist[l][:, jj_, :]),
                          reads=[("dram", "ffnhist", l, jj_)], writes=[(Un, ui_, "h")])
            for pb in range(8):
                wv_t, wv_res = W.get((p, "wup", l, pb))
                wg_t, wg_res = W.get((p, "wup", l, pb + 8))
                vv = wv_t[:].rearrange("p (k n) -> p k n", n=512)
                vg = wg_t[:].rearrange("p (k n) -> p k n", n=512)
                wsc = lambda k, jj: sm[:, SM_FCW + k * 64 + jj:SM_FCW + k * 64 + jj + 1]
                bsc = lambda jj: sm[:, SM_FCB + jj:SM_FCB + jj + 1]
                for cc in range(4):
                    j = pb * 4 + cc
                    ci = rot["u"] % 3
                    ui = ci
                    rot["u"] += 1
                    for (wview, wres_, Ub, Un, jj, cvb, cres) in ((vv, wv_res, Uv[ui], "Uv", j, cvv[ci], ("cvv", ci)),
                                                                 (vg, wg_res, Ug[ui], "Ug", 32 + j, cvg[ci], ("cvg", ci))):
                        UH, UB = (Un, ui, "h"), (Un, ui, "b")
                        if j == 0:
                            hist_in(Un, ui, jj)
                        if j + 1 < 32:
                            hist_in(Un, (ui + 1) % 3, jj + 1)

                        def ev(b, c0, n, Ub=Ub, UB=UB, cvb=cvb, cres=cres, jj=jj):
                            S.add("act", lambda e: e.copy(out=Ub[:, 64:64 + TP], in_=ps_t[b][:, 0:TP]), writes=[PSR(b), UB])
                            S.add("act", lambda e: e.activation(out=cvb[:, 0:TP], in_=ps_t[b][:, 0:TP], func=AF.Identity, scale=wsc(2, jj), bias=bsc(jj)),
                                  reads=[SMR], writes=[PSR(b), cres])
                        proj(wview, wres_, cc * 128, h, HR, NCH, [(0, TP)], ev)
                        S.add("pool", lambda e, Ub=Ub, jj=jj: e.tensor_copy(out=ffnhist[l][:, jj, :], in_=Ub[:, 62 + TP:64 + TP]),
                              reads=[UB], writes=[("dram", "ffnhist", l, jj)])
                        for k in range(2):
                            S.add("dve", lambda e, Ub=Ub, cvb=cvb, jj=jj, k=k: e.scalar_tensor_tensor(out=cvb[:, 0:TP], in0=Ub[:, 62 + k:62 + k + TP], scalar=wsc(k, jj),
                                                                                                      in1=cvb[:, 0:TP], op0=ALU.mult, op1=ALU.add),
                                  reads=[UH, UB, SMR], writes=[cres])

                    def tail(ci=ci, j=j):
                        S.add("act", lambda e: e.activation(out=ggb[ci][:, 0:TP], in_=cvg[ci][:, 0:TP], func=AF.Gelu_apprx_tanh),
                              reads=[("cvg", ci)], writes=[("gg", ci)])
                        S.add("dve", lambda e: e.tensor_tensor(out=A[:, j, 0:TP], in0=cvv[ci][:, 0:TP], in1=ggb[ci][:, 0:TP], op=ALU.mult),
                              reads=[("cvv", ci), ("gg", ci)], writes=[("A", j)])
                    if pend:
                        pend.pop()()
                    pend.append(tail)
                if has_s and pend_t:
                    pend_t.pop(0)()
                if has_s:
                    for half, (wview, wres_) in enumerate(((vv, wv_res), (vg, wg_res))):
                        jj0 = pb * 4 + 32 * half
                        Uh = Us[half][:].rearrange("p c (b k) -> p c b k", k=6)
                        b = banks.alloc()

                        def mm(e, b=b, wview=wview):
                            r = None
                            for c4 in range(4):
                                for k in range(NCH):
                                    r = e.matmul(ps_t[b][:, c4 * NS:(c4 + 1) * NS], wview[:, k, c4 * 128:(c4 + 1) * 128], h[:, k, TP:TP + NS],
                                                 start=(k == 0), stop=(k == NCH - 1))
                            return r
                        S.add("pe", mm, reads=[wres_] + HR, writes=[PSR(b)])
                        S.add("pool", lambda e, Uh=Uh, jj0=jj0: e.tensor_copy(out=Uh[:, :, :, 0:2], in_=FH[:, jj0:jj0 + 4, :].rearrange("p c (b k) -> p c b k", k=2)),
                              reads=["FH"], writes=[("Us", half)])
                        S.add("act", lambda e, Uh=Uh, b=b: e.copy(out=Uh[:, :, :, 2:6], in_=ps_t[b][:, 0:4 * NS].rearrange("p (c b t) -> p c b t", c=4, t=4)),
                              writes=[PSR(b), ("Us", half)])
                        for c4 in range(4):
                            S.add("act", lambda e, b=b, c4=c4, half=half, jj0=jj0: e.activation(out=cvs[half][:, c4, :], in_=ps_t[b][:, c4 * NS:(c4 + 1) * NS],
                                                                                               func=AF.Identity, scale=wsc(2, jj0 + c4), bias=bsc(jj0 + c4)),
                                  reads=[SMR], writes=[PSR(b), ("cvs", half)])
                        banks.free(b)
                        S.add("pool", lambda e, Uh=Uh, half=half, pb=pb: e.tensor_copy(
                            out=SFs[pb % 2][:, half * 4:(half + 1) * 4, :].rearrange("p c (b k) -> p c b k", k=2), in_=Uh[:, :, :, 4:6]),
                            reads=[("Us", half)], writes=[("SFs", pb % 2)])
                        for c4 in range(4):
                            cv4 = cvs[half][:, c4, :].rearrange("p (b t) -> p b t", t=4)
                            for k in range(2):
                                S.add("dve", lambda e, Uh=Uh, cv4=cv4, c4=c4, k=k, jj0=jj0: e.scalar_tensor_tensor(out=cv4, in0=Uh[:, c4, :, k:k + 4], scalar=wsc(k, jj0 + c4),
                                                                                                                 in1=cv4, op0=ALU.mult, op1=ALU.add),
                                      reads=[("Us", half), SMR], writes=[("cvs", half)])
                    S.add("act", lambda e: e.activation(out=cvs[1][:], in_=cvs[1][:], func=AF.Gelu_apprx_tanh), writes=[("cvs", 1)])
                    S.add("dve", lambda e, pb=pb: e.tensor_tensor(out=A[:, pb * 4:(pb + 1) * 4, TP:TP + NS], in0=cvs[0][:], in1=cvs[1][:], op=ALU.mult),
                          reads=[("cvs", 0), ("cvs", 1)], writes=[("A", pb * 4 + i) for i in range(4)])
                W.release()
                W.release()
                if has_s:
                    def tout(pb=pb):
                        jl = [pb * 4 + i for i in range(4)] + [32 + pb * 4 + i for i in range(4)]
                        transpose_out(lambda k, pb=pb: SFs[pb % 2][:, k, :], [NB * 2] * 8,
                                      [o_sf[l][:, jj * 128:(jj + 1) * 128] for jj in jl], [("SFs", pb % 2)], "sf")
                    pend_t.append(tout)
            while pend_t:
                pend_t.pop(0)()
            if last_pass:
                S.add("dve", lambda e, l=l: e.tensor_copy(out=pst[:, 0:128].rearrange("p (k c) -> p k c", c=64),
                                                          in_=ffnhist[l][:].rearrange("p c k -> p k c")),
                      reads=[("dram", "ffnhist", l, q_) for q_ in range(64)], writes=["pst"])
                transpose_out(lambda k: pst[:, 0:128], [128], [o_pf[l].rearrange("k (c p) -> (k c) p", p=128)], ["pst"], "pf")
            if pend:
                pend.pop()()
            if l == DEPTH - 1 and p + 1 < NPASS:
                S.add("sp", lambda e, p=p: e.dma_start(out=tokx[:], in_=xp[(p + 1) * TP:(p + 2) * TP, :].rearrange("(j r) d -> r j d", r=128)),
                      writes=["tokx"], dsem="xin")
            ARs = keys("A", 32)
            for oc in range(8):
                wt, wres = W.get((p, "wdn", l, oc))
                wv = wt[:].rearrange("p (k n) -> p k n", n=128)

                def ev(b, c0, n, oc=oc):
                    S.add("act", lambda e: e.activation(out=sq[:, oc, c0:c0 + n], in_=ps_t[b][:, 0:n], func=AF.Square), writes=[PSR(b), ("sq", oc)])
                    S.add("act", lambda e: e.activation(out=mbuf[:, oc, c0:c0 + n], in_=ps_t[b][:, 0:n], func=AF.Identity,
                                                        scale=sm[:, SM_NFPOST + oc:SM_NFPOST + oc + 1]),
                          reads=[SMR], writes=[PSR(b), ("mbuf", oc)])
                proj(wv, wres, 0, A, ARs, 32, subs, ev)
                W.release()
            postnorm_residual(l, T, subs, l + 1 < DEPTH)

    def finish_pass(p, has_s):
        for j in range(4):
            for half in range(2):
                b = banks.alloc()

                def tr(e, b=b, j=j, half=half):
                    r = None
                    for c4 in range(4):
                        c = half * 4 + c4
                        r = e.transpose(ps_t[b][:, c4 * 128:(c4 + 1) * 128], x[:, c, j * 128:(j + 1) * 128], identf[:])
                    return r
                S.add("pe", tr, reads=XK + ["identf"], writes=[PSR(b)])
                S.add("act", lambda e, b=b, j=j, half=half: e.copy(out=tok[:, j, half * 512:(half + 1) * 512], in_=ps_t[b][:, 0:512]),
                      writes=[PSR(b), "tok"])
                banks.free(b)
        S.add("sp", lambda e, p=p: e.dma_start(out=yp[p * TP:(p + 1) * TP, :].rearrange("(j r) d -> r j d", r=128), in_=tok[:]),
              reads=["tok"], dsem="yout")
        if has_s:
            for half in range(2):
                b = banks.alloc()

                def tr(e, b=b, half=half):
                    r = None
                    for c4 in range(4):
                        c = half * 4 + c4
                        r = e.transpose(ps_t[b][0:NS, c4 * 128:(c4 + 1) * 128], x[:, c, TP:TP + NS], identf[:])
                    return r
                S.add("pe", tr, reads=XK + ["identf"], writes=[PSR(b)])
                S.add("act", lambda e, b=b, half=half: e.copy(out=toks[0:NS, half * 512:(half + 1) * 512], in_=ps_t[b][0:NS, 0:512]),
                      writes=[PSR(b), "toks"])
                banks.free(b)
            S.add("sp", lambda e: e.dma_start(out=ys, in_=toks[0:NS, :]), reads=["toks"], dsem="yout2")

    for p in range(NPASS):
        do_pass(p)
    assert W.pos == len(seq)
    S.emit()
    return nc


def _t5_bucket(d):
    n = max(d, 0)
    if n < 16:
        return n
    import math
    nf = np.float32(max(n, 1))
    v = np.float32(np.log(nf / np.float32(16)) / np.float32(math.log(128 / 16))) * np.float32(16)
    return min(16 + int(v), 31)


_NC_CACHE = {}


def kernel(x_prompt, x_sample, state_lru_h, state_lru_conv, cache_win_k, cache_win_v, state_ffn_conv,
           norm_mix_pre, norm_mix_post, norm_ffn_pre, norm_ffn_post, w_in, conv_lru_w, conv_lru_b,
           lru_wr, lru_br, lru_wi, lru_bi, lru_lambda, w_lru_o, w_attn_o, w_out, attn_sink, rel_bias,
           w_up, ffn_conv_w, ffn_conv_b, w_down):
    f32 = lambda a: np.ascontiguousarray(np.asarray(a, dtype=np.float32))
    x_prompt, x_sample = f32(x_prompt), f32(x_sample)
    n = 8

    def blk(w, nb, ncols):
        L = w.shape[0]
        kc = w.shape[1] // 128
        return f32(np.asarray(w).reshape(L, kc, 128, nb, ncols).transpose(0, 3, 2, 1, 4).reshape(L * nb, 128, kc * ncols))

    win_r = blk(f32(w_in), 9, 512)
    wlo_r = blk(f32(w_lru_o), 2, 512)
    wao_r = blk(f32(w_attn_o), 2, 512)
    wout_r = blk(f32(w_out), 2, 512)
    wup_r = blk(f32(w_up), 16, 512)
    wdn_r = blk(f32(w_down), 8, 128)
    gates_r = f32(np.stack([f32(lru_wr), f32(lru_wi)], axis=1).transpose(0, 3, 1, 2, 4).reshape(DEPTH, 128, 2048))

    def pc(v):
        v = f32(v)
        return v.reshape(v.shape[0], -1, 128).transpose(0, 2, 1)

    def pck(v):
        v = f32(v)
        L, K = v.shape[0], v.shape[1]
        return v.reshape(L, K, -1, 128).transpose(0, 3, 1, 2).reshape(L, 128, -1)

    smalls = f32(np.concatenate([pc(norm_mix_pre), pc(norm_mix_post), pc(norm_ffn_pre), pc(norm_ffn_post),
                                 pck(conv_lru_w), pc(conv_lru_b), pc(lru_br), pc(lru_bi), pc(lru_lambda),
                                 pck(ffn_conv_w), pc(ffn_conv_b)], axis=2))
    assert smalls.shape == (DEPTH, 128, SM_N), smalls.shape
    sk = f32(attn_sink)
    sinks = np.zeros((DEPTH, 128, 10), np.float32)
    sinks[:, :, 0:8] = sk[:, None, :]
    for kv in range(2):
        for g in range(4):
            sinks[:, g * 4:(g + 1) * 4, 8 + kv] = sk[:, kv * 4 + g][:, None]
    rb = f32(rel_bias)
    bidx = np.array([_t5_bucket(d) for d in range(128)])
    qq = np.arange(128)[:, None]
    jj = np.arange(256)[None, :]
    dd = qq + 128 - jj
    valid = (dd >= 0) & (dd < 128)
    gat = rb[bidx[np.clip(dd, 0, 127)]]
    biasp = np.where(valid[:, :, None], gat, np.float32(NEG)).transpose(0, 2, 1)
    biasp = f32(biasp).reshape(128, 8 * 256)
    tt = np.arange(4)[:, None]
    js = np.arange(132)[None, :]
    ds = tt + 128 - js
    vs = (ds >= 0) & (ds < 128)
    gs = rb[bidx[np.clip(ds, 0, 127)]]
    bs = np.where(vs[:, :, None], gs, np.float32(NEG))
    biass = np.zeros((16, 2, 132), np.float32)
    for kv in range(2):
        for g in range(4):
            biass[g * 4:(g + 1) * 4, kv, :] = bs[:, :, kv * 4 + g]
    biass = f32(biass).reshape(16, 2 * 132)
    ident = np.eye(128, dtype=np.float32)

    st_h, st_c, ckk, cvv, st_f = f32(state_lru_h), f32(state_lru_conv), f32(cache_win_k), f32(cache_win_v), f32(state_ffn_conv)
    in_maps = []
    for i in range(n):
        sl = slice(i * NB, (i + 1) * NB)
        in_maps.append({
            "xp": x_prompt[i], "xs": f32(x_sample[sl].reshape(NS, D)),
            "st_h": f32(st_h[:, sl]), "st_c": f32(st_c[:, sl].reshape(DEPTH, NB * 3, D)),
            "ck": f32(ckk[:, sl].reshape(DEPTH, NB, 128, 256)), "cv": f32(cvv[:, sl].reshape(DEPTH, NB, 128, 256)),
            "st_f": f32(st_f[:, sl].reshape(DEPTH, NB * 2, 8192)),
            "win_r": win_r, "gates_r": gates_r, "wlo_r": wlo_r, "wao_r": wao_r, "wout_r": wout_r, "wup_r": wup_r, "wdn_r": wdn_r,
            "smalls": smalls, "sinks": sinks, "biasp": biasp, "biass": biass, "ident": ident,
        })
    if "nc" not in _NC_CACHE:
        _NC_CACHE["nc"] = build()
    nc = _NC_CACHE["nc"]
    res = run_bass_kernel_spmd(nc, in_maps, core_ids=list(range(n)))
    R = res.results
    y_prompt = np.stack([R[i]["yp"] for i in range(n)], axis=0)
    y_sample = np.concatenate([R[i]["ys"].reshape(NB, 4, D) for i in range(n)], axis=0)
    p_lru_h = np.stack([R[i]["o_plh"] for i in range(n)], axis=1)
    p_lru_conv = np.stack([R[i]["o_plc"] for i in range(n)], axis=1)
    p_win_k = np.stack([R[i]["o_pk"].reshape(DEPTH, 128, 2, 128) for i in range(n)], axis=1)
    p_win_v = np.stack([R[i]["o_pv"].reshape(DEPTH, 128, 2, 128) for i in range(n)], axis=1)
    p_ffn = np.stack([R[i]["o_pf"] for i in range(n)], axis=1)
    s_lru_h = np.concatenate([R[i]["o_slh"] for i in range(n)], axis=1)
    s_lru_conv = np.concatenate([R[i]["o_slc"].reshape(DEPTH, NB, 3, D) for i in range(n)], axis=1)
    s_win_k = np.concatenate([R[i]["o_sk"].reshape(DEPTH, NB, 128, 2, 128) for i in range(n)], axis=1)
    s_win_v = np.concatenate([R[i]["o_sv"].reshape(DEPTH, NB, 128, 2, 128) for i in range(n)], axis=1)
    s_ffn = np.concatenate([R[i]["o_sf"].reshape(DEPTH, NB, 2, 8192) for i in range(n)], axis=1)
    outs = (y_prompt, y_sample, p_lru_h, p_lru_conv, p_win_k, p_win_v, p_ffn, s_lru_h, s_lru_conv, s_win_k, s_win_v, s_ffn)
    return tuple(np.ascontiguousarray(o, dtype=np.float32) for o in outs)
```

```python
import numpy as np
import concourse.bass as bass
import concourse.mybir as mybir
from concourse.bass_utils import run_bass_kernel_spmd

F32 = mybir.dt.float32
BF16 = mybir.dt.bfloat16
AF = mybir.ActivationFunctionType
ALU = mybir.AluOpType
AX = mybir.AxisListType

D = 1024
NCH = 8
SEQ = 2048
TP = 512
NPASS = SEQ // TP
NB = 16
NS = 64
DEPTH = 2
NEG = -30000.0
EPS = 1e-6
QSCALE = 128 ** -0.5
NSLOT = 5
GR = 256
SB_BASE = 16512
SB_TOP = 229344

SM_NMP, SM_NMPOST, SM_NFP, SM_NFPOST = 0, 8, 16, 24
SM_CLW = 32
SM_CLB = 64
SM_BR, SM_BI, SM_LAM = 72, 80, 88
SM_FCW = 96
SM_FCB = 288
SM_N = 352


class _Op:
    __slots__ = ("id", "eng", "fn", "deps", "signals", "dsem", "sidx", "ninst")


class Sched:
    ENGS = ("pe", "act", "dve", "pool", "sp")

    def __init__(self, nc):
        self.nc = nc
        self.ops = []
        self.last_writer = {}
        self.readers = {}
        self.dma_sems = {}
        self.alias = {}

    def reg(self, key, off, nbytes):
        g0 = off // GR
        g1 = (off + nbytes + GR - 1) // GR
        self.alias[key] = [("g", g) for g in range(g0, g1)]

    def _expand(self, keys):
        out = []
        for k in keys:
            a = self.alias.get(k)
            if a is None:
                assert isinstance(k, tuple) and k[0] in ("ps", "w", "dram"), ("unregistered key", k)
                out.append(k)
            else:
                out.extend(a)
        return out

    def add(self, eng, fn, reads=(), writes=(), dsem=None, ndma=1):
        reads = self._expand(reads)
        writes = self._expand(writes)
        deps = set()
        for r in reads:
            lw = self.last_writer.get(r)
            if lw is not None:
                deps.add(lw)
        for w in writes:
            lw = self.last_writer.get(w)
            if lw is not None:
                deps.add(lw)
            for rd in self.readers.get(w, ()):
                deps.add(rd)
        op = _Op()
        op.id = len(self.ops)
        op.eng = eng
        op.fn = fn
        op.deps = deps
        op.dsem = dsem
        op.signals = dsem is not None
        op.sidx = None
        op.ninst = ndma
        deps.discard(op.id)
        self.ops.append(op)
        for r in reads:
            self.readers.setdefault(r, []).append(op.id)
        for w in writes:
            self.last_writer[w] = op.id
            self.readers[w] = []
        return op.id

    def emit(self):
        nc = self.nc
        ops = self.ops
        for op in ops:
            for d in op.deps:
                p = ops[d]
                if p.eng == "pe" and op.eng == "pe" and p.dsem is None:
                    continue
                p.signals = True
        esem = {e: nc.alloc_semaphore("s_" + e) for e in ("pe", "act", "dve", "pool")}
        ecount = {e: 0 for e in esem}
        dcount = {}
        for op in ops:
            if op.dsem is not None:
                if op.dsem not in self.dma_sems:
                    self.dma_sems[op.dsem] = nc.alloc_semaphore("d_%d" % len(self.dma_sems))
                    dcount[op.dsem] = 0
                dcount[op.dsem] += 16 * op.ninst
                op.sidx = (self.dma_sems[op.dsem], dcount[op.dsem])
            elif op.signals:
                ecount[op.eng] += 1
                op.sidx = (esem[op.eng], ecount[op.eng])
        final_waits = {k: (self.dma_sems[k], v) for k, v in dcount.items()}
        by_eng = {e: [op for op in ops if op.eng == e] for e in self.ENGS}

        def run(engine, ename):
            waited = {}
            for op in by_eng[ename]:
                need = {}
                for d in op.deps:
                    p = ops[d]
                    if p.eng == "pe" and ename == "pe" and p.dsem is None:
                        continue
                    sem, val = p.sidx
                    k = id(sem)
                    if waited.get(k, 0) >= val:
                        continue
                    if k not in need or need[k][1] < val:
                        need[k] = (sem, val)
                for k, (sem, val) in need.items():
                    engine.wait_ge(sem, val)
                    waited[k] = val
                r = op.fn(engine)
                insts = r if isinstance(r, (list, tuple)) else [r]
                if op.dsem is not None:
                    assert len(insts) == op.ninst, (len(insts), op.ninst)
                    for i in insts:
                        i.then_inc(op.sidx[0], 16)
                elif op.signals:
                    insts[-1].then_inc(op.sidx[0], 1)
            if ename == "sp":
                for k, (sem, val) in final_waits.items():
                    engine.wait_ge(sem, val)

        with nc.Block() as block:
            @block.sync
            def _(e):
                run(e, "sp")

            @block.gpsimd
            def _(e):
                run(e, "pool")

            @block.tensor
            def _(e):
                run(e, "pe")

            @block.scalar
            def _(e):
                run(e, "act")

            @block.vector
            def _(e):
                run(e, "dve")


class Banks:
    def __init__(self, tensors):
        self.t = tensors
        self.free_list = list(range(len(tensors)))

    def alloc(self):
        assert self.free_list, "out of PSUM banks"
        return self.free_list.pop(0)

    def free(self, b):
        assert b not in self.free_list
        self.free_list.append(b)


class WStream:
    def __init__(self, S, nc, seq, slots):
        self.S = S
        self.seq = seq
        self.slots = slots
        self.next_dma = 0
        self.released = 0
        self.pos = 0
        self._pump()

    def _pump(self):
        while self.next_dma < len(self.seq) and self.next_dma - NSLOT < self.released:
            n = self.next_dma
            key, src, ncol = self.seq[n]
            sl = n % NSLOT
            dst = self.slots[sl]
            nd = ncol // 2048

            def fn(e, src=src, dst=dst, nd=nd):
                return [e.dma_start(out=dst[:, i * 2048:(i + 1) * 2048], in_=src[:, i * 2048:(i + 1) * 2048])
                        for i in range(nd)]
            self.S.add("pool", fn, writes=[("w", sl)], dsem=("w", sl), ndma=nd)
            self.next_dma += 1

    def get(self, key):
        i = self.pos
        assert self.seq[i][0] == key, (self.seq[i][0], key)
        self.pos += 1
        self._pump()
        assert i < self.next_dma
        sl = i % NSLOT
        return self.slots[sl], ("w", sl)

    def release(self):
        self.released += 1
        self._pump()


def layer_block_keys(l):
    ks = [("win", l, 0), ("win", l, 1), ("win", l, 2), ("win", l, 3), ("win", l, 4)]
    ks += [("wlo", l, 0), ("win", l, 5), ("wlo", l, 1), ("win", l, 6)]
    ks += [("wao", l, 0), ("win", l, 7), ("wao", l, 1), ("win", l, 8)]
    ks += [("wout", l, 0), ("wout", l, 1)]
    for pb in range(8):
        ks += [("wup", l, pb), ("wup", l, pb + 8)]
    ks += [("wdn", l, oc) for oc in range(8)]
    return ks


def build():
    nc = bass.Bass("TRN2", target_bir_lowering=False)

    def din(name, shape):
        return nc.dram_tensor(name, list(shape), F32, kind="ExternalInput").ap()

    def dout(name, shape):
        return nc.dram_tensor(name, list(shape), F32, kind="ExternalOutput").ap()

    xp = din("xp", [SEQ, D])
    xs = din("xs", [NS, D])
    st_h = din("st_h", [DEPTH, NB, D])
    st_c = din("st_c", [DEPTH, NB * 3, D])
    ck = din("ck", [DEPTH, NB, 128, 256])
    cv = din("cv", [DEPTH, NB, 128, 256])
    st_f = din("st_f", [DEPTH, NB * 2, 8192])
    win_r = din("win_r", [DEPTH * 9, 128, 4096])
    gates_r = din("gates_r", [DEPTH, 128, 2048])
    wlo_r = din("wlo_r", [DEPTH * 2, 128, 4096])
    wao_r = din("wao_r", [DEPTH * 2, 128, 4096])
    wout_r = din("wout_r", [DEPTH * 2, 128, 4096])
    wup_r = din("wup_r", [DEPTH * 16, 128, 4096])
    wdn_r = din("wdn_r", [DEPTH * 8, 128, 4096])
    smalls_d = din("smalls", [DEPTH, 128, SM_N])
    sinks_d = din("sinks", [DEPTH, 128, 10])
    biasp_d = din("biasp", [128, 8 * 256])
    biass_d = din("biass", [16, 2 * 132])
    ident_d = din("ident", [128, 128])

    yp = dout("yp", [SEQ, D])
    ys = dout("ys", [NS, D])
    o_plh = dout("o_plh", [DEPTH, D])
    o_plc = dout("o_plc", [DEPTH, 3, D])
    o_pk = dout("o_pk", [DEPTH, 128, 256])
    o_pv = dout("o_pv", [DEPTH, 128, 256])
    o_pf = dout("o_pf", [DEPTH, 2, 8192])
    o_slh = dout("o_slh", [DEPTH, NB, D])
    o_slc = dout("o_slc", [DEPTH, NB * 3, D])
    o_sk = dout("o_sk", [DEPTH, NB, 128, 256])
    o_sv = dout("o_sv", [DEPTH, NB, 128, 256])
    o_sf = dout("o_sf", [DEPTH, NB * 2, 8192])

    S = Sched(nc)
    TMAX = TP + NS
    XRW = 3 + TP + NB * 7
    UW = 2 + TP + NB * 6

    def esz(dt):
        return 2 if dt == BF16 else 4

    def mk(name, shape, dt, off, key=None, chunks=None):
        assert off % 32 == 0
        nbytes = int(np.prod(shape[1:])) * esz(dt)
        assert SB_BASE + off + nbytes <= SB_TOP, (name, off, nbytes)
        t = nc.alloc_sbuf_tensor_at(name, list(shape), dt, offset=SB_BASE + off)
        key = key or name
        S.reg(key, off, nbytes)
        if chunks:
            cb = nbytes // chunks
            for c in range(chunks):
                S.reg((key, c), off + c * cb, cb)
        return t

    cur = [0]

    def P(name, shape, dt=F32, chunks=None, key=None):
        nbytes = int(np.prod(shape[1:])) * esz(dt)
        off = cur[0]
        cur[0] += (nbytes + GR - 1) // GR * GR
        return mk(name, shape, dt, off, key=key, chunks=chunks)

    identf = P("identf", [128, 128])
    identb = P("identb", [128, 128], BF16)
    ones_b = P("ones_b", [128, 128], BF16)
    epsc = P("epsc", [128, 1])
    smalls = [P("smalls%d" % l, [128, SM_N], key=("smalls", l)) for l in range(DEPTH)]
    lamc = [P("lamc%d" % l, [128, 16], key=("lamc", l)) for l in range(DEPTH)]
    sinks = [P("sinks%d" % l, [128, 10], key=("sinks", l)) for l in range(DEPTH)]
    biasp = P("biasp", [128, 8, 256])
    biass = P("biass", [16, 2, 132])
    convhist = [P("convhist%d" % l, [128, NCH, 3], key=("convhist", l)) for l in range(DEPTH)]
    hstate = [P("hstate%d" % l, [128, NCH], key=("hstate", l)) for l in range(DEPTH)]
    ffnhist = [P("ffnhist%d" % l, [128, 64, 2], key=("ffnhist", l)) for l in range(DEPTH)]
    khist = [P("khist%d" % l, [128, 2, 128], BF16, key=("khist", l)) for l in range(DEPTH)]
    vhist = [P("vhist%d" % l, [128, 256], BF16, key=("vhist", l)) for l in range(DEPTH)]
    gatesw = [P("gatesw%d" % l, [128, 2048], BF16, key=("gatesw", l)) for l in range(DEPTH)]
    x = P("x", [128, NCH, TMAX], chunks=NCH)
    h = P("h", [128, NCH, TMAX], BF16, chunks=NCH)
    rstd = P("rstd", [128, TMAX])
    sdv = P("sdv", [128, TMAX])
    stout = P("stout", [128, 1024])
    SFs = [P("SFs%d" % i, [128, 8, NB * 2], key=("SFs", i)) for i in range(2)]
    pst = P("pst", [128, 128])
    h0s = P("h0s", [128, NCH, NB])
    hs_last = P("hs_last", [128, NCH, NB])
    cs_stage = P("cs_stage", [128, NCH, NB * 3])
    tmp16 = P("tmp16", [128, NB])
    qTs = P("qTs", [128, 2, NB, 16], BF16)
    wslots = [P("wslot%d" % i, [128, 4096], BF16) for i in range(NSLOT)]
    SCR = cur[0]
    RA = SCR
    RB = SCR + 36864
    assert SB_BASE + RB + 61696 <= SB_TOP, (SCR, SB_TOP - SB_BASE)

    lro = mk("lro", [128, NCH, TMAX], BF16, RA + 0, chunks=NCH)
    qT = mk("qT", [128, 8, TMAX], BF16, RA + 9216, chunks=8)
    merged = mk("merged", [128, NCH, TMAX], BF16, RA + 9216, chunks=NCH)
    attn = mk("attn", [128, 8, TMAX], BF16, RA + 18432, chunks=8)
    kT = mk("kT", [128, 2, 128 + TP], BF16, RA + 27648)
    Vt = mk("Vt", [128, 5, 256], BF16, RA + 30208)
    kvnew = mk("kvnew", [64, 512], F32, RA + 32768)
    kvlast = mk("kvlast", [128, 512], F32, RA + 34816)
    A = mk("A", [128, 32, TMAX], BF16, RA + 0, chunks=32)
    sq = mk("sq", [128, NCH, TMAX], BF16, RB + 0, chunks=NCH)
    tok = mk("tok", [128, 4, D], F32, RB + 36864)
    tokx = mk("tokx", [128, 4, D], F32, RB + 20480)
    toks = mk("toks", [64, D], F32, RB + 16384)
    XR = mk("XR", [128, NCH, XRW], F32, RB + 0, chunks=NCH)
    LS = 20736
    xcg = [mk("xc%d" % s_, [128, 2, TMAX], F32, RB + 20224 + s_ * LS, key=("xc", s_), chunks=2) for s_ in range(2)]
    xcb = [mk("xcb%d" % s_, [128, 2, TMAX], BF16, RB + 20224 + s_ * LS + 4608, key=("xcb", s_), chunks=2) for s_ in range(2)]
    rr = [mk("rr%d" % s_, [128, 2, TMAX], F32, RB + 20224 + s_ * LS + 6912, key=("rr", s_), chunks=2) for s_ in range(2)]
    ii = [mk("ii%d" % s_, [128, 2, TMAX], F32, RB + 20224 + s_ * LS + 11520, key=("ii", s_), chunks=2) for s_ in range(2)]
    aa = [mk("aa%d" % s_, [128, 2, TMAX], F32, RB + 20224 + s_ * LS + 16128, key=("aa", s_), chunks=2) for s_ in range(2)]
    kTs = mk("kTs", [128, 2, NB, 132], BF16, RB + 0)
    Vs = mk("Vs", [128, NB, 256], BF16, RB + 8448)
    Vn4 = mk("Vn4", [4, NB, 256], BF16, RB + 16640)
    cstage = mk("cstage", [128, NB, 256], F32, RB + 24832)
    o3 = RB + 41216
    NSET = 8
    SBW = 64 + 260
    Sbuf, Pn, PT, stat = [], [], [], []
    for i in range(NSET):
        o = o3 + i * 2816
        Sbuf.append(mk("Sbuf%d" % i, [128, SBW], F32, o, key=("Sb", i)))
        S.reg(("Sb", i, "h"), o, 256)
        S.reg(("Sb", i, "b"), o + 256, 4 * 260)
        Pn.append(mk("Pn%d" % i, [128, 256], BF16, o + 1536, key=("Pn", i)))
        PT.append(mk("PT%d" % i, [128, 2, 128], BF16, o + 2048, key=("PT", i)))
        stat.append(mk("stat%d" % i, [128, 4], F32, o + 2560, key=("st", i)))
    sgA = [mk("sgA%d" % i, [128, TP], F32, RB + 9216 + i * 2048, key=("sgA", i)) for i in range(2)]
    sgB = [mk("sgB%d" % i, [128, TP], F32, RB + 13312 + i * 2048, key=("sgB", i)) for i in range(2)]
    t1b = [mk("t1b%d" % i, [128, TP], F32, RB + 17408 + i * 2048, key=("t1b", i)) for i in range(2)]
    mbuf = mk("mbuf", [128, NCH, TMAX], F32, RB + 40448, chunks=NCH)
    def mkU(name, i, off):
        t = mk("%s%d" % (name, i), [128, 64 + TP], F32, off, key=(name, i))
        S.reg((name, i, "h"), off, 256)
        S.reg((name, i, "b"), off + 256, 4 * TP)
        return t
    Uv = [mkU("Uv", 0, RB + 9216), mkU("Uv", 1, RB + 9216 + 2560), mkU("Uv", 2, RB + 41472)]
    Ug = [mkU("Ug", 0, RB + 14336), mkU("Ug", 1, RB + 14336 + 2560), mkU("Ug", 2, RB + 41472 + 2304)]
    cvv = [mk("cvv%d" % i, [128, TMAX], F32, RB + 19456 + i * 2304, key=("cvv", i)) for i in range(2)]
    cvg = [mk("cvg%d" % i, [128, TMAX], F32, RB + 24064 + i * 2304, key=("cvg", i)) for i in range(2)]
    ggb = [mk("ggb%d" % i, [128, TMAX], F32, RB + 28672 + i * 2304, key=("gg", i)) for i in range(2)]
    cvv.append(mk("cvv2", [128, TMAX], F32, RB + 0, key=("cvv", 2)))
    cvg.append(mk("cvg2", [128, TMAX], F32, RB + 2304, key=("cvg", 2)))
    ggb.append(mk("ggb2", [128, TMAX], F32, RB + 4608, key=("gg", 2)))
    FH = mk("FH", [128, 64, NB * 2], F32, RB + 33280)
    stf = mk("stf", [32, 2048], F32, RB + 0)
    Us = [mk("Us%d" % i, [128, 4, NB * 6], F32, RB + 58880 + i * 1536, key=("Us", i)) for i in range(2)]
    cvs = [mk("cvs%d" % i, [128, 4, NS], F32, RB + 6912 + i * 1024, key=("cvs", i)) for i in range(2)]
    print("SBUF map: persistent=%d scratch_avail=%d" % (SCR, SB_TOP - SB_BASE - SCR))

    ps_t = [nc.alloc_psum_tensor("ps%d" % i, [128, 512], F32) for i in range(8)]
    banks = Banks(ps_t)

    def PSR(b):
        return ("ps", b)

    def keys(name, n):
        return [(name, c) for c in range(n)]

    wsrc = {"win": (win_r, 9), "wlo": (wlo_r, 2), "wao": (wao_r, 2), "wout": (wout_r, 2),
            "wup": (wup_r, 16), "wdn": (wdn_r, 8)}
    seq = []
    for p in range(NPASS):
        for l in range(DEPTH):
            for key in layer_block_keys(l):
                t, n = wsrc[key[0]]
                seq.append(((p,) + key, t[l * n + key[2]], 4096))
    for l in range(DEPTH):
        S.add("pool", lambda e, l=l: e.dma_start(out=gatesw[l][:], in_=gates_r[l]), writes=[("gatesw", l)], dsem=("gw", l))
    W = WStream(S, nc, seq, wslots)

    S.add("sp", lambda e: e.dma_start(out=identf[:], in_=ident_d), writes=["identf"], dsem="c0")
    S.add("sp", lambda e: e.dma_start(out=biasp[:].rearrange("p h k -> p (h k)"), in_=biasp_d), writes=["biasp"], dsem="c1")
    S.add("sp", lambda e: e.dma_start(out=biass[:].rearrange("p h k -> p (h k)"), in_=biass_d), writes=["biass"], dsem="c2")
    for l in range(DEPTH):
        S.add("sp", lambda e, l=l: e.dma_start(out=smalls[l][:], in_=smalls_d[l]), writes=[("smalls", l)], dsem=("c3", l))
        S.add("sp", lambda e, l=l: e.dma_start(out=sinks[l][:], in_=sinks_d[l]), writes=[("sinks", l)], dsem=("c4", l))
    S.add("dve", lambda e: e.tensor_scalar(biasp[:], biasp[:], -1.0, None, ALU.mult), writes=["biasp"])
    S.add("dve", lambda e: e.tensor_scalar(biass[:], biass[:], -1.0, None, ALU.mult), writes=["biass"])
    for l in range(DEPTH):
        S.add("dve", lambda e, l=l: e.tensor_scalar(sinks[l][:], sinks[l][:], -1.0, None, ALU.mult), writes=[("sinks", l)])
    S.add("dve", lambda e: e.tensor_copy(out=identb[:], in_=identf[:]), reads=["identf"], writes=["identb"])
    S.add("dve", lambda e: e.memset(ones_b[:], 1.0), writes=["ones_b"])
    S.add("dve", lambda e: e.memset(epsc[:], EPS), writes=["epsc"])
    for l in range(DEPTH):
        S.add("act", lambda e, l=l: e.activation(out=lamc[l][:, 0:8], in_=smalls[l][:, SM_LAM:SM_LAM + 8], func=AF.Exp, scale=-1.0),
              reads=[("smalls", l)], writes=[("lamc", l)])
        S.add("act", lambda e, l=l: e.activation(out=lamc[l][:, 0:8], in_=lamc[l][:, 0:8], func=AF.Ln, bias=1.0),
              writes=[("lamc", l)])
        S.add("dve", lambda e, l=l: e.tensor_scalar(lamc[l][:, 8:16], lamc[l][:, 0:8], -16.0, None, ALU.mult), writes=[("lamc", l)])
        S.add("dve", lambda e, l=l: e.tensor_scalar(lamc[l][:, 0:8], lamc[l][:, 0:8], -8.0, None, ALU.mult), writes=[("lamc", l)])

    rot = {"S": 0, "sg": 0, "u": 0}
    XK = keys("x", NCH)
    HR = keys("h", NCH)

    def transpose_out(src_fn, ncols, dst_aps, rd, tag):
        i = 0
        while i < len(dst_aps):
            grp = list(range(i, min(i + 4, len(dst_aps))))
            b = banks.alloc()

            def tr(e, grp=grp, b=b):
                r = None
                for gi, k in enumerate(grp):
                    n = ncols[k]
                    r = e.transpose(ps_t[b][0:n, gi * 128:(gi + 1) * 128], src_fn(k), identf[:])
                return r
            S.add("pe", tr, reads=list(rd) + ["identf"], writes=[PSR(b)])
            nmax = max(ncols[k] for k in grp)
            S.add("act", lambda e, b=b, grp=grp, nmax=nmax: e.copy(out=stout[0:nmax, 0:128 * len(grp)], in_=ps_t[b][0:nmax, 0:128 * len(grp)]),
                  writes=[PSR(b), "stout"])
            banks.free(b)
            for gi, k in enumerate(grp):
                n = ncols[k]
                S.add("sp", lambda e, gi=gi, k=k, n=n: e.dma_start(out=dst_aps[k], in_=stout[0:n, gi * 128:(gi + 1) * 128]),
                      reads=["stout"], dsem=("so", tag))
            i += 4

    def norm_stats(T, subs):
        for (c0, n) in subs:
            b = banks.alloc()
            for c in range(NCH):
                S.add("pe", lambda e, b=b, c0=c0, n=n, c=c: e.matmul(ps_t[b][:, 0:n], ones_b[:], sq[:, c, c0:c0 + n], start=(c == 0), stop=(c == NCH - 1)),
                      reads=[("sq", c), "ones_b"], writes=[PSR(b)])
            S.add("act", lambda e, b=b, c0=c0, n=n: e.activation(out=sdv[:, c0:c0 + n], in_=ps_t[b][:, 0:n], func=AF.Ln,
                                                                 scale=1.0 / D, bias=epsc[:, 0:1]),
                  reads=["epsc"], writes=[PSR(b), "sdv"])
            S.add("act", lambda e, c0=c0, n=n: e.activation(out=rstd[:, c0:c0 + n], in_=sdv[:, c0:c0 + n], func=AF.Exp, scale=-0.5),
                  reads=["sdv"], writes=["rstd"])
            banks.free(b)

    def split_eng(c):
        return "dve"

    def rmsnorm_to_h(l, gcol, T, subs, squares_done=False):
        if not squares_done:
            for c in range(NCH):
                S.add("act", lambda e, c=c: e.activation(out=sq[:, c, 0:T], in_=x[:, c, 0:T], func=AF.Square), reads=[("x", c)], writes=[("sq", c)])
        norm_stats(T, subs)
        for c in range(NCH):
            S.add("dve", lambda e, c=c: e.scalar_tensor_tensor(out=h[:, c, 0:T], in0=x[:, c, 0:T],
                                                                      scalar=smalls[l][:, gcol + c:gcol + c + 1], in1=rstd[:, 0:T],
                                                                      op0=ALU.mult, op1=ALU.mult),
                  reads=[("x", c), "rstd", ("smalls", l)], writes=[("h", c)])

    def postnorm_residual(l, T, subs, next_squares):
        norm_stats(T, subs)
        for c in range(NCH):
            eng = split_eng(c)
            S.add(eng, lambda e, c=c: e.tensor_tensor(out=mbuf[:, c, 0:T], in0=mbuf[:, c, 0:T], in1=rstd[:, 0:T], op=ALU.mult),
                  reads=["rstd"], writes=[("mbuf", c)])
            S.add(eng, lambda e, c=c: e.tensor_tensor(out=x[:, c, 0:T], in0=x[:, c, 0:T], in1=mbuf[:, c, 0:T], op=ALU.add),
                  reads=[("mbuf", c)], writes=[("x", c)])
            if next_squares:
                S.add("act", lambda e, c=c: e.activation(out=sq[:, c, 0:T], in_=x[:, c, 0:T], func=AF.Square), reads=[("x", c)], writes=[("sq", c)])

    def proj(wt, wres, col, rhs, rhs_res, nk, subs, evac):
        for (c0, n) in subs:
            b = banks.alloc()
            for k in range(nk):
                S.add("pe", lambda e, b=b, c0=c0, n=n, k=k: e.matmul(ps_t[b][:, 0:n], wt[:, k, col:col + 128], rhs[:, k, c0:c0 + n],
                                                                     start=(k == 0), stop=(k == nk - 1)),
                      reads=[wres, rhs_res[k]], writes=[PSR(b)])
            evac(b, c0, n)
            banks.free(b)

    def attn_waves(units):
        waves = [units[i:i + 4] for i in range(0, len(units), 4)]

        def sets(w, i):
            return (w % 2) * 4 + i

        def front(w):
            wv = waves[w]
            bl = []
            for i, u in enumerate(wv):
                b = banks.alloc()
                bl.append(b)
                NQ, NK = u["NQ"], u["NK"]
                S.add("pe", lambda e, u=u, b=b, NQ=NQ, NK=NK: e.matmul(ps_t[b][0:NQ, 0:NK], u["qap"], u["kap"], start=True, stop=True),
                      reads=u["rd"], writes=[PSR(b)])
            for i, u in enumerate(wv):
                si = sets(w, i)
                NQ = u["NQ"]
                S.add("pool", lambda e, u=u, si=si, NQ=NQ: e.tensor_copy(out=Sbuf[si][0:NQ, 63:64], in_=u["sink_ap"]), reads=u["rd"], writes=[("Sb", si, "h")])
            for i, u in enumerate(wv):
                si = sets(w, i)
                b = bl[i]
                NQ, NK = u["NQ"], u["NK"]
                S.add("dve", lambda e, u=u, si=si, b=b, NQ=NQ, NK=NK: e.tensor_tensor(out=Sbuf[si][0:NQ, 64:64 + NK], in0=u["bias_ap"], in1=ps_t[b][0:NQ, 0:NK], op=ALU.subtract),
                      reads=[u["bias_key"]], writes=[PSR(b), ("Sb", si, "b")])
                banks.free(b)
            for i, u in enumerate(wv):
                si = sets(w, i)
                NQ, NK = u["NQ"], u["NK"]
                S.add("dve", lambda e, si=si, NQ=NQ, NK=NK: e.tensor_reduce(out=stat[si][0:NQ, 0:1], in_=Sbuf[si][0:NQ, 63:64 + NK], axis=AX.X, op=ALU.min),
                      reads=[("Sb", si)], writes=[("st", si)])

        def mid(w):
            wv = waves[w]
            for i, u in enumerate(wv):
                si = sets(w, i)
                NQ, NK = u["NQ"], u["NK"]
                S.add("act", lambda e, si=si, NQ=NQ, NK=NK: e.activation(out=Sbuf[si][0:NQ, 63:64 + NK], in_=Sbuf[si][0:NQ, 63:64 + NK], func=AF.Exp,
                                                                         bias=stat[si][0:NQ, 0:1], scale=-1.0, accum_out=stat[si][0:NQ, 2:3]),
                      writes=[("Sb", si), ("st", si)])
            for i, u in enumerate(wv):
                si = sets(w, i)
                NQ = u["NQ"]
                S.add("dve", lambda e, si=si, NQ=NQ: e.reciprocal(out=stat[si][0:NQ, 3:4], in_=stat[si][0:NQ, 2:3]), writes=[("st", si)])
            for i, u in enumerate(wv):
                si = sets(w, i)
                NQ, NK = u["NQ"], u["NK"]
                S.add("act", lambda e, si=si, NQ=NQ, NK=NK: e.activation(out=Pn[si][0:NQ, 0:NK], in_=Sbuf[si][0:NQ, 64:64 + NK], func=AF.Copy, scale=stat[si][0:NQ, 3:4]),
                      reads=[("Sb", si), ("st", si)], writes=[("Pn", si)])

        def back(w):
            wv = waves[w]
            tbl = []
            for i, u in enumerate(wv):
                si = sets(w, i)
                tb = banks.alloc()
                tbl.append(tb)
                NQ = u["NQ"]

                def tr(e, u=u, si=si, tb=tb, NQ=NQ):
                    tps = ps_t[tb].bitcast(BF16)
                    r = None
                    for vi, (vap, k0, nk) in enumerate(u["vblocks"]):
                        r = e.transpose(tps[0:nk, vi * 128:vi * 128 + NQ], Pn[si][0:NQ, k0:k0 + nk], identb[0:NQ, 0:NQ])
                    return r
                S.add("pe", tr, reads=[("Pn", si), "identb"], writes=[PSR(tb)])
            for i, u in enumerate(wv):
                si = sets(w, i)
                tb = tbl[i]
                NQ = u["NQ"]
                vb = u["vblocks"]
                if NQ == 128 and all(nk == 128 for (_, _, nk) in vb):
                    nv = len(vb)
                    S.add("act", lambda e, si=si, tb=tb, nv=nv: e.copy(out=PT[si][:, 0:nv, :],
                                                                      in_=ps_t[tb].bitcast(BF16)[:, 0:nv * 128].rearrange("p (v q) -> p v q", q=128)),
                          writes=[PSR(tb), ("PT", si)])
                else:
                    nv = len(vb)
                    S.add("act", lambda e, si=si, tb=tb, nv=nv, NQ=NQ: e.copy(
                        out=PT[si][:, 0:nv, 0:NQ], in_=ps_t[tb].bitcast(BF16)[:, 0:nv * 128].rearrange("p (v q) -> p v q", q=128)[:, :, 0:NQ]),
                        writes=[PSR(tb), ("PT", si)])
                banks.free(tb)
            for i, u in enumerate(wv):
                si = sets(w, i)
                g = u["grp"]
                if g.get("ob") is None:
                    g["ob"] = banks.alloc()
                ob = g["ob"]
                NQ = u["NQ"]
                c0 = u["out_c0"]

                def pv(e, u=u, si=si, ob=ob, NQ=NQ, c0=c0):
                    r = None
                    vb = u["vblocks"]
                    for vi, (vap, k0, nk) in enumerate(vb):
                        r = e.matmul(ps_t[ob][:, c0:c0 + NQ], vap, PT[si][0:nk, vi, 0:NQ], start=(vi == 0), stop=(vi == len(vb) - 1))
                    return r
                S.add("pe", pv, reads=[("PT", si)] + list(u["rd"]), writes=[PSR(ob)])
                if u["last"]:
                    u["evac"](ob)
                    banks.free(ob)
                    g["ob"] = None

        nw = len(waves)
        front(0)
        for w in range(nw):
            mid(w)
            if w + 1 < nw:
                front(w + 1)
            back(w)

    def do_pass(p):
        has_s = (p == 0)
        T = TP + (NS if has_s else 0)
        subs = [(0, TP)] + ([(TP, NS)] if has_s else [])
        last_pass = (p == NPASS - 1)
        XRs = XR[:, :, 3 + TP:3 + TP + NB * 7].rearrange("p c (b k) -> p c b k", k=7)

        if p == 0:
            S.add("sp", lambda e, p=p: e.dma_start(out=tokx[:], in_=xp[p * TP:(p + 1) * TP, :].rearrange("(j r) d -> r j d", r=128)),
                  writes=["tokx"], dsem="xin")
        if has_s:
            S.add("sp", lambda e: e.dma_start(out=toks[:], in_=xs), writes=["toks"], dsem="xin2")
        for c in range(NCH):
            b = banks.alloc()

            def tr(e, b=b, c=c):
                r = None
                for j in range(4):
                    r = e.transpose(ps_t[b][:, j * 128:(j + 1) * 128], tokx[:, j, c * 128:(c + 1) * 128], identf[:])
                return r
            S.add("pe", tr, reads=["tokx", "identf"], writes=[PSR(b)])
            S.add("act", lambda e, b=b, c=c: e.copy(out=x[:, c, 0:TP], in_=ps_t[b][:, 0:TP]), writes=[PSR(b), ("x", c)])
            banks.free(b)
            if has_s:
                b = banks.alloc()
                S.add("pe", lambda e, b=b, c=c: e.transpose(ps_t[b][:, 0:NS], toks[:, c * 128:(c + 1) * 128], identf[0:NS, 0:NS]),
                      reads=["toks", "identf"], writes=[PSR(b)])
                S.add("act", lambda e, b=b, c=c: e.copy(out=x[:, c, TP:TP + NS], in_=ps_t[b][:, 0:NS]), writes=[PSR(b), ("x", c)])
                banks.free(b)

        for l in range(DEPTH):
            do_layer(p, l, T, subs, has_s, last_pass, XRs)
        finish_pass(p, has_s)

    def do_layer(p, l, T, subs, has_s, last_pass, XRs):
        if True:
            sm = smalls[l]
            SMR = ("smalls", l)

            rmsnorm_to_h(l, SM_NMP, T, subs, squares_done=(l > 0))

            XRall = keys("XR", NCH)
            if has_s:
                S.add("sp", lambda e, l=l: e.dma_start(out=stout[0:48, 0:D], in_=st_c[l]), writes=["stout"], dsem="stin")
                for c in range(NCH):
                    b = banks.alloc()
                    S.add("pe", lambda e, b=b, c=c: e.transpose(ps_t[b][:, 0:48], stout[0:48, c * 128:(c + 1) * 128], identf[0:48, 0:48]),
                          reads=["stout", "identf"], writes=[PSR(b)])
                    S.add("act", lambda e, b=b, c=c: e.copy(out=XRs[:, c, :, 0:3], in_=ps_t[b][:, 0:48].rearrange("p (b k) -> p b k", k=3)),
                          writes=[PSR(b), ("XR", c)])
                    banks.free(b)
                S.add("sp", lambda e, l=l: e.dma_start(out=stout[0:16, 0:D], in_=st_h[l]), writes=["stout"], dsem="stin")
                for c in range(NCH):
                    b = banks.alloc()
                    S.add("pe", lambda e, b=b, c=c: e.transpose(ps_t[b][:, 0:16], stout[0:16, c * 128:(c + 1) * 128], identf[0:16, 0:16]),
                          reads=["stout", "identf"], writes=[PSR(b)])
                    S.add("act", lambda e, b=b, c=c: e.copy(out=h0s[:, c, :], in_=ps_t[b][:, 0:16]), writes=[PSR(b), "h0s"])
                    banks.free(b)

            if p == 0:
                S.add("dve", lambda e: e.memset(XR[:, :, 0:3], 0.0), writes=XRall)
            else:
                S.add("dve", lambda e, l=l: e.tensor_copy(out=XR[:, :, 0:3], in_=convhist[l][:]),
                      reads=[("convhist", l)], writes=XRall)
            for blk in range(2):
                wt, wres = W.get((p, "win", l, blk))
                wv = wt[:].rearrange("p (k n) -> p k n", n=512)
                for cc in range(4):
                    c = blk * 4 + cc

                    def ev(b, c0, n, c=c):
                        if c0 == 0:
                            S.add("act", lambda e: e.copy(out=XR[:, c, 3:3 + TP], in_=ps_t[b][:, 0:TP]), writes=[PSR(b), ("XR", c)])
                        else:
                            S.add("act", lambda e: e.copy(out=XRs[:, c, :, 3:7], in_=ps_t[b][:, 0:NS].rearrange("p (b t) -> p b t", t=4)),
                                  writes=[PSR(b), ("XR", c)])
                    proj(wv, wres, cc * 128, h, HR, NCH, subs, ev)
                W.release()
            S.add("dve", lambda e, l=l: e.tensor_copy(out=convhist[l][:], in_=XR[:, :, TP:TP + 3]), reads=XRall, writes=[("convhist", l)])
            if has_s:
                S.add("dve", lambda e: e.tensor_copy(out=cs_stage[:].rearrange("p c (b k) -> p c b k", k=3), in_=XRs[:, :, :, 4:7]),
                      reads=XRall, writes=["cs_stage"])

            for blk in (2, 3):
                wt, wres = W.get((p, "win", l, blk))
                wv = wt[:].rearrange("p (k n) -> p k n", n=512)
                for cc in range(4):
                    hh = (blk - 2) * 4 + cc

                    def ev(b, c0, n, hh=hh):
                        S.add("act", lambda e: e.activation(out=qT[:, hh, c0:c0 + n], in_=ps_t[b][:, 0:n], func=AF.Copy, scale=QSCALE),
                              writes=[PSR(b), ("qT", hh)])
                    proj(wv, wres, cc * 128, h, HR, NCH, subs, ev)
                W.release()
            wkv_t, wkv_res = W.get((p, "win", l, 4))
            wkv = wkv_t[:].rearrange("p (k n) -> p k n", n=512)
            if p == 0:
                S.add("dve", lambda e: e.memset(kT[:, :, 0:128], 0.0), writes=["kT"])
                S.add("dve", lambda e: e.memset(Vt[:, 0, :], 0.0), writes=["Vt"])
            else:
                S.add("dve", lambda e, l=l: e.tensor_copy(out=kT[:, :, 0:128], in_=khist[l][:]), reads=[("khist", l)], writes=["kT"])
                S.add("dve", lambda e, l=l: e.tensor_copy(out=Vt[:, 0, :], in_=vhist[l][:]), reads=[("vhist", l)], writes=["Vt"])
            for kv in range(2):
                def ev(b, c0, n, kv=kv):
                    S.add("act", lambda e: e.copy(out=kT[:, kv, 128:128 + TP], in_=ps_t[b][:, 0:TP]), writes=[PSR(b), "kT"])
                proj(wkv, wkv_res, kv * 128, h, HR, NCH, [(0, TP)], ev)
            for j in range(4):
                full = last_pass and j == 3
                b = banks.alloc()
                c0w, nw = (0, 512) if full else (256, 256)

                def mm(e, b=b, j=j, c0w=c0w, nw=nw):
                    r = None
                    for k in range(NCH):
                        r = e.matmul(ps_t[b][:, 0:nw], h[:, k, j * 128:(j + 1) * 128], wkv[:, k, c0w:c0w + nw], start=(k == 0), stop=(k == NCH - 1))
                    return r
                S.add("pe", mm, reads=[wkv_res] + HR, writes=[PSR(b)])
                voff = 256 if full else 0
                S.add("act", lambda e, b=b, j=j, voff=voff: e.copy(out=Vt[:, j + 1, :], in_=ps_t[b][:, voff:voff + 256]), writes=[PSR(b), "Vt"])
                if full:
                    S.add("dve", lambda e, b=b: e.tensor_copy(out=kvlast[:], in_=ps_t[b][:, 0:512]), writes=[PSR(b), "kvlast"])
                    S.add("sp", lambda e, l=l: e.dma_start(out=o_pk[l], in_=kvlast[:, 0:256]), reads=["kvlast"], dsem="okv")
                    S.add("sp", lambda e, l=l: e.dma_start(out=o_pv[l], in_=kvlast[:, 256:512]), reads=["kvlast"], dsem="okv")
                banks.free(b)
            S.add("dve", lambda e, l=l: e.tensor_copy(out=khist[l][:], in_=kT[:, :, TP:TP + 128]), reads=["kT"], writes=[("khist", l)])
            S.add("dve", lambda e, l=l: e.tensor_copy(out=vhist[l][:], in_=Vt[:, 4, :]), reads=["Vt"], writes=[("vhist", l)])
            if has_s:
                b = banks.alloc()

                def mm(e, b=b):
                    r = None
                    for k in range(NCH):
                        r = e.matmul(ps_t[b][0:NS, 0:512], h[:, k, TP:TP + NS], wkv[:, k, 0:512], start=(k == 0), stop=(k == NCH - 1))
                    return r
                S.add("pe", mm, reads=[wkv_res] + HR, writes=[PSR(b)])
                S.add("act", lambda e, b=b: e.copy(out=kvnew[:], in_=ps_t[b][0:NS, 0:512]), writes=[PSR(b), "kvnew"])
                banks.free(b)
                def kvout(e, l=l):
                    r = []
                    for bb in range(NB):
                        r.append(e.dma_start(out=o_sk[l][bb, 124:128, :], in_=kvnew[bb * 4:(bb + 1) * 4, 0:256]))
                        r.append(e.dma_start(out=o_sv[l][bb, 124:128, :], in_=kvnew[bb * 4:(bb + 1) * 4, 256:512]))
                    return r
                S.add("sp", kvout, reads=["kvnew"], writes=[("dram", "osv", l)], dsem="okv2", ndma=2 * NB)
                S.add("sp", lambda e, l=l: e.dma_start(out=o_sk[l][:, 0:124, :], in_=ck[l][:, 4:128, :]), dsem="d2d")
                S.add("sp", lambda e, l=l: e.dma_start(out=o_sv[l][:, 0:124, :], in_=cv[l][:, 4:128, :]), dsem="d2d")

            gv = gatesw[l][:].rearrange("p (g c n) -> p g c n", g=2, n=128)

            def K(name, st, cc):
                return ((name, st), cc)

            def convS(g):
                st = g % 2
                for cc in range(2):
                    c = g * 2 + cc
                    wcol = lambda k, c=c: sm[:, SM_CLW + k * 8 + c:SM_CLW + k * 8 + c + 1]
                    bcol = sm[:, SM_CLB + c:SM_CLB + c + 1]
                    xo = xcg[st]
                    S.add("dve", lambda e, c=c, cc=cc, xo=xo, wcol=wcol, bcol=bcol: e.tensor_scalar(xo[:, cc, 0:TP], XR[:, c, 3:3 + TP], wcol(3), bcol, ALU.mult, ALU.add),
                          reads=[("XR", c), SMR], writes=[K("xc", st, cc)])
                    for k in range(3):
                        S.add("dve", lambda e, c=c, cc=cc, k=k, xo=xo, wcol=wcol: e.scalar_tensor_tensor(out=xo[:, cc, 0:TP], in0=XR[:, c, k:k + TP], scalar=wcol(k),
                                                                                                       in1=xo[:, cc, 0:TP], op0=ALU.mult, op1=ALU.add),
                              reads=[("XR", c), SMR], writes=[K("xc", st, cc)])
                    if has_s:
                        xcs = xo[:, cc, TP:TP + NS].rearrange("p (b t) -> p b t", t=4)
                        S.add("dve", lambda e, c=c, wcol=wcol, bcol=bcol, xcs=xcs: e.tensor_scalar(xcs, XRs[:, c, :, 3:7], wcol(3), bcol, ALU.mult, ALU.add),
                              reads=[("XR", c), SMR], writes=[K("xc", st, cc)])
                        for k in range(3):
                            S.add("dve", lambda e, c=c, k=k, wcol=wcol, xcs=xcs: e.scalar_tensor_tensor(out=xcs, in0=XRs[:, c, :, k:k + 4], scalar=wcol(k),
                                                                                                      in1=xcs, op0=ALU.mult, op1=ALU.add),
                                  reads=[("XR", c), SMR], writes=[K("xc", st, cc)])
                    S.add("act", lambda e, cc=cc, xo=xo, st=st: e.copy(out=xcb[st][:, cc, 0:T], in_=xo[:, cc, 0:T]),
                          reads=[K("xc", st, cc)], writes=[K("xcb", st, cc)])

            def gatesS(g):
                st = g % 2
                for cc in range(2):
                    c = g * 2 + cc
                    for (c0, n) in subs:
                        for gi_, dst, dkey, bcolbase in ((0, rr[st], "rr", SM_BR), (1, ii[st], "ii", SM_BI)):
                            b = banks.alloc()
                            S.add("pe", lambda e, b=b, gi_=gi_, c=c, cc=cc, c0=c0, n=n, st=st: e.matmul(ps_t[b][:, 0:n], gv[:, gi_, c, :], xcb[st][:, cc, c0:c0 + n], start=True, stop=True),
                                  reads=[("gatesw", l), K("xcb", st, cc)], writes=[PSR(b)])
                            S.add("act", lambda e, b=b, dst=dst, c=c, cc=cc, c0=c0, n=n, bcolbase=bcolbase: e.activation(
                                out=dst[:, cc, c0:c0 + n], in_=ps_t[b][:, 0:n], func=AF.Sigmoid, bias=sm[:, bcolbase + c:bcolbase + c + 1], scale=1.0),
                                reads=[SMR], writes=[PSR(b), K(dkey, st, cc)])
                            banks.free(b)

            def expS(g):
                st = g % 2
                for cc in range(2):
                    c = g * 2 + cc
                    S.add("act", lambda e, c=c, cc=cc, st=st: e.activation(out=aa[st][:, cc, 0:T], in_=rr[st][:, cc, 0:T], func=AF.Exp, scale=lamc[l][:, c:c + 1]),
                          reads=[K("rr", st, cc), ("lamc", l)], writes=[K("aa", st, cc)])
                for cc in range(2):
                    S.add("dve", lambda e, cc=cc, st=st: e.scalar_tensor_tensor(out=rr[st][:, cc, 0:T], in0=aa[st][:, cc, 0:T], scalar=0.99999994,
                                                                                in1=aa[st][:, cc, 0:T], op0=ALU.min, op1=ALU.mult),
                          reads=[K("aa", st, cc)], writes=[K("rr", st, cc)])
                for cc in range(2):
                    S.add("act", lambda e, cc=cc, st=st: e.activation(out=rr[st][:, cc, 0:T], in_=rr[st][:, cc, 0:T], func=AF.Ln, scale=-1.0, bias=1.0),
                          writes=[K("rr", st, cc)])
                for cc in range(2):
                    S.add("act", lambda e, cc=cc, st=st: e.activation(out=rr[st][:, cc, 0:T], in_=rr[st][:, cc, 0:T], func=AF.Exp, scale=0.5),
                          writes=[K("rr", st, cc)])

            def dveS(g):
                st = g % 2
                xo = xcg[st]
                for cc in range(2):
                    c = g * 2 + cc
                    if p == 0:
                        S.add("dve", lambda e, cc=cc, st=st: e.memset(rr[st][:, cc, 0:1], 1.0), writes=[K("rr", st, cc)])
                    S.add("dve", lambda e, cc=cc, st=st, xo=xo: e.tensor_tensor(out=ii[st][:, cc, 0:T], in0=ii[st][:, cc, 0:T], in1=xo[:, cc, 0:T], op=ALU.mult),
                          reads=[K("xc", st, cc)], writes=[K("ii", st, cc)])
                    S.add("dve", lambda e, cc=cc, st=st: e.tensor_tensor(out=ii[st][:, cc, 0:T], in0=ii[st][:, cc, 0:T], in1=rr[st][:, cc, 0:T], op=ALU.mult),
                          reads=[K("rr", st, cc)], writes=[K("ii", st, cc)])
                    init = 0.0 if p == 0 else hstate[l][:, c:c + 1]
                    S.add("dve", lambda e, cc=cc, st=st, xo=xo, init=init: e.tensor_tensor_scan(out=xo[:, cc, 0:TP], data0=aa[st][:, cc, 0:TP], data1=ii[st][:, cc, 0:TP],
                                                                                                 initial=init, op0=ALU.mult, op1=ALU.add),
                          reads=[K("aa", st, cc), K("ii", st, cc), ("hstate", l)], writes=[K("xc", st, cc)])
                    if has_s:
                        aas = aa[st][:, cc, TP:TP + NS].rearrange("p (b t) -> p b t", t=4)
                        iis = ii[st][:, cc, TP:TP + NS].rearrange("p (b t) -> p b t", t=4)
                        S.add("dve", lambda e, c=c, aas=aas: e.tensor_tensor(out=tmp16[:], in0=aas[:, :, 0], in1=h0s[:, c, :], op=ALU.mult),
                              reads=[K("aa", st, cc), "h0s"], writes=["tmp16"])
                        S.add("dve", lambda e, iis=iis: e.tensor_tensor(out=iis[:, :, 0], in0=iis[:, :, 0], in1=tmp16[:], op=ALU.add),
                              reads=["tmp16"], writes=[K("ii", st, cc)])
                        S.add("dve", lambda e, aas=aas: e.memset(aas[:, :, 0], 0.0), writes=[K("aa", st, cc)])
                        S.add("dve", lambda e, cc=cc, st=st, xo=xo: e.tensor_tensor_scan(out=xo[:, cc, TP:TP + NS], data0=aa[st][:, cc, TP:TP + NS], data1=ii[st][:, cc, TP:TP + NS],
                                                                                         initial=0.0, op0=ALU.mult, op1=ALU.add),
                              reads=[K("aa", st, cc), K("ii", st, cc)], writes=[K("xc", st, cc)])
                    S.add("act", lambda e, c=c, cc=cc, xo=xo: e.copy(out=lro[:, c, 0:T], in_=xo[:, cc, 0:T]), reads=[K("xc", st, cc)], writes=[("lro", c)])
                XCK = [K("xc", st, cc) for cc in range(2)]
                S.add("dve", lambda e, g=g, xo=xo: e.tensor_copy(out=hstate[l][:, g * 2:(g + 1) * 2], in_=xo[:, :, TP - 1]), reads=XCK, writes=[("hstate", l)])
                if has_s:
                    S.add("dve", lambda e, g=g, xo=xo: e.tensor_copy(out=hs_last[:, g * 2:(g + 1) * 2, :],
                                                                     in_=xo[:, :, TP:TP + NS].rearrange("p c (b t) -> p c b t", t=4)[:, :, :, 3]),
                          reads=XCK, writes=["hs_last"])

            convS(0)
            gatesS(0)
            convS(1)
            expS(0)
            gatesS(1)
            dveS(0)
            convS(2)
            expS(1)
            gatesS(2)
            dveS(1)
            convS(3)
            expS(2)
            gatesS(3)
            dveS(2)
            expS(3)
            dveS(3)
            if has_s:
                transpose_out(lambda k: hs_last[:, k, :], [NB] * NCH,
                              [o_slh[l][:, k * 128:(k + 1) * 128] for k in range(NCH)], ["hs_last"], "slh")
                transpose_out(lambda k: cs_stage[:, k, :], [NB * 3] * NCH,
                              [o_slc[l][:, k * 128:(k + 1) * 128] for k in range(NCH)], ["cs_stage"], "slc")
            if last_pass:
                transpose_out(lambda k, l=l: hstate[l][:, :], [NCH], [o_plh[l].rearrange("(c p) -> c p", p=128)], [("hstate", l)], "plh")
                S.add("dve", lambda e, l=l: e.tensor_copy(out=pst[:, 0:24].rearrange("p (k c) -> p k c", c=NCH),
                                                          in_=convhist[l][:].rearrange("p c k -> p k c")),
                      reads=[("convhist", l)], writes=["pst"])
                transpose_out(lambda k: pst[:, 0:24], [24], [o_plc[l].rearrange("k (c p) -> (k c) p", p=128)], ["pst"], "plc")

            if has_s:
                S.add("sp", lambda e, l=l: e.dma_start(out=cstage[:], in_=ck[l].rearrange("b k d -> k b d")), writes=["cstage"], dsem="stin3")
                S.add("pool", lambda e, l=l: e.dma_start(out=Vs[:], in_=cv[l].rearrange("b k d -> k b d")), writes=["Vs"], dsem="vs_in")
                for kv in range(2):
                    def ev(b, c0, n, kv=kv):
                        S.add("act", lambda e: e.copy(out=kTs[:, kv, :, 128:132], in_=ps_t[b][:, 0:NS].rearrange("p (b t) -> p b t", t=4)),
                              writes=[PSR(b), "kTs"])
                    proj(wkv, wkv_res, kv * 128, h, HR, NCH, [(TP, NS)], ev)
                S.add("pool", lambda e, l=l: e.dma_start(out=Vn4[:], in_=o_sv[l][:, 124:128, :].rearrange("b t d -> t b d")),
                      reads=[("dram", "osv", l)], writes=["Vn4"], dsem="vn4")
                for kv in range(2):
                    S.add("dve", lambda e, kv=kv: e.tensor_copy(
                        out=qTs[:, kv, :, :].rearrange("p b (g t) -> p b g t", t=4),
                        in_=qT[:, kv * 4:kv * 4 + 4, TP:TP + NS].rearrange("p g (b t) -> p b g t", t=4)),
                        reads=[("qT", kv * 4 + g) for g in range(4)], writes=["qTs"])
            W.release()

            units = []
            for hh in range(8):
                kv = hh // 4
                grp = {}
                for j in range(4):
                    first_blk = (p == 0 and j == 0)
                    if first_blk:
                        kap = kT[:, kv, 128:256]
                        bias_ap = biasp[:, hh, 128:256]
                        vbl = [(Vt[:, 1, kv * 128:(kv + 1) * 128], 0, 128)]
                        NK = 128
                    else:
                        kap = kT[:, kv, j * 128:j * 128 + 256]
                        bias_ap = biasp[:, hh, :]
                        vbl = [(Vt[:, j, kv * 128:(kv + 1) * 128], 0, 128), (Vt[:, j + 1, kv * 128:(kv + 1) * 128], 128, 128)]
                        NK = 256

                    def evp(ob, hh=hh):
                        S.add("act", lambda e: e.copy(out=attn[:, hh, 0:TP], in_=ps_t[ob][:, 0:TP]), writes=[PSR(ob), ("attn", hh)])
                    units.append(dict(qap=qT[:, hh, j * 128:(j + 1) * 128], kap=kap, NQ=128, NK=NK, bias_ap=bias_ap, bias_key="biasp",
                                      sink_ap=sinks[l][:, hh:hh + 1], vblocks=vbl, rd=[("qT", hh), "kT", "Vt", ("sinks", l)],
                                      grp=grp, out_c0=j * 128, last=(j == 3), evac=evp))
            attn_waves(units)
            units = []
            if has_s:
                for bb in range(NB):
                    b = banks.alloc()

                    def tr(e, b=b, bb=bb):
                        r = None
                        for kv in range(2):
                            r = e.transpose(ps_t[b][:, kv * 128:(kv + 1) * 128], cstage[:, bb, kv * 128:(kv + 1) * 128], identf[:])
                        return r
                    S.add("pe", tr, reads=["cstage", "identf"], writes=[PSR(b)])
                    S.add("act", lambda e, b=b, bb=bb: e.copy(out=kTs[:, :, bb, 0:128], in_=ps_t[b][:, 0:256].rearrange("p (v k) -> p v k", k=128)),
                          writes=[PSR(b), "kTs"])
                    banks.free(b)
                for kv in range(2):
                    grp = {}
                    for bb in range(NB):
                        vbl = [(Vs[:, bb, kv * 128:(kv + 1) * 128], 0, 128), (Vn4[0:4, bb, kv * 128:(kv + 1) * 128], 128, 4)]

                        def evs(ob, kv=kv):
                            S.add("act", lambda e: e.copy(
                                out=attn[:, kv * 4:kv * 4 + 4, TP:TP + NS].rearrange("p g (b t) -> p b g t", t=4),
                                in_=ps_t[ob][:, 0:256].rearrange("p (b g t) -> p b g t", g=4, t=4)),
                                writes=[PSR(ob)] + [("attn", kv * 4 + g) for g in range(4)])
                        units.append(dict(qap=qTs[:, kv, bb, :], kap=kTs[:, kv, bb, :], NQ=16, NK=132, bias_ap=biass[:, kv, :], bias_key="biass",
                                          sink_ap=sinks[l][0:16, 8 + kv:9 + kv], vblocks=vbl, rd=["qTs", "kTs", "Vs", "Vn4", ("sinks", l)],
                                          grp=grp, out_c0=bb * 16, last=(bb == NB - 1), evac=evs))
            if units:
                attn_waves(units)

            LR = keys("lro", NCH)
            AR = keys("attn", 8)

            def mkmm(bk, wv_, src, col, c0, n):
                def mm(e):
                    r = None
                    for k in range(NCH):
                        r = e.matmul(ps_t[bk][:, 0:n], wv_[:, k, col:col + 128], src[:, k, c0:c0 + n], start=(k == 0), stop=(k == NCH - 1))
                    return r
                return mm
            for br, (wname, gbase, src, SR) in enumerate((("wlo", 5, lro, LR), ("wao", 7, attn, AR))):
                for hf in range(2):
                    wo_t, wo_res = W.get((p, wname, l, hf))
                    wg_t, wg_res = W.get((p, "win", l, gbase + hf))
                    v_o = wo_t[:].rearrange("p (k n) -> p k n", n=512)
                    v_g = wg_t[:].rearrange("p (k n) -> p k n", n=512)
                    for cc in range(4):
                        oc = hf * 4 + cc
                        for (c0, n) in subs:
                            gi = rot["sg"] % 2
                            rot["sg"] += 1
                            bO, bG = banks.alloc(), banks.alloc()
                            S.add("pe", mkmm(bG, v_g, h, cc * 128, c0, n), reads=[wg_res] + HR, writes=[PSR(bG)])
                            for k in range(NCH):
                                S.add("pe", lambda e, bO=bO, v_o=v_o, src=src, cc=cc, c0=c0, n=n, k=k: e.matmul(
                                    ps_t[bO][:, 0:n], v_o[:, k, cc * 128:cc * 128 + 128], src[:, k, c0:c0 + n], start=(k == 0), stop=(k == NCH - 1)),
                                    reads=[wo_res, SR[k]], writes=[PSR(bO)])
                            S.add("act", lambda e, gi=gi, bG=bG, n=n: e.activation(out=sgA[gi][:, 0:n], in_=ps_t[bG][:, 0:n], func=AF.Sigmoid),
                                  writes=[PSR(bG), ("sgA", gi)])
                            if br == 0:
                                S.add("dve", lambda e, gi=gi, bO=bO, oc=oc, c0=c0, n=n: e.tensor_tensor(out=merged[:, oc, c0:c0 + n], in0=sgA[gi][:, 0:n], in1=ps_t[bO][:, 0:n], op=ALU.mult),
                                      reads=[("sgA", gi)], writes=[PSR(bO), ("merged", oc)])
                            else:
                                S.add("dve", lambda e, gi=gi, bO=bO, n=n: e.tensor_tensor(out=t1b[gi][:, 0:n], in0=sgA[gi][:, 0:n], in1=ps_t[bO][:, 0:n], op=ALU.mult),
                                      reads=[("sgA", gi)], writes=[PSR(bO), ("t1b", gi)])
                                S.add("dve", lambda e, gi=gi, oc=oc, c0=c0, n=n: e.tensor_tensor(out=merged[:, oc, c0:c0 + n], in0=t1b[gi][:, 0:n], in1=merged[:, oc, c0:c0 + n], op=ALU.add),
                                      reads=[("t1b", gi)], writes=[("merged", oc)])
                            banks.free(bO)
                            banks.free(bG)
                    W.release()
                    W.release()
            MR = keys("merged", NCH)
            for hf in range(2):
                wt, wres = W.get((p, "wout", l, hf))
                wv = wt[:].rearrange("p (k n) -> p k n", n=512)
                for cc in range(4):
                    oc = hf * 4 + cc

                    def ev(b, c0, n, oc=oc):
                        S.add("act", lambda e: e.activation(out=sq[:, oc, c0:c0 + n], in_=ps_t[b][:, 0:n], func=AF.Square), writes=[PSR(b), ("sq", oc)])
                        S.add("act", lambda e: e.activation(out=mbuf[:, oc, c0:c0 + n], in_=ps_t[b][:, 0:n], func=AF.Identity,
                                                            scale=sm[:, SM_NMPOST + oc:SM_NMPOST + oc + 1]),
                              reads=[SMR], writes=[PSR(b), ("mbuf", oc)])
                    proj(wv, wres, cc * 128, merged, MR, NCH, subs, ev)
                W.release()
            postnorm_residual(l, T, subs, True)

            rmsnorm_to_h(l, SM_NFP, T, subs, squares_done=True)
            if has_s:
                for q4 in range(4):
                    S.add("sp", lambda e, l=l, q4=q4: e.dma_start(out=stf[:], in_=st_f[l][:, q4 * 2048:(q4 + 1) * 2048]),
                          writes=["stf"], dsem="stin2")
                    for g4 in range(4):
                        b = banks.alloc()

                        def tr(e, b=b, g4=g4):
                            r = None
                            for i4 in range(4):
                                jj = g4 * 4 + i4
                                r = e.transpose(ps_t[b][:, i4 * 32:(i4 + 1) * 32], stf[0:32, jj * 128:(jj + 1) * 128], identf[0:32, 0:32])
                            return r
                        S.add("pe", tr, reads=["stf", "identf"], writes=[PSR(b)])
                        j0 = q4 * 16 + g4 * 4
                        S.add("act", lambda e, b=b, j0=j0: e.copy(out=FH[:, j0:j0 + 4, :], in_=ps_t[b][:, 0:128].rearrange("p (j n) -> p j n", n=32)),
                              writes=[PSR(b), "FH"])
                        banks.free(b)
            pend = []
            pend_t = []

            def hist_in(Un, ui_, jj_):
                Ub_ = (Uv if Un == "Uv" else Ug)[ui_]
                if p == 0:
                    S.add("pool", lambda e: e.memset(Ub_[:, 62:64], 0.0), writes=[(Un, ui_, "h")])
                else:
                    S.add("pool", lambda e: e.tensor_copy(out=Ub_[:, 62:64], in_=ffnhist[l][:, jj_, :]),
                          reads=[("dram", "ffnhist", l, jj_)], writes=[(Un, ui_, "h")])
            for pb in range(8):
                wv_t, wv_res = W.get((p, "wup", l, pb))
                wg_t, wg_res = W.get((p, "wup", l, pb + 8))
                vv = wv_t[:].rearrange("p (k n) -> p k n", n=512)
                vg = wg_t[:].rearrange("p (k n) -> p k n", n=512)
                wsc = lambda k, jj: sm[:, SM_FCW + k * 64 + jj:SM_FCW + k * 64 + jj + 1]
                bsc = lambda jj: sm[:, SM_FCB + jj:SM_FCB + jj + 1]
                for cc in range(4):
                    j = pb * 4 + cc
                    ci = rot["u"] % 3
                    ui = ci
                    rot["u"] += 1
                    for (wview, wres_, Ub, Un, jj, cvb, cres) in ((vv, wv_res, Uv[ui], "Uv", j, cvv[ci], ("cvv", ci)),
                                                                 (vg, wg_res, Ug[ui], "Ug", 32 + j, cvg[ci], ("cvg", ci))):
                        UH, UB = (Un, ui, "h"), (Un, ui, "b")
                        if j == 0:
                            hist_in(Un, ui, jj)
                        if j + 1 < 32:
                            hist_in(Un, (ui + 1) % 3, jj + 1)

                        def ev(b, c0, n, Ub=Ub, UB=UB, cvb=cvb, cres=cres, jj=jj):
                            S.add("act", lambda e: e.copy(out=Ub[:, 64:64 + TP], in_=ps_t[b][:, 0:TP]), writes=[PSR(b), UB])
                            S.add("act", lambda e: e.activation(out=cvb[:, 0:TP], in_=ps_t[b][:, 0:TP], func=AF.Identity, scale=wsc(2, jj), bias=bsc(jj)),
                                  reads=[SMR], writes=[PSR(b), cres])
                        proj(wview, wres_, cc * 128, h, HR, NCH, [(0, TP)], ev)
                        S.add("pool", lambda e, Ub=Ub, jj=jj: e.tensor_copy(out=ffnhist[l][:, jj, :], in_=Ub[:, 62 + TP:64 + TP]),
                              reads=[UB], writes=[("dram", "ffnhist", l, jj)])
                        for k in range(2):
                            S.add("dve", lambda e, Ub=Ub, cvb=cvb, jj=jj, k=k: e.scalar_tensor_tensor(out=cvb[:, 0:TP], in0=Ub[:, 62 + k:62 + k + TP], scalar=wsc(k, jj),
                                                                                                      in1=cvb[:, 0:TP], op0=ALU.mult, op1=ALU.add),
                                  reads=[UH, UB, SMR], writes=[cres])

                    def tail(ci=ci, j=j):
                        S.add("act", lambda e: e.activation(out=ggb[ci][:, 0:TP], in_=cvg[ci][:, 0:TP], func=AF.Gelu_apprx_tanh),
                              reads=[("cvg", ci)], writes=[("gg", ci)])
                        S.add("dve", lambda e: e.tensor_tensor(out=A[:, j, 0:TP], in0=cvv[ci][:, 0:TP], in1=ggb[ci][:, 0:TP], op=ALU.mult),
                              reads=[("cvv", ci), ("gg", ci)], writes=[("A", j)])
                    if pend:
                        pend.pop()()
                    pend.append(tail)
                if has_s and pend_t:
                    pend_t.pop(0)()
                if has_s:
                    for half, (wview, wres_) in enumerate(((vv, wv_res), (vg, wg_res))):
                        jj0 = pb * 4 + 32 * half
                        Uh = Us[half][:].rearrange("p c (b k) -> p c b k", k=6)
                        b = banks.alloc()

                        def mm(e, b=b, wview=wview):
                            r = None
                            for c4 in range(4):
                                for k in range(NCH):
                                    r = e.matmul(ps_t[b][:, c4 * NS:(c4 + 1) * NS], wview[:, k, c4 * 128:(c4 + 1) * 128], h[:, k, TP:TP + NS],
                                                 start=(k == 0), stop=(k == NCH - 1))
                            return r
                        S.add("pe", mm, reads=[wres_] + HR, writes=[PSR(b)])
                        S.add("pool", lambda e, Uh=Uh, jj0=jj0: e.tensor_copy(out=Uh[:, :, :, 0:2], in_=FH[:, jj0:jj0 + 4, :].rearrange("p c (b k) -> p c b k", k=2)),
                              reads=["FH"], writes=[("Us", half)])
                        S.add("act", lambda e, Uh=Uh, b=b: e.copy(out=Uh[:, :, :, 2:6], in_=ps_t[b][:, 0:4 * NS].rearrange("p (c b t) -> p c b t", c=4, t=4)),
                              writes=[PSR(b), ("Us", half)])
                        for c4 in range(4):
                            S.add("act", lambda e, b=b, c4=c4, half=half, jj0=jj0: e.activation(out=cvs[half][:, c4, :], in_=ps_t[b][:, c4 * NS:(c4 + 1) * NS],
                                                                                               func=AF.Identity, scale=wsc(2, jj0 + c4), bias=bsc(jj0 + c4)),
                                  reads=[SMR], writes=[PSR(b), ("cvs", half)])
                        banks.free(b)
                        S.add("pool", lambda e, Uh=Uh, half=half, pb=pb: e.tensor_copy(
                            out=SFs[pb % 2][:, half * 4:(half + 1) * 4, :].rearrange("p c (b k) -> p c b k", k=2), in_=Uh[:, :, :, 4:6]),
                            reads=[("Us", half)], writes=[("SFs", pb % 2)])
                        for c4 in range(4):
                            cv4 = cvs[half][:, c4, :].rearrange("p (b t) -> p b t", t=4)
                            for k in range(2):
                                S.add("dve", lambda e, Uh=Uh, cv4=cv4, c4=c4, k=k, jj0=jj0: e.scalar_tensor_tensor(out=cv4, in0=Uh[:, c4, :, k:k + 4], scalar=wsc(k, jj0 + c4),
                                                                                                                 in1=cv4, op0=ALU.mult, op1=ALU.add),
                                      reads=[("Us", half), SMR], writes=[("cvs", half)])
                    S.add("act", lambda e: e.activation(out=cvs[1][:], in_=cvs[1][:], func=AF.Gelu_apprx_tanh), writes=[("cvs", 1)])
                    S.add("dve", lambda e, pb=pb: e.tensor_tensor(out=A[:, pb * 4:(pb + 1) * 4, TP:TP + NS], in0=cvs[0][:], in1=cvs[1][:], op=ALU.mult),
                          reads=[("cvs", 0), ("cvs", 1)], writes=[("A", pb * 4 + i) for i in range(4)])
                W.release()
                W.release()
                if has_s:
                    def tout(pb=pb):
                        jl = [pb * 4 + i for i in range(4)] + [32 + pb * 4 + i for i in range(4)]
                        transpose_out(lambda k, pb=pb: SFs[pb % 2][:, k, :], [NB * 2] * 8,
                                      [o_sf[l][:, jj * 128:(jj + 1) * 128] for jj in jl], [("SFs", pb % 2)], "sf")
                    pend_t.append(tout)
            while pend_t:
                pend_t.pop(0)()
            if last_pass:
                S.add("dve", lambda e, l=l: e.tensor_copy(out=pst[:, 0:128].rearrange("p (k c) -> p k c", c=64),
                                                          in_=ffnhist[l][:].rearrange("p c k -> p k c")),
                      reads=[("dram", "ffnhist", l, q_) for q_ in range(64)], writes=["pst"])
                transpose_out(lambda k: pst[:, 0:128], [128], [o_pf[l].rearrange("k (c p) -> (k c) p", p=128)], ["pst"], "pf")
            if pend:
                pend.pop()()
            if l == DEPTH - 1 and p + 1 < NPASS:
                S.add("sp", lambda e, p=p: e.dma_start(out=tokx[:], in_=xp[(p + 1) * TP:(p + 2) * TP, :].rearrange("(j r) d -> r j d", r=128)),
                      writes=["tokx"], dsem="xin")
            ARs = keys("A", 32)
            for oc in range(8):
                wt, wres = W.get((p, "wdn", l, oc))
                wv = wt[:].rearrange("p (k n) -> p k n", n=128)

                def ev(b, c0, n, oc=oc):
                    S.add("act", lambda e: e.activation(out=sq[:, oc, c0:c0 + n], in_=ps_t[b][:, 0:n], func=AF.Square), writes=[PSR(b), ("sq", oc)])
                    S.add("act", lambda e: e.activation(out=mbuf[:, oc, c0:c0 + n], in_=ps_t[b][:, 0:n], func=AF.Identity,
                                                        scale=sm[:, SM_NFPOST + oc:SM_NFPOST + oc + 1]),
                          reads=[SMR], writes=[PSR(b), ("mbuf", oc)])
                proj(wv, wres, 0, A, ARs, 32, subs, ev)
                W.release()
            postnorm_residual(l, T, subs, l + 1 < DEPTH)

    def finish_pass(p, has_s):
        for j in range(4):
            for half in range(2):
                b = banks.alloc()

                def tr(e, b=b, j=j, half=half):
                    r = None
                    for c4 in range(4):
                        c = half * 4 + c4
                        r = e.transpose(ps_t[b][:, c4 * 128:(c4 + 1) * 128], x[:, c, j * 128:(j + 1) * 128], identf[:])
                    return r
                S.add("pe", tr, reads=XK + ["identf"], writes=[PSR(b)])
                S.add("act", lambda e, b=b, j=j, half=half: e.copy(out=tok[:, j, half * 512:(half + 1) * 512], in_=ps_t[b][:, 0:512]),
                      writes=[PSR(b), "tok"])
                banks.free(b)
        S.add("sp", lambda e, p=p: e.dma_start(out=yp[p * TP:(p + 1) * TP, :].rearrange("(j r) d -> r j d", r=128), in_=tok[:]),
              reads=["tok"], dsem="yout")
        if has_s:
            for half in range(2):
                b = banks.alloc()

                def tr(e, b=b, half=half):
                    r = None
                    for c4 in range(4):
                        c = half * 4 + c4
                        r = e.transpose(ps_t[b][0:NS, c4 * 128:(c4 + 1) * 128], x[:, c, TP:TP + NS], identf[:])
                    return r
                S.add("pe", tr, reads=XK + ["identf"], writes=[PSR(b)])
                S.add("act", lambda e, b=b, half=half: e.copy(out=toks[0:NS, half * 512:(half + 1) * 512], in_=ps_t[b][0:NS, 0:512]),
                      writes=[PSR(b), "toks"])
                banks.free(b)
            S.add("sp", lambda e: e.dma_start(out=ys, in_=toks[0:NS, :]), reads=["toks"], dsem="yout2")

    for p in range(NPASS):
        do_pass(p)
    assert W.pos == len(seq)
    S.emit()
    return nc


def _t5_bucket(d):
    n = max(d, 0)
    if n < 16:
        return n
    import math
    nf = np.float32(max(n, 1))
    v = np.float32(np.log(nf / np.float32(16)) / np.float32(math.log(128 / 16))) * np.float32(16)
    return min(16 + int(v), 31)


_NC_CACHE = {}


def kernel(x_prompt, x_sample, state_lru_h, state_lru_conv, cache_win_k, cache_win_v, state_ffn_conv,
           norm_mix_pre, norm_mix_post, norm_ffn_pre, norm_ffn_post, w_in, conv_lru_w, conv_lru_b,
           lru_wr, lru_br, lru_wi, lru_bi, lru_lambda, w_lru_o, w_attn_o, w_out, attn_sink, rel_bias,
           w_up, ffn_conv_w, ffn_conv_b, w_down):
    f32 = lambda a: np.ascontiguousarray(np.asarray(a, dtype=np.float32))
    x_prompt, x_sample = f32(x_prompt), f32(x_sample)
    n = 8

    def blk(w, nb, ncols):
        L = w.shape[0]
        kc = w.shape[1] // 128
        return f32(np.asarray(w).reshape(L, kc, 128, nb, ncols).transpose(0, 3, 2, 1, 4).reshape(L * nb, 128, kc * ncols))

    win_r = blk(f32(w_in), 9, 512)
    wlo_r = blk(f32(w_lru_o), 2, 512)
    wao_r = blk(f32(w_attn_o), 2, 512)
    wout_r = blk(f32(w_out), 2, 512)
    wup_r = blk(f32(w_up), 16, 512)
    wdn_r = blk(f32(w_down), 8, 128)
    gates_r = f32(np.stack([f32(lru_wr), f32(lru_wi)], axis=1).transpose(0, 3, 1, 2, 4).reshape(DEPTH, 128, 2048))

    def pc(v):
        v = f32(v)
        return v.reshape(v.shape[0], -1, 128).transpose(0, 2, 1)

    def pck(v):
        v = f32(v)
        L, K = v.shape[0], v.shape[1]
        return v.reshape(L, K, -1, 128).transpose(0, 3, 1, 2).reshape(L, 128, -1)

    smalls = f32(np.concatenate([pc(norm_mix_pre), pc(norm_mix_post), pc(norm_ffn_pre), pc(norm_ffn_post),
                                 pck(conv_lru_w), pc(conv_lru_b), pc(lru_br), pc(lru_bi), pc(lru_lambda),
                                 pck(ffn_conv_w), pc(ffn_conv_b)], axis=2))
    assert smalls.shape == (DEPTH, 128, SM_N), smalls.shape
    sk = f32(attn_sink)
    sinks = np.zeros((DEPTH, 128, 10), np.float32)
    sinks[:, :, 0:8] = sk[:, None, :]
    for kv in range(2):
        for g in range(4):
            sinks[:, g * 4:(g + 1) * 4, 8 + kv] = sk[:, kv * 4 + g][:, None]
    rb = f32(rel_bias)
    bidx = np.array([_t5_bucket(d) for d in range(128)])
    qq = np.arange(128)[:, None]
    jj = np.arange(256)[None, :]
    dd = qq + 128 - jj
    valid = (dd >= 0) & (dd < 128)
    gat = rb[bidx[np.clip(dd, 0, 127)]]
    biasp = np.where(valid[:, :, None], gat, np.float32(NEG)).transpose(0, 2, 1)
    biasp = f32(biasp).reshape(128, 8 * 256)
    tt = np.arange(4)[:, None]
    js = np.arange(132)[None, :]
    ds = tt + 128 - js
    vs = (ds >= 0) & (ds < 128)
    gs = rb[bidx[np.clip(ds, 0, 127)]]
    bs = np.where(vs[:, :, None], gs, np.float32(NEG))
    biass = np.zeros((16, 2, 132), np.float32)
    for kv in range(2):
        for g in range(4):
            biass[g * 4:(g + 1) * 4, kv, :] = bs[:, :, kv * 4 + g]
    biass = f32(biass).reshape(16, 2 * 132)
    ident = np.eye(128, dtype=np.float32)

    st_h, st_c, ckk, cvv, st_f = f32(state_lru_h), f32(state_lru_conv), f32(cache_win_k), f32(cache_win_v), f32(state_ffn_conv)
    in_maps = []
    for i in range(n):
        sl = slice(i * NB, (i + 1) * NB)
        in_maps.append({
            "xp": x_prompt[i], "xs": f32(x_sample[sl].reshape(NS, D)),
            "st_h": f32(st_h[:, sl]), "st_c": f32(st_c[:, sl].reshape(DEPTH, NB * 3, D)),
            "ck": f32(ckk[:, sl].reshape(DEPTH, NB, 128, 256)), "cv": f32(cvv[:, sl].reshape(DEPTH, NB, 128, 256)),
            "st_f": f32(st_f[:, sl].reshape(DEPTH, NB * 2, 8192)),
            "win_r": win_r, "gates_r": gates_r, "wlo_r": wlo_r, "wao_r": wao_r, "wout_r": wout_r, "wup_r": wup_r, "wdn_r": wdn_r,
            "smalls": smalls, "sinks": sinks, "biasp": biasp, "biass": biass, "ident": ident,
        })
    if "nc" not in _NC_CACHE:
        _NC_CACHE["nc"] = build()
    nc = _NC_CACHE["nc"]
    res = run_bass_kernel_spmd(nc, in_maps, core_ids=list(range(n)))
    R = res.results
    y_prompt = np.stack([R[i]["yp"] for i in range(n)], axis=0)
    y_sample = np.concatenate([R[i]["ys"].reshape(NB, 4, D) for i in range(n)], axis=0)
    p_lru_h = np.stack([R[i]["o_plh"] for i in range(n)], axis=1)
    p_lru_conv = np.stack([R[i]["o_plc"] for i in range(n)], axis=1)
    p_win_k = np.stack([R[i]["o_pk"].reshape(DEPTH, 128, 2, 128) for i in range(n)], axis=1)
    p_win_v = np.stack([R[i]["o_pv"].reshape(DEPTH, 128, 2, 128) for i in range(n)], axis=1)
    p_ffn = np.stack([R[i]["o_pf"] for i in range(n)], axis=1)
    s_lru_h = np.concatenate([R[i]["o_slh"] for i in range(n)], axis=1)
    s_lru_conv = np.concatenate([R[i]["o_slc"].reshape(DEPTH, NB, 3, D) for i in range(n)], axis=1)
    s_win_k = np.concatenate([R[i]["o_sk"].reshape(DEPTH, NB, 128, 2, 128) for i in range(n)], axis=1)
    s_win_v = np.concatenate([R[i]["o_sv"].reshape(DEPTH, NB, 128, 2, 128) for i in range(n)], axis=1)
    s_ffn = np.concatenate([R[i]["o_sf"].reshape(DEPTH, NB, 2, 8192) for i in range(n)], axis=1)
    outs = (y_prompt, y_sample, p_lru_h, p_lru_conv, p_win_k, p_win_v, p_ffn, s_lru_h, s_lru_conv, s_win_k, s_win_v, s_ffn)
    return tuple(np.ascontiguousarray(o, dtype=np.float32) for o in outs)
```

```python
import numpy as np
import concourse.bass as bass
import concourse.mybir as mybir
from concourse.bass_utils import run_bass_kernel_spmd

F32 = mybir.dt.float32
BF16 = mybir.dt.bfloat16
AF = mybir.ActivationFunctionType
ALU = mybir.AluOpType
AX = mybir.AxisListType

D = 1024
NCH = 8
SEQ = 2048
TP = 512
NPASS = SEQ // TP
NB = 16
NS = 64
DEPTH = 2
NEG = -30000.0
EPS = 1e-6
QSCALE = 128 ** -0.5
NSLOT = 5
GR = 256
SB_BASE = 16512
SB_TOP = 229344

SM_NMP, SM_NMPOST, SM_NFP, SM_NFPOST = 0, 8, 16, 24
SM_CLW = 32
SM_CLB = 64
SM_BR, SM_BI, SM_LAM = 72, 80, 88
SM_FCW = 96
SM_FCB = 288
SM_N = 352


class _Op:
    __slots__ = ("id", "eng", "fn", "deps", "signals", "dsem", "sidx", "ninst")


class Sched:
    ENGS = ("pe", "act", "dve", "pool", "sp")

    def __init__(self, nc):
        self.nc = nc
        self.ops = []
        self.last_writer = {}
        self.readers = {}
        self.dma_sems = {}
        self.alias = {}

    def reg(self, key, off, nbytes):
        g0 = off // GR
        g1 = (off + nbytes + GR - 1) // GR
        self.alias[key] = [("g", g) for g in range(g0, g1)]

    def _expand(self, keys):
        out = []
        for k in keys:
            a = self.alias.get(k)
            if a is None:
                assert isinstance(k, tuple) and k[0] in ("ps", "w", "dram"), ("unregistered key", k)
                out.append(k)
            else:
                out.extend(a)
        return out

    def add(self, eng, fn, reads=(), writes=(), dsem=None, ndma=1):
        reads = self._expand(reads)
        writes = self._expand(writes)
        deps = set()
        for r in reads:
            lw = self.last_writer.get(r)
            if lw is not None:
                deps.add(lw)
        for w in writes:
            lw = self.last_writer.get(w)
            if lw is not None:
                deps.add(lw)
            for rd in self.readers.get(w, ()):
                deps.add(rd)
        op = _Op()
        op.id = len(self.ops)
        op.eng = eng
        op.fn = fn
        op.deps = deps
        op.dsem = dsem
        op.signals = dsem is not None
        op.sidx = None
        op.ninst = ndma
        deps.discard(op.id)
        self.ops.append(op)
        for r in reads:
            self.readers.setdefault(r, []).append(op.id)
        for w in writes:
            self.last_writer[w] = op.id
            self.readers[w] = []
        return op.id

    def emit(self):
        nc = self.nc
        ops = self.ops
        for op in ops:
            for d in op.deps:
                p = ops[d]
                if p.eng == "pe" and op.eng == "pe" and p.dsem is None:
                    continue
                p.signals = True
        esem = {e: nc.alloc_semaphore("s_" + e) for e in ("pe", "act", "dve", "pool")}
        ecount = {e: 0 for e in esem}
        dcount = {}
        for op in ops:
            if op.dsem is not None:
                if op.dsem not in self.dma_sems:
                    self.dma_sems[op.dsem] = nc.alloc_semaphore("d_%d" % len(self.dma_sems))
                    dcount[op.dsem] = 0
                dcount[op.dsem] += 16 * op.ninst
                op.sidx = (self.dma_sems[op.dsem], dcount[op.dsem])
            elif op.signals:
                ecount[op.eng] += 1
                op.sidx = (esem[op.eng], ecount[op.eng])
        final_waits = {k: (self.dma_sems[k], v) for k, v in dcount.items()}
        by_eng = {e: [op for op in ops if op.eng == e] for e in self.ENGS}

        def run(engine, ename):
            waited = {}
            for op in by_eng[ename]:
                need = {}
                for d in op.deps:
                    p = ops[d]
                    if p.eng == "pe" and ename == "pe" and p.dsem is None:
                        continue
                    sem, val = p.sidx
                    k = id(sem)
                    if waited.get(k, 0) >= val:
                        continue
                    if k not in need or need[k][1] < val:
                        need[k] = (sem, val)
                for k, (sem, val) in need.items():
                    engine.wait_ge(sem, val)
                    waited[k] = val
                r = op.fn(engine)
                insts = r if isinstance(r, (list, tuple)) else [r]
                if op.dsem is not None:
                    assert len(insts) == op.ninst, (len(insts), op.ninst)
                    for i in insts:
                        i.then_inc(op.sidx[0], 16)
                elif op.signals:
                    insts[-1].then_inc(op.sidx[0], 1)
            if ename == "sp":
                for k, (sem, val) in final_waits.items():
                    engine.wait_ge(sem, val)

        with nc.Block() as block:
            @block.sync
            def _(e):
                run(e, "sp")

            @block.gpsimd
            def _(e):
                run(e, "pool")

            @block.tensor
            def _(e):
                run(e, "pe")

            @block.scalar
            def _(e):
                run(e, "act")

            @block.vector
            def _(e):
                run(e, "dve")


class Banks:
    def __init__(self, tensors):
        self.t = tensors
        self.free_list = list(range(len(tensors)))

    def alloc(self):
        assert self.free_list, "out of PSUM banks"
        return self.free_list.pop(0)

    def free(self, b):
        assert b not in self.free_list
        self.free_list.append(b)


class WStream:
    def __init__(self, S, nc, seq, slots, first_reads=()):
        self.S = S
        self.seq = seq
        self.slots = slots
        self.first_reads = list(first_reads)
        self.next_dma = 0
        self.released = 0
        self.pos = 0
        self._pump()

    def _pump(self):
        while self.next_dma < len(self.seq) and self.next_dma - NSLOT < self.released:
            n = self.next_dma
            key, src, ncol = self.seq[n]
            sl = n % NSLOT
            dst = self.slots[sl]
            nd = ncol // 2048

            def fn(e, src=src, dst=dst, nd=nd):
                return [e.dma_start(out=dst[:, i * 2048:(i + 1) * 2048], in_=src[:, i * 2048:(i + 1) * 2048])
                        for i in range(nd)]
            self.S.add("pool", fn, reads=(self.first_reads if n == 0 else ()), writes=[("w", sl)], dsem=("w", sl), ndma=nd)
            self.next_dma += 1

    def get(self, key):
        i = self.pos
        assert self.seq[i][0] == key, (self.seq[i][0], key)
        self.pos += 1
        self._pump()
        assert i < self.next_dma
        sl = i % NSLOT
        return self.slots[sl], ("w", sl)

    def release(self):
        self.released += 1
        self._pump()


def layer_block_keys(l):
    ks = [("win", l, 0), ("win", l, 1), ("win", l, 2), ("win", l, 3), ("win", l, 4)]
    ks += [("wlo", l, 0), ("win", l, 5), ("wlo", l, 1), ("win", l, 6)]
    ks += [("wao", l, 0), ("win", l, 7), ("wao", l, 1), ("win", l, 8)]
    ks += [("wout", l, 0), ("wout", l, 1)]
    for pb in range(8):
        ks += [("wup", l, pb), ("wup", l, pb + 8)]
    ks += [("wdn", l, oc) for oc in range(8)]
    return ks


def build():
    nc = bass.Bass("TRN2", target_bir_lowering=False)

    def din(name, shape):
        return nc.dram_tensor(name, list(shape), F32, kind="ExternalInput").ap()

    def dout(name, shape):
        return nc.dram_tensor(name, list(shape), F32, kind="ExternalOutput").ap()

    xp = din("xp", [SEQ, D])
    xs = din("xs", [NS, D])
    st_h = din("st_h", [DEPTH, NB, D])
    st_c = din("st_c", [DEPTH, NB * 3, D])
    ck = din("ck", [DEPTH, NB, 128, 256])
    cv = din("cv", [DEPTH, NB, 128, 256])
    st_f = din("st_f", [DEPTH, NB * 2, 8192])
    win_r = din("win_r", [DEPTH * 9, 128, 4096])
    gates_r = din("gates_r", [DEPTH, 128, 2048])
    wlo_r = din("wlo_r", [DEPTH * 2, 128, 4096])
    wao_r = din("wao_r", [DEPTH * 2, 128, 4096])
    wout_r = din("wout_r", [DEPTH * 2, 128, 4096])
    wup_r = din("wup_r", [DEPTH * 16, 128, 4096])
    wdn_r = din("wdn_r", [DEPTH * 8, 128, 4096])
    smalls_d = din("smalls", [DEPTH, 128, SM_N])
    sinks_d = din("sinks", [DEPTH, 128, 10])
    biasp_d = din("biasp", [128, 8 * 256])
    biass_d = din("biass", [16, 2 * 132])
    ident_d = din("ident", [128, 128])

    yp = dout("yp", [SEQ, D])
    ys = dout("ys", [NS, D])
    o_plh = dout("o_plh", [DEPTH, D])
    o_plc = dout("o_plc", [DEPTH, 3, D])
    o_pk = dout("o_pk", [DEPTH, 128, 256])
    o_pv = dout("o_pv", [DEPTH, 128, 256])
    o_pf = dout("o_pf", [DEPTH, 2, 8192])
    o_slh = dout("o_slh", [DEPTH, NB, D])
    o_slc = dout("o_slc", [DEPTH, NB * 3, D])
    o_sk = dout("o_sk", [DEPTH, NB, 128, 256])
    o_sv = dout("o_sv", [DEPTH, NB, 128, 256])
    o_sf = dout("o_sf", [DEPTH, NB * 2, 8192])

    S = Sched(nc)
    TMAX = TP + NS
    XRW = 3 + TP + NB * 7
    UW = 2 + TP + NB * 6

    def esz(dt):
        return 2 if dt == BF16 else 4

    def mk(name, shape, dt, off, key=None, chunks=None):
        assert off % 32 == 0
        nbytes = int(np.prod(shape[1:])) * esz(dt)
        assert SB_BASE + off + nbytes <= SB_TOP, (name, off, nbytes)
        t = nc.alloc_sbuf_tensor_at(name, list(shape), dt, offset=SB_BASE + off)
        key = key or name
        S.reg(key, off, nbytes)
        if chunks:
            cb = nbytes // chunks
            for c in range(chunks):
                S.reg((key, c), off + c * cb, cb)
        return t

    cur = [0]

    def P(name, shape, dt=F32, chunks=None, key=None):
        nbytes = int(np.prod(shape[1:])) * esz(dt)
        off = cur[0]
        cur[0] += (nbytes + GR - 1) // GR * GR
        return mk(name, shape, dt, off, key=key, chunks=chunks)

    identf = P("identf", [128, 128])
    identb = P("identb", [128, 128], BF16)
    ones_b = P("ones_b", [128, 128], BF16)
    epsc = P("epsc", [128, 1])
    smalls = [P("smalls%d" % l, [128, SM_N], key=("smalls", l)) for l in range(DEPTH)]
    lamc = [P("lamc%d" % l, [128, 16], key=("lamc", l)) for l in range(DEPTH)]
    sinks = [P("sinks%d" % l, [128, 10], key=("sinks", l)) for l in range(DEPTH)]
    biasp = P("biasp", [128, 8, 256])
    biass = P("biass", [16, 2, 132])
    convhist = [P("convhist%d" % l, [128, NCH, 3], key=("convhist", l)) for l in range(DEPTH)]
    hstate = [P("hstate%d" % l, [128, NCH], key=("hstate", l)) for l in range(DEPTH)]
    ffnhist = [P("ffnhist%d" % l, [128, 64, 2], key=("ffnhist", l)) for l in range(DEPTH)]
    khist = [P("khist%d" % l, [128, 2, 128], BF16, key=("khist", l)) for l in range(DEPTH)]
    vhist = [P("vhist%d" % l, [128, 256], BF16, key=("vhist", l)) for l in range(DEPTH)]
    gatesw = [P("gatesw%d" % l, [128, 2048], BF16, key=("gatesw", l)) for l in range(DEPTH)]
    x = P("x", [128, NCH, TMAX], chunks=NCH)
    h = P("h", [128, NCH, TMAX], BF16, chunks=NCH)
    rstd = P("rstd", [128, TMAX])
    sdv = P("sdv", [128, TMAX])
    stout = P("stout", [128, 1024])
    SFs = [P("SFs%d" % i, [128, 8, NB * 2], key=("SFs", i)) for i in range(2)]
    pst = P("pst", [128, 128])
    h0s = P("h0s", [128, NCH, NB])
    hs_last = P("hs_last", [128, NCH, NB])
    cs_stage = P("cs_stage", [128, NCH, NB * 3])
    tmp16 = P("tmp16", [128, NB])
    qTs = P("qTs", [128, 2, NB, 16], BF16)
    wslots = [P("wslot%d" % i, [128, 4096], BF16) for i in range(NSLOT)]
    SCR = cur[0]
    RA = SCR
    RB = SCR + 36864
    assert SB_BASE + RB + 61696 <= SB_TOP, (SCR, SB_TOP - SB_BASE)

    lro = mk("lro", [128, NCH, TMAX], BF16, RA + 0, chunks=NCH)
    qT = mk("qT", [128, 8, TMAX], BF16, RA + 9216, chunks=8)
    merged = mk("merged", [128, NCH, TMAX], BF16, RA + 9216, chunks=NCH)
    attn = mk("attn", [128, 8, TMAX], BF16, RA + 18432, chunks=8)
    kT = mk("kT", [128, 2, 128 + TP], BF16, RA + 27648)
    Vt = mk("Vt", [128, 5, 256], BF16, RA + 30208)
    kvnew = mk("kvnew", [64, 512], F32, RA + 32768)
    kvlast = mk("kvlast", [128, 512], F32, RA + 34816)
    A = mk("A", [128, 32, TMAX], BF16, RA + 0, chunks=32)
    sq = mk("sq", [128, NCH, TMAX], BF16, RB + 0, chunks=NCH)
    tok = mk("tok", [128, 4, D], F32, RB + 36864)
    tokx = mk("tokx", [128, 4, D], F32, RB + 20480)
    toks = mk("toks", [64, D], F32, RB + 16384)
    XR = mk("XR", [128, NCH, XRW], F32, RB + 0, chunks=NCH)
    LS = 20736
    xcg = [mk("xc%d" % s_, [128, 2, TMAX], F32, RB + 20224 + s_ * LS, key=("xc", s_), chunks=2) for s_ in range(2)]
    xcb = [mk("xcb%d" % s_, [128, 2, TMAX], BF16, RB + 20224 + s_ * LS + 4608, key=("xcb", s_), chunks=2) for s_ in range(2)]
    rr = [mk("rr%d" % s_, [128, 2, TMAX], F32, RB + 20224 + s_ * LS + 6912, key=("rr", s_), chunks=2) for s_ in range(2)]
    ii = [mk("ii%d" % s_, [128, 2, TMAX], F32, RB + 20224 + s_ * LS + 11520, key=("ii", s_), chunks=2) for s_ in range(2)]
    aa = [mk("aa%d" % s_, [128, 2, TMAX], F32, RB + 20224 + s_ * LS + 16128, key=("aa", s_), chunks=2) for s_ in range(2)]
    kTs = mk("kTs", [128, 2, NB, 132], BF16, RB + 0)
    Vs = mk("Vs", [128, NB, 256], BF16, RB + 8448)
    Vn4 = mk("Vn4", [4, NB, 256], BF16, RB + 16640)
    cstage = mk("cstage", [128, NB, 256], F32, RB + 24832)
    o3 = RB + 41216
    NSET = 8
    SBW = 64 + 260
    Sbuf, Pn, PT, stat = [], [], [], []
    for i in range(NSET):
        o = o3 + i * 2816
        Sbuf.append(mk("Sbuf%d" % i, [128, SBW], F32, o, key=("Sb", i)))
        S.reg(("Sb", i, "h"), o, 256)
        S.reg(("Sb", i, "b"), o + 256, 4 * 260)
        Pn.append(mk("Pn%d" % i, [128, 256], BF16, o + 1536, key=("Pn", i)))
        PT.append(mk("PT%d" % i, [128, 2, 128], BF16, o + 2048, key=("PT", i)))
        stat.append(mk("stat%d" % i, [128, 4], F32, o + 2560, key=("st", i)))
    sgA = [mk("sgA%d" % i, [128, TP], F32, RB + 9216 + i * 2048, key=("sgA", i)) for i in range(2)]
    sgB = [mk("sgB%d" % i, [128, TP], F32, RB + 13312 + i * 2048, key=("sgB", i)) for i in range(2)]
    t1b = [mk("t1b%d" % i, [128, TP], F32, RB + 17408 + i * 2048, key=("t1b", i)) for i in range(2)]
    mbuf = mk("mbuf", [128, NCH, TMAX], F32, RB + 40448, chunks=NCH)
    def mkU(name, i, off):
        t = mk("%s%d" % (name, i), [128, 64 + TP], F32, off, key=(name, i))
        S.reg((name, i, "h"), off, 256)
        S.reg((name, i, "b"), off + 256, 4 * TP)
        return t
    Uv = [mkU("Uv", 0, RB + 9216), mkU("Uv", 1, RB + 9216 + 2560), mkU("Uv", 2, RB + 41472)]
    Ug = [mkU("Ug", 0, RB + 14336), mkU("Ug", 1, RB + 14336 + 2560), mkU("Ug", 2, RB + 41472 + 2304)]
    cvv = [mk("cvv%d" % i, [128, TMAX], F32, RB + 19456 + i * 2304, key=("cvv", i)) for i in range(2)]
    cvg = [mk("cvg%d" % i, [128, TMAX], F32, RB + 24064 + i * 2304, key=("cvg", i)) for i in range(2)]
    ggb = [mk("ggb%d" % i, [128, TMAX], F32, RB + 28672 + i * 2304, key=("gg", i)) for i in range(2)]
    cvv.append(mk("cvv2", [128, TMAX], F32, RB + 0, key=("cvv", 2)))
    cvg.append(mk("cvg2", [128, TMAX], F32, RB + 2304, key=("cvg", 2)))
    ggb.append(mk("ggb2", [128, TMAX], F32, RB + 4608, key=("gg", 2)))
    FH = mk("FH", [128, 64, NB * 2], F32, RB + 33280)
    stf = mk("stf", [32, 2048], F32, RB + 0)
    Us = [mk("Us%d" % i, [128, 4, NB * 6], F32, RB + 58880 + i * 1536, key=("Us", i)) for i in range(2)]
    cvs = [mk("cvs%d" % i, [128, 4, NS], F32, RB + 6912 + i * 1024, key=("cvs", i)) for i in range(2)]
    print("SBUF map: persistent=%d scratch_avail=%d" % (SCR, SB_TOP - SB_BASE - SCR))

    ps_t = [nc.alloc_psum_tensor("ps%d" % i, [128, 512], F32) for i in range(8)]
    banks = Banks(ps_t)

    def PSR(b):
        return ("ps", b)

    def keys(name, n):
        return [(name, c) for c in range(n)]

    wsrc = {"win": (win_r, 9), "wlo": (wlo_r, 2), "wao": (wao_r, 2), "wout": (wout_r, 2),
            "wup": (wup_r, 16), "wdn": (wdn_r, 8)}
    seq = []
    for p in range(NPASS):
        for l in range(DEPTH):
            for key in layer_block_keys(l):
                t, n = wsrc[key[0]]
                seq.append(((p,) + key, t[l * n + key[2]], 4096))
    S.add("sp", lambda e: e.dma_start(out=tokx[:], in_=xp[0:TP, :].rearrange("(j r) d -> r j d", r=128)), writes=["tokx"], dsem="xin")
    W = WStream(S, nc, seq, wslots, first_reads=["tokx"])
    for l in range(DEPTH):
        S.add("pool", lambda e, l=l: e.dma_start(out=gatesw[l][:], in_=gates_r[l]), writes=[("gatesw", l)], dsem=("gw", l))

    S.add("sp", lambda e: e.dma_start(out=identf[:], in_=ident_d), writes=["identf"], dsem="c0")
    S.add("sp", lambda e: e.dma_start(out=biasp[:].rearrange("p h k -> p (h k)"), in_=biasp_d), writes=["biasp"], dsem="c1")
    S.add("sp", lambda e: e.dma_start(out=biass[:].rearrange("p h k -> p (h k)"), in_=biass_d), writes=["biass"], dsem="c2")
    for l in range(DEPTH):
        S.add("sp", lambda e, l=l: e.dma_start(out=smalls[l][:], in_=smalls_d[l]), writes=[("smalls", l)], dsem=("c3", l))
        S.add("sp", lambda e, l=l: e.dma_start(out=sinks[l][:], in_=sinks_d[l]), writes=[("sinks", l)], dsem=("c4", l))
    S.add("dve", lambda e: e.tensor_scalar(biasp[:], biasp[:], -1.0, None, ALU.mult), writes=["biasp"])
    S.add("dve", lambda e: e.tensor_scalar(biass[:], biass[:], -1.0, None, ALU.mult), writes=["biass"])
    for l in range(DEPTH):
        S.add("dve", lambda e, l=l: e.tensor_scalar(sinks[l][:], sinks[l][:], -1.0, None, ALU.mult), writes=[("sinks", l)])
    S.add("dve", lambda e: e.tensor_copy(out=identb[:], in_=identf[:]), reads=["identf"], writes=["identb"])
    S.add("dve", lambda e: e.memset(ones_b[:], 1.0), writes=["ones_b"])
    S.add("dve", lambda e: e.memset(epsc[:], EPS), writes=["epsc"])
    for l in range(DEPTH):
        S.add("act", lambda e, l=l: e.activation(out=lamc[l][:, 0:8], in_=smalls[l][:, SM_LAM:SM_LAM + 8], func=AF.Exp, scale=-1.0),
              reads=[("smalls", l)], writes=[("lamc", l)])
        S.add("act", lambda e, l=l: e.activation(out=lamc[l][:, 0:8], in_=lamc[l][:, 0:8], func=AF.Ln, bias=1.0),
              writes=[("lamc", l)])
        S.add("dve", lambda e, l=l: e.tensor_scalar(lamc[l][:, 8:16], lamc[l][:, 0:8], -16.0, None, ALU.mult), writes=[("lamc", l)])
        S.add("dve", lambda e, l=l: e.tensor_scalar(lamc[l][:, 0:8], lamc[l][:, 0:8], -8.0, None, ALU.mult), writes=[("lamc", l)])

    rot = {"S": 0, "sg": 0, "u": 0}
    XK = keys("x", NCH)
    HR = keys("h", NCH)

    def transpose_out(src_fn, ncols, dst_aps, rd, tag):
        i = 0
        while i < len(dst_aps):
            grp = list(range(i, min(i + 4, len(dst_aps))))
            b = banks.alloc()

            def tr(e, grp=grp, b=b):
                r = None
                for gi, k in enumerate(grp):
                    n = ncols[k]
                    r = e.transpose(ps_t[b][0:n, gi * 128:(gi + 1) * 128], src_fn(k), identf[:])
                return r
            S.add("pe", tr, reads=list(rd) + ["identf"], writes=[PSR(b)])
            nmax = max(ncols[k] for k in grp)
            S.add("act", lambda e, b=b, grp=grp, nmax=nmax: e.copy(out=stout[0:nmax, 0:128 * len(grp)], in_=ps_t[b][0:nmax, 0:128 * len(grp)]),
                  writes=[PSR(b), "stout"])
            banks.free(b)
            for gi, k in enumerate(grp):
                n = ncols[k]
                S.add("sp", lambda e, gi=gi, k=k, n=n: e.dma_start(out=dst_aps[k], in_=stout[0:n, gi * 128:(gi + 1) * 128]),
                      reads=["stout"], dsem=("so", tag))
            i += 4

    def norm_stats(T, subs):
        for (c0, n) in subs:
            b = banks.alloc()
            for c in range(NCH):
                S.add("pe", lambda e, b=b, c0=c0, n=n, c=c: e.matmul(ps_t[b][:, 0:n], ones_b[:], sq[:, c, c0:c0 + n], start=(c == 0), stop=(c == NCH - 1)),
                      reads=[("sq", c), "ones_b"], writes=[PSR(b)])
            S.add("act", lambda e, b=b, c0=c0, n=n: e.activation(out=sdv[:, c0:c0 + n], in_=ps_t[b][:, 0:n], func=AF.Ln,
                                                                 scale=1.0 / D, bias=epsc[:, 0:1]),
                  reads=["epsc"], writes=[PSR(b), "sdv"])
            S.add("act", lambda e, c0=c0, n=n: e.activation(out=rstd[:, c0:c0 + n], in_=sdv[:, c0:c0 + n], func=AF.Exp, scale=-0.5),
                  reads=["sdv"], writes=["rstd"])
            banks.free(b)

    def split_eng(c):
        return "dve"

    def rmsnorm_to_h(l, gcol, T, subs, squares_done=False):
        if not squares_done:
            for c in range(NCH):
                S.add("act", lambda e, c=c: e.activation(out=sq[:, c, 0:T], in_=x[:, c, 0:T], func=AF.Square), reads=[("x", c)], writes=[("sq", c)])
        norm_stats(T, subs)
        for c in range(NCH):
            S.add("dve", lambda e, c=c: e.scalar_tensor_tensor(out=h[:, c, 0:T], in0=x[:, c, 0:T],
                                                                      scalar=smalls[l][:, gcol + c:gcol + c + 1], in1=rstd[:, 0:T],
                                                                      op0=ALU.mult, op1=ALU.mult),
                  reads=[("x", c), "rstd", ("smalls", l)], writes=[("h", c)])

    def postnorm_residual(l, T, subs, next_squares):
        norm_stats(T, subs)
        for c in range(NCH):
            eng = split_eng(c)
            S.add(eng, lambda e, c=c: e.tensor_tensor(out=mbuf[:, c, 0:T], in0=mbuf[:, c, 0:T], in1=rstd[:, 0:T], op=ALU.mult),
                  reads=["rstd"], writes=[("mbuf", c)])
            S.add(eng, lambda e, c=c: e.tensor_tensor(out=x[:, c, 0:T], in0=x[:, c, 0:T], in1=mbuf[:, c, 0:T], op=ALU.add),
                  reads=[("mbuf", c)], writes=[("x", c)])
            if next_squares:
                S.add("act", lambda e, c=c: e.activation(out=sq[:, c, 0:T], in_=x[:, c, 0:T], func=AF.Square), reads=[("x", c)], writes=[("sq", c)])

    def proj(wt, wres, col, rhs, rhs_res, nk, subs, evac):
        for (c0, n) in subs:
            b = banks.alloc()
            for k in range(nk):
                S.add("pe", lambda e, b=b, c0=c0, n=n, k=k: e.matmul(ps_t[b][:, 0:n], wt[:, k, col:col + 128], rhs[:, k, c0:c0 + n],
                                                                     start=(k == 0), stop=(k == nk - 1)),
                      reads=[wres, rhs_res[k]], writes=[PSR(b)])
            evac(b, c0, n)
            banks.free(b)

    def attn_waves(units):
        waves = [units[i:i + 4] for i in range(0, len(units), 4)]

        def sets(w, i):
            return (w % 2) * 4 + i

        def front(w):
            wv = waves[w]
            bl = []
            for i, u in enumerate(wv):
                b = banks.alloc()
                bl.append(b)
                NQ, NK = u["NQ"], u["NK"]
                S.add("pe", lambda e, u=u, b=b, NQ=NQ, NK=NK: e.matmul(ps_t[b][0:NQ, 0:NK], u["qap"], u["kap"], start=True, stop=True),
                      reads=u["rd"], writes=[PSR(b)])
            for i, u in enumerate(wv):
                si = sets(w, i)
                NQ = u["NQ"]
                S.add("pool", lambda e, u=u, si=si, NQ=NQ: e.tensor_copy(out=Sbuf[si][0:NQ, 63:64], in_=u["sink_ap"]), reads=u["rd"], writes=[("Sb", si, "h")])
            for i, u in enumerate(wv):
                si = sets(w, i)
                b = bl[i]
                NQ, NK = u["NQ"], u["NK"]
                S.add("dve", lambda e, u=u, si=si, b=b, NQ=NQ, NK=NK: e.tensor_tensor(out=Sbuf[si][0:NQ, 64:64 + NK], in0=u["bias_ap"], in1=ps_t[b][0:NQ, 0:NK], op=ALU.subtract),
                      reads=[u["bias_key"]], writes=[PSR(b), ("Sb", si, "b")])
                banks.free(b)
            for i, u in enumerate(wv):
                si = sets(w, i)
                NQ, NK = u["NQ"], u["NK"]
                S.add("dve", lambda e, si=si, NQ=NQ, NK=NK: e.tensor_reduce(out=stat[si][0:NQ, 0:1], in_=Sbuf[si][0:NQ, 63:64 + NK], axis=AX.X, op=ALU.min),
                      reads=[("Sb", si)], writes=[("st", si)])

        def mid(w):
            wv = waves[w]
            for i, u in enumerate(wv):
                si = sets(w, i)
                NQ, NK = u["NQ"], u["NK"]
                S.add("act", lambda e, si=si, NQ=NQ, NK=NK: e.activation(out=Sbuf[si][0:NQ, 63:64 + NK], in_=Sbuf[si][0:NQ, 63:64 + NK], func=AF.Exp,
                                                                         bias=stat[si][0:NQ, 0:1], scale=-1.0, accum_out=stat[si][0:NQ, 2:3]),
                      writes=[("Sb", si), ("st", si)])
            for i, u in enumerate(wv):
                si = sets(w, i)
                NQ = u["NQ"]
                S.add("dve", lambda e, si=si, NQ=NQ: e.reciprocal(out=stat[si][0:NQ, 3:4], in_=stat[si][0:NQ, 2:3]), writes=[("st", si)])
            for i, u in enumerate(wv):
                si = sets(w, i)
                NQ, NK = u["NQ"], u["NK"]
                S.add("act", lambda e, si=si, NQ=NQ, NK=NK: e.activation(out=Pn[si][0:NQ, 0:NK], in_=Sbuf[si][0:NQ, 64:64 + NK], func=AF.Copy, scale=stat[si][0:NQ, 3:4]),
                      reads=[("Sb", si), ("st", si)], writes=[("Pn", si)])

        def back(w):
            wv = waves[w]
            tbl = []
            for i, u in enumerate(wv):
                si = sets(w, i)
                tb = banks.alloc()
                tbl.append(tb)
                NQ = u["NQ"]

                def tr(e, u=u, si=si, tb=tb, NQ=NQ):
                    tps = ps_t[tb].bitcast(BF16)
                    r = None
                    for vi, (vap, k0, nk) in enumerate(u["vblocks"]):
                        r = e.transpose(tps[0:nk, vi * 128:vi * 128 + NQ], Pn[si][0:NQ, k0:k0 + nk], identb[0:NQ, 0:NQ])
                    return r
                S.add("pe", tr, reads=[("Pn", si), "identb"], writes=[PSR(tb)])
            for i, u in enumerate(wv):
                si = sets(w, i)
                tb = tbl[i]
                NQ = u["NQ"]
                vb = u["vblocks"]
                if NQ == 128 and all(nk == 128 for (_, _, nk) in vb):
                    nv = len(vb)
                    S.add("act", lambda e, si=si, tb=tb, nv=nv: e.copy(out=PT[si][:, 0:nv, :],
                                                                      in_=ps_t[tb].bitcast(BF16)[:, 0:nv * 128].rearrange("p (v q) -> p v q", q=128)),
                          writes=[PSR(tb), ("PT", si)])
                else:
                    nv = len(vb)
                    S.add("act", lambda e, si=si, tb=tb, nv=nv, NQ=NQ: e.copy(
                        out=PT[si][:, 0:nv, 0:NQ], in_=ps_t[tb].bitcast(BF16)[:, 0:nv * 128].rearrange("p (v q) -> p v q", q=128)[:, :, 0:NQ]),
                        writes=[PSR(tb), ("PT", si)])
                banks.free(tb)
            for i, u in enumerate(wv):
                si = sets(w, i)
                g = u["grp"]
                if g.get("ob") is None:
                    g["ob"] = banks.alloc()
                ob = g["ob"]
                NQ = u["NQ"]
                c0 = u["out_c0"]

                def pv(e, u=u, si=si, ob=ob, NQ=NQ, c0=c0):
                    r = None
                    vb = u["vblocks"]
                    for vi, (vap, k0, nk) in enumerate(vb):
                        r = e.matmul(ps_t[ob][:, c0:c0 + NQ], vap, PT[si][0:nk, vi, 0:NQ], start=(vi == 0), stop=(vi == len(vb) - 1))
                    return r
                S.add("pe", pv, reads=[("PT", si)] + list(u["rd"]), writes=[PSR(ob)])
                if u["last"]:
                    u["evac"](ob)
                    banks.free(ob)
                    g["ob"] = None

        nw = len(waves)
        front(0)
        for w in range(nw):
            mid(w)
            if w + 1 < nw:
                front(w + 1)
            back(w)

    def do_pass(p):
        has_s = (p == 0)
        T = TP + (NS if has_s else 0)
        subs = [(0, TP)] + ([(TP, NS)] if has_s else [])
        last_pass = (p == NPASS - 1)
        XRs = XR[:, :, 3 + TP:3 + TP + NB * 7].rearrange("p c (b k) -> p c b k", k=7)

        if has_s:
            S.add("sp", lambda e: e.dma_start(out=toks[:], in_=xs), writes=["toks"], dsem="xin2")
        for c in range(NCH):
            b = banks.alloc()

            def tr(e, b=b, c=c):
                r = None
                for j in range(4):
                    r = e.transpose(ps_t[b][:, j * 128:(j + 1) * 128], tokx[:, j, c * 128:(c + 1) * 128], identf[:])
                return r
            S.add("pe", tr, reads=["tokx", "identf"], writes=[PSR(b)])
            S.add("act", lambda e, b=b, c=c: e.copy(out=x[:, c, 0:TP], in_=ps_t[b][:, 0:TP]), writes=[PSR(b), ("x", c)])
            banks.free(b)
            if has_s:
                b = banks.alloc()
                S.add("pe", lambda e, b=b, c=c: e.transpose(ps_t[b][:, 0:NS], toks[:, c * 128:(c + 1) * 128], identf[0:NS, 0:NS]),
                      reads=["toks", "identf"], writes=[PSR(b)])
                S.add("act", lambda e, b=b, c=c: e.copy(out=x[:, c, TP:TP + NS], in_=ps_t[b][:, 0:NS]), writes=[PSR(b), ("x", c)])
                banks.free(b)

        for l in range(DEPTH):
            do_layer(p, l, T, subs, has_s, last_pass, XRs)
        finish_pass(p, has_s)

    def do_layer(p, l, T, subs, has_s, last_pass, XRs):
        if True:
            sm = smalls[l]
            SMR = ("smalls", l)

            rmsnorm_to_h(l, SM_NMP, T, subs, squares_done=(l > 0))

            XRall = keys("XR", NCH)
            if has_s:
                S.add("sp", lambda e, l=l: e.dma_start(out=stout[0:48, 0:D], in_=st_c[l]), writes=["stout"], dsem="stin")
                for c in range(NCH):
                    b = banks.alloc()
                    S.add("pe", lambda e, b=b, c=c: e.transpose(ps_t[b][:, 0:48], stout[0:48, c * 128:(c + 1) * 128], identf[0:48, 0:48]),
                          reads=["stout", "identf"], writes=[PSR(b)])
                    S.add("act", lambda e, b=b, c=c: e.copy(out=XRs[:, c, :, 0:3], in_=ps_t[b][:, 0:48].rearrange("p (b k) -> p b k", k=3)),
                          writes=[PSR(b), ("XR", c)])
                    banks.free(b)
                S.add("sp", lambda e, l=l: e.dma_start(out=stout[0:16, 0:D], in_=st_h[l]), writes=["stout"], dsem="stin")
                for c in range(NCH):
                    b = banks.alloc()
                    S.add("pe", lambda e, b=b, c=c: e.transpose(ps_t[b][:, 0:16], stout[0:16, c * 128:(c + 1) * 128], identf[0:16, 0:16]),
                          reads=["stout", "identf"], writes=[PSR(b)])
                    S.add("act", lambda e, b=b, c=c: e.copy(out=h0s[:, c, :], in_=ps_t[b][:, 0:16]), writes=[PSR(b), "h0s"])
                    banks.free(b)

            if p == 0:
                S.add("dve", lambda e: e.memset(XR[:, :, 0:3], 0.0), writes=XRall)
            else:
                S.add("dve", lambda e, l=l: e.tensor_copy(out=XR[:, :, 0:3], in_=convhist[l][:]),
                      reads=[("convhist", l)], writes=XRall)
            for blk in range(2):
                wt, wres = W.get((p, "win", l, blk))
                wv = wt[:].rearrange("p (k n) -> p k n", n=512)
                for cc in range(4):
                    c = blk * 4 + cc

                    def ev(b, c0, n, c=c):
                        if c0 == 0:
                            S.add("act", lambda e: e.copy(out=XR[:, c, 3:3 + TP], in_=ps_t[b][:, 0:TP]), writes=[PSR(b), ("XR", c)])
                        else:
                            S.add("act", lambda e: e.copy(out=XRs[:, c, :, 3:7], in_=ps_t[b][:, 0:NS].rearrange("p (b t) -> p b t", t=4)),
                                  writes=[PSR(b), ("XR", c)])
                    proj(wv, wres, cc * 128, h, HR, NCH, subs, ev)
                W.release()
            S.add("dve", lambda e, l=l: e.tensor_copy(out=convhist[l][:], in_=XR[:, :, TP:TP + 3]), reads=XRall, writes=[("convhist", l)])
            if has_s:
                S.add("dve", lambda e: e.tensor_copy(out=cs_stage[:].rearrange("p c (b k) -> p c b k", k=3), in_=XRs[:, :, :, 4:7]),
                      reads=XRall, writes=["cs_stage"])

            for blk in (2, 3):
                wt, wres = W.get((p, "win", l, blk))
                wv = wt[:].rearrange("p (k n) -> p k n", n=512)
                for cc in range(4):
                    hh = (blk - 2) * 4 + cc

                    def ev(b, c0, n, hh=hh):
                        S.add("act", lambda e: e.activation(out=qT[:, hh, c0:c0 + n], in_=ps_t[b][:, 0:n], func=AF.Copy, scale=QSCALE),
                              writes=[PSR(b), ("qT", hh)])
                    proj(wv, wres, cc * 128, h, HR, NCH, subs, ev)
                W.release()
            wkv_t, wkv_res = W.get((p, "win", l, 4))
            wkv = wkv_t[:].rearrange("p (k n) -> p k n", n=512)
            if p == 0:
                S.add("dve", lambda e: e.memset(kT[:, :, 0:128], 0.0), writes=["kT"])
                S.add("dve", lambda e: e.memset(Vt[:, 0, :], 0.0), writes=["Vt"])
            else:
                S.add("dve", lambda e, l=l: e.tensor_copy(out=kT[:, :, 0:128], in_=khist[l][:]), reads=[("khist", l)], writes=["kT"])
                S.add("dve", lambda e, l=l: e.tensor_copy(out=Vt[:, 0, :], in_=vhist[l][:]), reads=[("vhist", l)], writes=["Vt"])
            for kv in range(2):
                def ev(b, c0, n, kv=kv):
                    S.add("act", lambda e: e.copy(out=kT[:, kv, 128:128 + TP], in_=ps_t[b][:, 0:TP]), writes=[PSR(b), "kT"])
                proj(wkv, wkv_res, kv * 128, h, HR, NCH, [(0, TP)], ev)
            for j in range(4):
                full = last_pass and j == 3
                b = banks.alloc()
                c0w, nw = (0, 512) if full else (256, 256)

                def mm(e, b=b, j=j, c0w=c0w, nw=nw):
                    r = None
                    for k in range(NCH):
                        r = e.matmul(ps_t[b][:, 0:nw], h[:, k, j * 128:(j + 1) * 128], wkv[:, k, c0w:c0w + nw], start=(k == 0), stop=(k == NCH - 1))
                    return r
                S.add("pe", mm, reads=[wkv_res] + HR, writes=[PSR(b)])
                voff = 256 if full else 0
                S.add("act", lambda e, b=b, j=j, voff=voff: e.copy(out=Vt[:, j + 1, :], in_=ps_t[b][:, voff:voff + 256]), writes=[PSR(b), "Vt"])
                if full:
                    S.add("dve", lambda e, b=b: e.tensor_copy(out=kvlast[:], in_=ps_t[b][:, 0:512]), writes=[PSR(b), "kvlast"])
                    S.add("sp", lambda e, l=l: e.dma_start(out=o_pk[l], in_=kvlast[:, 0:256]), reads=["kvlast"], dsem="okv")
                    S.add("sp", lambda e, l=l: e.dma_start(out=o_pv[l], in_=kvlast[:, 256:512]), reads=["kvlast"], dsem="okv")
                banks.free(b)
            S.add("dve", lambda e, l=l: e.tensor_copy(out=khist[l][:], in_=kT[:, :, TP:TP + 128]), reads=["kT"], writes=[("khist", l)])
            S.add("dve", lambda e, l=l: e.tensor_copy(out=vhist[l][:], in_=Vt[:, 4, :]), reads=["Vt"], writes=[("vhist", l)])
            if has_s:
                b = banks.alloc()

                def mm(e, b=b):
                    r = None
                    for k in range(NCH):
                        r = e.matmul(ps_t[b][0:NS, 0:512], h[:, k, TP:TP + NS], wkv[:, k, 0:512], start=(k == 0), stop=(k == NCH - 1))
                    return r
                S.add("pe", mm, reads=[wkv_res] + HR, writes=[PSR(b)])
                S.add("act", lambda e, b=b: e.copy(out=kvnew[:], in_=ps_t[b][0:NS, 0:512]), writes=[PSR(b), "kvnew"])
                banks.free(b)
                def kvout(e, l=l):
                    r = []
                    for bb in range(NB):
                        r.append(e.dma_start(out=o_sk[l][bb, 124:128, :], in_=kvnew[bb * 4:(bb + 1) * 4, 0:256]))
                        r.append(e.dma_start(out=o_sv[l][bb, 124:128, :], in_=kvnew[bb * 4:(bb + 1) * 4, 256:512]))
                    return r
                S.add("sp", kvout, reads=["kvnew"], writes=[("dram", "osv", l)], dsem="okv2", ndma=2 * NB)
                S.add("sp", lambda e, l=l: e.dma_start(out=o_sk[l][:, 0:124, :], in_=ck[l][:, 4:128, :]), dsem="d2d")
                S.add("sp", lambda e, l=l: e.dma_start(out=o_sv[l][:, 0:124, :], in_=cv[l][:, 4:128, :]), dsem="d2d")

            gv = gatesw[l][:].rearrange("p (g c n) -> p g c n", g=2, n=128)

            def K(name, st, cc):
                return ((name, st), cc)

            def convS(g):
                st = g % 2
                for cc in range(2):
                    c = g * 2 + cc
                    wcol = lambda k, c=c: sm[:, SM_CLW + k * 8 + c:SM_CLW + k * 8 + c + 1]
                    bcol = sm[:, SM_CLB + c:SM_CLB + c + 1]
                    xo = xcg[st]
                    S.add("dve", lambda e, c=c, cc=cc, xo=xo, wcol=wcol, bcol=bcol: e.tensor_scalar(xo[:, cc, 0:TP], XR[:, c, 3:3 + TP], wcol(3), bcol, ALU.mult, ALU.add),
                          reads=[("XR", c), SMR], writes=[K("xc", st, cc)])
                    for k in range(3):
                        S.add("dve", lambda e, c=c, cc=cc, k=k, xo=xo, wcol=wcol: e.scalar_tensor_tensor(out=xo[:, cc, 0:TP], in0=XR[:, c, k:k + TP], scalar=wcol(k),
                                                                                                       in1=xo[:, cc, 0:TP], op0=ALU.mult, op1=ALU.add),
                              reads=[("XR", c), SMR], writes=[K("xc", st, cc)])
                    if has_s:
                        xcs = xo[:, cc, TP:TP + NS].rearrange("p (b t) -> p b t", t=4)
                        S.add("dve", lambda e, c=c, wcol=wcol, bcol=bcol, xcs=xcs: e.tensor_scalar(xcs, XRs[:, c, :, 3:7], wcol(3), bcol, ALU.mult, ALU.add),
                              reads=[("XR", c), SMR], writes=[K("xc", st, cc)])
                        for k in range(3):
                            S.add("dve", lambda e, c=c, k=k, wcol=wcol, xcs=xcs: e.scalar_tensor_tensor(out=xcs, in0=XRs[:, c, :, k:k + 4], scalar=wcol(k),
                                                                                                      in1=xcs, op0=ALU.mult, op1=ALU.add),
                                  reads=[("XR", c), SMR], writes=[K("xc", st, cc)])
                    S.add("act", lambda e, cc=cc, xo=xo, st=st: e.copy(out=xcb[st][:, cc, 0:T], in_=xo[:, cc, 0:T]),
                          reads=[K("xc", st, cc)], writes=[K("xcb", st, cc)])

            def gatesS(g):
                st = g % 2
                for cc in range(2):
                    c = g * 2 + cc
                    for (c0, n) in subs:
                        for gi_, dst, dkey, bcolbase in ((0, rr[st], "rr", SM_BR), (1, ii[st], "ii", SM_BI)):
                            b = banks.alloc()
                            S.add("pe", lambda e, b=b, gi_=gi_, c=c, cc=cc, c0=c0, n=n, st=st: e.matmul(ps_t[b][:, 0:n], gv[:, gi_, c, :], xcb[st][:, cc, c0:c0 + n], start=True, stop=True),
                                  reads=[("gatesw", l), K("xcb", st, cc)], writes=[PSR(b)])
                            S.add("act", lambda e, b=b, dst=dst, c=c, cc=cc, c0=c0, n=n, bcolbase=bcolbase: e.activation(
                                out=dst[:, cc, c0:c0 + n], in_=ps_t[b][:, 0:n], func=AF.Sigmoid, bias=sm[:, bcolbase + c:bcolbase + c + 1], scale=1.0),
                                reads=[SMR], writes=[PSR(b), K(dkey, st, cc)])
                            banks.free(b)

            def expS(g):
                st = g % 2
                for cc in range(2):
                    c = g * 2 + cc
                    S.add("act", lambda e, c=c, cc=cc, st=st: e.activation(out=aa[st][:, cc, 0:T], in_=rr[st][:, cc, 0:T], func=AF.Exp, scale=lamc[l][:, c:c + 1]),
                          reads=[K("rr", st, cc), ("lamc", l)], writes=[K("aa", st, cc)])
                for cc in range(2):
                    S.add("dve", lambda e, cc=cc, st=st: e.scalar_tensor_tensor(out=rr[st][:, cc, 0:T], in0=aa[st][:, cc, 0:T], scalar=0.99999994,
                                                                                in1=aa[st][:, cc, 0:T], op0=ALU.min, op1=ALU.mult),
                          reads=[K("aa", st, cc)], writes=[K("rr", st, cc)])
                for cc in range(2):
                    S.add("act", lambda e, cc=cc, st=st: e.activation(out=rr[st][:, cc, 0:T], in_=rr[st][:, cc, 0:T], func=AF.Ln, scale=-1.0, bias=1.0),
                          writes=[K("rr", st, cc)])
                for cc in range(2):
                    S.add("act", lambda e, cc=cc, st=st: e.activation(out=rr[st][:, cc, 0:T], in_=rr[st][:, cc, 0:T], func=AF.Exp, scale=0.5),
                          writes=[K("rr", st, cc)])

            def dveS(g):
                st = g % 2
                xo = xcg[st]
                for cc in range(2):
                    c = g * 2 + cc
                    if p == 0:
                        S.add("dve", lambda e, cc=cc, st=st: e.memset(rr[st][:, cc, 0:1], 1.0), writes=[K("rr", st, cc)])
                    S.add("dve", lambda e, cc=cc, st=st, xo=xo: e.tensor_tensor(out=ii[st][:, cc, 0:T], in0=ii[st][:, cc, 0:T], in1=xo[:, cc, 0:T], op=ALU.mult),
                          reads=[K("xc", st, cc)], writes=[K("ii", st, cc)])
                    S.add("dve", lambda e, cc=cc, st=st: e.tensor_tensor(out=ii[st][:, cc, 0:T], in0=ii[st][:, cc, 0:T], in1=rr[st][:, cc, 0:T], op=ALU.mult),
                          reads=[K("rr", st, cc)], writes=[K("ii", st, cc)])
                    init = 0.0 if p == 0 else hstate[l][:, c:c + 1]
                    S.add("dve", lambda e, cc=cc, st=st, xo=xo, init=init: e.tensor_tensor_scan(out=xo[:, cc, 0:TP], data0=aa[st][:, cc, 0:TP], data1=ii[st][:, cc, 0:TP],
                                                                                                 initial=init, op0=ALU.mult, op1=ALU.add),
                          reads=[K("aa", st, cc), K("ii", st, cc), ("hstate", l)], writes=[K("xc", st, cc)])
                    if has_s:
                        aas = aa[st][:, cc, TP:TP + NS].rearrange("p (b t) -> p b t", t=4)
                        iis = ii[st][:, cc, TP:TP + NS].rearrange("p (b t) -> p b t", t=4)
                        S.add("dve", lambda e, c=c, aas=aas: e.tensor_tensor(out=tmp16[:], in0=aas[:, :, 0], in1=h0s[:, c, :], op=ALU.mult),
                              reads=[K("aa", st, cc), "h0s"], writes=["tmp16"])
                        S.add("dve", lambda e, iis=iis: e.tensor_tensor(out=iis[:, :, 0], in0=iis[:, :, 0], in1=tmp16[:], op=ALU.add),
                              reads=["tmp16"], writes=[K("ii", st, cc)])
                        S.add("dve", lambda e, aas=aas: e.memset(aas[:, :, 0], 0.0), writes=[K("aa", st, cc)])
                        S.add("dve", lambda e, cc=cc, st=st, xo=xo: e.tensor_tensor_scan(out=xo[:, cc, TP:TP + NS], data0=aa[st][:, cc, TP:TP + NS], data1=ii[st][:, cc, TP:TP + NS],
                                                                                         initial=0.0, op0=ALU.mult, op1=ALU.add),
                              reads=[K("aa", st, cc), K("ii", st, cc)], writes=[K("xc", st, cc)])
                    S.add("act", lambda e, c=c, cc=cc, xo=xo: e.copy(out=lro[:, c, 0:T], in_=xo[:, cc, 0:T]), reads=[K("xc", st, cc)], writes=[("lro", c)])
                XCK = [K("xc", st, cc) for cc in range(2)]
                S.add("dve", lambda e, g=g, xo=xo: e.tensor_copy(out=hstate[l][:, g * 2:(g + 1) * 2], in_=xo[:, :, TP - 1]), reads=XCK, writes=[("hstate", l)])
                if has_s:
                    S.add("dve", lambda e, g=g, xo=xo: e.tensor_copy(out=hs_last[:, g * 2:(g + 1) * 2, :],
                                                                     in_=xo[:, :, TP:TP + NS].rearrange("p c (b t) -> p c b t", t=4)[:, :, :, 3]),
                          reads=XCK, writes=["hs_last"])

            convS(0)
            gatesS(0)
            convS(1)
            expS(0)
            gatesS(1)
            dveS(0)
            convS(2)
            expS(1)
            gatesS(2)
            dveS(1)
            convS(3)
            expS(2)
            gatesS(3)
            dveS(2)
            expS(3)
            dveS(3)
            if has_s:
                transpose_out(lambda k: hs_last[:, k, :], [NB] * NCH,
                              [o_slh[l][:, k * 128:(k + 1) * 128] for k in range(NCH)], ["hs_last"], "slh")
                transpose_out(lambda k: cs_stage[:, k, :], [NB * 3] * NCH,
                              [o_slc[l][:, k * 128:(k + 1) * 128] for k in range(NCH)], ["cs_stage"], "slc")
            if last_pass:
                transpose_out(lambda k, l=l: hstate[l][:, :], [NCH], [o_plh[l].rearrange("(c p) -> c p", p=128)], [("hstate", l)], "plh")
                S.add("dve", lambda e, l=l: e.tensor_copy(out=pst[:, 0:24].rearrange("p (k c) -> p k c", c=NCH),
                                                          in_=convhist[l][:].rearrange("p c k -> p k c")),
                      reads=[("convhist", l)], writes=["pst"])
                transpose_out(lambda k: pst[:, 0:24], [24], [o_plc[l].rearrange("k (c p) -> (k c) p", p=128)], ["pst"], "plc")

            if has_s:
                S.add("sp", lambda e, l=l: e.dma_start(out=cstage[:], in_=ck[l].rearrange("b k d -> k b d")), writes=["cstage"], dsem="stin3")
                S.add("pool", lambda e, l=l: e.dma_start(out=Vs[:], in_=cv[l].rearrange("b k d -> k b d")), writes=["Vs"], dsem="vs_in")
                for kv in range(2):
                    def ev(b, c0, n, kv=kv):
                        S.add("act", lambda e: e.copy(out=kTs[:, kv, :, 128:132], in_=ps_t[b][:, 0:NS].rearrange("p (b t) -> p b t", t=4)),
                              writes=[PSR(b), "kTs"])
                    proj(wkv, wkv_res, kv * 128, h, HR, NCH, [(TP, NS)], ev)
                S.add("pool", lambda e, l=l: e.dma_start(out=Vn4[:], in_=o_sv[l][:, 124:128, :].rearrange("b t d -> t b d")),
                      reads=[("dram", "osv", l)], writes=["Vn4"], dsem="vn4")
                for kv in range(2):
                    S.add("dve", lambda e, kv=kv: e.tensor_copy(
                        out=qTs[:, kv, :, :].rearrange("p b (g t) -> p b g t", t=4),
                        in_=qT[:, kv * 4:kv * 4 + 4, TP:TP + NS].rearrange("p g (b t) -> p b g t", t=4)),
                        reads=[("qT", kv * 4 + g) for g in range(4)], writes=["qTs"])
            W.release()

            units = []
            for hh in range(8):
                kv = hh // 4
                grp = {}
                for j in range(4):
                    first_blk = (p == 0 and j == 0)
                    if first_blk:
                        kap = kT[:, kv, 128:256]
                        bias_ap = biasp[:, hh, 128:256]
                        vbl = [(Vt[:, 1, kv * 128:(kv + 1) * 128], 0, 128)]
                        NK = 128
                    else:
                        kap = kT[:, kv, j * 128:j * 128 + 256]
                        bias_ap = biasp[:, hh, :]
                        vbl = [(Vt[:, j, kv * 128:(kv + 1) * 128], 0, 128), (Vt[:, j + 1, kv * 128:(kv + 1) * 128], 128, 128)]
                        NK = 256

                    def evp(ob, hh=hh):
                        S.add("act", lambda e: e.copy(out=attn[:, hh, 0:TP], in_=ps_t[ob][:, 0:TP]), writes=[PSR(ob), ("attn", hh)])
                    units.append(dict(qap=qT[:, hh, j * 128:(j + 1) * 128], kap=kap, NQ=128, NK=NK, bias_ap=bias_ap, bias_key="biasp",
                                      sink_ap=sinks[l][:, hh:hh + 1], vblocks=vbl, rd=[("qT", hh), "kT", "Vt", ("sinks", l)],
                                      grp=grp, out_c0=j * 128, last=(j == 3), evac=evp))
            attn_waves(units)
            units = []
            if has_s:
                for bb in range(NB):
                    b = banks.alloc()

                    def tr(e, b=b, bb=bb):
                        r = None
                        for kv in range(2):
                            r = e.transpose(ps_t[b][:, kv * 128:(kv + 1) * 128], cstage[:, bb, kv * 128:(kv + 1) * 128], identf[:])
                        return r
                    S.add("pe", tr, reads=["cstage", "identf"], writes=[PSR(b)])
                    S.add("act", lambda e, b=b, bb=bb: e.copy(out=kTs[:, :, bb, 0:128], in_=ps_t[b][:, 0:256].rearrange("p (v k) -> p v k", k=128)),
                          writes=[PSR(b), "kTs"])
                    banks.free(b)
                for kv in range(2):
                    grp = {}
                    for bb in range(NB):
                        vbl = [(Vs[:, bb, kv * 128:(kv + 1) * 128], 0, 128), (Vn4[0:4, bb, kv * 128:(kv + 1) * 128], 128, 4)]

                        def evs(ob, kv=kv):
                            S.add("act", lambda e: e.copy(
                                out=attn[:, kv * 4:kv * 4 + 4, TP:TP + NS].rearrange("p g (b t) -> p b g t", t=4),
                                in_=ps_t[ob][:, 0:256].rearrange("p (b g t) -> p b g t", g=4, t=4)),
                                writes=[PSR(ob)] + [("attn", kv * 4 + g) for g in range(4)])
                        units.append(dict(qap=qTs[:, kv, bb, :], kap=kTs[:, kv, bb, :], NQ=16, NK=132, bias_ap=biass[:, kv, :], bias_key="biass",
                                          sink_ap=sinks[l][0:16, 8 + kv:9 + kv], vblocks=vbl, rd=["qTs", "kTs", "Vs", "Vn4", ("sinks", l)],
                                          grp=grp, out_c0=bb * 16, last=(bb == NB - 1), evac=evs))
            if units:
                attn_waves(units)

            LR = keys("lro", NCH)
            AR = keys("attn", 8)

            def mkmm(bk, wv_, src, col, c0, n):
                def mm(e):
                    r = None
                    for k in range(NCH):
                        r = e.matmul(ps_t[bk][:, 0:n], wv_[:, k, col:col + 128], src[:, k, c0:c0 + n], start=(k == 0), stop=(k == NCH - 1))
                    return r
                return mm
            for br, (wname, gbase, src, SR) in enumerate((("wlo", 5, lro, LR), ("wao", 7, attn, AR))):
                for hf in range(2):
                    wo_t, wo_res = W.get((p, wname, l, hf))
                    wg_t, wg_res = W.get((p, "win", l, gbase + hf))
                    v_o = wo_t[:].rearrange("p (k n) -> p k n", n=512)
                    v_g = wg_t[:].rearrange("p (k n) -> p k n", n=512)
                    for cc in range(4):
                        oc = hf * 4 + cc
                        for (c0, n) in subs:
                            gi = rot["sg"] % 2
                            rot["sg"] += 1
                            bO, bG = banks.alloc(), banks.alloc()
                            S.add("pe", mkmm(bG, v_g, h, cc * 128, c0, n), reads=[wg_res] + HR, writes=[PSR(bG)])
                            for k in range(NCH):
                                S.add("pe", lambda e, bO=bO, v_o=v_o, src=src, cc=cc, c0=c0, n=n, k=k: e.matmul(
                                    ps_t[bO][:, 0:n], v_o[:, k, cc * 128:cc * 128 + 128], src[:, k, c0:c0 + n], start=(k == 0), stop=(k == NCH - 1)),
                                    reads=[wo_res, SR[k]], writes=[PSR(bO)])
                            S.add("act", lambda e, gi=gi, bG=bG, n=n: e.activation(out=sgA[gi][:, 0:n], in_=ps_t[bG][:, 0:n], func=AF.Sigmoid),
                                  writes=[PSR(bG), ("sgA", gi)])
                            if br == 0:
                                S.add("dve", lambda e, gi=gi, bO=bO, oc=oc, c0=c0, n=n: e.tensor_tensor(out=merged[:, oc, c0:c0 + n], in0=sgA[gi][:, 0:n], in1=ps_t[bO][:, 0:n], op=ALU.mult),
                                      reads=[("sgA", gi)], writes=[PSR(bO), ("merged", oc)])
                            else:
                                S.add("dve", lambda e, gi=gi, bO=bO, n=n: e.tensor_tensor(out=t1b[gi][:, 0:n], in0=sgA[gi][:, 0:n], in1=ps_t[bO][:, 0:n], op=ALU.mult),
                                      reads=[("sgA", gi)], writes=[PSR(bO), ("t1b", gi)])
                                S.add("dve", lambda e, gi=gi, oc=oc, c0=c0, n=n: e.tensor_tensor(out=merged[:, oc, c0:c0 + n], in0=t1b[gi][:, 0:n], in1=merged[:, oc, c0:c0 + n], op=ALU.add),
                                      reads=[("t1b", gi)], writes=[("merged", oc)])
                            banks.free(bO)
                            banks.free(bG)
                    W.release()
                    W.release()
            MR = keys("merged", NCH)
            for hf in range(2):
                wt, wres = W.get((p, "wout", l, hf))
                wv = wt[:].rearrange("p (k n) -> p k n", n=512)
                for cc in range(4):
                    oc = hf * 4 + cc

                    def ev(b, c0, n, oc=oc):
                        S.add("act", lambda e: e.activation(out=sq[:, oc, c0:c0 + n], in_=ps_t[b][:, 0:n], func=AF.Square), writes=[PSR(b), ("sq", oc)])
                        S.add("act", lambda e: e.activation(out=mbuf[:, oc, c0:c0 + n], in_=ps_t[b][:, 0:n], func=AF.Identity,
                                                            scale=sm[:, SM_NMPOST + oc:SM_NMPOST + oc + 1]),
                              reads=[SMR], writes=[PSR(b), ("mbuf", oc)])
                    proj(wv, wres, cc * 128, merged, MR, NCH, subs, ev)
                W.release()
            postnorm_residual(l, T, subs, True)

            rmsnorm_to_h(l, SM_NFP, T, subs, squares_done=True)
            if has_s:
                for q4 in range(4):
                    S.add("sp", lambda e, l=l, q4=q4: e.dma_start(out=stf[:], in_=st_f[l][:, q4 * 2048:(q4 + 1) * 2048]),
                          writes=["stf"], dsem="stin2")
                    for g4 in range(4):
                        b = banks.alloc()

                        def tr(e, b=b, g4=g4):
                            r = None
                            for i4 in range(4):
                                jj = g4 * 4 + i4
                                r = e.transpose(ps_t[b][:, i4 * 32:(i4 + 1) * 32], stf[0:32, jj * 128:(jj + 1) * 128], identf[0:32, 0:32])
                            return r
                        S.add("pe", tr, reads=["stf", "identf"], writes=[PSR(b)])
                        j0 = q4 * 16 + g4 * 4
                        S.add("act", lambda e, b=b, j0=j0: e.copy(out=FH[:, j0:j0 + 4, :], in_=ps_t[b][:, 0:128].rearrange("p (j n) -> p j n", n=32)),
                              writes=[PSR(b), "FH"])
                        banks.free(b)
            pend = []
            pend_t = []

            def hist_in(Un, ui_, jj_):
                Ub_ = (Uv if Un == "Uv" else Ug)[ui_]
                if p == 0:
                    S.add("pool", lambda e: e.memset(Ub_[:, 62:64], 0.0), writes=[(Un, ui_, "h")])
                else:
                    S.add("pool", lambda e: e.tensor_copy(out=Ub_[:, 62:64], in_=ffnhist[l][:, jj_, :]),
                          reads=[("dram", "ffnhist", l, jj_)], writes=[(Un, ui_, "h")])
            for pb in range(8):
                wv_t, wv_res = W.get((p, "wup", l, pb))
                wg_t, wg_res = W.get((p, "wup", l, pb + 8))
                vv = wv_t[:].rearrange("p (k n) -> p k n", n=512)
                vg = wg_t[:].rearrange("p (k n) -> p k n", n=512)
                wsc = lambda k, jj: sm[:, SM_FCW + k * 64 + jj:SM_FCW + k * 64 + jj + 1]
                bsc = lambda jj: sm[:, SM_FCB + jj:SM_FCB + jj + 1]
                for cc in range(4):
                    j = pb * 4 + cc
                    ci = rot["u"] % 3
                    ui = ci
                    rot["u"] += 1
                    for (wview, wres_, Ub, Un, jj, cvb, cres) in ((vv, wv_res, Uv[ui], "Uv", j, cvv[ci], ("cvv", ci)),
                                                                 (vg, wg_res, Ug[ui], "Ug", 32 + j, cvg[ci], ("cvg", ci))):
                        UH, UB = (Un, ui, "h"), (Un, ui, "b")
                        if j == 0:
                            hist_in(Un, ui, jj)
                        if j + 1 < 32:
                            hist_in(Un, (ui + 1) % 3, jj + 1)

                        def ev(b, c0, n, Ub=Ub, UB=UB, cvb=cvb, cres=cres, jj=jj):
                            S.add("act", lambda e: e.copy(out=Ub[:, 64:64 + TP], in_=ps_t[b][:, 0:TP]), writes=[PSR(b), UB])
                            S.add("act", lambda e: e.activation(out=cvb[:, 0:TP], in_=ps_t[b][:, 0:TP], func=AF.Identity, scale=wsc(2, jj), bias=bsc(jj)),
                                  reads=[SMR], writes=[PSR(b), cres])
                        proj(wview, wres_, cc * 128, h, HR, NCH, [(0, TP)], ev)
                        S.add("pool", lambda e, Ub=Ub, jj=jj: e.tensor_copy(out=ffnhist[l][:, jj, :], in_=Ub[:, 62 + TP:64 + TP]),
                              reads=[UB], writes=[("dram", "ffnhist", l, jj)])
                        for k in range(2):
                            S.add("dve", lambda e, Ub=Ub, cvb=cvb, jj=jj, k=k: e.scalar_tensor_tensor(out=cvb[:, 0:TP], in0=Ub[:, 62 + k:62 + k + TP], scalar=wsc(k, jj),
                                                                                                      in1=cvb[:, 0:TP], op0=ALU.mult, op1=ALU.add),
                                  reads=[UH, UB, SMR], writes=[cres])

                    def tail(ci=ci, j=j):
                        S.add("act", lambda e: e.activation(out=ggb[ci][:, 0:TP], in_=cvg[ci][:, 0:TP], func=AF.Gelu_apprx_tanh),
                              reads=[("cvg", ci)], writes=[("gg", ci)])
                        S.add("dve", lambda e: e.tensor_tensor(out=A[:, j, 0:TP], in0=cvv[ci][:, 0:TP], in1=ggb[ci][:, 0:TP], op=ALU.mult),
                              reads=[("cvv", ci), ("gg", ci)], writes=[("A", j)])
                    if pend:
                        pend.pop()()
                    pend.append(tail)
                if has_s and pend_t:
                    pend_t.pop(0)()
                if has_s:
                    for half, (wview, wres_) in enumerate(((vv, wv_res), (vg, wg_res))):
                        jj0 = pb * 4 + 32 * half
                        Uh = Us[half][:].rearrange("p c (b k) -> p c b k", k=6)
                        b = banks.alloc()

                        def mm(e, b=b, wview=wview):
                            r = None
                            for c4 in range(4):
                                for k in range(NCH):
                                    r = e.matmul(ps_t[b][:, c4 * NS:(c4 + 1) * NS], wview[:, k, c4 * 128:(c4 + 1) * 128], h[:, k, TP:TP + NS],
                                                 start=(k == 0), stop=(k == NCH - 1))
                            return r
                        S.add("pe", mm, reads=[wres_] + HR, writes=[PSR(b)])
                        S.add("pool", lambda e, Uh=Uh, jj0=jj0: e.tensor_copy(out=Uh[:, :, :, 0:2], in_=FH[:, jj0:jj0 + 4, :].rearrange("p c (b k) -> p c b k", k=2)),
                              reads=["FH"], writes=[("Us", half)])
                        S.add("act", lambda e, Uh=Uh, b=b: e.copy(out=Uh[:, :, :, 2:6], in_=ps_t[b][:, 0:4 * NS].rearrange("p (c b t) -> p c b t", c=4, t=4)),
                              writes=[PSR(b), ("Us", half)])
                        for c4 in range(4):
                            S.add("act", lambda e, b=b, c4=c4, half=half, jj0=jj0: e.activation(out=cvs[half][:, c4, :], in_=ps_t[b][:, c4 * NS:(c4 + 1) * NS],
                                                                                               func=AF.Identity, scale=wsc(2, jj0 + c4), bias=bsc(jj0 + c4)),
                                  reads=[SMR], writes=[PSR(b), ("cvs", half)])
                        banks.free(b)
                        S.add("pool", lambda e, Uh=Uh, half=half, pb=pb: e.tensor_copy(
                            out=SFs[pb % 2][:, half * 4:(half + 1) * 4, :].rearrange("p c (b k) -> p c b k", k=2), in_=Uh[:, :, :, 4:6]),
                            reads=[("Us", half)], writes=[("SFs", pb % 2)])
                        for c4 in range(4):
                            cv4 = cvs[half][:, c4, :].rearrange("p (b t) -> p b t", t=4)
                            for k in range(2):
                                S.add("dve", lambda e, Uh=Uh, cv4=cv4, c4=c4, k=k, jj0=jj0: e.scalar_tensor_tensor(out=cv4, in0=Uh[:, c4, :, k:k + 4], scalar=wsc(k, jj0 + c4),
                                                                                                                 in1=cv4, op0=ALU.mult, op1=ALU.add),
                                      reads=[("Us", half), SMR], writes=[("cvs", half)])
                    S.add("act", lambda e: e.activation(out=cvs[1][:], in_=cvs[1][:], func=AF.Gelu_apprx_tanh), writes=[("cvs", 1)])
                    S.add("dve", lambda e, pb=pb: e.tensor_tensor(out=A[:, pb * 4:(pb + 1) * 4, TP:TP + NS], in0=cvs[0][:], in1=cvs[1][:], op=ALU.mult),
                          reads=[("cvs", 0), ("cvs", 1)], writes=[("A", pb * 4 + i) for i in range(4)])
                W.release()
                W.release()
                if has_s:
                    def tout(pb=pb):
                        jl = [pb * 4 + i for i in range(4)] + [32 + pb * 4 + i for i in range(4)]
                        transpose_out(lambda k, pb=pb: SFs[pb % 2][:, k, :], [NB * 2] * 8,
                                      [o_sf[l][:, jj * 128:(jj + 1) * 128] for jj in jl], [("SFs", pb % 2)], "sf")
                    pend_t.append(tout)
            while pend_t:
                pend_t.pop(0)()
            if last_pass:
                S.add("dve", lambda e, l=l: e.tensor_copy(out=pst[:, 0:128].rearrange("p (k c) -> p k c", c=64),
                                                          in_=ffnhist[l][:].rearrange("p c k -> p k c")),
                      reads=[("dram", "ffnhist", l, q_) for q_ in range(64)], writes=["pst"])
                transpose_out(lambda k: pst[:, 0:128], [128], [o_pf[l].rearrange("k (c p) -> (k c) p", p=128)], ["pst"], "pf")
            if pend:
                pend.pop()()
            if l == DEPTH - 1 and p + 1 < NPASS:
                S.add("sp", lambda e, p=p: e.dma_start(out=tokx[:], in_=xp[(p + 1) * TP:(p + 2) * TP, :].rearrange("(j r) d -> r j d", r=128)),
                      writes=["tokx"], dsem="xin")
            ARs = keys("A", 32)
            for oc in range(8):
                wt, wres = W.get((p, "wdn", l, oc))
                wv = wt[:].rearrange("p (k n) -> p k n", n=128)

                def ev(b, c0, n, oc=oc):
                    S.add("act", lambda e: e.activation(out=sq[:, oc, c0:c0 + n], in_=ps_t[b][:, 0:n], func=AF.Square), writes=[PSR(b), ("sq", oc)])
                    S.add("act", lambda e: e.activation(out=mbuf[:, oc, c0:c0 + n], in_=ps_t[b][:, 0:n], func=AF.Identity,
                                                        scale=sm[:, SM_NFPOST + oc:SM_NFPOST + oc + 1]),
                          reads=[SMR], writes=[PSR(b), ("mbuf", oc)])
                proj(wv, wres, 0, A, ARs, 32, subs, ev)
                W.release()
            postnorm_residual(l, T, subs, l + 1 < DEPTH)

    def finish_pass(p, has_s):
        for j in range(4):
            for half in range(2):
                b = banks.alloc()

                def tr(e, b=b, j=j, half=half):
                    r = None
                    for c4 in range(4):
                        c = half * 4 + c4
                        r = e.transpose(ps_t[b][:, c4 * 128:(c4 + 1) * 128], x[:, c, j * 128:(j + 1) * 128], identf[:])
                    return r
                S.add("pe", tr, reads=XK + ["identf"], writes=[PSR(b)])
                S.add("dve", lambda e, b=b, j=j, half=half: e.tensor_copy(out=tok[:, j, half * 512:(half + 1) * 512], in_=ps_t[b][:, 0:512]),
                      writes=[PSR(b), "tok"])
                banks.free(b)
        S.add("sp", lambda e, p=p: e.dma_start(out=yp[p * TP:(p + 1) * TP, :].rearrange("(j r) d -> r j d", r=128), in_=tok[:]),
              reads=["tok"], dsem="yout")
        if has_s:
            for half in range(2):
                b = banks.alloc()

                def tr(e, b=b, half=half):
                    r = None
                    for c4 in range(4):
                        c = half * 4 + c4
                        r = e.transpose(ps_t[b][0:NS, c4 * 128:(c4 + 1) * 128], x[:, c, TP:TP + NS], identf[:])
                    return r
                S.add("pe", tr, reads=XK + ["identf"], writes=[PSR(b)])
                S.add("act", lambda e, b=b, half=half: e.copy(out=toks[0:NS, half * 512:(half + 1) * 512], in_=ps_t[b][0:NS, 0:512]),
                      writes=[PSR(b), "toks"])
                banks.free(b)
            S.add("sp", lambda e: e.dma_start(out=ys, in_=toks[0:NS, :]), reads=["toks"], dsem="yout2")

    for p in range(NPASS):
        do_pass(p)
    assert W.pos == len(seq)
    S.emit()
    return nc


def _t5_bucket(d):
    n = max(d, 0)
    if n < 16:
        return n
    import math
    nf = np.float32(max(n, 1))
    v = np.float32(np.log(nf / np.float32(16)) / np.float32(math.log(128 / 16))) * np.float32(16)
    return min(16 + int(v), 31)


_NC_CACHE = {}


def kernel(x_prompt, x_sample, state_lru_h, state_lru_conv, cache_win_k, cache_win_v, state_ffn_conv,
           norm_mix_pre, norm_mix_post, norm_ffn_pre, norm_ffn_post, w_in, conv_lru_w, conv_lru_b,
           lru_wr, lru_br, lru_wi, lru_bi, lru_lambda, w_lru_o, w_attn_o, w_out, attn_sink, rel_bias,
           w_up, ffn_conv_w, ffn_conv_b, w_down):
    f32 = lambda a: np.ascontiguousarray(np.asarray(a, dtype=np.float32))
    x_prompt, x_sample = f32(x_prompt), f32(x_sample)
    n = 8

    def blk(w, nb, ncols):
        L = w.shape[0]
        kc = w.shape[1] // 128
        return f32(np.asarray(w).reshape(L, kc, 128, nb, ncols).transpose(0, 3, 2, 1, 4).reshape(L * nb, 128, kc * ncols))

    win_r = blk(f32(w_in), 9, 512)
    wlo_r = blk(f32(w_lru_o), 2, 512)
    wao_r = blk(f32(w_attn_o), 2, 512)
    wout_r = blk(f32(w_out), 2, 512)
    wup_r = blk(f32(w_up), 16, 512)
    wdn_r = blk(f32(w_down), 8, 128)
    gates_r = f32(np.stack([f32(lru_wr), f32(lru_wi)], axis=1).transpose(0, 3, 1, 2, 4).reshape(DEPTH, 128, 2048))

    def pc(v):
        v = f32(v)
        return v.reshape(v.shape[0], -1, 128).transpose(0, 2, 1)

    def pck(v):
        v = f32(v)
        L, K = v.shape[0], v.shape[1]
        return v.reshape(L, K, -1, 128).transpose(0, 3, 1, 2).reshape(L, 128, -1)

    smalls = f32(np.concatenate([pc(norm_mix_pre), pc(norm_mix_post), pc(norm_ffn_pre), pc(norm_ffn_post),
                                 pck(conv_lru_w), pc(conv_lru_b), pc(lru_br), pc(lru_bi), pc(lru_lambda),
                                 pck(ffn_conv_w), pc(ffn_conv_b)], axis=2))
    assert smalls.shape == (DEPTH, 128, SM_N), smalls.shape
    sk = f32(attn_sink)
    sinks = np.zeros((DEPTH, 128, 10), np.float32)
    sinks[:, :, 0:8] = sk[:, None, :]
    for kv in range(2):
        for g in range(4):
            sinks[:, g * 4:(g + 1) * 4, 8 + kv] = sk[:, kv * 4 + g][:, None]
    rb = f32(rel_bias)
    bidx = np.array([_t5_bucket(d) for d in range(128)])
    qq = np.arange(128)[:, None]
    jj = np.arange(256)[None, :]
    dd = qq + 128 - jj
    valid = (dd >= 0) & (dd < 128)
    gat = rb[bidx[np.clip(dd, 0, 127)]]
    biasp = np.where(valid[:, :, None], gat, np.float32(NEG)).transpose(0, 2, 1)
    biasp = f32(biasp).reshape(128, 8 * 256)
    tt = np.arange(4)[:, None]
    js = np.arange(132)[None, :]
    ds = tt + 128 - js
    vs = (ds >= 0) & (ds < 128)
    gs = rb[bidx[np.clip(ds, 0, 127)]]
    bs = np.where(vs[:, :, None], gs, np.float32(NEG))
    biass = np.zeros((16, 2, 132), np.float32)
    for kv in range(2):
        for g in range(4):
            biass[g * 4:(g + 1) * 4, kv, :] = bs[:, :, kv * 4 + g]
    biass = f32(biass).reshape(16, 2 * 132)
    ident = np.eye(128, dtype=np.float32)

    st_h, st_c, ckk, cvv, st_f = f32(state_lru_h), f32(state_lru_conv), f32(cache_win_k), f32(cache_win_v), f32(state_ffn_conv)
    in_maps = []
    for i in range(n):
        sl = slice(i * NB, (i + 1) * NB)
        in_maps.append({
            "xp": x_prompt[i], "xs": f32(x_sample[sl].reshape(NS, D)),
            "st_h": f32(st_h[:, sl]), "st_c": f32(st_c[:, sl].reshape(DEPTH, NB * 3, D)),
            "ck": f32(ckk[:, sl].reshape(DEPTH, NB, 128, 256)), "cv": f32(cvv[:, sl].reshape(DEPTH, NB, 128, 256)),
            "st_f": f32(st_f[:, sl].reshape(DEPTH, NB * 2, 8192)),
            "win_r": win_r, "gates_r": gates_r, "wlo_r": wlo_r, "wao_r": wao_r, "wout_r": wout_r, "wup_r": wup_r, "wdn_r": wdn_r,
            "smalls": smalls, "sinks": sinks, "biasp": biasp, "biass": biass, "ident": ident,
        })
    if "nc" not in _NC_CACHE:
        _NC_CACHE["nc"] = build()
    nc = _NC_CACHE["nc"]
    res = run_bass_kernel_spmd(nc, in_maps, core_ids=list(range(n)))
    R = res.results
    y_prompt = np.stack([R[i]["yp"] for i in range(n)], axis=0)
    y_sample = np.concatenate([R[i]["ys"].reshape(NB, 4, D) for i in range(n)], axis=0)
    p_lru_h = np.stack([R[i]["o_plh"] for i in range(n)], axis=1)
    p_lru_conv = np.stack([R[i]["o_plc"] for i in range(n)], axis=1)
    p_win_k = np.stack([R[i]["o_pk"].reshape(DEPTH, 128, 2, 128) for i in range(n)], axis=1)
    p_win_v = np.stack([R[i]["o_pv"].reshape(DEPTH, 128, 2, 128) for i in range(n)], axis=1)
    p_ffn = np.stack([R[i]["o_pf"] for i in range(n)], axis=1)
    s_lru_h = np.concatenate([R[i]["o_slh"] for i in range(n)], axis=1)
    s_lru_conv = np.concatenate([R[i]["o_slc"].reshape(DEPTH, NB, 3, D) for i in range(n)], axis=1)
    s_win_k = np.concatenate([R[i]["o_sk"].reshape(DEPTH, NB, 128, 2, 128) for i in range(n)], axis=1)
    s_win_v = np.concatenate([R[i]["o_sv"].reshape(DEPTH, NB, 128, 2, 128) for i in range(n)], axis=1)
    s_ffn = np.concatenate([R[i]["o_sf"].reshape(DEPTH, NB, 2, 8192) for i in range(n)], axis=1)
    outs = (y_prompt, y_sample, p_lru_h, p_lru_conv, p_win_k, p_win_v, p_ffn, s_lru_h, s_lru_conv, s_win_k, s_win_v, s_ffn)
    return tuple(np.ascontiguousarray(o, dtype=np.float32) for o in outs)
```

```python
import numpy as np
import concourse.bass as bass
import concourse.mybir as mybir
from concourse.bass_utils import run_bass_kernel_spmd

F32 = mybir.dt.float32
BF16 = mybir.dt.bfloat16
AF = mybir.ActivationFunctionType
ALU = mybir.AluOpType
AX = mybir.AxisListType

D = 1024
NCH = 8
SEQ = 2048
TP = 512
NPASS = SEQ // TP
NB = 16
NS = 64
DEPTH = 2
NEG = -30000.0
EPS = 1e-6
QSCALE = 128 ** -0.5
NSLOT = 5
GR = 256
SB_BASE = 16512
SB_TOP = 229344

SM_NMP, SM_NMPOST, SM_NFP, SM_NFPOST = 0, 8, 16, 24
SM_CLW = 32
SM_CLB = 64
SM_BR, SM_BI, SM_LAM = 72, 80, 88
SM_FCW = 96
SM_FCB = 288
SM_N = 352


class _Op:
    __slots__ = ("id", "eng", "fn", "deps", "signals", "dsem", "sidx", "ninst")


class Sched:
    ENGS = ("pe", "act", "dve", "pool", "sp")

    def __init__(self, nc):
        self.nc = nc
        self.ops = []
        self.last_writer = {}
        self.readers = {}
        self.dma_sems = {}
        self.alias = {}

    def reg(self, key, off, nbytes):
        g0 = off // GR
        g1 = (off + nbytes + GR - 1) // GR
        self.alias[key] = [("g", g) for g in range(g0, g1)]

    def _expand(self, keys):
        out = []
        for k in keys:
            a = self.alias.get(k)
            if a is None:
                assert isinstance(k, tuple) and k[0] in ("ps", "w", "dram"), ("unregistered key", k)
                out.append(k)
            else:
                out.extend(a)
        return out

    def add(self, eng, fn, reads=(), writes=(), dsem=None, ndma=1):
        reads = self._expand(reads)
        writes = self._expand(writes)
        deps = set()
        for r in reads:
            lw = self.last_writer.get(r)
            if lw is not None:
                deps.add(lw)
        for w in writes:
            lw = self.last_writer.get(w)
            if lw is not None:
                deps.add(lw)
            for rd in self.readers.get(w, ()):
                deps.add(rd)
        op = _Op()
        op.id = len(self.ops)
        op.eng = eng
        op.fn = fn
        op.deps = deps
        op.dsem = dsem
        op.signals = dsem is not None
        op.sidx = None
        op.ninst = ndma
        deps.discard(op.id)
        self.ops.append(op)
        for r in reads:
            self.readers.setdefault(r, []).append(op.id)
        for w in writes:
            self.last_writer[w] = op.id
            self.readers[w] = []
        return op.id

    def emit(self):
        nc = self.nc
        ops = self.ops
        for op in ops:
            for d in op.deps:
                p = ops[d]
                if p.eng == "pe" and op.eng == "pe" and p.dsem is None:
                    continue
                p.signals = True
        esem = {e: nc.alloc_semaphore("s_" + e) for e in ("pe", "act", "dve", "pool")}
        ecount = {e: 0 for e in esem}
        dcount = {}
        for op in ops:
            if op.dsem is not None:
                if op.dsem not in self.dma_sems:
                    self.dma_sems[op.dsem] = nc.alloc_semaphore("d_%d" % len(self.dma_sems))
                    dcount[op.dsem] = 0
                dcount[op.dsem] += 16 * op.ninst
                op.sidx = (self.dma_sems[op.dsem], dcount[op.dsem])
            elif op.signals:
                ecount[op.eng] += 1
                op.sidx = (esem[op.eng], ecount[op.eng])
        final_waits = {k: (self.dma_sems[k], v) for k, v in dcount.items()}
        by_eng = {e: [op for op in ops if op.eng == e] for e in self.ENGS}

        def run(engine, ename):
            waited = {}
            for op in by_eng[ename]:
                need = {}
                for d in op.deps:
                    p = ops[d]
                    if p.eng == "pe" and ename == "pe" and p.dsem is None:
                        continue
                    sem, val = p.sidx
                    k = id(sem)
                    if waited.get(k, 0) >= val:
                        continue
                    if k not in need or need[k][1] < val:
                        need[k] = (sem, val)
                for k, (sem, val) in need.items():
                    engine.wait_ge(sem, val)
                    waited[k] = val
                r = op.fn(engine)
                insts = r if isinstance(r, (list, tuple)) else [r]
                if op.dsem is not None:
                    assert len(insts) == op.ninst, (len(insts), op.ninst)
                    for i in insts:
                        i.then_inc(op.sidx[0], 16)
                elif op.signals:
                    insts[-1].then_inc(op.sidx[0], 1)
            if ename == "sp":
                for k, (sem, val) in final_waits.items():
                    engine.wait_ge(sem, val)

        with nc.Block() as block:
            @block.sync
            def _(e):
                run(e, "sp")

            @block.gpsimd
            def _(e):
                run(e, "pool")

            @block.tensor
            def _(e):
                run(e, "pe")

            @block.scalar
            def _(e):
                run(e, "act")

            @block.vector
            def _(e):
                run(e, "dve")


class Banks:
    def __init__(self, tensors):
        self.t = tensors
        self.free_list = list(range(len(tensors)))

    def alloc(self):
        assert self.free_list, "out of PSUM banks"
        return self.free_list.pop(0)

    def free(self, b):
        assert b not in self.free_list
        self.free_list.append(b)


class WStream:
    def __init__(self, S, nc, seq, slots, first_reads=()):
        self.S = S
        self.seq = seq
        self.slots = slots
        self.first_reads = list(first_reads)
        self.next_dma = 0
        self.released = 0
        self.pos = 0
        self._pump()

    def _pump(self):
        while self.next_dma < len(self.seq) and self.next_dma - NSLOT < self.released:
            n = self.next_dma
            key, src, ncol = self.seq[n]
            sl = n % NSLOT
            dst = self.slots[sl]
            nd = ncol // 2048

            def fn(e, src=src, dst=dst, nd=nd):
                return [e.dma_start(out=dst[:, i * 2048:(i + 1) * 2048], in_=src[:, i * 2048:(i + 1) * 2048])
                        for i in range(nd)]
            self.S.add("pool", fn, reads=(self.first_reads if n == 0 else ()), writes=[("w", sl)], dsem=("w", sl), ndma=nd)
            self.next_dma += 1

    def get(self, key):
        i = self.pos
        assert self.seq[i][0] == key, (self.seq[i][0], key)
        self.pos += 1
        self._pump()
        assert i < self.next_dma
        sl = i % NSLOT
        return self.slots[sl], ("w", sl)

    def release(self):
        self.released += 1
        self._pump()


def layer_block_keys(l):
    ks = [("win", l, 0), ("win", l, 1), ("win", l, 2), ("win", l, 3), ("win", l, 4)]
    ks += [("wlo", l, 0), ("win", l, 5), ("wlo", l, 1), ("win", l, 6)]
    ks += [("wao", l, 0), ("win", l, 7), ("wao", l, 1), ("win", l, 8)]
    ks += [("wout", l, 0), ("wout", l, 1)]
    for pb in range(8):
        ks += [("wup", l, pb), ("wup", l, pb + 8)]
    ks += [("wdn", l, oc) for oc in range(8)]
    return ks


def build():
    nc = bass.Bass("TRN2", target_bir_lowering=False)

    def din(name, shape):
        return nc.dram_tensor(name, list(shape), F32, kind="ExternalInput").ap()

    def dout(name, shape):
        return nc.dram_tensor(name, list(shape), F32, kind="ExternalOutput").ap()

    xp = din("xp", [SEQ, D])
    xs = din("xs", [NS, D])
    st_h = din("st_h", [DEPTH, NB, D])
    st_c = din("st_c", [DEPTH, NB * 3, D])
    ck = din("ck", [DEPTH, NB, 128, 256])
    cv = din("cv", [DEPTH, NB, 128, 256])
    st_f = din("st_f", [DEPTH, NB * 2, 8192])
    win_r = din("win_r", [DEPTH * 9, 128, 4096])
    gates_r = din("gates_r", [DEPTH, 128, 2048])
    wlo_r = din("wlo_r", [DEPTH * 2, 128, 4096])
    wao_r = din("wao_r", [DEPTH * 2, 128, 4096])
    wout_r = din("wout_r", [DEPTH * 2, 128, 4096])
    wup_r = din("wup_r", [DEPTH * 16, 128, 4096])
    wdn_r = din("wdn_r", [DEPTH * 8, 128, 4096])
    smalls_d = din("smalls", [DEPTH, 128, SM_N])
    sinks_d = din("sinks", [DEPTH, 128, 10])
    biasp_d = din("biasp", [128, 8 * 256])
    biass_d = din("biass", [16, 2 * 132])
    ident_d = din("ident", [128, 128])

    yp = dout("yp", [SEQ, D])
    ys = dout("ys", [NS, D])
    o_plh = dout("o_plh", [DEPTH, D])
    o_plc = dout("o_plc", [DEPTH, 3, D])
    o_pk = dout("o_pk", [DEPTH, 128, 256])
    o_pv = dout("o_pv", [DEPTH, 128, 256])
    o_pf = dout("o_pf", [DEPTH, 2, 8192])
    o_slh = dout("o_slh", [DEPTH, NB, D])
    o_slc = dout("o_slc", [DEPTH, NB * 3, D])
    o_sk = dout("o_sk", [DEPTH, NB, 128, 256])
    o_sv = dout("o_sv", [DEPTH, NB, 128, 256])
    o_sf = dout("o_sf", [DEPTH, NB * 2, 8192])

    S = Sched(nc)
    TMAX = TP + NS
    XRW = 3 + TP + NB * 7
    UW = 2 + TP + NB * 6

    def esz(dt):
        return 2 if dt == BF16 else 4

    def mk(name, shape, dt, off, key=None, chunks=None):
        assert off % 32 == 0
        nbytes = int(np.prod(shape[1:])) * esz(dt)
        assert SB_BASE + off + nbytes <= SB_TOP, (name, off, nbytes)
        t = nc.alloc_sbuf_tensor_at(name, list(shape), dt, offset=SB_BASE + off)
        key = key or name
        S.reg(key, off, nbytes)
        if chunks:
            cb = nbytes // chunks
            for c in range(chunks):
                S.reg((key, c), off + c * cb, cb)
        return t

    cur = [0]

    def P(name, shape, dt=F32, chunks=None, key=None):
        nbytes = int(np.prod(shape[1:])) * esz(dt)
        off = cur[0]
        cur[0] += (nbytes + GR - 1) // GR * GR
        return mk(name, shape, dt, off, key=key, chunks=chunks)

    identf = P("identf", [128, 128])
    identb = P("identb", [128, 128], BF16)
    ones_b = P("ones_b", [128, 128], BF16)
    epsc = P("epsc", [128, 1])
    smalls = [P("smalls%d" % l, [128, SM_N], key=("smalls", l)) for l in range(DEPTH)]
    lamc = [P("lamc%d" % l, [128, 16], key=("lamc", l)) for l in range(DEPTH)]
    sinks = [P("sinks%d" % l, [128, 10], key=("sinks", l)) for l in range(DEPTH)]
    biasp = P("biasp", [128, 8, 256])
    biass = P("biass", [16, 2, 132])
    convhist = [P("convhist%d" % l, [128, NCH, 3], key=("convhist", l)) for l in range(DEPTH)]
    hstate = [P("hstate%d" % l, [128, NCH], key=("hstate", l)) for l in range(DEPTH)]
    ffnhist = [P("ffnhist%d" % l, [128, 64, 2], key=("ffnhist", l)) for l in range(DEPTH)]
    khist = [P("khist%d" % l, [128, 2, 128], BF16, key=("khist", l)) for l in range(DEPTH)]
    vhist = [P("vhist%d" % l, [128, 256], BF16, key=("vhist", l)) for l in range(DEPTH)]
    gatesw = [P("gatesw%d" % l, [128, 2048], BF16, key=("gatesw", l)) for l in range(DEPTH)]
    x = P("x", [128, NCH, TMAX], chunks=NCH)
    h = P("h", [128, NCH, TMAX], BF16, chunks=NCH)
    rstd = P("rstd", [128, TMAX])
    sdv = P("sdv", [128, TMAX])
    stout = P("stout", [128, 1024])
    SFs = [P("SFs%d" % i, [128, 8, NB * 2], key=("SFs", i)) for i in range(2)]
    pst = P("pst", [128, 128])
    h0s = P("h0s", [128, NCH, NB])
    hs_last = P("hs_last", [128, NCH, NB])
    cs_stage = P("cs_stage", [128, NCH, NB * 3])
    tmp16 = P("tmp16", [128, NB])
    qTs = P("qTs", [128, 2, NB, 16], BF16)
    wslots = [P("wslot%d" % i, [128, 4096], BF16) for i in range(NSLOT)]
    SCR = cur[0]
    RA = SCR
    RB = SCR + 36864
    assert SB_BASE + RB + 61696 <= SB_TOP, (SCR, SB_TOP - SB_BASE)

    lro = mk("lro", [128, NCH, TMAX], BF16, RA + 0, chunks=NCH)
    qT = mk("qT", [128, 8, TMAX], BF16, RA + 9216, chunks=8)
    merged = mk("merged", [128, NCH, TMAX], BF16, RA + 9216, chunks=NCH)
    attn = mk("attn", [128, 8, TMAX], BF16, RA + 18432, chunks=8)
    kT = mk("kT", [128, 2, 128 + TP], BF16, RA + 27648)
    Vt = mk("Vt", [128, 5, 256], BF16, RA + 30208)
    kvnew = mk("kvnew", [64, 512], F32, RA + 32768)
    kvlast = mk("kvlast", [128, 512], F32, RA + 34816)
    A = mk("A", [128, 32, TMAX], BF16, RA + 0, chunks=32)
    sq = mk("sq", [128, NCH, TMAX], BF16, RB + 0, chunks=NCH)
    tok = mk("tok", [128, 4, D], F32, RB + 36864)
    tokx = mk("tokx", [128, 4, D], F32, RB + 20480)
    toks = mk("toks", [64, D], F32, RB + 16384)
    XR = mk("XR", [128, NCH, XRW], F32, RB + 0, chunks=NCH)
    LS = 20736
    xcg = [mk("xc%d" % s_, [128, 2, TMAX], F32, RB + 20224 + s_ * LS, key=("xc", s_), chunks=2) for s_ in range(2)]
    xcb = [mk("xcb%d" % s_, [128, 2, TMAX], BF16, RB + 20224 + s_ * LS + 4608, key=("xcb", s_), chunks=2) for s_ in range(2)]
    rr = [mk("rr%d" % s_, [128, 2, TMAX], F32, RB + 20224 + s_ * LS + 6912, key=("rr", s_), chunks=2) for s_ in range(2)]
    ii = [mk("ii%d" % s_, [128, 2, TMAX], F32, RB + 20224 + s_ * LS + 11520, key=("ii", s_), chunks=2) for s_ in range(2)]
    aa = [mk("aa%d" % s_, [128, 2, TMAX], F32, RB + 20224 + s_ * LS + 16128, key=("aa", s_), chunks=2) for s_ in range(2)]
    kTs = mk("kTs", [128, 2, NB, 132], BF16, RB + 0)
    Vs = mk("Vs", [128, NB, 256], BF16, RB + 8448)
    Vn4 = mk("Vn4", [4, NB, 256], BF16, RB + 16640)
    cstage = mk("cstage", [128, NB, 256], F32, RB + 24832)
    o3 = RB + 41216
    NSET = 8
    SBW = 64 + 260
    Sbuf, Pn, PT, stat = [], [], [], []
    for i in range(NSET):
        o = o3 + i * 2816
        Sbuf.append(mk("Sbuf%d" % i, [128, SBW], F32, o, key=("Sb", i)))
        S.reg(("Sb", i, "h"), o, 256)
        S.reg(("Sb", i, "b"), o + 256, 4 * 260)
        Pn.append(mk("Pn%d" % i, [128, 256], BF16, o + 1536, key=("Pn", i)))
        PT.append(mk("PT%d" % i, [128, 2, 128], BF16, o + 2048, key=("PT", i)))
        stat.append(mk("stat%d" % i, [128, 4], F32, o + 2560, key=("st", i)))
    sgA = [mk("sgA%d" % i, [128, TP], F32, RB + 9216 + i * 2048, key=("sgA", i)) for i in range(2)]
    sgB = [mk("sgB%d" % i, [128, TP], F32, RB + 13312 + i * 2048, key=("sgB", i)) for i in range(2)]
    t1b = [mk("t1b%d" % i, [128, TP], F32, RB + 17408 + i * 2048, key=("t1b", i)) for i in range(2)]
    mbuf = mk("mbuf", [128, NCH, TMAX], F32, RB + 40448, chunks=NCH)
    def mkU(name, i, off):
        t = mk("%s%d" % (name, i), [128, 64 + TP], F32, off, key=(name, i))
        S.reg((name, i, "h"), off, 256)
        S.reg((name, i, "b"), off + 256, 4 * TP)
        return t
    Uv = [mkU("Uv", 0, RB + 9216), mkU("Uv", 1, RB + 9216 + 2560), mkU("Uv", 2, RB + 41472)]
    Ug = [mkU("Ug", 0, RB + 14336), mkU("Ug", 1, RB + 14336 + 2560), mkU("Ug", 2, RB + 41472 + 2304)]
    cvv = [mk("cvv%d" % i, [128, TMAX], F32, RB + 19456 + i * 2304, key=("cvv", i)) for i in range(2)]
    cvg = [mk("cvg%d" % i, [128, TMAX], F32, RB + 24064 + i * 2304, key=("cvg", i)) for i in range(2)]
    ggb = [mk("ggb%d" % i, [128, TMAX], F32, RB + 28672 + i * 2304, key=("gg", i)) for i in range(2)]
    cvv.append(mk("cvv2", [128, TMAX], F32, RB + 0, key=("cvv", 2)))
    cvg.append(mk("cvg2", [128, TMAX], F32, RB + 2304, key=("cvg", 2)))
    ggb.append(mk("ggb2", [128, TMAX], F32, RB + 4608, key=("gg", 2)))
    FH = mk("FH", [128, 64, NB * 2], F32, RB + 33280)
    stf = mk("stf", [32, 2048], F32, RB + 0)
    Us = [mk("Us%d" % i, [128, 4, NB * 6], F32, RB + 58880 + i * 1536, key=("Us", i)) for i in range(2)]
    cvs = [mk("cvs%d" % i, [128, 4, NS], F32, RB + 6912 + i * 1024, key=("cvs", i)) for i in range(2)]
    print("SBUF map: persistent=%d scratch_avail=%d" % (SCR, SB_TOP - SB_BASE - SCR))

    ps_t = [nc.alloc_psum_tensor("ps%d" % i, [128, 512], F32) for i in range(8)]
    banks = Banks(ps_t)

    def PSR(b):
        return ("ps", b)

    def keys(name, n):
        return [(name, c) for c in range(n)]

    wsrc = {"win": (win_r, 9), "wlo": (wlo_r, 2), "wao": (wao_r, 2), "wout": (wout_r, 2),
            "wup": (wup_r, 16), "wdn": (wdn_r, 8)}
    seq = []
    for p in range(NPASS):
        for l in range(DEPTH):
            for key in layer_block_keys(l):
                t, n = wsrc[key[0]]
                seq.append(((p,) + key, t[l * n + key[2]], 4096))
    S.add("sp", lambda e: e.dma_start(out=tokx[:], in_=xp[0:TP, :].rearrange("(j r) d -> r j d", r=128)), writes=["tokx"], dsem="xin")
    W = WStream(S, nc, seq, wslots, first_reads=["tokx"])
    for l in range(DEPTH):
        S.add("pool", lambda e, l=l: e.dma_start(out=gatesw[l][:], in_=gates_r[l]), writes=[("gatesw", l)], dsem=("gw", l))

    S.add("sp", lambda e: e.dma_start(out=identf[:], in_=ident_d), writes=["identf"], dsem="c0")
    S.add("sp", lambda e: e.dma_start(out=biasp[:].rearrange("p h k -> p (h k)"), in_=biasp_d), writes=["biasp"], dsem="c1")
    S.add("sp", lambda e: e.dma_start(out=biass[:].rearrange("p h k -> p (h k)"), in_=biass_d), writes=["biass"], dsem="c2")
    for l in range(DEPTH):
        S.add("sp", lambda e, l=l: e.dma_start(out=smalls[l][:], in_=smalls_d[l]), writes=[("smalls", l)], dsem=("c3", l))
        S.add("sp", lambda e, l=l: e.dma_start(out=sinks[l][:], in_=sinks_d[l]), writes=[("sinks", l)], dsem=("c4", l))
    S.add("dve", lambda e: e.tensor_scalar(biasp[:], biasp[:], -1.0, None, ALU.mult), writes=["biasp"])
    S.add("dve", lambda e: e.tensor_scalar(biass[:], biass[:], -1.0, None, ALU.mult), writes=["biass"])
    for l in range(DEPTH):
        S.add("dve", lambda e, l=l: e.tensor_scalar(sinks[l][:], sinks[l][:], -1.0, None, ALU.mult), writes=[("sinks", l)])
    S.add("dve", lambda e: e.tensor_copy(out=identb[:], in_=identf[:]), reads=["identf"], writes=["identb"])
    S.add("dve", lambda e: e.memset(ones_b[:], 1.0), writes=["ones_b"])
    S.add("dve", lambda e: e.memset(epsc[:], EPS), writes=["epsc"])
    for l in range(DEPTH):
        S.add("act", lambda e, l=l: e.activation(out=lamc[l][:, 0:8], in_=smalls[l][:, SM_LAM:SM_LAM + 8], func=AF.Exp, scale=-1.0),
              reads=[("smalls", l)], writes=[("lamc", l)])
        S.add("act", lambda e, l=l: e.activation(out=lamc[l][:, 0:8], in_=lamc[l][:, 0:8], func=AF.Ln, bias=1.0),
              writes=[("lamc", l)])
        S.add("dve", lambda e, l=l: e.tensor_scalar(lamc[l][:, 8:16], lamc[l][:, 0:8], -16.0, None, ALU.mult), writes=[("lamc", l)])
        S.add("dve", lambda e, l=l: e.tensor_scalar(lamc[l][:, 0:8], lamc[l][:, 0:8], -8.0, None, ALU.mult), writes=[("lamc", l)])

    rot = {"S": 0, "sg": 0, "u": 0}
    XK = keys("x", NCH)
    HR = keys("h", NCH)

    def transpose_out(src_fn, ncols, dst_aps, rd, tag):
        i = 0
        while i < len(dst_aps):
            grp = list(range(i, min(i + 4, len(dst_aps))))
            b = banks.alloc()

            def tr(e, grp=grp, b=b):
                r = None
                for gi, k in enumerate(grp):
                    n = ncols[k]
                    r = e.transpose(ps_t[b][0:n, gi * 128:(gi + 1) * 128], src_fn(k), identf[:])
                return r
            S.add("pe", tr, reads=list(rd) + ["identf"], writes=[PSR(b)])
            nmax = max(ncols[k] for k in grp)
            S.add("act", lambda e, b=b, grp=grp, nmax=nmax: e.copy(out=stout[0:nmax, 0:128 * len(grp)], in_=ps_t[b][0:nmax, 0:128 * len(grp)]),
                  writes=[PSR(b), "stout"])
            banks.free(b)
            for gi, k in enumerate(grp):
                n = ncols[k]
                S.add("sp", lambda e, gi=gi, k=k, n=n: e.dma_start(out=dst_aps[k], in_=stout[0:n, gi * 128:(gi + 1) * 128]),
                      reads=["stout"], dsem=("so", tag))
            i += 4

    def norm_stats(T, subs):
        for (c0, n) in subs:
            b = banks.alloc()
            for c in range(NCH):
                S.add("pe", lambda e, b=b, c0=c0, n=n, c=c: e.matmul(ps_t[b][:, 0:n], ones_b[:], sq[:, c, c0:c0 + n], start=(c == 0), stop=(c == NCH - 1)),
                      reads=[("sq", c), "ones_b"], writes=[PSR(b)])
            S.add("act", lambda e, b=b, c0=c0, n=n: e.activation(out=sdv[:, c0:c0 + n], in_=ps_t[b][:, 0:n], func=AF.Ln,
                                                                 scale=1.0 / D, bias=epsc[:, 0:1]),
                  reads=["epsc"], writes=[PSR(b), "sdv"])
            S.add("act", lambda e, c0=c0, n=n: e.activation(out=rstd[:, c0:c0 + n], in_=sdv[:, c0:c0 + n], func=AF.Exp, scale=-0.5),
                  reads=["sdv"], writes=["rstd"])
            banks.free(b)

    def split_eng(c):
        return "dve"

    def rmsnorm_to_h(l, gcol, T, subs, squares_done=False):
        if not squares_done:
            for c in range(NCH):
                S.add("act", lambda e, c=c: e.activation(out=sq[:, c, 0:T], in_=x[:, c, 0:T], func=AF.Square), reads=[("x", c)], writes=[("sq", c)])
        norm_stats(T, subs)
        for c in range(NCH):
            S.add("dve", lambda e, c=c: e.scalar_tensor_tensor(out=h[:, c, 0:T], in0=x[:, c, 0:T],
                                                                      scalar=smalls[l][:, gcol + c:gcol + c + 1], in1=rstd[:, 0:T],
                                                                      op0=ALU.mult, op1=ALU.mult),
                  reads=[("x", c), "rstd", ("smalls", l)], writes=[("h", c)])

    def postnorm_residual(l, T, subs, next_squares):
        norm_stats(T, subs)
        for c in range(NCH):
            eng = split_eng(c)
            S.add(eng, lambda e, c=c: e.tensor_tensor(out=mbuf[:, c, 0:T], in0=mbuf[:, c, 0:T], in1=rstd[:, 0:T], op=ALU.mult),
                  reads=["rstd"], writes=[("mbuf", c)])
            S.add(eng, lambda e, c=c: e.tensor_tensor(out=x[:, c, 0:T], in0=x[:, c, 0:T], in1=mbuf[:, c, 0:T], op=ALU.add),
                  reads=[("mbuf", c)], writes=[("x", c)])
            if next_squares:
                S.add("act", lambda e, c=c: e.activation(out=sq[:, c, 0:T], in_=x[:, c, 0:T], func=AF.Square), reads=[("x", c)], writes=[("sq", c)])

    def proj(wt, wres, col, rhs, rhs_res, nk, subs, evac):
        for (c0, n) in subs:
            b = banks.alloc()
            for k in range(nk):
                S.add("pe", lambda e, b=b, c0=c0, n=n, k=k: e.matmul(ps_t[b][:, 0:n], wt[:, k, col:col + 128], rhs[:, k, c0:c0 + n],
                                                                     start=(k == 0), stop=(k == nk - 1)),
                      reads=[wres, rhs_res[k]], writes=[PSR(b)])
            evac(b, c0, n)
            banks.free(b)

    def attn_waves(units):
        waves = [units[i:i + 4] for i in range(0, len(units), 4)]

        def sets(w, i):
            return (w % 2) * 4 + i

        def front(w):
            wv = waves[w]
            bl = []
            for i, u in enumerate(wv):
                b = banks.alloc()
                bl.append(b)
                NQ, NK = u["NQ"], u["NK"]
                S.add("pe", lambda e, u=u, b=b, NQ=NQ, NK=NK: e.matmul(ps_t[b][0:NQ, 0:NK], u["qap"], u["kap"], start=True, stop=True),
                      reads=u["rd"], writes=[PSR(b)])
            for i, u in enumerate(wv):
                si = sets(w, i)
                NQ = u["NQ"]
                S.add("pool", lambda e, u=u, si=si, NQ=NQ: e.tensor_copy(out=Sbuf[si][0:NQ, 63:64], in_=u["sink_ap"]), reads=u["rd"], writes=[("Sb", si, "h")])
            for i, u in enumerate(wv):
                si = sets(w, i)
                b = bl[i]
                NQ, NK = u["NQ"], u["NK"]
                S.add("dve", lambda e, u=u, si=si, b=b, NQ=NQ, NK=NK: e.tensor_tensor(out=Sbuf[si][0:NQ, 64:64 + NK], in0=u["bias_ap"], in1=ps_t[b][0:NQ, 0:NK], op=ALU.subtract),
                      reads=[u["bias_key"]], writes=[PSR(b), ("Sb", si, "b")])
                banks.free(b)
            for i, u in enumerate(wv):
                si = sets(w, i)
                NQ, NK = u["NQ"], u["NK"]
                S.add("dve", lambda e, si=si, NQ=NQ, NK=NK: e.tensor_reduce(out=stat[si][0:NQ, 0:1], in_=Sbuf[si][0:NQ, 63:64 + NK], axis=AX.X, op=ALU.min),
                      reads=[("Sb", si)], writes=[("st", si)])

        def mid(w):
            wv = waves[w]
            for i, u in enumerate(wv):
                si = sets(w, i)
                NQ, NK = u["NQ"], u["NK"]
                S.add("act", lambda e, si=si, NQ=NQ, NK=NK: e.activation(out=Sbuf[si][0:NQ, 63:64 + NK], in_=Sbuf[si][0:NQ, 63:64 + NK], func=AF.Exp,
                                                                         bias=stat[si][0:NQ, 0:1], scale=-1.0, accum_out=stat[si][0:NQ, 2:3]),
                      writes=[("Sb", si), ("st", si)])
            for i, u in enumerate(wv):
                si = sets(w, i)
                NQ = u["NQ"]
                S.add("dve", lambda e, si=si, NQ=NQ: e.reciprocal(out=stat[si][0:NQ, 3:4], in_=stat[si][0:NQ, 2:3]), writes=[("st", si)])
            for i, u in enumerate(wv):
                si = sets(w, i)
                NQ, NK = u["NQ"], u["NK"]
                S.add("act", lambda e, si=si, NQ=NQ, NK=NK: e.activation(out=Pn[si][0:NQ, 0:NK], in_=Sbuf[si][0:NQ, 64:64 + NK], func=AF.Copy, scale=stat[si][0:NQ, 3:4]),
                      reads=[("Sb", si), ("st", si)], writes=[("Pn", si)])

        def back(w):
            wv = waves[w]
            tbl = []
            for i, u in enumerate(wv):
                si = sets(w, i)
                tb = banks.alloc()
                tbl.append(tb)
                NQ = u["NQ"]

                def tr(e, u=u, si=si, tb=tb, NQ=NQ):
                    tps = ps_t[tb].bitcast(BF16)
                    r = None
                    for vi, (vap, k0, nk) in enumerate(u["vblocks"]):
                        r = e.transpose(tps[0:nk, vi * 128:vi * 128 + NQ], Pn[si][0:NQ, k0:k0 + nk], identb[0:NQ, 0:NQ])
                    return r
                S.add("pe", tr, reads=[("Pn", si), "identb"], writes=[PSR(tb)])
            for i, u in enumerate(wv):
                si = sets(w, i)
                tb = tbl[i]
                NQ = u["NQ"]
                vb = u["vblocks"]
                if NQ == 128 and all(nk == 128 for (_, _, nk) in vb):
                    nv = len(vb)
                    S.add("act", lambda e, si=si, tb=tb, nv=nv: e.copy(out=PT[si][:, 0:nv, :],
                                                                      in_=ps_t[tb].bitcast(BF16)[:, 0:nv * 128].rearrange("p (v q) -> p v q", q=128)),
                          writes=[PSR(tb), ("PT", si)])
                else:
                    nv = len(vb)
                    S.add("act", lambda e, si=si, tb=tb, nv=nv, NQ=NQ: e.copy(
                        out=PT[si][:, 0:nv, 0:NQ], in_=ps_t[tb].bitcast(BF16)[:, 0:nv * 128].rearrange("p (v q) -> p v q", q=128)[:, :, 0:NQ]),
                        writes=[PSR(tb), ("PT", si)])
                banks.free(tb)
            for i, u in enumerate(wv):
                si = sets(w, i)
                g = u["grp"]
                if g.get("ob") is None:
                    g["ob"] = banks.alloc()
                ob = g["ob"]
                NQ = u["NQ"]
                c0 = u["out_c0"]

                def pv(e, u=u, si=si, ob=ob, NQ=NQ, c0=c0):
                    r = None
                    vb = u["vblocks"]
                    for vi, (vap, k0, nk) in enumerate(vb):
                        r = e.matmul(ps_t[ob][:, c0:c0 + NQ], vap, PT[si][0:nk, vi, 0:NQ], start=(vi == 0), stop=(vi == len(vb) - 1))
                    return r
                S.add("pe", pv, reads=[("PT", si)] + list(u["rd"]), writes=[PSR(ob)])
                if u["last"]:
                    u["evac"](ob)
                    banks.free(ob)
                    g["ob"] = None

        nw = len(waves)
        front(0)
        for w in range(nw):
            mid(w)
            if w + 1 < nw:
                front(w + 1)
            back(w)

    def do_pass(p):
        has_s = (p == 0)
        T = TP + (NS if has_s else 0)
        subs = [(0, TP)] + ([(TP, NS)] if has_s else [])
        last_pass = (p == NPASS - 1)
        XRs = XR[:, :, 3 + TP:3 + TP + NB * 7].rearrange("p c (b k) -> p c b k", k=7)

        if has_s:
            S.add("sp", lambda e: e.dma_start(out=toks[:], in_=xs), writes=["toks"], dsem="xin2")
        for c in range(NCH):
            b = banks.alloc()

            def tr(e, b=b, c=c):
                r = None
                for j in range(4):
                    r = e.transpose(ps_t[b][:, j * 128:(j + 1) * 128], tokx[:, j, c * 128:(c + 1) * 128], identf[:])
                return r
            S.add("pe", tr, reads=["tokx", "identf"], writes=[PSR(b)])
            S.add("dve", lambda e, b=b, c=c: e.tensor_copy(out=x[:, c, 0:TP], in_=ps_t[b][:, 0:TP]), writes=[PSR(b), ("x", c)])
            banks.free(b)
            if has_s:
                b = banks.alloc()
                S.add("pe", lambda e, b=b, c=c: e.transpose(ps_t[b][:, 0:NS], toks[:, c * 128:(c + 1) * 128], identf[0:NS, 0:NS]),
                      reads=["toks", "identf"], writes=[PSR(b)])
                S.add("act", lambda e, b=b, c=c: e.copy(out=x[:, c, TP:TP + NS], in_=ps_t[b][:, 0:NS]), writes=[PSR(b), ("x", c)])
                banks.free(b)

        for l in range(DEPTH):
            do_layer(p, l, T, subs, has_s, last_pass, XRs)
        finish_pass(p, has_s)

    def do_layer(p, l, T, subs, has_s, last_pass, XRs):
        if True:
            sm = smalls[l]
            SMR = ("smalls", l)

            rmsnorm_to_h(l, SM_NMP, T, subs, squares_done=(l > 0))

            XRall = keys("XR", NCH)
            if has_s:
                S.add("sp", lambda e, l=l: e.dma_start(out=stout[0:48, 0:D], in_=st_c[l]), writes=["stout"], dsem="stin")
                for c in range(NCH):
                    b = banks.alloc()
                    S.add("pe", lambda e, b=b, c=c: e.transpose(ps_t[b][:, 0:48], stout[0:48, c * 128:(c + 1) * 128], identf[0:48, 0:48]),
                          reads=["stout", "identf"], writes=[PSR(b)])
                    S.add("act", lambda e, b=b, c=c: e.copy(out=XRs[:, c, :, 0:3], in_=ps_t[b][:, 0:48].rearrange("p (b k) -> p b k", k=3)),
                          writes=[PSR(b), ("XR", c)])
                    banks.free(b)
                S.add("sp", lambda e, l=l: e.dma_start(out=stout[0:16, 0:D], in_=st_h[l]), writes=["stout"], dsem="stin")
                for c in range(NCH):
                    b = banks.alloc()
                    S.add("pe", lambda e, b=b, c=c: e.transpose(ps_t[b][:, 0:16], stout[0:16, c * 128:(c + 1) * 128], identf[0:16, 0:16]),
                          reads=["stout", "identf"], writes=[PSR(b)])
                    S.add("act", lambda e, b=b, c=c: e.copy(out=h0s[:, c, :], in_=ps_t[b][:, 0:16]), writes=[PSR(b), "h0s"])
                    banks.free(b)

            if p == 0:
                S.add("dve", lambda e: e.memset(XR[:, :, 0:3], 0.0), writes=XRall)
            else:
                S.add("dve", lambda e, l=l: e.tensor_copy(out=XR[:, :, 0:3], in_=convhist[l][:]),
                      reads=[("convhist", l)], writes=XRall)
            for blk in range(2):
                wt, wres = W.get((p, "win", l, blk))
                wv = wt[:].rearrange("p (k n) -> p k n", n=512)
                for cc in range(4):
                    c = blk * 4 + cc

                    def ev(b, c0, n, c=c):
                        if c0 == 0:
                            S.add("act", lambda e: e.copy(out=XR[:, c, 3:3 + TP], in_=ps_t[b][:, 0:TP]), writes=[PSR(b), ("XR", c)])
                        else:
                            S.add("act", lambda e: e.copy(out=XRs[:, c, :, 3:7], in_=ps_t[b][:, 0:NS].rearrange("p (b t) -> p b t", t=4)),
                                  writes=[PSR(b), ("XR", c)])
                    proj(wv, wres, cc * 128, h, HR, NCH, subs, ev)
                W.release()
            S.add("dve", lambda e, l=l: e.tensor_copy(out=convhist[l][:], in_=XR[:, :, TP:TP + 3]), reads=XRall, writes=[("convhist", l)])
            if has_s:
                S.add("dve", lambda e: e.tensor_copy(out=cs_stage[:].rearrange("p c (b k) -> p c b k", k=3), in_=XRs[:, :, :, 4:7]),
                      reads=XRall, writes=["cs_stage"])

            for blk in (2, 3):
                wt, wres = W.get((p, "win", l, blk))
                wv = wt[:].rearrange("p (k n) -> p k n", n=512)
                for cc in range(4):
                    hh = (blk - 2) * 4 + cc

                    def ev(b, c0, n, hh=hh):
                        S.add("act", lambda e: e.activation(out=qT[:, hh, c0:c0 + n], in_=ps_t[b][:, 0:n], func=AF.Copy, scale=QSCALE),
                              writes=[PSR(b), ("qT", hh)])
                    proj(wv, wres, cc * 128, h, HR, NCH, subs, ev)
                W.release()
            wkv_t, wkv_res = W.get((p, "win", l, 4))
            wkv = wkv_t[:].rearrange("p (k n) -> p k n", n=512)
            if p == 0:
                S.add("dve", lambda e: e.memset(kT[:, :, 0:128], 0.0), writes=["kT"])
                S.add("dve", lambda e: e.memset(Vt[:, 0, :], 0.0), writes=["Vt"])
            else:
                S.add("dve", lambda e, l=l: e.tensor_copy(out=kT[:, :, 0:128], in_=khist[l][:]), reads=[("khist", l)], writes=["kT"])
                S.add("dve", lambda e, l=l: e.tensor_copy(out=Vt[:, 0, :], in_=vhist[l][:]), reads=[("vhist", l)], writes=["Vt"])
            for kv in range(2):
                def ev(b, c0, n, kv=kv):
                    S.add("act", lambda e: e.copy(out=kT[:, kv, 128:128 + TP], in_=ps_t[b][:, 0:TP]), writes=[PSR(b), "kT"])
                proj(wkv, wkv_res, kv * 128, h, HR, NCH, [(0, TP)], ev)
            for j in range(4):
                full = last_pass and j == 3
                b = banks.alloc()
                c0w, nw = (0, 512) if full else (256, 256)

                def mm(e, b=b, j=j, c0w=c0w, nw=nw):
                    r = None
                    for k in range(NCH):
                        r = e.matmul(ps_t[b][:, 0:nw], h[:, k, j * 128:(j + 1) * 128], wkv[:, k, c0w:c0w + nw], start=(k == 0), stop=(k == NCH - 1))
                    return r
                S.add("pe", mm, reads=[wkv_res] + HR, writes=[PSR(b)])
                voff = 256 if full else 0
                S.add("act", lambda e, b=b, j=j, voff=voff: e.copy(out=Vt[:, j + 1, :], in_=ps_t[b][:, voff:voff + 256]), writes=[PSR(b), "Vt"])
                if full:
                    S.add("dve", lambda e, b=b: e.tensor_copy(out=kvlast[:], in_=ps_t[b][:, 0:512]), writes=[PSR(b), "kvlast"])
                    S.add("sp", lambda e, l=l: e.dma_start(out=o_pk[l], in_=kvlast[:, 0:256]), reads=["kvlast"], dsem="okv")
                    S.add("sp", lambda e, l=l: e.dma_start(out=o_pv[l], in_=kvlast[:, 256:512]), reads=["kvlast"], dsem="okv")
                banks.free(b)
            S.add("dve", lambda e, l=l: e.tensor_copy(out=khist[l][:], in_=kT[:, :, TP:TP + 128]), reads=["kT"], writes=[("khist", l)])
            S.add("dve", lambda e, l=l: e.tensor_copy(out=vhist[l][:], in_=Vt[:, 4, :]), reads=["Vt"], writes=[("vhist", l)])
            if has_s:
                b = banks.alloc()

                def mm(e, b=b):
                    r = None
                    for k in range(NCH):
                        r = e.matmul(ps_t[b][0:NS, 0:512], h[:, k, TP:TP + NS], wkv[:, k, 0:512], start=(k == 0), stop=(k == NCH - 1))
                    return r
                S.add("pe", mm, reads=[wkv_res] + HR, writes=[PSR(b)])
                S.add("act", lambda e, b=b: e.copy(out=kvnew[:], in_=ps_t[b][0:NS, 0:512]), writes=[PSR(b), "kvnew"])
                banks.free(b)
                def kvout(e, l=l):
                    r = []
                    for bb in range(NB):
                        r.append(e.dma_start(out=o_sk[l][bb, 124:128, :], in_=kvnew[bb * 4:(bb + 1) * 4, 0:256]))
                        r.append(e.dma_start(out=o_sv[l][bb, 124:128, :], in_=kvnew[bb * 4:(bb + 1) * 4, 256:512]))
                    return r
                S.add("sp", kvout, reads=["kvnew"], writes=[("dram", "osv", l)], dsem="okv2", ndma=2 * NB)
                S.add("sp", lambda e, l=l: e.dma_start(out=o_sk[l][:, 0:124, :], in_=ck[l][:, 4:128, :]), dsem="d2d")
                S.add("sp", lambda e, l=l: e.dma_start(out=o_sv[l][:, 0:124, :], in_=cv[l][:, 4:128, :]), dsem="d2d")

            gv = gatesw[l][:].rearrange("p (g c n) -> p g c n", g=2, n=128)

            def K(name, st, cc):
                return ((name, st), cc)

            def convS(g):
                st = g % 2
                for cc in range(2):
                    c = g * 2 + cc
                    wcol = lambda k, c=c: sm[:, SM_CLW + k * 8 + c:SM_CLW + k * 8 + c + 1]
                    bcol = sm[:, SM_CLB + c:SM_CLB + c + 1]
                    xo = xcg[st]
                    S.add("dve", lambda e, c=c, cc=cc, xo=xo, wcol=wcol, bcol=bcol: e.tensor_scalar(xo[:, cc, 0:TP], XR[:, c, 3:3 + TP], wcol(3), bcol, ALU.mult, ALU.add),
                          reads=[("XR", c), SMR], writes=[K("xc", st, cc)])
                    for k in range(3):
                        S.add("dve", lambda e, c=c, cc=cc, k=k, xo=xo, wcol=wcol: e.scalar_tensor_tensor(out=xo[:, cc, 0:TP], in0=XR[:, c, k:k + TP], scalar=wcol(k),
                                                                                                       in1=xo[:, cc, 0:TP], op0=ALU.mult, op1=ALU.add),
                              reads=[("XR", c), SMR], writes=[K("xc", st, cc)])
                    if has_s:
                        xcs = xo[:, cc, TP:TP + NS].rearrange("p (b t) -> p b t", t=4)
                        S.add("dve", lambda e, c=c, wcol=wcol, bcol=bcol, xcs=xcs: e.tensor_scalar(xcs, XRs[:, c, :, 3:7], wcol(3), bcol, ALU.mult, ALU.add),
                              reads=[("XR", c), SMR], writes=[K("xc", st, cc)])
                        for k in range(3):
                            S.add("dve", lambda e, c=c, k=k, wcol=wcol, xcs=xcs: e.scalar_tensor_tensor(out=xcs, in0=XRs[:, c, :, k:k + 4], scalar=wcol(k),
                                                                                                      in1=xcs, op0=ALU.mult, op1=ALU.add),
                                  reads=[("XR", c), SMR], writes=[K("xc", st, cc)])
                    S.add("act", lambda e, cc=cc, xo=xo, st=st: e.copy(out=xcb[st][:, cc, 0:T], in_=xo[:, cc, 0:T]),
                          reads=[K("xc", st, cc)], writes=[K("xcb", st, cc)])

            def gatesS(g):
                st = g % 2
                for cc in range(2):
                    c = g * 2 + cc
                    for (c0, n) in subs:
                        for gi_, dst, dkey, bcolbase in ((0, rr[st], "rr", SM_BR), (1, ii[st], "ii", SM_BI)):
                            b = banks.alloc()
                            S.add("pe", lambda e, b=b, gi_=gi_, c=c, cc=cc, c0=c0, n=n, st=st: e.matmul(ps_t[b][:, 0:n], gv[:, gi_, c, :], xcb[st][:, cc, c0:c0 + n], start=True, stop=True),
                                  reads=[("gatesw", l), K("xcb", st, cc)], writes=[PSR(b)])
                            S.add("act", lambda e, b=b, dst=dst, c=c, cc=cc, c0=c0, n=n, bcolbase=bcolbase: e.activation(
                                out=dst[:, cc, c0:c0 + n], in_=ps_t[b][:, 0:n], func=AF.Sigmoid, bias=sm[:, bcolbase + c:bcolbase + c + 1], scale=1.0),
                                reads=[SMR], writes=[PSR(b), K(dkey, st, cc)])
                            banks.free(b)

            def expS(g):
                st = g % 2
                for cc in range(2):
                    c = g * 2 + cc
                    S.add("act", lambda e, c=c, cc=cc, st=st: e.activation(out=aa[st][:, cc, 0:T], in_=rr[st][:, cc, 0:T], func=AF.Exp, scale=lamc[l][:, c:c + 1]),
                          reads=[K("rr", st, cc), ("lamc", l)], writes=[K("aa", st, cc)])
                for cc in range(2):
                    S.add("dve", lambda e, cc=cc, st=st: e.scalar_tensor_tensor(out=rr[st][:, cc, 0:T], in0=aa[st][:, cc, 0:T], scalar=0.99999994,
                                                                                in1=aa[st][:, cc, 0:T], op0=ALU.min, op1=ALU.mult),
                          reads=[K("aa", st, cc)], writes=[K("rr", st, cc)])
                for cc in range(2):
                    S.add("act", lambda e, cc=cc, st=st: e.activation(out=rr[st][:, cc, 0:T], in_=rr[st][:, cc, 0:T], func=AF.Ln, scale=-1.0, bias=1.0),
                          writes=[K("rr", st, cc)])
                for cc in range(2):
                    S.add("act", lambda e, cc=cc, st=st: e.activation(out=rr[st][:, cc, 0:T], in_=rr[st][:, cc, 0:T], func=AF.Exp, scale=0.5),
                          writes=[K("rr", st, cc)])

            def dveS(g):
                st = g % 2
                xo = xcg[st]
                for cc in range(2):
                    c = g * 2 + cc
                    if p == 0:
                        S.add("dve", lambda e, cc=cc, st=st: e.memset(rr[st][:, cc, 0:1], 1.0), writes=[K("rr", st, cc)])
                    S.add("dve", lambda e, cc=cc, st=st, xo=xo: e.tensor_tensor(out=ii[st][:, cc, 0:T], in0=ii[st][:, cc, 0:T], in1=xo[:, cc, 0:T], op=ALU.mult),
                          reads=[K("xc", st, cc)], writes=[K("ii", st, cc)])
                    S.add("dve", lambda e, cc=cc, st=st: e.tensor_tensor(out=ii[st][:, cc, 0:T], in0=ii[st][:, cc, 0:T], in1=rr[st][:, cc, 0:T], op=ALU.mult),
                          reads=[K("rr", st, cc)], writes=[K("ii", st, cc)])
                    init = 0.0 if p == 0 else hstate[l][:, c:c + 1]
                    S.add("dve", lambda e, cc=cc, st=st, xo=xo, init=init: e.tensor_tensor_scan(out=xo[:, cc, 0:TP], data0=aa[st][:, cc, 0:TP], data1=ii[st][:, cc, 0:TP],
                                                                                                 initial=init, op0=ALU.mult, op1=ALU.add),
                          reads=[K("aa", st, cc), K("ii", st, cc), ("hstate", l)], writes=[K("xc", st, cc)])
                    if has_s:
                        aas = aa[st][:, cc, TP:TP + NS].rearrange("p (b t) -> p b t", t=4)
                        iis = ii[st][:, cc, TP:TP + NS].rearrange("p (b t) -> p b t", t=4)
                        S.add("dve", lambda e, c=c, aas=aas: e.tensor_tensor(out=tmp16[:], in0=aas[:, :, 0], in1=h0s[:, c, :], op=ALU.mult),
                              reads=[K("aa", st, cc), "h0s"], writes=["tmp16"])
                        S.add("dve", lambda e, iis=iis: e.tensor_tensor(out=iis[:, :, 0], in0=iis[:, :, 0], in1=tmp16[:], op=ALU.add),
                              reads=["tmp16"], writes=[K("ii", st, cc)])
                        S.add("dve", lambda e, aas=aas: e.memset(aas[:, :, 0], 0.0), writes=[K("aa", st, cc)])
                        S.add("dve", lambda e, cc=cc, st=st, xo=xo: e.tensor_tensor_scan(out=xo[:, cc, TP:TP + NS], data0=aa[st][:, cc, TP:TP + NS], data1=ii[st][:, cc, TP:TP + NS],
                                                                                         initial=0.0, op0=ALU.mult, op1=ALU.add),
                              reads=[K("aa", st, cc), K("ii", st, cc)], writes=[K("xc", st, cc)])
                    S.add("act", lambda e, c=c, cc=cc, xo=xo: e.copy(out=lro[:, c, 0:T], in_=xo[:, cc, 0:T]), reads=[K("xc", st, cc)], writes=[("lro", c)])
                XCK = [K("xc", st, cc) for cc in range(2)]
                S.add("dve", lambda e, g=g, xo=xo: e.tensor_copy(out=hstate[l][:, g * 2:(g + 1) * 2], in_=xo[:, :, TP - 1]), reads=XCK, writes=[("hstate", l)])
                if has_s:
                    S.add("dve", lambda e, g=g, xo=xo: e.tensor_copy(out=hs_last[:, g * 2:(g + 1) * 2, :],
                                                                     in_=xo[:, :, TP:TP + NS].rearrange("p c (b t) -> p c b t", t=4)[:, :, :, 3]),
                          reads=XCK, writes=["hs_last"])

            convS(0)
            gatesS(0)
            convS(1)
            expS(0)
            gatesS(1)
            dveS(0)
            convS(2)
            expS(1)
            gatesS(2)
            dveS(1)
            convS(3)
            expS(2)
            gatesS(3)
            dveS(2)
            expS(3)
            dveS(3)
            if has_s:
                transpose_out(lambda k: hs_last[:, k, :], [NB] * NCH,
                              [o_slh[l][:, k * 128:(k + 1) * 128] for k in range(NCH)], ["hs_last"], "slh")
                transpose_out(lambda k: cs_stage[:, k, :], [NB * 3] * NCH,
                              [o_slc[l][:, k * 128:(k + 1) * 128] for k in range(NCH)], ["cs_stage"], "slc")
            if last_pass:
                transpose_out(lambda k, l=l: hstate[l][:, :], [NCH], [o_plh[l].rearrange("(c p) -> c p", p=128)], [("hstate", l)], "plh")
                S.add("dve", lambda e, l=l: e.tensor_copy(out=pst[:, 0:24].rearrange("p (k c) -> p k c", c=NCH),
                                                          in_=convhist[l][:].rearrange("p c k -> p k c")),
                      reads=[("convhist", l)], writes=["pst"])
                transpose_out(lambda k: pst[:, 0:24], [24], [o_plc[l].rearrange("k (c p) -> (k c) p", p=128)], ["pst"], "plc")

            if has_s:
                S.add("sp", lambda e, l=l: e.dma_start(out=cstage[:], in_=ck[l].rearrange("b k d -> k b d")), writes=["cstage"], dsem="stin3")
                S.add("pool", lambda e, l=l: e.dma_start(out=Vs[:], in_=cv[l].rearrange("b k d -> k b d")), writes=["Vs"], dsem="vs_in")
                for kv in range(2):
                    def ev(b, c0, n, kv=kv):
                        S.add("act", lambda e: e.copy(out=kTs[:, kv, :, 128:132], in_=ps_t[b][:, 0:NS].rearrange("p (b t) -> p b t", t=4)),
                              writes=[PSR(b), "kTs"])
                    proj(wkv, wkv_res, kv * 128, h, HR, NCH, [(TP, NS)], ev)
                S.add("pool", lambda e, l=l: e.dma_start(out=Vn4[:], in_=o_sv[l][:, 124:128, :].rearrange("b t d -> t b d")),
                      reads=[("dram", "osv", l)], writes=["Vn4"], dsem="vn4")
                for kv in range(2):
                    S.add("dve", lambda e, kv=kv: e.tensor_copy(
                        out=qTs[:, kv, :, :].rearrange("p b (g t) -> p b g t", t=4),
                        in_=qT[:, kv * 4:kv * 4 + 4, TP:TP + NS].rearrange("p g (b t) -> p b g t", t=4)),
                        reads=[("qT", kv * 4 + g) for g in range(4)], writes=["qTs"])
            W.release()

            units = []
            for hh in range(8):
                kv = hh // 4
                grp = {}
                for j in range(4):
                    first_blk = (p == 0 and j == 0)
                    if first_blk:
                        kap = kT[:, kv, 128:256]
                        bias_ap = biasp[:, hh, 128:256]
                        vbl = [(Vt[:, 1, kv * 128:(kv + 1) * 128], 0, 128)]
                        NK = 128
                    else:
                        kap = kT[:, kv, j * 128:j * 128 + 256]
                        bias_ap = biasp[:, hh, :]
                        vbl = [(Vt[:, j, kv * 128:(kv + 1) * 128], 0, 128), (Vt[:, j + 1, kv * 128:(kv + 1) * 128], 128, 128)]
                        NK = 256

                    def evp(ob, hh=hh):
                        S.add("act", lambda e: e.copy(out=attn[:, hh, 0:TP], in_=ps_t[ob][:, 0:TP]), writes=[PSR(ob), ("attn", hh)])
                    units.append(dict(qap=qT[:, hh, j * 128:(j + 1) * 128], kap=kap, NQ=128, NK=NK, bias_ap=bias_ap, bias_key="biasp",
                                      sink_ap=sinks[l][:, hh:hh + 1], vblocks=vbl, rd=[("qT", hh), "kT", "Vt", ("sinks", l)],
                                      grp=grp, out_c0=j * 128, last=(j == 3), evac=evp))
            attn_waves(units)
            units = []
            if has_s:
                for bb in range(NB):
                    b = banks.alloc()

                    def tr(e, b=b, bb=bb):
                        r = None
                        for kv in range(2):
                            r = e.transpose(ps_t[b][:, kv * 128:(kv + 1) * 128], cstage[:, bb, kv * 128:(kv + 1) * 128], identf[:])
                        return r
                    S.add("pe", tr, reads=["cstage", "identf"], writes=[PSR(b)])
                    S.add("act", lambda e, b=b, bb=bb: e.copy(out=kTs[:, :, bb, 0:128], in_=ps_t[b][:, 0:256].rearrange("p (v k) -> p v k", k=128)),
                          writes=[PSR(b), "kTs"])
                    banks.free(b)
                for kv in range(2):
                    grp = {}
                    for bb in range(NB):
                        vbl = [(Vs[:, bb, kv * 128:(kv + 1) * 128], 0, 128), (Vn4[0:4, bb, kv * 128:(kv + 1) * 128], 128, 4)]

                        def evs(ob, kv=kv):
                            S.add("act", lambda e: e.copy(
                                out=attn[:, kv * 4:kv * 4 + 4, TP:TP + NS].rearrange("p g (b t) -> p b g t", t=4),
                                in_=ps_t[ob][:, 0:256].rearrange("p (b g t) -> p b g t", g=4, t=4)),
                                writes=[PSR(ob)] + [("attn", kv * 4 + g) for g in range(4)])
                        units.append(dict(qap=qTs[:, kv, bb, :], kap=kTs[:, kv, bb, :], NQ=16, NK=132, bias_ap=biass[:, kv, :], bias_key="biass",
                                          sink_ap=sinks[l][0:16, 8 + kv:9 + kv], vblocks=vbl, rd=["qTs", "kTs", "Vs", "Vn4", ("sinks", l)],
                                          grp=grp, out_c0=bb * 16, last=(bb == NB - 1), evac=evs))
            if units:
                attn_waves(units)

            LR = keys("lro", NCH)
            AR = keys("attn", 8)

            def mkmm(bk, wv_, src, col, c0, n):
                def mm(e):
                    r = None
                    for k in range(NCH):
                        r = e.matmul(ps_t[bk][:, 0:n], wv_[:, k, col:col + 128], src[:, k, c0:c0 + n], start=(k == 0), stop=(k == NCH - 1))
                    return r
                return mm
            for br, (wname, gbase, src, SR) in enumerate((("wlo", 5, lro, LR), ("wao", 7, attn, AR))):
                for hf in range(2):
                    wo_t, wo_res = W.get((p, wname, l, hf))
                    wg_t, wg_res = W.get((p, "win", l, gbase + hf))
                    v_o = wo_t[:].rearrange("p (k n) -> p k n", n=512)
                    v_g = wg_t[:].rearrange("p (k n) -> p k n", n=512)
                    for cc in range(4):
                        oc = hf * 4 + cc
                        for (c0, n) in subs:
                            gi = rot["sg"] % 2
                            rot["sg"] += 1
                            bO, bG = banks.alloc(), banks.alloc()
                            S.add("pe", mkmm(bG, v_g, h, cc * 128, c0, n), reads=[wg_res] + HR, writes=[PSR(bG)])
                            for k in range(NCH):
                                S.add("pe", lambda e, bO=bO, v_o=v_o, src=src, cc=cc, c0=c0, n=n, k=k: e.matmul(
                                    ps_t[bO][:, 0:n], v_o[:, k, cc * 128:cc * 128 + 128], src[:, k, c0:c0 + n], start=(k == 0), stop=(k == NCH - 1)),
                                    reads=[wo_res, SR[k]], writes=[PSR(bO)])
                            S.add("act", lambda e, gi=gi, bG=bG, n=n: e.activation(out=sgA[gi][:, 0:n], in_=ps_t[bG][:, 0:n], func=AF.Sigmoid),
                                  writes=[PSR(bG), ("sgA", gi)])
                            if br == 0:
                                S.add("dve", lambda e, gi=gi, bO=bO, oc=oc, c0=c0, n=n: e.tensor_tensor(out=merged[:, oc, c0:c0 + n], in0=sgA[gi][:, 0:n], in1=ps_t[bO][:, 0:n], op=ALU.mult),
                                      reads=[("sgA", gi)], writes=[PSR(bO), ("merged", oc)])
                            else:
                                S.add("dve", lambda e, gi=gi, bO=bO, n=n: e.tensor_tensor(out=t1b[gi][:, 0:n], in0=sgA[gi][:, 0:n], in1=ps_t[bO][:, 0:n], op=ALU.mult),
                                      reads=[("sgA", gi)], writes=[PSR(bO), ("t1b", gi)])
                                S.add("dve", lambda e, gi=gi, oc=oc, c0=c0, n=n: e.tensor_tensor(out=merged[:, oc, c0:c0 + n], in0=t1b[gi][:, 0:n], in1=merged[:, oc, c0:c0 + n], op=ALU.add),
                                      reads=[("t1b", gi)], writes=[("merged", oc)])
                            banks.free(bO)
                            banks.free(bG)
                    W.release()
                    W.release()
            MR = keys("merged", NCH)
            for hf in range(2):
                wt, wres = W.get((p, "wout", l, hf))
                wv = wt[:].rearrange("p (k n) -> p k n", n=512)
                for cc in range(4):
                    oc = hf * 4 + cc

                    def ev(b, c0, n, oc=oc):
                        S.add("act", lambda e: e.activation(out=sq[:, oc, c0:c0 + n], in_=ps_t[b][:, 0:n], func=AF.Square), writes=[PSR(b), ("sq", oc)])
                        S.add("act", lambda e: e.activation(out=mbuf[:, oc, c0:c0 + n], in_=ps_t[b][:, 0:n], func=AF.Identity,
                                                            scale=sm[:, SM_NMPOST + oc:SM_NMPOST + oc + 1]),
                              reads=[SMR], writes=[PSR(b), ("mbuf", oc)])
                    proj(wv, wres, cc * 128, merged, MR, NCH, subs, ev)
                W.release()
            postnorm_residual(l, T, subs, True)

            rmsnorm_to_h(l, SM_NFP, T, subs, squares_done=True)
            if has_s:
                for q4 in range(4):
                    S.add("sp", lambda e, l=l, q4=q4: e.dma_start(out=stf[:], in_=st_f[l][:, q4 * 2048:(q4 + 1) * 2048]),
                          writes=["stf"], dsem="stin2")
                    for g4 in range(4):
                        b = banks.alloc()

                        def tr(e, b=b, g4=g4):
                            r = None
                            for i4 in range(4):
                                jj = g4 * 4 + i4
                                r = e.transpose(ps_t[b][:, i4 * 32:(i4 + 1) * 32], stf[0:32, jj * 128:(jj + 1) * 128], identf[0:32, 0:32])
                            return r
                        S.add("pe", tr, reads=["stf", "identf"], writes=[PSR(b)])
                        j0 = q4 * 16 + g4 * 4
                        S.add("act", lambda e, b=b, j0=j0: e.copy(out=FH[:, j0:j0 + 4, :], in_=ps_t[b][:, 0:128].rearrange("p (j n) -> p j n", n=32)),
                              writes=[PSR(b), "FH"])
                        banks.free(b)
            pend = []
            pend_t = []

            def hist_in(Un, ui_, jj_):
                Ub_ = (Uv if Un == "Uv" else Ug)[ui_]
                if p == 0:
                    S.add("pool", lambda e: e.memset(Ub_[:, 62:64], 0.0), writes=[(Un, ui_, "h")])
                else:
                    S.add("pool", lambda e: e.tensor_copy(out=Ub_[:, 62:64], in_=ffnhist[l][:, jj_, :]),
                          reads=[("dram", "ffnhist", l, jj_)], writes=[(Un, ui_, "h")])
            for pb in range(8):
                wv_t, wv_res = W.get((p, "wup", l, pb))
                wg_t, wg_res = W.get((p, "wup", l, pb + 8))
                vv = wv_t[:].rearrange("p (k n) -> p k n", n=512)
                vg = wg_t[:].rearrange("p (k n) -> p k n", n=512)
                wsc = lambda k, jj: sm[:, SM_FCW + k * 64 + jj:SM_FCW + k * 64 + jj + 1]
                bsc = lambda jj: sm[:, SM_FCB + jj:SM_FCB + jj + 1]
                for cc in range(4):
                    j = pb * 4 + cc
                    ci = rot["u"] % 3
                    ui = ci
                    rot["u"] += 1
                    for (wview, wres_, Ub, Un, jj, cvb, cres) in ((vv, wv_res, Uv[ui], "Uv", j, cvv[ci], ("cvv", ci)),
                                                                 (vg, wg_res, Ug[ui], "Ug", 32 + j, cvg[ci], ("cvg", ci))):
                        UH, UB = (Un, ui, "h"), (Un, ui, "b")
                        if j == 0:
                            hist_in(Un, ui, jj)
                        if j + 1 < 32:
                            hist_in(Un, (ui + 1) % 3, jj + 1)

                        def ev(b, c0, n, Ub=Ub, UB=UB, cvb=cvb, cres=cres, jj=jj):
                            S.add("act", lambda e: e.copy(out=Ub[:, 64:64 + TP], in_=ps_t[b][:, 0:TP]), writes=[PSR(b), UB])
                            S.add("act", lambda e: e.activation(out=cvb[:, 0:TP], in_=ps_t[b][:, 0:TP], func=AF.Identity, scale=wsc(2, jj), bias=bsc(jj)),
                                  reads=[SMR], writes=[PSR(b), cres])
                        proj(wview, wres_, cc * 128, h, HR, NCH, [(0, TP)], ev)
                        S.add("pool", lambda e, Ub=Ub, jj=jj: e.tensor_copy(out=ffnhist[l][:, jj, :], in_=Ub[:, 62 + TP:64 + TP]),
                              reads=[UB], writes=[("dram", "ffnhist", l, jj)])
                        for k in range(2):
                            S.add("dve", lambda e, Ub=Ub, cvb=cvb, jj=jj, k=k: e.scalar_tensor_tensor(out=cvb[:, 0:TP], in0=Ub[:, 62 + k:62 + k + TP], scalar=wsc(k, jj),
                                                                                                      in1=cvb[:, 0:TP], op0=ALU.mult, op1=ALU.add),
                                  reads=[UH, UB, SMR], writes=[cres])

                    def tail(ci=ci, j=j):
                        S.add("act", lambda e: e.activation(out=ggb[ci][:, 0:TP], in_=cvg[ci][:, 0:TP], func=AF.Gelu_apprx_tanh),
                              reads=[("cvg", ci)], writes=[("gg", ci)])
                        S.add("dve", lambda e: e.tensor_tensor(out=A[:, j, 0:TP], in0=cvv[ci][:, 0:TP], in1=ggb[ci][:, 0:TP], op=ALU.mult),
                              reads=[("cvv", ci), ("gg", ci)], writes=[("A", j)])
                    if pend:
                        pend.pop()()
                    pend.append(tail)
                if has_s and pend_t:
                    pend_t.pop(0)()
                if has_s:
                    for half, (wview, wres_) in enumerate(((vv, wv_res), (vg, wg_res))):
                        jj0 = pb * 4 + 32 * half
                        Uh = Us[half][:].rearrange("p c (b k) -> p c b k", k=6)
                        b = banks.alloc()

                        def mm(e, b=b, wview=wview):
                            r = None
                            for c4 in range(4):
                                for k in range(NCH):
                                    r = e.matmul(ps_t[b][:, c4 * NS:(c4 + 1) * NS], wview[:, k, c4 * 128:(c4 + 1) * 128], h[:, k, TP:TP + NS],
                                                 start=(k == 0), stop=(k == NCH - 1))
                            return r
                        S.add("pe", mm, reads=[wres_] + HR, writes=[PSR(b)])
                        S.add("pool", lambda e, Uh=Uh, jj0=jj0: e.tensor_copy(out=Uh[:, :, :, 0:2], in_=FH[:, jj0:jj0 + 4, :].rearrange("p c (b k) -> p c b k", k=2)),
                              reads=["FH"], writes=[("Us", half)])
                        S.add("act", lambda e, Uh=Uh, b=b: e.copy(out=Uh[:, :, :, 2:6], in_=ps_t[b][:, 0:4 * NS].rearrange("p (c b t) -> p c b t", c=4, t=4)),
                              writes=[PSR(b), ("Us", half)])
                        for c4 in range(4):
                            S.add("act", lambda e, b=b, c4=c4, half=half, jj0=jj0: e.activation(out=cvs[half][:, c4, :], in_=ps_t[b][:, c4 * NS:(c4 + 1) * NS],
                                                                                               func=AF.Identity, scale=wsc(2, jj0 + c4), bias=bsc(jj0 + c4)),
                                  reads=[SMR], writes=[PSR(b), ("cvs", half)])
                        banks.free(b)
                        S.add("pool", lambda e, Uh=Uh, half=half, pb=pb: e.tensor_copy(
                            out=SFs[pb % 2][:, half * 4:(half + 1) * 4, :].rearrange("p c (b k) -> p c b k", k=2), in_=Uh[:, :, :, 4:6]),
                            reads=[("Us", half)], writes=[("SFs", pb % 2)])
                        for c4 in range(4):
                            cv4 = cvs[half][:, c4, :].rearrange("p (b t) -> p b t", t=4)
                            for k in range(2):
                                S.add("dve", lambda e, Uh=Uh, cv4=cv4, c4=c4, k=k, jj0=jj0: e.scalar_tensor_tensor(out=cv4, in0=Uh[:, c4, :, k:k + 4], scalar=wsc(k, jj0 + c4),
                                                                                                                 in1=cv4, op0=ALU.mult, op1=ALU.add),
                                      reads=[("Us", half), SMR], writes=[("cvs", half)])
                    S.add("act", lambda e: e.activation(out=cvs[1][:], in_=cvs[1][:], func=AF.Gelu_apprx_tanh), writes=[("cvs", 1)])
                    S.add("dve", lambda e, pb=pb: e.tensor_tensor(out=A[:, pb * 4:(pb + 1) * 4, TP:TP + NS], in0=cvs[0][:], in1=cvs[1][:], op=ALU.mult),
                          reads=[("cvs", 0), ("cvs", 1)], writes=[("A", pb * 4 + i) for i in range(4)])
                W.release()
                W.release()
                if has_s:
                    def tout(pb=pb):
                        jl = [pb * 4 + i for i in range(4)] + [32 + pb * 4 + i for i in range(4)]
                        transpose_out(lambda k, pb=pb: SFs[pb % 2][:, k, :], [NB * 2] * 8,
                                      [o_sf[l][:, jj * 128:(jj + 1) * 128] for jj in jl], [("SFs", pb % 2)], "sf")
                    pend_t.append(tout)
            while pend_t:
                pend_t.pop(0)()
            if last_pass:
                S.add("dve", lambda e, l=l: e.tensor_copy(out=pst[:, 0:128].rearrange("p (k c) -> p k c", c=64),
                                                          in_=ffnhist[l][:].rearrange("p c k -> p k c")),
                      reads=[("dram", "ffnhist", l, q_) for q_ in range(64)], writes=["pst"])
                transpose_out(lambda k: pst[:, 0:128], [128], [o_pf[l].rearrange("k (c p) -> (k c) p", p=128)], ["pst"], "pf")
            if pend:
                pend.pop()()
            if l == DEPTH - 1 and p + 1 < NPASS:
                S.add("sp", lambda e, p=p: e.dma_start(out=tokx[:], in_=xp[(p + 1) * TP:(p + 2) * TP, :].rearrange("(j r) d -> r j d", r=128)),
                      writes=["tokx"], dsem="xin")
            ARs = keys("A", 32)
            for oc in range(8):
                wt, wres = W.get((p, "wdn", l, oc))
                wv = wt[:].rearrange("p (k n) -> p k n", n=128)

                def ev(b, c0, n, oc=oc):
                    S.add("act", lambda e: e.activation(out=sq[:, oc, c0:c0 + n], in_=ps_t[b][:, 0:n], func=AF.Square), writes=[PSR(b), ("sq", oc)])
                    S.add("act", lambda e: e.activation(out=mbuf[:, oc, c0:c0 + n], in_=ps_t[b][:, 0:n], func=AF.Identity,
                                                        scale=sm[:, SM_NFPOST + oc:SM_NFPOST + oc + 1]),
                          reads=[SMR], writes=[PSR(b), ("mbuf", oc)])
                proj(wv, wres, 0, A, ARs, 32, subs, ev)
                W.release()
            postnorm_residual(l, T, subs, l + 1 < DEPTH)

    def finish_pass(p, has_s):
        for j in range(4):
            for half in range(2):
                b = banks.alloc()

                def tr(e, b=b, j=j, half=half):
                    r = None
                    for c4 in range(4):
                        c = half * 4 + c4
                        r = e.transpose(ps_t[b][:, c4 * 128:(c4 + 1) * 128], x[:, c, j * 128:(j + 1) * 128], identf[:])
                    return r
                S.add("pe", tr, reads=XK + ["identf"], writes=[PSR(b)])
                S.add("dve", lambda e, b=b, j=j, half=half: e.tensor_copy(out=tok[:, j, half * 512:(half + 1) * 512], in_=ps_t[b][:, 0:512]),
                      writes=[PSR(b), "tok"])
                banks.free(b)
        S.add("sp", lambda e, p=p: e.dma_start(out=yp[p * TP:(p + 1) * TP, :].rearrange("(j r) d -> r j d", r=128), in_=tok[:]),
              reads=["tok"], dsem="yout")
        if has_s:
            for half in range(2):
                b = banks.alloc()

                def tr(e, b=b, half=half):
                    r = None
                    for c4 in range(4):
                        c = half * 4 + c4
                        r = e.transpose(ps_t[b][0:NS, c4 * 128:(c4 + 1) * 128], x[:, c, TP:TP + NS], identf[:])
                    return r
                S.add("pe", tr, reads=XK + ["identf"], writes=[PSR(b)])
                S.add("act", lambda e, b=b, half=half: e.copy(out=toks[0:NS, half * 512:(half + 1) * 512], in_=ps_t[b][0:NS, 0:512]),
                      writes=[PSR(b), "toks"])
                banks.free(b)
            S.add("sp", lambda e: e.dma_start(out=ys, in_=toks[0:NS, :]), reads=["toks"], dsem="yout2")

    for p in range(NPASS):
        do_pass(p)
    assert W.pos == len(seq)
    S.emit()
    return nc


def _t5_bucket(d):
    n = max(d, 0)
    if n < 16:
        return n
    import math
    nf = np.float32(max(n, 1))
    v = np.float32(np.log(nf / np.float32(16)) / np.float32(math.log(128 / 16))) * np.float32(16)
    return min(16 + int(v), 31)


_NC_CACHE = {}


def kernel(x_prompt, x_sample, state_lru_h, state_lru_conv, cache_win_k, cache_win_v, state_ffn_conv,
           norm_mix_pre, norm_mix_post, norm_ffn_pre, norm_ffn_post, w_in, conv_lru_w, conv_lru_b,
           lru_wr, lru_br, lru_wi, lru_bi, lru_lambda, w_lru_o, w_attn_o, w_out, attn_sink, rel_bias,
           w_up, ffn_conv_w, ffn_conv_b, w_down):
    f32 = lambda a: np.ascontiguousarray(np.asarray(a, dtype=np.float32))
    x_prompt, x_sample = f32(x_prompt), f32(x_sample)
    n = 8

    def blk(w, nb, ncols):
        L = w.shape[0]
        kc = w.shape[1] // 128
        return f32(np.asarray(w).reshape(L, kc, 128, nb, ncols).transpose(0, 3, 2, 1, 4).reshape(L * nb, 128, kc * ncols))

    win_r = blk(f32(w_in), 9, 512)
    wlo_r = blk(f32(w_lru_o), 2, 512)
    wao_r = blk(f32(w_attn_o), 2, 512)
    wout_r = blk(f32(w_out), 2, 512)
    wup_r = blk(f32(w_up), 16, 512)
    wdn_r = blk(f32(w_down), 8, 128)
    gates_r = f32(np.stack([f32(lru_wr), f32(lru_wi)], axis=1).transpose(0, 3, 1, 2, 4).reshape(DEPTH, 128, 2048))

    def pc(v):
        v = f32(v)
        return v.reshape(v.shape[0], -1, 128).transpose(0, 2, 1)

    def pck(v):
        v = f32(v)
        L, K = v.shape[0], v.shape[1]
        return v.reshape(L, K, -1, 128).transpose(0, 3, 1, 2).reshape(L, 128, -1)

    smalls = f32(np.concatenate([pc(norm_mix_pre), pc(norm_mix_post), pc(norm_ffn_pre), pc(norm_ffn_post),
                                 pck(conv_lru_w), pc(conv_lru_b), pc(lru_br), pc(lru_bi), pc(lru_lambda),
                                 pck(ffn_conv_w), pc(ffn_conv_b)], axis=2))
    assert smalls.shape == (DEPTH, 128, SM_N), smalls.shape
    sk = f32(attn_sink)
    sinks = np.zeros((DEPTH, 128, 10), np.float32)
    sinks[:, :, 0:8] = sk[:, None, :]
    for kv in range(2):
        for g in range(4):
            sinks[:, g * 4:(g + 1) * 4, 8 + kv] = sk[:, kv * 4 + g][:, None]
    rb = f32(rel_bias)
    bidx = np.array([_t5_bucket(d) for d in range(128)])
    qq = np.arange(128)[:, None]
    jj = np.arange(256)[None, :]
    dd = qq + 128 - jj
    valid = (dd >= 0) & (dd < 128)
    gat = rb[bidx[np.clip(dd, 0, 127)]]
    biasp = np.where(valid[:, :, None], gat, np.float32(NEG)).transpose(0, 2, 1)
    biasp = f32(biasp).reshape(128, 8 * 256)
    tt = np.arange(4)[:, None]
    js = np.arange(132)[None, :]
    ds = tt + 128 - js
    vs = (ds >= 0) & (ds < 128)
    gs = rb[bidx[np.clip(ds, 0, 127)]]
    bs = np.where(vs[:, :, None], gs, np.float32(NEG))
    biass = np.zeros((16, 2, 132), np.float32)
    for kv in range(2):
        for g in range(4):
            biass[g * 4:(g + 1) * 4, kv, :] = bs[:, :, kv * 4 + g]
    biass = f32(biass).reshape(16, 2 * 132)
    ident = np.eye(128, dtype=np.float32)

    st_h, st_c, ckk, cvv, st_f = f32(state_lru_h), f32(state_lru_conv), f32(cache_win_k), f32(cache_win_v), f32(state_ffn_conv)
    in_maps = []
    for i in range(n):
        sl = slice(i * NB, (i + 1) * NB)
        in_maps.append({
            "xp": x_prompt[i], "xs": f32(x_sample[sl].reshape(NS, D)),
            "st_h": f32(st_h[:, sl]), "st_c": f32(st_c[:, sl].reshape(DEPTH, NB * 3, D)),
            "ck": f32(ckk[:, sl].reshape(DEPTH, NB, 128, 256)), "cv": f32(cvv[:, sl].reshape(DEPTH, NB, 128, 256)),
            "st_f": f32(st_f[:, sl].reshape(DEPTH, NB * 2, 8192)),
            "win_r": win_r, "gates_r": gates_r, "wlo_r": wlo_r, "wao_r": wao_r, "wout_r": wout_r, "wup_r": wup_r, "wdn_r": wdn_r,
            "smalls": smalls, "sinks": sinks, "biasp": biasp, "biass": biass, "ident": ident,
        })
    if "nc" not in _NC_CACHE:
        _NC_CACHE["nc"] = build()
    nc = _NC_CACHE["nc"]
    res = run_bass_kernel_spmd(nc, in_maps, core_ids=list(range(n)))
    R = res.results
    y_prompt = np.stack([R[i]["yp"] for i in range(n)], axis=0)
    y_sample = np.concatenate([R[i]["ys"].reshape(NB, 4, D) for i in range(n)], axis=0)
    p_lru_h = np.stack([R[i]["o_plh"] for i in range(n)], axis=1)
    p_lru_conv = np.stack([R[i]["o_plc"] for i in range(n)], axis=1)
    p_win_k = np.stack([R[i]["o_pk"].reshape(DEPTH, 128, 2, 128) for i in range(n)], axis=1)
    p_win_v = np.stack([R[i]["o_pv"].reshape(DEPTH, 128, 2, 128) for i in range(n)], axis=1)
    p_ffn = np.stack([R[i]["o_pf"] for i in range(n)], axis=1)
    s_lru_h = np.concatenate([R[i]["o_slh"] for i in range(n)], axis=1)
    s_lru_conv = np.concatenate([R[i]["o_slc"].reshape(DEPTH, NB, 3, D) for i in range(n)], axis=1)
    s_win_k = np.concatenate([R[i]["o_sk"].reshape(DEPTH, NB, 128, 2, 128) for i in range(n)], axis=1)
    s_win_v = np.concatenate([R[i]["o_sv"].reshape(DEPTH, NB, 128, 2, 128) for i in range(n)], axis=1)
    s_ffn = np.concatenate([R[i]["o_sf"].reshape(DEPTH, NB, 2, 8192) for i in range(n)], axis=1)
    outs = (y_prompt, y_sample, p_lru_h, p_lru_conv, p_win_k, p_win_v, p_ffn, s_lru_h, s_lru_conv, s_win_k, s_win_v, s_ffn)
    return tuple(np.ascontiguousarray(o, dtype=np.float32) for o in outs)
```
